# Optimizing a Trainium2 kernel written in Bass

```python
import math
import jax
import jax.numpy as jnp
from jax import lax
import numpy as np

D_MODEL = 1024
BATCH = 8
SEQ = 4096
DEPTH = 4

GRID_W = 64
CTX_LEN = 256
N_MIXERS = 3
N_MOD = 9
NORM_EPS = 1e-6
D_FF = 2816
FFT_GROUPS = 8
FFT_GROUP_W = D_MODEL // FFT_GROUPS
RET_HEADS = D_MODEL // 256
RET_DK = D_MODEL // RET_HEADS
RET_DV = 2 * RET_DK
RET_HK = RET_HEADS * RET_DK
RET_HV = RET_HEADS * RET_DV
RET_CHUNK = 128
ROPE_BASE = 10000.0
ROPE_FREQS = RET_DK // 4
S5_GROUP = 16
S5_GROUPS = D_MODEL // S5_GROUP
S5_STATE = 64
S5_DT_MIN = 0.001
S5_DT_MAX = 0.1
S5_C_STD = 0.5
N_FNET = (DEPTH + 2) // N_MIXERS
N_RET = (DEPTH + 1) // N_MIXERS
N_S5 = DEPTH // N_MIXERS

kernel_name = 'hybrid_fourier_retention_s5_macaron_dit'


def _rmsnorm(h, g):
    h32 = h.astype(jnp.float32)
    y = h32 * lax.rsqrt(jnp.mean(h32 * h32, axis=-1, keepdims=True) + NORM_EPS)
    return (y * g.astype(jnp.float32)).astype(h.dtype)


def _modnorm(h, g, shift, scale):
    return _rmsnorm(h, g) * (1.0 + scale) + shift


def _ffn_half(h, g, shift, scale, gate, w_in, w_out):
    a, b = jnp.split(_modnorm(h, g, shift, scale) @ w_in, 2, axis=-1)
    return h + 0.5 * gate * ((jax.nn.silu(a) * b) @ w_out)


def _fourier_mixer(u, w_out):
    b, n, d = u.shape
    ug = u.astype(jnp.float32).reshape(b, n, FFT_GROUPS, FFT_GROUP_W)
    f = jnp.real(jnp.fft.fft2(ug, axes=(1, 3), norm='ortho'))
    return f.reshape(b, n, d).astype(u.dtype) @ w_out


def _heads(t, dh):
    b, n, _ = t.shape
    return jnp.swapaxes(t.reshape(b, n, -1, dh).astype(jnp.float32), 1, 2)


def _rotate(t, cos, sin):
    t1, t2 = jnp.split(t, 2, axis=-1)
    return jnp.concatenate([t1 * cos - t2 * sin, t2 * cos + t1 * sin], axis=-1)


def _rope_2d(t, rope):
    cos_r, sin_r, cos_c, sin_c = rope
    t_row, t_col = jnp.split(t, 2, axis=-1)
    return jnp.concatenate([_rotate(t_row, cos_r, sin_r), _rotate(t_col, cos_c, sin_c)], axis=-1)


def _ret_project(u, w_in, rope):
    q, k, v, g = jnp.split(u @ w_in, [RET_HK, 2 * RET_HK, 2 * RET_HK + RET_HV], axis=-1)
    q, k = _heads(q, RET_DK), _heads(k, RET_DK)
    if rope is not None:
        q, k = _rope_2d(q, rope), _rope_2d(k, rope)
    return q, k * RET_DK ** -0.5, _heads(v, RET_DV), g


def _ret_project_kv(u, w_in):
    k, v = jnp.split(u @ w_in[:, RET_HK:2 * RET_HK + RET_HV], [RET_HK], axis=-1)
    return _heads(k, RET_DK) * RET_DK ** -0.5, _heads(v, RET_DV)


def _ret_scan(q, k, v, log_gamma, s0, strict, reverse):
    if reverse:
        q, k, v = jnp.flip(q, 2), jnp.flip(k, 2), jnp.flip(v, 2)
    b, h, n, _ = q.shape
    dv = v.shape[-1]
    nc = n // RET_CHUNK

    def chunks(t):
        return jnp.moveaxis(t.reshape(b, h, nc, RET_CHUNK, t.shape[-1]), 2, 0)

    idx = jnp.arange(RET_CHUNK, dtype=jnp.float32)
    diff = idx[:, None] - idx[None, :]
    keep = diff > 0 if strict else diff >= 0
    intra = jnp.where(keep, jnp.exp(jnp.maximum(diff, 0.0)[None] * log_gamma[:, None, None]), 0.0)
    q_dec = jnp.exp((idx + 1.0)[None, :] * log_gamma[:, None])[None, :, :, None]
    k_dec = jnp.exp((RET_CHUNK - 1.0 - idx)[None, :] * log_gamma[:, None])[None, :, :, None]
    c_dec = jnp.exp(RET_CHUNK * log_gamma)[None, :, None, None]

    def step(s, qkv):
        qc, kc, vc = qkv
        scores = jnp.einsum('bhld,bhmd->bhlm', qc, kc) * intra
        o = jnp.einsum('bhlm,bhme->bhle', scores, vc) + jnp.einsum('bhld,bhde->bhle', qc * q_dec, s)
        s = c_dec * s + jnp.einsum('bhld,bhle->bhde', kc * k_dec, vc)
        return s, o

    s_fin, o = lax.scan(step, s0, (chunks(q), chunks(k), chunks(v)))
    o = jnp.moveaxis(o, 0, 2).reshape(b, h, n, dv)
    if reverse:
        o = jnp.flip(o, 2)
    return o, s_fin


def _ret_final_state(k, v, log_gamma, reverse):
    n = k.shape[2]
    pos = jnp.arange(n, dtype=jnp.float32)
    expo = pos if reverse else (n - 1.0) - pos
    w = jnp.exp(expo[None, :] * log_gamma[:, None])
    return jnp.einsum('bhnd,bhne,hn->bhde', k, v, w)


def _ret_out(o, g, w_out):
    mu = jnp.mean(o, axis=-1, keepdims=True)
    var = jnp.mean(jnp.square(o - mu), axis=-1, keepdims=True)
    o = (o - mu) * lax.rsqrt(var + NORM_EPS)
    b, h, n, dv = o.shape
    o = jnp.swapaxes(o, 1, 2).reshape(b, n, h * dv).astype(g.dtype)
    return (jax.nn.silu(g) * o) @ w_out


def _retention_mixer(u, uc, w_in, w_out, decay_logit, rope, need_ctx_out):
    log_gamma = jax.nn.log_sigmoid(decay_logit.astype(jnp.float32))
    q, k, v, g = _ret_project(u, w_in, rope)
    yc = None
    if need_ctx_out:
        qc, kc, vc, gc = _ret_project(uc, w_in, None)
        zero = jnp.zeros(kc.shape[:2] + (RET_DK, RET_DV), jnp.float32)
        oc_f, s_f = _ret_scan(qc, kc, vc, log_gamma[0], zero, False, False)
        oc_b, s_b = _ret_scan(qc, kc, vc, log_gamma[1], zero, True, True)
        yc = _ret_out(oc_f + oc_b, gc, w_out)
    else:
        kc, vc = _ret_project_kv(uc, w_in)
        s_f = _ret_final_state(kc, vc, log_gamma[0], False)
        s_b = _ret_final_state(kc, vc, log_gamma[1], True)
    o_f, _ = _ret_scan(q, k, v, log_gamma[0], s_f, False, False)
    o_b, _ = _ret_scan(q, k, v, log_gamma[1], s_b, True, True)
    return _ret_out(o_f + o_b, g, w_out), yc


def _lin_combine(left, right):
    a_l, b_l = left
    a_r, b_r = right
    return a_r * a_l, a_r * b_l + b_r


def _s5_discretise(lam_re, lam_im, log_dt, b_re, b_im):
    lam = lax.complex(lam_re.astype(jnp.float32), lam_im.astype(jnp.float32))
    dt = jnp.exp(log_dt.astype(jnp.float32))[:, None]
    a_bar = jnp.exp(lam * dt)
    b_bar = ((a_bar - 1.0) / lam)[..., None] * lax.complex(b_re.astype(jnp.float32),
                                                          b_im.astype(jnp.float32))
    return a_bar, b_bar


def _s5_drive(u, b_bar):
    b, n, _ = u.shape
    ug = u.astype(jnp.float32).reshape(b, n, S5_GROUPS, S5_GROUP)
    return lax.complex(jnp.einsum('bngc,gpc->nbgp', ug, jnp.real(b_bar)),
                       jnp.einsum('bngc,gpc->nbgp', ug, jnp.imag(b_bar)))


def _s5_scan(a_bar, bu, reverse):
    a = jnp.broadcast_to(a_bar, (bu.shape[0], 1) + a_bar.shape)
    _, h = lax.associative_scan(_lin_combine, (a, bu), reverse=reverse, axis=0)
    return h


def _s5_read(h, c_re, c_im):
    n, b = h.shape[:2]
    y = (jnp.einsum('gcp,nbgp->bngc', c_re, jnp.real(h))
         - jnp.einsum('gcp,nbgp->bngc', c_im, jnp.imag(h)))
    return y.reshape(b, n, D_MODEL)


def _s5_glu(y, w_glu, dtype):
    a, b = jnp.split(jax.nn.gelu(y).astype(dtype) @ w_glu, 2, axis=-1)
    return a * jax.nn.sigmoid(b)


def _s5_mixer(u, uc, lam_re, lam_im, log_dt, b_re, b_im, c_re, c_im, d_skip, w_glu, need_ctx_out):
    d32 = d_skip.astype(jnp.float32)
    y = u.astype(jnp.float32) * d32
    yc = uc.astype(jnp.float32) * d32 if need_ctx_out else None
    for dirn in range(2):
        reverse = dirn == 1
        a_bar, b_bar = _s5_discretise(lam_re[dirn], lam_im[dirn], log_dt[dirn], b_re[dirn], b_im[dirn])
        hc = _s5_scan(a_bar, _s5_drive(uc, b_bar), reverse)
        s_init = hc[0] if reverse else hc[-1]
        bu = _s5_drive(u, b_bar)
        bu = bu.at[-1 if reverse else 0].add(a_bar * s_init)
        hl = _s5_scan(a_bar, bu, reverse)
        cr, ci = c_re[dirn].astype(jnp.float32), c_im[dirn].astype(jnp.float32)
        y = y + _s5_read(hl, cr, ci)
        if need_ctx_out:
            yc = yc + _s5_read(hc, cr, ci)
    out = _s5_glu(y, w_glu, u.dtype)
    out_c = _s5_glu(yc, w_glu, uc.dtype) if need_ctx_out else None
    return out, out_c


def setup_inputs(seed: int = 0) -> dict:
    key = jax.random.key(seed)
    ks = jax.random.split(key, 23)
    f32 = jnp.float32
    d = D_MODEL

    def nrm(k, shape, scale):
        return scale * jax.random.normal(k, shape, f32)

    g, p, cg = S5_GROUPS, S5_STATE, S5_GROUP
    base_logit = jnp.log(2.0 ** (5.0 + jnp.arange(RET_HEADS, dtype=f32)) - 1.0)
    return {
        'x': nrm(ks[0], (BATCH, SEQ, d), 1.0),
        'c': nrm(ks[1], (BATCH, d), 1.0),
        'ctx': nrm(ks[2], (BATCH, CTX_LEN, d), 1.0),
        'c_ctx': nrm(ks[3], (d,), 1.0),
        'w_mod': nrm(ks[4], (DEPTH, d, N_MOD * d), 0.5 * d ** -0.5),
        'b_mod': nrm(ks[5], (DEPTH, N_MOD * d), 0.02),
        'norm_g': 1.0 + nrm(ks[6], (DEPTH, 3, d), 0.02),
        'ffn_w_in': nrm(ks[7], (DEPTH, 2, d, 2 * D_FF), d ** -0.5),
        'ffn_w_out': nrm(ks[8], (DEPTH, 2, D_FF, d), D_FF ** -0.5),
        'fnet_w_out': nrm(ks[9], (N_FNET, d, d), d ** -0.5),
        'ret_w_in': nrm(ks[10], (N_RET, d, 2 * RET_HK + 2 * RET_HV), d ** -0.5),
        'ret_w_out': nrm(ks[11], (N_RET, RET_HV, d), RET_HV ** -0.5),
        'ret_decay_logit': base_logit + nrm(ks[12], (N_RET, 2, RET_HEADS), 0.1),
        's5_lam_re': -0.5 + nrm(ks[13], (N_S5, 2, g, p), 0.01),
        's5_lam_im': jnp.pi * jnp.arange(p, dtype=f32) + nrm(ks[14], (N_S5, 2, g, p), 0.01),
        's5_log_dt': jax.random.uniform(ks[15], (N_S5, 2, g), f32,
                                        math.log(S5_DT_MIN), math.log(S5_DT_MAX)),
        's5_b_re': nrm(ks[16], (N_S5, 2, g, p, cg), (2 * cg) ** -0.5),
        's5_b_im': nrm(ks[17], (N_S5, 2, g, p, cg), (2 * cg) ** -0.5),
        's5_c_re': nrm(ks[18], (N_S5, 2, g, cg, p), S5_C_STD),
        's5_c_im': nrm(ks[19], (N_S5, 2, g, cg, p), S5_C_STD),
        's5_d': nrm(ks[20], (N_S5, d), 1.0),
        's5_w_glu': nrm(ks[21], (N_S5, d, 2 * d), d ** -0.5),
        'final_norm_g': 1.0 + nrm(ks[22], (d,), 0.02),
    }


def reference(x, c, ctx, c_ctx, w_mod, b_mod, norm_g, ffn_w_in, ffn_w_out, fnet_w_out,
              ret_w_in, ret_w_out, ret_decay_logit, s5_lam_re, s5_lam_im, s5_log_dt,
              s5_b_re, s5_b_im, s5_c_re, s5_c_im, s5_d, s5_w_glu, final_norm_g):
    bsz, n_lat, _ = x.shape
    rows = n_lat // GRID_W
    r_idx, c_idx = jnp.meshgrid(jnp.arange(rows, dtype=jnp.float32),
                                jnp.arange(GRID_W, dtype=jnp.float32), indexing='ij')
    inv_freq = jnp.exp(-math.log(ROPE_BASE) * jnp.arange(ROPE_FREQS, dtype=jnp.float32) / ROPE_FREQS)
    ang_r = r_idx.reshape(-1)[:, None] * inv_freq
    ang_c = c_idx.reshape(-1)[:, None] * inv_freq
    rope = (jnp.cos(ang_r), jnp.sin(ang_r), jnp.cos(ang_c), jnp.sin(ang_c))

    silu_c = jax.nn.silu(c)
    silu_cc = jax.nn.silu(c_ctx)
    h, hc = x, ctx
    for i in range(DEPTH):
        kind = i % N_MIXERS
        j = i // N_MIXERS
        last = i == DEPTH - 1
        ctx_used = (not last) or kind != 0
        m = (silu_c @ w_mod[i] + b_mod[i]).reshape(bsz, N_MOD, 1, D_MODEL)
        mc = (silu_cc @ w_mod[i] + b_mod[i]).reshape(N_MOD, D_MODEL)

        h = _ffn_half(h, norm_g[i, 0], m[:, 0], m[:, 1], m[:, 2], ffn_w_in[i, 0], ffn_w_out[i, 0])
        u = _modnorm(h, norm_g[i, 1], m[:, 3], m[:, 4])
        if ctx_used:
            hc = _ffn_half(hc, norm_g[i, 0], mc[0], mc[1], mc[2], ffn_w_in[i, 0], ffn_w_out[i, 0])
            uc = _modnorm(hc, norm_g[i, 1], mc[3], mc[4])

        if kind == 0:
            y = _fourier_mixer(u, fnet_w_out[j])
            yc = _fourier_mixer(uc, fnet_w_out[j]) if not last else None
        elif kind == 1:
            y, yc = _retention_mixer(u, uc, ret_w_in[j], ret_w_out[j], ret_decay_logit[j], rope, not last)
        else:
            y, yc = _s5_mixer(u, uc, s5_lam_re[j], s5_lam_im[j], s5_log_dt[j], s5_b_re[j], s5_b_im[j],
                              s5_c_re[j], s5_c_im[j], s5_d[j], s5_w_glu[j], not last)

        h = h + m[:, 5] * y
        h = _ffn_half(h, norm_g[i, 2], m[:, 6], m[:, 7], m[:, 8], ffn_w_in[i, 1], ffn_w_out[i, 1])
        if not last:
            hc = hc + mc[5] * yc
            hc = _ffn_half(hc, norm_g[i, 2], mc[6], mc[7], mc[8], ffn_w_in[i, 1], ffn_w_out[i, 1])
    return _rmsnorm(h, final_norm_g)
```

```python
import math
import numpy as np
import concourse.bass as bass
import concourse.mybir as mybir
from concourse.bass_utils import run_bass_kernel_spmd

F32 = mybir.dt.float32
BF16 = mybir.dt.bfloat16
AF = mybir.ActivationFunctionType
ALU = mybir.AluOpType
AX = mybir.AxisListType


class Res:
    __slots__ = ("name", "w", "r")

    def __init__(self, name):
        self.name = name
        self.w = None
        self.r = {}


class Prog:
    ENG = ("pe", "act", "dve", "pool", "sp")

    def __init__(self, nc, n_dma=40, same_sync=True):
        self.nc = nc
        self.items = {e: [] for e in self.ENG}
        self.cnt = {e: 0 for e in self.ENG}
        self.sems = {}
        for e in ("pe", "act", "dve", "pool"):
            self.sems[e] = nc.alloc_semaphore(name="s_" + e)
        self.n_dma = n_dma
        for i in range(n_dma):
            self.sems[("d", i)] = nc.alloc_semaphore(name="d%d" % i)
        self.dma_cum = [0] * n_dma
        q = n_dma // 5
        self.dma_pool = {"sp": (0, 2 * q), "pool": (2 * q, 4 * q), "act": (4 * q, n_dma)}
        self.dma_pi = {"sp": 0, "pool": 0, "act": 0}
        self.waited = {e: {} for e in self.ENG}
        self.same_sync = same_sync
        self.nops = 0

    def _need(self, eng, tok):
        if tok is None:
            return
        key, val = tok
        if val <= 0:
            return
        if key == eng and (eng == "pe" or (not self.same_sync and eng != "pool")):
            return
        if key == eng and val > self.cnt[eng]:
            return
        if self.waited[eng].get(key, 0) >= val:
            return
        self.waited[eng][key] = val
        self.items[eng].append(("wait", key, val))

    def _deps(self, eng, reads, writes):
        for r in reads:
            self._need(eng, r.w)
        for w in writes:
            self._need(eng, w.w)
            for k, v in w.r.items():
                self._need(eng, (k, v))

    def _mark(self, tok, reads, writes):
        k, v = tok
        for r in reads:
            if r.r.get(k, 0) < v:
                r.r[k] = v
        for w in writes:
            w.w = tok
            w.r = {}

    def op(self, eng, fn, reads=(), writes=(), signal=True):
        self._deps(eng, reads, writes)
        tok = (eng, self.cnt[eng] + 1)
        if signal:
            self.cnt[eng] += 1
            self.items[eng].append(("op", fn, eng, 1))
        else:
            self.items[eng].append(("op", fn, None, 0))
        self._mark(tok, reads, writes)
        self.nops += 1

    def dma(self, eng, out, in_, reads=(), writes=(), **kw):
        lo, hi = self.dma_pool[eng]
        s = lo + self.dma_pi[eng] % (hi - lo)
        self.dma_pi[eng] += 1
        key = ("d", s)
        self._need(eng, (key, self.dma_cum[s]))
        self._deps(eng, reads, writes)
        self.dma_cum[s] += 16
        tok = (key, self.dma_cum[s])
        self.items[eng].append(("op", lambda e: e.dma_start(out=out, in_=in_, **kw), key, 16))
        self._mark(tok, reads, writes)
        self.nops += 1

    def finish(self, eng="sp"):
        for s in range(self.n_dma):
            self._need(eng, (("d", s), self.dma_cum[s]))
        for e in ("pe", "act", "dve", "pool"):
            self._need(eng, (e, self.cnt[e]))

    def emit(self):
        nc = self.nc
        sems = self.sems

        def replay(name):
            def f(eng):
                for it in self.items[name]:
                    if it[0] == "wait":
                        eng.wait_ge(sems[it[1]], it[2])
                    else:
                        ins = it[1](eng)
                        if it[2] is not None:
                            ins.then_inc(sems[it[2]], it[3])
            return f

        with nc.Block() as block:
            block.tensor(replay("pe"))
            block.scalar(replay("act"))
            block.vector(replay("dve"))
            block.gpsimd(replay("pool"))
            block.sync(replay("sp"))


D = 1024
NCH = 8
SEQ = 4096
CTX = 256
T = SEQ + CTX
DEPTH = 4
NMOD = 9
DFF = 2816
FCH = DFF // 128
EPS = 1e-6
TILES = [(0, 256, True)] + [(256 + 512 * i, 512, False) for i in range(8)]


class K:
    def __init__(self, nc, P):
        self.nc = nc
        self.P = P
        self._uid = 0

    def uname(self, name):
        self._uid += 1
        return "%s_%d" % (name, self._uid)

    def mm(self, out, lhsT, rhs, start, stop, reads, writes, signal=True):
        self.P.op("pe", lambda e: e.matmul(out, lhsT, rhs, start=start, stop=stop), reads, writes, signal)

    def tr(self, out, in_, ident, reads, writes):
        self.P.op("pe", lambda e: e.transpose(out, in_, ident), reads, writes)

    def act(self, out, in_, func, reads, writes, bias=None, scale=None):
        kw = {}
        if bias is not None:
            kw["bias"] = bias
        if scale is not None:
            kw["scale"] = scale
        self.P.op("act", lambda e: e.activation(out=out, in_=in_, func=func, **kw), reads, writes)

    def tt(self, eng, out, in0, in1, op, reads, writes):
        self.P.op(eng, lambda e: e.tensor_tensor(out=out, in0=in0, in1=in1, op=op), reads, writes)

    def ts(self, eng, out, in0, s1, s2, op0, op1, reads, writes):
        if op1 is None:
            self.P.op(eng, lambda e: e.tensor_single_scalar(out=out, in_=in0, scalar=s1, op=op0), reads, writes)
        else:
            self.P.op(eng, lambda e: e.tensor_scalar(out=out, in0=in0, scalar1=s1, scalar2=s2, op0=op0, op1=op1),
                      reads, writes)

    def stt(self, eng, out, in0, scalar, in1, op0, op1, reads, writes):
        self.P.op(eng, lambda e: e.scalar_tensor_tensor(out=out, in0=in0, scalar=scalar, in1=in1, op0=op0, op1=op1),
                  reads, writes)

    def copy(self, eng, out, in_, reads, writes):
        if eng == "act":
            self.P.op(eng, lambda e: e.copy(out=out, in_=in_), reads, writes)
        else:
            self.P.op(eng, lambda e: e.tensor_copy(out=out, in_=in_), reads, writes)

    def memset(self, eng, ap, val, writes):
        self.P.op(eng, lambda e: e.memset(ap, val), (), writes)

    def dma(self, eng, out, in_, reads, writes, **kw):
        self.P.dma(eng, out, in_, reads, writes, **kw)


def _barrier(P):
    for e in P.ENG:
        for s in range(P.n_dma):
            P._need(e, (("d", s), P.dma_cum[s]))
        for e2 in ("pe", "act", "dve", "pool"):
            if e2 != e:
                P._need(e, (e2, P.cnt[e2]))


Prog.barrier = _barrier

VC_C, VC_CC, VC_FNG, VC_S5D, VC_NG, VC_BM = 0, 8, 16, 24, 32, 128
NVEC = 416


def declare_io(nc):
    io = {}

    def inp(name, shape):
        io[name] = nc.dram_tensor(name, list(shape), F32, kind="ExternalInput").ap()

    inp("x", [SEQ, D])
    inp("c", [8, 128])
    inp("ctx", [CTX, D])
    inp("c_ctx", [8, 128])
    inp("w_mod", [DEPTH, D, NMOD * D])
    inp("b_mod", [288, 128])
    inp("norm_g", [96, 128])
    inp("ffn_w_in", [DEPTH, 2, D, 2 * DFF])
    inp("ffn_w_out", [DEPTH, 2, DFF, D])
    inp("fnet_w_out", [2, D, D])
    inp("ret_w_in", [D, 6144])
    inp("ret_w_out", [2048, D])
    inp("ret_decay_logit", [2, 4])
    inp("s5_lam_re", [2, 64, 64])
    inp("s5_lam_im", [2, 64, 64])
    inp("s5_log_dt", [2, 64])
    inp("s5_b_re", [2, 64, 64, 16])
    inp("s5_b_im", [2, 64, 64, 16])
    inp("s5_c_re", [2, 64, 16, 64])
    inp("s5_c_im", [2, 64, 16, 64])
    inp("s5_d", [8, 128])
    inp("s5_w_glu", [D, 2 * D])
    inp("final_norm_g", [8, 128])
    io["out"] = nc.dram_tensor("out", [SEQ, D], F32, kind="ExternalOutput").ap()
    io["hT"] = nc.dram_tensor("hT", [D, T], F32).ap()
    declare_fnet_consts(nc, io)
    declare_ret_consts(nc, io)
    declare_s5_consts(nc, io)
    io["uT_d"] = nc.dram_tensor("uT_d", [D, T], BF16).ap()
    return io


def alloc_consts(k, es):
    nc = k.nc
    sb = lambda name, shape, dt: es.enter_context(nc.sbuf_tensor(k.uname(name), shape, dt))
    k.ident = sb("ident", [128, 128], F32)
    k.ones_b = sb("ones_b", [128, 128], BF16)
    k.vecT = sb("vecT", [128, NVEC], F32)
    k.sc = sb("sc", [128, 8, 2], BF16)
    k.mod = sb("mod", [128, DEPTH, 72, 2], F32)
    k.A = sb("Asc", [128, DEPTH, 3, 8, 2], F32)
    k.B = sb("Bsh", [128, DEPTH, 3, 8, 2], F32)
    k.G = sb("Ggt", [128, DEPTH, 3, 8, 2], F32)
    k.R_const = Res("const")
    k.R_vec = Res("vecT")
    k.R_mod = Res("mod")
    k.R_hT = [[Res("hT%d_%d" % (i, j)) for j in range(NCH)] for i in range(len(TILES))]
    P = k.P
    k.memset("pool", k.ident[:, :], 0.0, [k.R_const])
    P.op("pool", lambda e: e.affine_select(out=k.ident[:, :], in_=k.ident[:, :], pattern=[[-1, 128]],
                                            compare_op=ALU.not_equal, fill=1.0, base=0, channel_multiplier=1),
         [k.R_const], [k.R_const])
    k.memset("dve", k.ones_b[:, :], 1.0, [k.R_const])


def prologue(k, io):
    nc, P = k.nc, k.P
    from contextlib import ExitStack
    with ExitStack() as es:
        sb = lambda name, shape, dt: es.enter_context(nc.sbuf_tensor(k.uname(name), shape, dt))
        ps = lambda name, shape: es.enter_context(nc.psum_tensor(k.uname(name), shape, F32))
        stg = [sb("vstg%d" % i, [128, 128], F32) for i in range(4)]
        R_stg = [Res("vstg%d" % i) for i in range(4)]
        tp = [ps("tp%d" % i, [128, 512]) for i in range(2)]
        R_tp = [Res("tp%d" % i) for i in range(2)]
        rows0 = [(io["c"], 8, 0), (io["c_ctx"], 8, 8), (io["final_norm_g"], 8, 16), (io["s5_d"], 8, 24),
                 (io["norm_g"], 96, 32)]
        for ap, n, r0 in rows0:
            k.dma("sp", stg[0][r0:r0 + n, :], ap, [], [R_stg[0]])
        for b, (r0, n) in enumerate([(0, 128), (128, 128), (256, 32)]):
            k.dma("sp", stg[b + 1][0:n, :], io["b_mod"][r0:r0 + n, :], [], [R_stg[b + 1]])
        col = 0
        for b, n in enumerate([128, 128, 128, 32]):
            k.tr(tp[b % 2][:, 0:n], stg[b][0:n, :], k.ident[0:n, 0:n], [R_stg[b], k.R_const], [R_tp[b % 2]])
            k.copy("dve", k.vecT[:, col:col + n], tp[b % 2][:, 0:n], [R_tp[b % 2]], [k.R_vec])
            col += n
        k.act(k.sc[:, :, 0], k.vecT[:, VC_C:VC_C + 8], AF.Silu, [k.R_vec], [k.R_mod])
        k.act(k.sc[:, :, 1], k.vecT[:, VC_CC:VC_CC + 8], AF.Silu, [k.R_vec], [k.R_mod])

        wblk = [sb("wmodblk%d" % i, [128, 8, 1024], BF16) for i in range(3)]
        R_wblk = [Res("wmodblk%d" % i) for i in range(3)]
        mps = [ps("modps%d" % i, [128, 72, 2]) for i in range(2)]
        R_mps = [Res("modps%d" % i) for i in range(2)]
        nb = 0
        for li in range(DEPTH):
            wv = io["w_mod"][li].rearrange("(c p) n -> p c n", p=128)
            for kk in range(NMOD):
                b = nb % 3
                nb += 1
                for h2 in range(2):
                    k.dma("pool", wblk[b][:, h2 * 4:(h2 + 1) * 4, :], wv[:, h2 * 4:(h2 + 1) * 4, kk * 1024:(kk + 1) * 1024],
                          [], [R_wblk[b]])
                for j in range(8):
                    for kc in range(8):
                        k.mm(mps[li % 2][:, kk * 8 + j, :], wblk[b][:, kc, j * 128:(j + 1) * 128], k.sc[:, kc, :],
                             kc == 0, kc == 7, [R_wblk[b], k.R_mod], [R_mps[li % 2]], signal=(kc == 7))
            for r in range(2):
                k.tt("dve", k.mod[:, li, :, r], mps[li % 2][:, :, r], k.vecT[:, VC_BM + li * 72:VC_BM + (li + 1) * 72],
                     ALU.add, [R_mps[li % 2], k.R_vec], [k.R_mod])
            for s in range(3):
                gcol = VC_NG + (li * 3 + s) * 8
                for r in range(2):
                    sh = k.mod[:, li, (3 * s) * 8:(3 * s) * 8 + 8, r]
                    scl = k.mod[:, li, (3 * s + 1) * 8:(3 * s + 1) * 8 + 8, r]
                    gt = k.mod[:, li, (3 * s + 2) * 8:(3 * s + 2) * 8 + 8, r]
                    k.stt("dve", k.A[:, li, s, :, r], scl, 1.0, k.vecT[:, gcol:gcol + 8], ALU.add, ALU.mult,
                          [k.R_mod, k.R_vec], [k.R_mod])
                    k.copy("dve", k.B[:, li, s, :, r], sh, [k.R_mod], [k.R_mod])
                    if s == 1:
                        k.copy("dve", k.G[:, li, s, :, r], gt, [k.R_mod], [k.R_mod])
                    else:
                        k.ts("dve", k.G[:, li, s, :, r], gt, 0.5, None, ALU.mult, None, [k.R_mod], [k.R_mod])

        xs = [sb("xs%d" % i, [128, 4, D], F32) for i in range(2)]
        R_xs = [Res("xs%d" % i) for i in range(2)]
        hst = [sb("hst%d" % i, [128, 8, 512], F32) for i in range(2)]
        R_hst = [Res("hst%d" % i) for i in range(2)]
        ncp = 0
        for ti, (t0, N, isctx) in enumerate(TILES):
            b = ti % 2
            ns = N // 128
            src = io["ctx"] if isctx else io["x"][t0 - CTX:t0 - CTX + N, :]
            k.dma("sp", xs[b][:, 0:ns, :], src.rearrange("(s p) d -> p s d", p=128), [], [R_xs[b]])
            for j in range(8):
                pb = ncp % 2
                for s in range(ns):
                    k.tr(tp[pb][:, s * 128:(s + 1) * 128], xs[b][:, s, j * 128:(j + 1) * 128], k.ident[:, :],
                         [R_xs[b], k.R_const], [R_tp[pb]])
                k.copy("act" if ncp % 2 else "dve", hst[b][:, j, 0:N], tp[pb][:, 0:N], [R_tp[pb]], [R_hst[b]])
                ncp += 1
            k.dma("sp", io["hT"][:, t0:t0 + N].rearrange("(c p) t -> p c t", p=128), hst[b][:, :, 0:N],
                  [R_hst[b]], k.R_hT[ti])
    P.barrier()


def ffn_phase(k, io, li, half):
    nc, P = k.nc, k.P
    s = 0 if half == 0 else 2
    from contextlib import ExitStack
    with ExitStack() as es:
        sb = lambda name, shape, dt: es.enter_context(nc.sbuf_tensor(k.uname(name), shape, dt))
        ps = lambda name, shape: es.enter_context(nc.psum_tensor(k.uname(name), shape, F32))
        w_in = sb("w_in", [128, 8, 2 * DFF], BF16)
        w_out = sb("w_out", [128, FCH, D], BF16)
        hbuf = sb("hbuf", [128, 8, 512], F32)
        uT = sb("uT", [128, 8, 512], BF16)
        gT = sb("gT", [128, FCH, 512], BF16)
        rstd = sb("rstd", [128, 512], F32)
        sqb = [sb("sqb%d" % i, [128, 512], BF16) for i in range(2)]
        tmp = [sb("tmp%d" % i, [128, 512], F32) for i in range(2)]
        sg = [sb("sg%d" % i, [128, 512], F32) for i in range(2)]
        hres = [sb("hres%d" % i, [128, 512], F32) for i in range(3)]
        pa = [ps("pa%d" % i, [128, 512]) for i in range(2)]
        pb = [ps("pb%d" % i, [128, 512]) for i in range(2)]
        py = [ps("py%d" % i, [128, 512]) for i in range(2)]
        pss = ps("pss", [128, 512])
        NWB = 11
        R_win = [Res("w_in%d" % i) for i in range(2 * NWB)]
        R_wout = [Res("w_out%d" % i) for i in range(2)]
        R_hbuf, R_uT, R_rstd, R_pss = Res("hbuf"), Res("uT"), Res("rstd"), Res("pss")
        R_gT = [Res("gT%d" % i) for i in range(FCH)]
        R_sqb = [Res("sqb%d" % i) for i in range(2)]
        R_tmp = [Res("tmp%d" % i) for i in range(2)]
        R_sg = [Res("sg%d" % i) for i in range(2)]
        R_hres = [Res("hres%d" % i) for i in range(3)]
        R_pa = [Res("pa%d" % i) for i in range(2)]
        R_pb = [Res("pb%d" % i) for i in range(2)]
        R_py = [Res("py%d" % i) for i in range(2)]

        wi = io["ffn_w_in"][li, half].rearrange("(c p) n -> p c n", p=128)
        wo = io["ffn_w_out"][li, half].rearrange("(c p) n -> p c n", p=128)
        for blk in range(NWB):
            for ab in range(2):
                c0 = ab * DFF + blk * 256
                k.dma("pool", w_in[:, :, c0:c0 + 256], wi[:, :, c0:c0 + 256], [], [R_win[ab * NWB + blk]])
        for hh in range(2):
            k.dma("pool", w_out[:, hh * 11:(hh + 1) * 11, :], wo[:, hh * 11:(hh + 1) * 11, :], [], [R_wout[hh]])

        def A1(ti):
            t0, N, isctx = TILES[ti]
            k.dma("sp", hbuf[:, :, 0:N], io["hT"][:, t0:t0 + N].rearrange("(c p) t -> p c t", p=128),
                  k.R_hT[ti], [R_hbuf])
            for j in range(8):
                k.act(sqb[j % 2][:, 0:N], hbuf[:, j, 0:N], AF.Square, [R_hbuf], [R_sqb[j % 2]])
                k.mm(pss[:, 0:N], k.ones_b[:, :], sqb[j % 2][:, 0:N], j == 0, j == 7, [R_sqb[j % 2], k.R_const], [R_pss])
            k.ts("dve", rstd[:, 0:N], pss[:, 0:N], 1.0 / D, EPS, ALU.mult, ALU.add, [R_pss], [R_rstd])
            k.act(rstd[:, 0:N], rstd[:, 0:N], AF.Sqrt, [R_rstd], [R_rstd])
            P.op("dve", (lambda o, i: (lambda e: e.reciprocal(out=o, in_=i)))(rstd[:, 0:N], rstd[:, 0:N]), [R_rstd], [R_rstd])

        def A2(ti):
            t0, N, isctx = TILES[ti]
            r = 1 if isctx else 0
            for j in range(8):
                k.stt("dve", tmp[j % 2][:, 0:N], hbuf[:, j, 0:N], k.A[:, li, s, j, r:r + 1], rstd[:, 0:N], ALU.mult, ALU.mult,
                      [R_hbuf, R_rstd, k.R_mod], [R_tmp[j % 2]])
                k.act(uT[:, j, 0:N], tmp[j % 2][:, 0:N], AF.Identity, [R_tmp[j % 2], k.R_mod], [R_uT],
                      bias=k.B[:, li, s, j, r:r + 1])

        def Bst(ti, c_lo, c_hi):
            t0, N, isctx = TILES[ti]
            for c in range(c_lo, c_hi):
                bk = c % 2
                for ab, pp, Rp in ((0, pa, R_pa), (1, pb, R_pb)):
                    col = ab * DFF + c * 128
                    Rw = R_win[ab * NWB + c // 2]
                    for kc in range(8):
                        k.mm(pp[bk][:, 0:N], w_in[:, kc, col:col + 128], uT[:, kc, 0:N], kc == 0, kc == 7,
                             [Rw, R_uT], [Rp[bk]], signal=(kc == 7))
                k.act(sg[bk][:, 0:N], pa[bk][:, 0:N], AF.Silu, [R_pa[bk]], [R_sg[bk]])
                k.tt("dve", gT[:, c, 0:N], sg[bk][:, 0:N], pb[bk][:, 0:N], ALU.mult, [R_sg[bk], R_pb[bk]], [R_gT[c]])

        def Cst(ti):
            t0, N, isctx = TILES[ti]
            r = 1 if isctx else 0
            for j in range(8):
                bk = j % 2
                hb = j % 3
                k.dma("sp", hres[hb][:, 0:N], io["hT"][j * 128:(j + 1) * 128, t0:t0 + N], [k.R_hT[ti][j]], [R_hres[hb]])
                for c in range(FCH):
                    k.mm(py[bk][:, 0:N], w_out[:, c, j * 128:(j + 1) * 128], gT[:, c, 0:N], c == 0, c == FCH - 1,
                         [R_wout[c // 11], R_gT[c]], [R_py[bk]], signal=(c == FCH - 1))
                k.stt("dve", hres[hb][:, 0:N], py[bk][:, 0:N], k.G[:, li, s, j, r:r + 1], hres[hb][:, 0:N], ALU.mult, ALU.add,
                      [R_py[bk], R_hres[hb], k.R_mod], [R_hres[hb]])
                k.dma("sp", io["hT"][j * 128:(j + 1) * 128, t0:t0 + N], hres[hb][:, 0:N], [R_hres[hb]], [k.R_hT[ti][j]])

        tiles = list(range(len(TILES)))
        if k.skip_ctx(li):
            tiles = tiles[1:]
        stop = k.dbg.get("ffn_stop", "C")
        tiles = tiles[:k.dbg.get("ffn_tiles", 99)]
        if stop != "w":
            A1(tiles[0])
            A2(tiles[0])
        for n_, ti in enumerate(tiles):
            if stop in ("w", "A"):
                break
            Bst(ti, 0, 11)
            if n_ + 1 < len(tiles):
                A1(tiles[n_ + 1])
            Bst(ti, 11, FCH)
            if n_ + 1 < len(tiles):
                A2(tiles[n_ + 1])
            if stop == "B":
                continue
            Cst(ti)
    P.barrier()


def final_phase(k, io):
    nc, P = k.nc, k.P
    from contextlib import ExitStack
    with ExitStack() as es:
        sb = lambda name, shape, dt: es.enter_context(nc.sbuf_tensor(k.uname(name), shape, dt))
        ps = lambda name, shape: es.enter_context(nc.psum_tensor(k.uname(name), shape, F32))
        hbuf = [sb("fhbuf%d" % i, [128, 8, 512], F32) for i in range(2)]
        R_hbuf = [Res("fhbuf%d" % i) for i in range(2)]
        sqb = [sb("fsqb%d" % i, [128, 512], BF16) for i in range(2)]
        R_sqb = [Res("fsqb%d" % i) for i in range(2)]
        rstd = sb("frstd", [128, 512], F32)
        R_rstd = Res("frstd")
        xn = [sb("fxn%d" % i, [128, 512], F32) for i in range(2)]
        R_xn = [Res("fxn%d" % i) for i in range(2)]
        ost = [sb("fost%d" % i, [128, 4, D], F32) for i in range(2)]
        R_ost = [Res("fost%d" % i) for i in range(2)]
        pss = ps("fpss", [128, 512])
        R_pss = Res("fpss")
        tp = [ps("ftp%d" % i, [128, 512]) for i in range(2)]
        R_tp = [Res("ftp%d" % i) for i in range(2)]
        R_out = Res("out")
        ncp = 0
        for n_, ti in enumerate(range(1, len(TILES))):
            t0, N, _ = TILES[ti]
            b = n_ % 2
            k.dma("sp", hbuf[b][:, :, 0:N], io["hT"][:, t0:t0 + N].rearrange("(c p) t -> p c t", p=128),
                  k.R_hT[ti], [R_hbuf[b]])
            for j in range(8):
                k.act(sqb[j % 2][:, 0:N], hbuf[b][:, j, 0:N], AF.Square, [R_hbuf[b]], [R_sqb[j % 2]])
                k.mm(pss[:, 0:N], k.ones_b[:, :], sqb[j % 2][:, 0:N], j == 0, j == 7, [R_sqb[j % 2], k.R_const], [R_pss])
            k.ts("dve", rstd[:, 0:N], pss[:, 0:N], 1.0 / D, EPS, ALU.mult, ALU.add, [R_pss], [R_rstd])
            k.act(rstd[:, 0:N], rstd[:, 0:N], AF.Sqrt, [R_rstd], [R_rstd])
            P.op("dve", (lambda o, i: (lambda e: e.reciprocal(out=o, in_=i)))(rstd[:, 0:N], rstd[:, 0:N]), [R_rstd], [R_rstd])
            for j in range(8):
                k.stt("dve", xn[j % 2][:, 0:N], hbuf[b][:, j, 0:N], k.vecT[:, VC_FNG + j:VC_FNG + j + 1], rstd[:, 0:N],
                      ALU.mult, ALU.mult, [R_hbuf[b], R_rstd, k.R_vec], [R_xn[j % 2]])
                pbk = ncp % 2
                for s_ in range(4):
                    k.tr(tp[pbk][:, s_ * 128:(s_ + 1) * 128], xn[j % 2][:, s_ * 128:(s_ + 1) * 128], k.ident[:, :],
                         [R_xn[j % 2], k.R_const], [R_tp[pbk]])
                k.copy("act" if ncp % 2 else "dve",
                       ost[b][:, :, j * 128:(j + 1) * 128], tp[pbk][:, :].rearrange("p (s f) -> p s f", f=128),
                       [R_tp[pbk]], [R_ost[b]])
                ncp += 1
            k.dma("sp", io["out"][t0 - CTX:t0 - CTX + N, :].rearrange("(s p) d -> p s d", p=128), ost[b][:, :, :],
                  [R_ost[b]], [R_out])
    P.barrier()


def _skip_ctx(self, li):
    return li == DEPTH - 1


K.skip_ctx = _skip_ctx

MIXERS = {}


def build_program(layers=range(DEPTH), mixers=True, do_final=True, ffn=True, dbg=None):
    from contextlib import ExitStack
    nc = bass.Bass("TRN2", target_bir_lowering=False)
    io = declare_io(nc)
    P = Prog(nc, n_dma=40, same_sync=bool((dbg or {}).get("same_sync", True)))
    k = K(nc, P)
    with ExitStack() as es:
        alloc_consts(k, es)
        k.dbg = dbg or {}
        prologue(k, io)
        for li in layers:
            if ffn:
                ffn_phase(k, io, li, 0)
            if mixers and (li % 3) in MIXERS:
                MIXERS[li % 3](k, io, li)
            if ffn:
                ffn_phase(k, io, li, 1)
        if do_final:
            final_phase(k, io)
        P.finish("sp")
        P.emit()
    return nc


def make_in_maps(inputs, cores=range(8)):
    f = lambda a: np.ascontiguousarray(np.asarray(a, dtype=np.float32))
    shared = {
        "c_ctx": f(inputs["c_ctx"]).reshape(8, 128),
        "w_mod": f(inputs["w_mod"]),
        "b_mod": f(inputs["b_mod"]).reshape(288, 128),
        "norm_g": f(inputs["norm_g"]).reshape(96, 128),
        "ffn_w_in": f(inputs["ffn_w_in"]),
        "ffn_w_out": f(inputs["ffn_w_out"]),
        "fnet_w_out": f(inputs["fnet_w_out"]),
        "ret_w_in": f(inputs["ret_w_in"]).reshape(D, 6144),
        "ret_w_out": f(inputs["ret_w_out"]).reshape(2048, D),
        "ret_decay_logit": f(inputs["ret_decay_logit"]).reshape(2, 4),
        "s5_lam_re": f(inputs["s5_lam_re"]).reshape(2, 64, 64),
        "s5_lam_im": f(inputs["s5_lam_im"]).reshape(2, 64, 64),
        "s5_log_dt": f(inputs["s5_log_dt"]).reshape(2, 64),
        "s5_b_re": f(inputs["s5_b_re"]).reshape(2, 64, 64, 16),
        "s5_b_im": f(inputs["s5_b_im"]).reshape(2, 64, 64, 16),
        "s5_c_re": f(inputs["s5_c_re"]).reshape(2, 64, 16, 64),
        "s5_c_im": f(inputs["s5_c_im"]).reshape(2, 64, 16, 64),
        "s5_d": f(inputs["s5_d"]).reshape(8, 128),
        "s5_w_glu": f(inputs["s5_w_glu"]).reshape(D, 2 * D),
        "final_norm_g": f(inputs["final_norm_g"]).reshape(8, 128),
    }
    shared.update(fnet_host_consts())
    shared.update(ret_host_consts())
    shared.update(s5_host_consts())
    x, c, ctx = f(inputs["x"]), f(inputs["c"]), f(inputs["ctx"])
    maps = []
    for b in cores:
        m = dict(shared)
        m["x"] = x[b]
        m["c"] = c[b].reshape(8, 128)
        m["ctx"] = ctx[b]
        maps.append(m)
    return maps


def kernel(**inputs):
    nc = build_program()
    maps = make_in_maps(inputs)
    res = run_bass_kernel_spmd(nc, maps, core_ids=list(range(8)))
    return np.stack([np.asarray(r["out"], dtype=np.float32) for r in res.results], axis=0)


def mk_modnorm(k, io, sb, ps):
    P = k.P
    hbuf = sb("mn_hbuf", [128, 8, 512], F32)
    sqb = [sb("mn_sqb%d" % i, [128, 512], BF16) for i in range(2)]
    tmp = [sb("mn_tmp%d" % i, [128, 512], F32) for i in range(2)]
    rstd = sb("mn_rstd", [128, 512], F32)
    pss = ps("mn_pss", [128, 512])
    R_hbuf, R_rstd, R_pss = Res("mn_hbuf"), Res("mn_rstd"), Res("mn_pss")
    R_sqb = [Res("mn_sqb%d" % i) for i in range(2)]
    R_tmp = [Res("mn_tmp%d" % i) for i in range(2)]

    def do(ti, li, s, uT_of, R_uT, chunks=range(8), raw_of=None):
        t0, N, isctx = TILES[ti]
        r = 1 if isctx else 0
        k.dma("sp", hbuf[:, :, 0:N], io["hT"][:, t0:t0 + N].rearrange("(c p) t -> p c t", p=128),
              k.R_hT[ti], [R_hbuf])
        for j in range(8):
            k.act(sqb[j % 2][:, 0:N], hbuf[:, j, 0:N], AF.Square, [R_hbuf], [R_sqb[j % 2]])
            k.mm(pss[:, 0:N], k.ones_b[:, :], sqb[j % 2][:, 0:N], j == 0, j == 7, [R_sqb[j % 2], k.R_const], [R_pss])
        k.ts("dve", rstd[:, 0:N], pss[:, 0:N], 1.0 / D, EPS, ALU.mult, ALU.add, [R_pss], [R_rstd])
        k.act(rstd[:, 0:N], rstd[:, 0:N], AF.Sqrt, [R_rstd], [R_rstd])
        P.op("dve", (lambda o, i: (lambda e: e.reciprocal(out=o, in_=i)))(rstd[:, 0:N], rstd[:, 0:N]), [R_rstd], [R_rstd])
        for n_, j in enumerate(chunks):
            k.stt("dve", tmp[n_ % 2][:, 0:N], hbuf[:, j, 0:N], k.A[:, li, s, j, r:r + 1], rstd[:, 0:N], ALU.mult, ALU.mult,
                  [R_hbuf, R_rstd, k.R_mod], [R_tmp[n_ % 2]])
            k.act(uT_of(j), tmp[n_ % 2][:, 0:N], AF.Identity, [R_tmp[n_ % 2], k.R_mod], [R_uT],
                  bias=k.B[:, li, s, j, r:r + 1])

    return do


def mk_resadd(k, io, sb, li):
    hres = [sb("ra_hres%d" % i, [128, 512], F32) for i in range(3)]
    R_hres = [Res("ra_hres%d" % i) for i in range(3)]
    cnt = [0]

    def do(ti, j, col0, N, y_ap, R_y, pre_reads=()):
        t0, _, isctx = TILES[ti]
        r = 1 if isctx else 0
        hb = cnt[0] % 3
        cnt[0] += 1
        k.dma("sp", hres[hb][:, 0:N], io["hT"][j * 128:(j + 1) * 128, t0 + col0:t0 + col0 + N], [k.R_hT[ti][j]], [R_hres[hb]])
        k.stt("dve", hres[hb][:, 0:N], y_ap, k.G[:, li, 1, j, r:r + 1], hres[hb][:, 0:N], ALU.mult, ALU.add,
              [R_y, R_hres[hb], k.R_mod] + list(pre_reads), [R_hres[hb]])
        k.dma("sp", io["hT"][j * 128:(j + 1) * 128, t0 + col0:t0 + col0 + N], hres[hb][:, 0:N], [R_hres[hb]], [k.R_hT[ti][j]])

    return do


def declare_fnet_consts(nc, io):
    io["dft_c"] = nc.dram_tensor("dft_c", [128, 256], BF16, kind="ExternalInput").ap()
    io["dft_cosN"] = nc.dram_tensor("dft_cosN", [SEQ, SEQ], BF16, kind="ExternalInput").ap()
    io["dft_sinN"] = nc.dram_tensor("dft_sinN", [SEQ, SEQ], BF16, kind="ExternalInput").ap()
    io["dft_cosC"] = nc.dram_tensor("dft_cosC", [CTX, CTX], BF16, kind="ExternalInput").ap()
    io["dft_sinC"] = nc.dram_tensor("dft_sinC", [CTX, CTX], BF16, kind="ExternalInput").ap()
    io["fn_perm"] = nc.dram_tensor("fn_perm", [128, 4 * 128], BF16, kind="ExternalInput").ap()


def fnet_host_consts():
    import ml_dtypes
    bf = ml_dtypes.bfloat16
    c = np.arange(128)
    ang = 2.0 * np.pi * ((c[:, None] * c[None, :]) % 128) / 128.0
    dft_c = np.concatenate([np.cos(ang), -np.sin(ang)], axis=1).astype(np.float32).astype(bf)

    def mats(n):
        i = np.arange(n, dtype=np.int64)
        idx = (i[:, None] * i[None, :]) % n
        a = 2.0 * np.pi * np.arange(n, dtype=np.float64) / n
        return np.cos(a).astype(np.float32)[idx].astype(bf), np.sin(a).astype(np.float32)[idx].astype(bf)

    cN, sN = mats(SEQ)
    cC, sC = mats(CTX)
    J = np.zeros((128, 128), np.float32)
    for p in range(1, 128):
        J[128 - p, p] = 1.0
    E = np.zeros((128, 128), np.float32)
    E[0, 0] = 1.0
    perm = np.concatenate([J, -J, E, -E], axis=1).astype(bf)
    return {"dft_c": dft_c, "dft_cosN": cN, "dft_sinN": sN, "dft_cosC": cC, "dft_sinC": sC, "fn_perm": perm}


def fnet_mixer(k, io, li):
    nc, P = k.nc, k.P
    jw = li // 3
    from contextlib import ExitStack
    with ExitStack() as es:
        sb = lambda name, shape, dt: es.enter_context(nc.sbuf_tensor(k.uname(name), shape, dt))
        ps = lambda name, shape: es.enter_context(nc.psum_tensor(k.uname(name), shape, F32))
        modnorm = mk_modnorm(k, io, sb, ps)
        resadd = mk_resadd(k, io, sb, li)
        dftc = sb("dftc", [128, 256], BF16)
        w_out = sb("fw_out", [128, 8, D], BF16)
        X = sb("fX", [128, 32, 4, 256], BF16)
        cosb = sb("fcos", [128, 32, 512], BF16)
        sinb = sb("fsin", [128, 32, 512], BF16)
        uT = sb("fuT", [128, 8, 512], BF16)
        uTg = sb("fuTg", [128, 4, 512], BF16)
        R_uTg = Res("fuTg")
        R_uTd = [Res("uTd%d" % i) for i in range(len(TILES))]
        fT = [sb("ffT%d" % i, [128, 512], BF16) for i in range(8)]
        px = [ps("fpx%d" % i, [128, 512]) for i in range(2)]
        pf = [ps("fpf%d" % i, [128, 512]) for i in range(2)]
        py = [ps("fpy%d" % i, [128, 512]) for i in range(2)]
        R_dftc, R_wout, R_uT = Res("dftc"), Res("fw_out"), Res("fuT")
        R_X = [Res("fX%d" % i) for i in range(32)]
        R_cos = [Res("fcos%d" % i) for i in range(8)]
        R_sin = [Res("fsin%d" % i) for i in range(8)]
        R_fT = [Res("ffT%d" % i) for i in range(8)]
        R_px = [Res("fpx%d" % i) for i in range(2)]
        R_pf = [Res("fpf%d" % i) for i in range(2)]
        R_py = [Res("fpy%d" % i) for i in range(2)]
        k.dma("sp", dftc[:, :], io["dft_c"], [], [R_dftc])
        k.dma("pool", w_out[:, :, :], io["fnet_w_out"][jw].rearrange("(c p) n -> p c n", p=128), [], [R_wout])
        cnt = {"x": 0, "f": 0, "y": 0}

        def run(tiles, nch, groups_list, cosN, sinN, W, norm, sweep_only=False):
            for ti in tiles:
                t0, N, isctx = TILES[ti]
                modnorm(ti, li, 1, lambda j: uT[:, j, 0:N], R_uT)
                k.dma("sp", io["uT_d"][:, t0:t0 + N].rearrange("(c p) t -> p c t", p=128), uT[:, :, 0:N], [R_uT], [R_uTd[ti]])
            if sweep_only:
                return
            for groups in groups_list:
                ng = len(groups)
                for ti in tiles:
                    t0, N, isctx = TILES[ti]
                    g0 = groups[0]
                    k.dma("sp", uTg[:, 0:ng, 0:N],
                          io["uT_d"][g0 * 128:(g0 + ng) * 128, t0:t0 + N].rearrange("(c p) t -> p c t", p=128),
                          [R_uTd[ti]], [R_uTg])
                    c0 = (t0 - TILES[tiles[0]][0]) // 128
                    for sbi in range(N // 128):
                        for g2 in range(0, ng, 2):
                            bk = cnt["x"] % 2
                            cnt["x"] += 1
                            for u_ in range(2):
                                k.mm(px[bk][:, u_ * 256:(u_ + 1) * 256], uTg[:, g2 + u_, sbi * 128:(sbi + 1) * 128], dftc[:, :],
                                     True, True, [R_uTg, R_dftc], [R_px[bk]])
                            k.copy("act" if cnt["x"] % 2 else "dve",
                                   X[:, c0 + sbi, g2:g2 + 2, :], px[bk][:, :].rearrange("p (a b) -> p a b", b=256),
                                   [R_px[bk]], [R_X[c0 + sbi]])
                cq = max(1, nch // 8)
                for kt in range((nch * 128) // W):
                    for q in range(nch // cq):
                        k.dma("sp", cosb[:, q * cq:(q + 1) * cq, 0:W],
                              cosN.rearrange("(c p) k -> p c k", p=128)[:, q * cq:(q + 1) * cq, kt * W:(kt + 1) * W],
                              [], [R_cos[q]])
                        k.dma("act", sinb[:, q * cq:(q + 1) * cq, 0:W],
                              sinN.rearrange("(c p) k -> p c k", p=128)[:, q * cq:(q + 1) * cq, kt * W:(kt + 1) * W],
                              [], [R_sin[q]])
                    pf4 = [pf[0], pf[1], px[0], px[1]]
                    R_pf4 = [R_pf[0], R_pf[1], R_px[0], R_px[1]]
                    for c in range(nch):
                        for gi, g in enumerate(groups):
                            last = (c == nch - 1)
                            k.mm(pf4[gi][:, 0:W], X[:, c, gi, 0:128], cosb[:, c, 0:W], c == 0, False,
                                 [R_X[c], R_cos[c // cq]], [R_pf4[gi]], signal=False)
                            k.mm(pf4[gi][:, 0:W], X[:, c, gi, 128:256], sinb[:, c, 0:W], False, last,
                                 [R_X[c], R_sin[c // cq]], [R_pf4[gi]], signal=(last or (c % cq == cq - 1 and gi == ng - 1)))
                    for gi, g in enumerate(groups):
                        if gi % 2:
                            k.act(fT[gi][:, 0:W], pf4[gi][:, 0:W], AF.Copy, [R_pf4[gi]], [R_fT[gi]], scale=norm)
                        else:
                            k.ts("dve", fT[gi][:, 0:W], pf4[gi][:, 0:W], norm, None, ALU.mult, None, [R_pf4[gi]], [R_fT[gi]])
                    ti = tiles[0] + (kt * W) // 512 if W == 512 else tiles[0]
                    for j in range(8):
                        bk = cnt["y"] % 2
                        cnt["y"] += 1
                        for gi, g in enumerate(groups):
                            k.mm(py[bk][:, 0:W], w_out[:, g, j * 128:(j + 1) * 128], fT[gi][:, 0:W], gi == 0, gi == ng - 1,
                                 [R_wout, R_fT[gi]], [R_py[bk]], signal=(gi == ng - 1))
                        resadd(ti, j, 0, W, py[bk][:, 0:W], R_py[bk])

        if not k.skip_ctx(li):
            run([0], 2, [list(range(4)), list(range(4, 8))], io["dft_cosC"], io["dft_sinC"], 256, 1.0 / math.sqrt(CTX * 128))
        if k.dbg.get("fnet_old"):
            run(list(range(1, 9)), 32, [list(range(4)), list(range(4, 8))], io["dft_cosN"], io["dft_sinN"], 512,
                1.0 / math.sqrt(SEQ * 128))
        else:
            run(list(range(1, 9)), 32, None, None, None, 512, None, sweep_only=True)
    P.barrier()
    if not k.dbg.get("fnet_old"):
        fnet_latent_folded(k, io, li)


def fnet_latent_folded(k, io, li):
    nc, P = k.nc, k.P
    jw = li // 3
    from contextlib import ExitStack
    R = lambda n: Res(n)
    norm = 1.0 / math.sqrt(SEQ * 128)
    NF = 17
    with ExitStack() as es:
        sb = lambda name, shape, dt: es.enter_context(nc.sbuf_tensor(k.uname(name), shape, dt))
        ps = lambda name, shape: es.enter_context(nc.psum_tensor(k.uname(name), shape, F32))
        resadd = mk_resadd(k, io, sb, li)
        dftc = sb("g_dftc", [128, 256], BF16)
        perm = sb("g_perm", [128, 4, 128], BF16)
        w_out = sb("g_wout", [128, 8, D], BF16)
        Xf = sb("g_Xf", [128, NF, 8, 256], BF16)
        px = [ps("g_px%d" % i, [128, 512]) for i in range(2)]
        pf = [ps("g_pf%d" % i, [128, 512]) for i in range(2)]
        py = [ps("g_py%d" % i, [128, 512]) for i in range(2)]
        R_c, R_wout = R("g_c"), R("g_wout")
        R_Xf = [R("g_Xf%d" % i) for i in range(NF)]
        R_px = [R("g_px0"), R("g_px1")]
        R_pf = [R("g_pf0"), R("g_pf1")]
        R_py = [R("g_py0"), R("g_py1")]
        k.dma("sp", dftc[:, :], io["dft_c"], [], [R_c])
        k.dma("sp", perm[:, :, :], io["fn_perm"].rearrange("p (a b) -> p a b", b=128), [], [R_c])
        k.dma("pool", w_out[:, :, :], io["fnet_w_out"][jw].rearrange("(c p) n -> p c n", p=128), [], [R_wout])
        with ExitStack() as es1:
            sb1 = lambda name, shape, dt: es1.enter_context(nc.sbuf_tensor(k.uname(name), shape, dt))
            Xup = sb1("g_Xup", [128, 16, 8, 256], BF16)
            uTg = sb1("g_uTg", [128, 8, 512], BF16)
            R_uTg = R("g_uTg")
            R_Xup = [R("g_Xup%d" % i) for i in range(16)]
            nx = [0]

            def load_u(ti):
                t0, N, _ = TILES[ti]
                k.dma("sp", uTg[:, :, 0:N], io["uT_d"][:, t0:t0 + N].rearrange("(c p) t -> p c t", p=128), [], [R_uTg])

            def evac(dst, src, Rs, Rd):
                nx[0] += 1
                k.copy("act" if nx[0] % 2 else "dve", dst, src, [Rs], [Rd])

            for ti in range(5, 9):
                t0 = TILES[ti][0]
                load_u(ti)
                for sbi in range(4):
                    cu = (t0 - CTX) // 128 + sbi - 16
                    for g2 in range(0, 8, 2):
                        bk = nx[0] % 2
                        for u_ in range(2):
                            k.mm(px[bk][:, u_ * 256:(u_ + 1) * 256], uTg[:, g2 + u_, sbi * 128:(sbi + 1) * 128], dftc[:, :],
                                 True, True, [R_uTg, R_c], [R_px[bk]])
                        evac(Xup[:, cu, g2:g2 + 2, :], px[bk][:, :].rearrange("p (a b) -> p a b", b=256), R_px[bk], R_Xup[cu])
            for ti in range(1, 5):
                t0 = TILES[ti][0]
                load_u(ti)
                for sbi in range(4):
                    c = (t0 - CTX) // 128 + sbi
                    for g2 in range(0, 8, 2):
                        bk = nx[0] % 2
                        for u_ in range(2):
                            g = g2 + u_
                            for w in range(2):
                                o_ = px[bk][:, u_ * 256 + w * 128:u_ * 256 + (w + 1) * 128]
                                rd = [R_uTg, R_c, R_Xup[15 - c]] + ([R_Xup[16 - c]] if c >= 1 else [])
                                k.mm(o_, uTg[:, g, sbi * 128:(sbi + 1) * 128], dftc[:, w * 128:(w + 1) * 128], True, False,
                                     rd, [R_px[bk]], signal=False)
                                k.mm(o_, perm[:, w, :], Xup[:, 15 - c, g, w * 128:(w + 1) * 128], False, c == 0,
                                     rd, [R_px[bk]], signal=(c == 0))
                                if c >= 1:
                                    k.mm(o_, perm[:, 2 + w, :], Xup[:, 16 - c, g, w * 128:(w + 1) * 128], False, True,
                                         rd, [R_px[bk]], signal=True)
                        evac(Xf[:, c, g2:g2 + 2, :], px[bk][:, :].rearrange("p (a b) -> p a b", b=256), R_px[bk], R_Xf[c])
            k.memset("dve", Xf[:, 16, :, :], 0.0, [R_Xf[16]])
            for g2 in range(0, 8, 2):
                bk = nx[0] % 2
                for u_ in range(2):
                    k.mm(px[bk][:, u_ * 256:u_ * 256 + 128], perm[:, 2, :], Xup[:, 0, g2 + u_, 0:128], True, True,
                         [R_c, R_Xup[0]], [R_px[bk]])
                evac(Xf[:, 16, g2:g2 + 2, 0:128], px[bk][:, :].rearrange("p (a b) -> p a b", b=256)[:, :, 0:128], R_px[bk], R_Xf[16])
        P.barrier()
        with ExitStack() as es2:
            sb2 = lambda name, shape, dt: es2.enter_context(nc.sbuf_tensor(k.uname(name), shape, dt))
            cosb = [sb2("g_cos%d" % i, [128, NF, 512], BF16) for i in range(2)]
            sinb = [sb2("g_sin%d" % i, [128, NF, 512], BF16) for i in range(2)]
            fT = [sb2("g_fT%d" % i, [128, 512], BF16) for i in range(8)]
            SUBS = [(0, 4), (4, 8), (8, 12), (12, NF)]
            sub_of = [0] * 4 + [1] * 4 + [2] * 4 + [3] * 5
            R_cos = [[R("g_cos%d_%d" % (i, q)) for q in range(4)] for i in range(2)]
            R_sin = [[R("g_sin%d_%d" % (i, q)) for q in range(4)] for i in range(2)]
            R_fT = [R("g_fT%d" % i) for i in range(8)]
            cv = io["dft_cosN"][0:NF * 128, :].rearrange("(c p) k -> p c k", p=128)
            sv = io["dft_sinN"][0:NF * 128, :].rearrange("(c p) k -> p c k", p=128)
            pf4 = [pf[0], pf[1], px[0], px[1]]
            R_pf4 = [R_pf[0], R_pf[1], R_px[0], R_px[1]]
            ny = 0
            for kt in range(8):
                par = kt % 2
                for qi, (a, b) in enumerate(SUBS):
                    k.dma("sp", cosb[par][:, a:b, :], cv[:, a:b, kt * 512:(kt + 1) * 512], [], [R_cos[par][qi]])
                    k.dma("act", sinb[par][:, a:b, :], sv[:, a:b, kt * 512:(kt + 1) * 512], [], [R_sin[par][qi]])
                for half in range(2):
                    for c in range(NF):
                        for gi in range(4):
                            g = 4 * half + gi
                            k.mm(pf4[gi][:, :], Xf[:, c, g, 0:128], cosb[par][:, c, :], c == 0, False,
                                 [R_Xf[c], R_cos[par][sub_of[c]]], [R_pf4[gi]], signal=False)
                            k.mm(pf4[gi][:, :], Xf[:, c, g, 128:256], sinb[par][:, c, :], False, c == NF - 1,
                                 [R_Xf[c], R_sin[par][sub_of[c]]], [R_pf4[gi]], signal=(c == NF - 1))
                    for gi in range(4):
                        g = 4 * half + gi
                        if gi % 2:
                            k.act(fT[g][:, :], pf4[gi][:, :], AF.Copy, [R_pf4[gi]], [R_fT[g]], scale=norm)
                        else:
                            k.ts("dve", fT[g][:, :], pf4[gi][:, :], norm, None, ALU.mult, None, [R_pf4[gi]], [R_fT[g]])
                ti = 1 + kt
                for j in range(8):
                    bk = ny % 2
                    ny += 1
                    for g in range(8):
                        k.mm(py[bk][:, :], w_out[:, g, j * 128:(j + 1) * 128], fT[g][:, :], g == 0, g == 7,
                             [R_wout, R_fT[g]], [R_py[bk]], signal=(g == 7))
                    resadd(ti, j, 0, 512, py[bk][:, :], R_py[bk])
    P.barrier()


MIXERS[0] = fnet_mixer


NCHK = T // 128


def declare_ret_consts(nc, io):
    io["ret_expo"] = nc.dram_tensor("ret_expo", [128, 4 * 128], F32, kind="ExternalInput").ap()
    io["ret_m01"] = nc.dram_tensor("ret_m01", [128, 2 * 128], F32, kind="ExternalInput").ap()
    io["ret_ramp"] = nc.dram_tensor("ret_ramp", [128, NCHK], F32, kind="ExternalInput").ap()
    io["ropeC"] = nc.dram_tensor("ropeC", [256, SEQ], F32, kind="ExternalInput").ap()
    io["ropeS"] = nc.dram_tensor("ropeS", [256, SEQ], F32, kind="ExternalInput").ap()
    io["zT_d"] = nc.dram_tensor("zT_d", [2048, T], BF16).ap()


def ret_host_consts():
    m = np.arange(128, dtype=np.float32)[:, None]
    l = np.arange(128, dtype=np.float32)[None, :]
    expo = np.concatenate([np.maximum(l - m, 0), l - m + 128, np.maximum(m - l, 0), m - l + 128], axis=1).astype(np.float32)
    m01 = np.concatenate([(m <= l), (m > l)], axis=1).astype(np.float32)
    ramp = np.broadcast_to(128.0 * np.arange(NCHK, dtype=np.float32)[None, :], (128, NCHK)).copy()
    t = np.arange(SEQ)
    inv_freq = np.exp(np.float32(-math.log(10000.0)) * np.arange(64, dtype=np.float32) / np.float32(64)).astype(np.float32)
    ang_r = ((t // 64).astype(np.float32)[:, None] * inv_freq[None, :]).astype(np.float32)
    ang_c = ((t % 64).astype(np.float32)[:, None] * inv_freq[None, :]).astype(np.float32)
    C = np.zeros((256, SEQ), np.float32)
    S = np.zeros((256, SEQ), np.float32)
    for d, ang in enumerate((ang_r, ang_c)):
        c, s = np.cos(ang).T.astype(np.float32), np.sin(ang).T.astype(np.float32)
        C[d * 128:d * 128 + 64] = c
        C[d * 128 + 64:d * 128 + 128] = c
        S[d * 128:d * 128 + 64] = -s
        S[d * 128 + 64:d * 128 + 128] = s
    return {"ret_expo": expo, "ret_m01": m01, "ret_ramp": ramp, "ropeC": C, "ropeS": S}


def ret_mixer(k, io, li):
    nc, P = k.nc, k.P
    from contextlib import ExitStack
    with ExitStack() as es:
        sb = lambda name, shape, dt: es.enter_context(nc.sbuf_tensor(k.uname(name), shape, dt))
        ps = lambda name, shape, dt=F32: es.enter_context(nc.psum_tensor(k.uname(name), shape, dt))
        modnorm = mk_modnorm(k, io, sb, ps)
        resadd = mk_resadd(k, io, sb, li)
        R = lambda n: Res(n)
        expo = sb("r_expo", [128, 4, 128], F32)
        m01 = sb("r_m01", [128, 2, 128], F32)
        ramp = sb("r_ramp", [128, NCHK], F32)
        ones1 = sb("r_ones1", [1, 128], F32)
        lg1 = sb("r_lg1", [1, 8], F32)
        lgam = sb("r_lgam", [128, 8], F32)
        Etab = sb("r_Etab", [128, 8, 2, 128], F32)
        Wdiag = sb("r_Wdiag", [128, 4, 128], F32)
        apow = sb("r_apow", [128, 8, NCHK], F32)
        R_tab = R("r_tab")
        pA = ps("r_pA", [128, 512])
        pB = ps("r_pB", [128, 512])
        psc_t = [ps("r_psc%d" % i, [128, 512]) for i in range(2)]
        NSC = 4
        psc = [psc_t[0], psc_t[1], pA, pB]
        po = ps("r_po", [128, 512])
        pg = ps("r_pg", [128, 512])
        pt = ps("r_pt", [128, 512], BF16)
        R_pA, R_pB, R_po, R_pg, R_pt = R("pA"), R("pB"), R("po"), R("pg"), R("pt")
        R_psc = [R("psc0"), R("psc1"), R_pA, R_pB]
        k.dma("sp", expo[:, :, :], io["ret_expo"].rearrange("p (a b) -> p a b", b=128), [], [R_tab])
        k.dma("sp", m01[:, :, :], io["ret_m01"].rearrange("p (a b) -> p a b", b=128), [], [R_tab])
        k.dma("sp", ramp[:, :], io["ret_ramp"], [], [R_tab])
        k.dma("sp", lg1[:, :], io["ret_decay_logit"].rearrange("(o a) b -> o (a b)", o=1), [], [R_tab])
        k.memset("dve", ones1[:, :], 1.0, [R_tab])
        k.mm(pA[:, 0:8], ones1[0:1, :], lg1[0:1, :], True, True, [R_tab], [R_pA])
        k.act(lgam[:, :], pA[:, 0:8], AF.Exp, [R_pA], [R_tab], scale=-1.0)
        k.ts("dve", lgam[:, :], lgam[:, :], 1.0, None, ALU.add, None, [R_tab], [R_tab])
        k.act(lgam[:, :], lgam[:, :], AF.Ln, [R_tab], [R_tab])
        k.ts("dve", lgam[:, :], lgam[:, :], -1.0, None, ALU.mult, None, [R_tab], [R_tab])
        for d in range(2):
            for h in range(4):
                col = d * 4 + h
                for w in range(2):
                    k.act(Etab[:, col, w, :], expo[:, d * 2 + w, :], AF.Exp, [R_tab], [R_tab], scale=lgam[:, col:col + 1])
                k.tt("dve", Etab[:, col, 0, :], Etab[:, col, 0, :], m01[:, d, :], ALU.mult, [R_tab], [R_tab])
                k.act(apow[:, col, :], ramp[:, :], AF.Exp, [R_tab], [R_tab], scale=lgam[:, col:col + 1])
        for h in range(4):
            k.tt("dve", Wdiag[:, h, :], Etab[:, h, 0, :], Etab[:, 4 + h, 0, :], ALU.add, [R_tab], [R_tab])

        uT = sb("r_uT", [128, 8, 512], BF16)
        R_uT = R("r_uT")
        R_uTd = [R("uTd%d" % i) for i in range(len(TILES))]
        for ti in range(len(TILES)):
            t0, N, _ = TILES[ti]
            modnorm(ti, li, 1, lambda j: uT[:, j, 0:N], R_uT)
            k.dma("sp", io["uT_d"][:, t0:t0 + N].rearrange("(c p) t -> p c t", p=128), uT[:, :, 0:N], [R_uT], [R_uTd[ti]])

        wq = sb("r_wq", [128, 8, 256], BF16)
        wk = sb("r_wk", [128, 8, 256], BF16)
        wqs = sb("r_wqs", [128, 8, 256], BF16)
        wks = sb("r_wks", [128, 8, 256], BF16)
        wv = sb("r_wv", [128, 8, 512], BF16)
        wg = sb("r_wg", [128, 8, 512], BF16)
        R_w = R("r_w")
        qT = sb("r_qT", [128, 2, T], BF16)
        kT = sb("r_kT", [128, 2, T], BF16)
        vtm = sb("r_vtm", [128, NCHK, 512], BF16)
        R_qT, R_kT = R("qT"), R("kT")
        R_v = [R("v%d" % i) for i in range(NCHK)]
        rc = sb("r_rc", [128, 2, 512], F32)
        rs = sb("r_rs", [128, 2, 512], F32)
        R_rope = R("rope")
        t1 = [sb("r_t1_%d" % i, [128, 512], F32) for i in range(2)]
        t2 = [sb("r_t2_%d" % i, [128, 512], F32) for i in range(2)]
        R_t1 = [R("t1a"), R("t1b")]
        R_t2 = [R("t2a"), R("t2b")]
        NPB = 24
        Pb = [sb("r_Pb%d" % i, [128, 128], BF16) for i in range(NPB)]
        R_Pb = [R("Pb%d" % i) for i in range(NPB)]
        Pt = [sb("r_Pt%d" % i, [128, 128], F32) for i in range(2)]
        R_Pt = [R("Pt0"), R("Pt1")]
        uTc = sb("r_uTc", [128, 8, 128], BF16)
        R_uTc = R("uTc")
        osb = sb("r_osb", [128, 512], F32)
        sgt = sb("r_sgt", [128, 512], F32)
        sqj = sb("r_sqj", [128, 512], F32)
        zb = sb("r_zb", [128, 512], BF16)
        zb2 = sb("r_zb2", [128, 512], BF16)
        zbs = [zb, zb2]
        R_zbs = [R("zb0"), R("zb1")]
        zTc = sb("r_zTc", [128, 4, 128], BF16)
        st = sb("r_st", [128, 4], F32)
        identb = sb("r_identb", [128, 128], BF16)
        R_osb, R_sgt, R_zb, R_zTc, R_st, R_sqj = R("osb"), R("sgt"), R("zb"), R("zTc"), R("st"), R("sqj")
        R_zTd = [R("zTd%d" % i) for i in range(NCHK)]
        k.copy("dve", identb[:, :], k.ident[:, :], [k.R_const], [R_tab])
        wi = io["ret_w_in"].rearrange("(c p) n -> p c n", p=128)

        def cb(c):
            return c - 2 if c >= 2 else 32 + c

        nblk = 0
        for h in range(4):
            for (dst, c0, n) in ((wq, h * 256, 256), (wk, 1024 + h * 256, 256), (wv, 2048 + h * 512, 512), (wg, 4096 + h * 512, 512)):
                k.dma("pool", dst[:, :, 0:n], wi[:, :, c0:c0 + n], [], [R_w])
            for (dst, c0) in ((wqs, h * 256), (wks, 1024 + h * 256)):
                for dk in range(2):
                    b0 = c0 + dk * 128
                    k.dma("pool", dst[:, :, dk * 128:dk * 128 + 64], wi[:, :, b0 + 64:b0 + 128], [], [R_w])
                    k.dma("pool", dst[:, :, dk * 128 + 64:dk * 128 + 128], wi[:, :, b0:b0 + 64], [], [R_w])
            for ti in range(len(TILES)):
                t0, N, isctx = TILES[ti]
                k.dma("sp", uT[:, :, 0:N], io["uT_d"][:, t0:t0 + N].rearrange("(c p) t -> p c t", p=128), [R_uTd[ti]], [R_uT])
                if not isctx:
                    p0 = t0 - CTX
                    k.dma("sp", rc[:, :, 0:N], io["ropeC"][:, p0:p0 + N].rearrange("(d p) t -> p d t", p=128), [], [R_rope])
                    k.dma("sp", rs[:, :, 0:N], io["ropeS"][:, p0:p0 + N].rearrange("(d p) t -> p d t", p=128), [], [R_rope])
                for (w_, ws_, dst, R_dst, scl) in ((wq, wqs, qT, R_qT, 1.0), (wk, wks, kT, R_kT, 0.0625)):
                    for dk in range(2):
                        for kc in range(8):
                            k.mm(pA[:, 0:N], w_[:, kc, dk * 128:(dk + 1) * 128], uT[:, kc, 0:N], kc == 0, kc == 7,
                                 [R_w, R_uT], [R_pA], signal=(kc == 7))
                        if isctx:
                            k.ts("dve", dst[:, dk, t0:t0 + N], pA[:, 0:N], scl, None, ALU.mult, None, [R_pA], [R_dst])
                            continue
                        for kc in range(8):
                            k.mm(pB[:, 0:N], ws_[:, kc, dk * 128:(dk + 1) * 128], uT[:, kc, 0:N], kc == 0, kc == 7,
                                 [R_w, R_uT], [R_pB], signal=(kc == 7))
                        b = dk
                        k.stt("dve", t1[b][:, 0:N], pA[:, 0:N], scl, rc[:, dk, 0:N], ALU.mult, ALU.mult, [R_pA, R_rope], [R_t1[b]])
                        k.stt("dve", t2[b][:, 0:N], pB[:, 0:N], scl, rs[:, dk, 0:N], ALU.mult, ALU.mult, [R_pB, R_rope], [R_t2[b]])
                        k.tt("pool", dst[:, dk, t0:t0 + N], t1[b][:, 0:N], t2[b][:, 0:N], ALU.add, [R_t1[b], R_t2[b]], [R_dst])
                for sbi in range(N // 128):
                    c = t0 // 128 + sbi
                    for kc in range(8):
                        k.mm(pg[:, :], uT[:, kc, sbi * 128:(sbi + 1) * 128], wv[:, kc, :], kc == 0, kc == 7,
                             [R_w, R_uT], [R_pg], signal=(kc == 7))
                    k.copy("act", vtm[:, c, :], pg[:, :], [R_pg], [R_v[c]])
            tasks = []
            for lc in range(NCHK):
                blocks = []
                for mc in range(NCHK):
                    terms = []
                    if mc == lc:
                        terms = ["diag"]
                    else:
                        if mc < lc:
                            terms.append((h, lc - mc - 1))
                        if cb(mc) > cb(lc):
                            terms.append((4 + h, cb(mc) - cb(lc) - 1))
                    if terms:
                        blocks.append((mc, terms))
                for bi, (mc, terms) in enumerate(blocks):
                    tasks.append((lc, bi, len(blocks), mc, terms))

            GB = 4
            groups_ = [tasks[i:i + GB] for i in range(0, len(tasks), GB)]

            def emit_scores(grp, gslot):
                sk = gslot % NSC
                for i_, (lc, bi, nb, mc, terms) in enumerate(grp):
                    for kc in range(2):
                        k.mm(psc[sk][:, i_ * 128:(i_ + 1) * 128], kT[:, kc, mc * 128:(mc + 1) * 128], qT[:, kc, lc * 128:(lc + 1) * 128],
                             kc == 0, kc == 1, [R_kT, R_qT], [R_psc[sk]], signal=(kc == 1 and i_ == len(grp) - 1))
                for i_, (lc, bi, nb, mc, terms) in enumerate(grp):
                    pk = (gslot * GB + i_) % NPB
                    sc_ap = psc[sk][:, i_ * 128:(i_ + 1) * 128]
                    if terms[0] == "diag":
                        k.tt("dve", Pb[pk][:, :], sc_ap, Wdiag[:, h, :], ALU.mult, [R_psc[sk], R_tab], [R_Pb[pk]])
                    elif len(terms) == 1:
                        col, n = terms[0]
                        k.stt("dve", Pb[pk][:, :], sc_ap, apow[:, col, n:n + 1], Etab[:, col, 1, :], ALU.mult, ALU.mult,
                              [R_psc[sk], R_tab], [R_Pb[pk]])
                    else:
                        for q_, (col, n) in enumerate(terms):
                            k.stt("dve", Pt[q_][:, :], sc_ap, apow[:, col, n:n + 1], Etab[:, col, 1, :], ALU.mult, ALU.mult,
                                  [R_psc[sk], R_tab], [R_Pt[q_]])
                        k.tt("dve", Pb[pk][:, :], Pt[0][:, :], Pt[1][:, :], ALU.add, [R_Pt[0], R_Pt[1]], [R_Pb[pk]])

            DLOOK = 3
            pending_fin = []

            def flush_fin(now):
                while pending_fin and pending_fin[0][0] <= now:
                    _, lc_ = pending_fin.pop(0)
                    zb_ = zbs[lc_ % 2]
                    for ec in range(4):
                        k.tr(pt[:, ec * 128:(ec + 1) * 128], zb_[:, ec * 128:(ec + 1) * 128], identb[:, :], [R_zbs[lc_ % 2], R_tab], [R_pt])
                    k.copy("act", zTc[:, :, :], pt[:, :].rearrange("p (a b) -> p a b", b=128), [R_pt], [R_zTc])
                    k.dma("sp", io["zT_d"][h * 512:(h + 1) * 512, lc_ * 128:(lc_ + 1) * 128].rearrange("(c p) t -> p c t", p=128),
                          zTc[:, :, :], [R_zTc], [R_zTd[lc_]])

            gbase = nblk
            pv_list = []
            for gi_ in range(len(groups_) + DLOOK):
                if gi_ < len(groups_):
                    emit_scores(groups_[gi_], gbase + gi_)
                if gi_ - DLOOK >= 0:
                    for i_, tk in enumerate(groups_[gi_ - DLOOK]):
                        pv_list.append((tk, ((gbase + gi_ - DLOOK) * GB + i_) % NPB, gi_))
                while pv_list:
                    (lc, bi, nb, mc, terms), pk, idx = pv_list.pop(0)
                    flush_fin(idx)
                    k.mm(po[:, :], Pb[pk][:, :], vtm[:, mc, :], bi == 0, bi == nb - 1, [R_Pb[pk], R_v[mc]], [R_po])
                    if bi != nb - 1:
                        continue
                    k.dma("sp", uTc[:, :, :], io["uT_d"][:, lc * 128:(lc + 1) * 128].rearrange("(c p) t -> p c t", p=128),
                          [R_uTd[0 if lc < 2 else 1 + (lc - 2) // 4]], [R_uTc])
                    for kc in range(8):
                        k.mm(pg[:, :], uTc[:, kc, :], wg[:, kc, :], kc == 0, kc == 7, [R_w, R_uTc], [R_pg], signal=(kc == 7))
                    k.act(sgt[:, :], pg[:, :], AF.Silu, [R_pg], [R_sgt])
                    k.copy("act", osb[:, :], po[:, :], [R_po], [R_osb])
                    P.op("dve", lambda e: e.reduce_sum(out=st[:, 0:1], in_=osb[:, :], axis=AX.X), [R_osb], [R_st])
                    k.ts("dve", st[:, 0:1], st[:, 0:1], 1.0 / 512, None, ALU.mult, None, [R_st], [R_st])
                    k.ts("dve", osb[:, :], osb[:, :], st[:, 0:1], None, ALU.subtract, None, [R_st, R_osb], [R_osb])
                    k.tt("dve", sqj[:, :], osb[:, :], osb[:, :], ALU.mult, [R_osb], [R_sqj])
                    P.op("dve", lambda e: e.reduce_sum(out=st[:, 1:2], in_=sqj[:, :], axis=AX.X), [R_sqj], [R_st])
                    k.ts("dve", st[:, 1:2], st[:, 1:2], 1.0 / 512, EPS, ALU.mult, ALU.add, [R_st], [R_st])
                    k.act(st[:, 1:2], st[:, 1:2], AF.Sqrt, [R_st], [R_st])
                    P.op("dve", lambda e: e.reciprocal(out=st[:, 2:3], in_=st[:, 1:2]), [R_st], [R_st])
                    k.stt("dve", zbs[lc % 2][:, :], osb[:, :], st[:, 2:3], sgt[:, :], ALU.mult, ALU.mult, [R_osb, R_st, R_sgt], [R_zbs[lc % 2]])
                    pending_fin.append((idx + 3, lc))
            flush_fin(10 ** 9)
            nblk += len(groups_) + DLOOK
    P.barrier()
    with ExitStack() as es:
        sb = lambda name, shape, dt: es.enter_context(nc.sbuf_tensor(k.uname(name), shape, dt))
        ps = lambda name, shape, dt=F32: es.enter_context(nc.psum_tensor(k.uname(name), shape, dt))
        resadd = mk_resadd(k, io, sb, li)
        pA = ps("r_pA2", [128, 512])
        pB = ps("r_pB2", [128, 512])
        R_pA, R_pB = R("pA2"), R("pB2")
        wo = sb("r_wo", [128, 16, D], BF16)
        zt = sb("r_zt", [128, 16, 512], BF16)
        R_wo, R_zt = R("r_wo"), R("r_zt")
        k.dma("pool", wo[:, :, :], io["ret_w_out"].rearrange("(c p) n -> p c n", p=128), [], [R_wo])
        for ti in range(len(TILES)):
            t0, N, isctx = TILES[ti]
            k.dma("sp", zt[:, :, 0:N], io["zT_d"][:, t0:t0 + N].rearrange("(c p) t -> p c t", p=128),
                  R_zTd[t0 // 128:(t0 + N) // 128], [R_zt])
            for j in range(8):
                pp, Rp = (pA, R_pA) if j % 2 == 0 else (pB, R_pB)
                for ec in range(16):
                    k.mm(pp[:, 0:N], wo[:, ec, j * 128:(j + 1) * 128], zt[:, ec, 0:N], ec == 0, ec == 15, [R_wo, R_zt], [Rp],
                         signal=(ec == 15))
                resadd(ti, j, 0, N, pp[:, 0:N], Rp)
    P.barrier()


MIXERS[1] = ret_mixer


TB = 64
NBLK = T // TB


def declare_s5_consts(nc, io):
    io["s5_rmask"] = nc.dram_tensor("s5_rmask", [128, 4], F32, kind="ExternalInput").ap()
    io["s5_emask"] = nc.dram_tensor("s5_emask", [128, 2], F32, kind="ExternalInput").ap()
    io["s5_cmask"] = nc.dram_tensor("s5_cmask", [128, 4 * 128], F32, kind="ExternalInput").ap()
    io["yf_d"] = nc.dram_tensor("yf_d", [2, D, T], F32).ap()


def s5_host_consts():
    r = np.arange(128)
    rmask = np.stack([(r // 32 == q4) for q4 in range(4)], axis=1).astype(np.float32)
    emask = np.stack([((r // 16) % 2 == e) for e in range(2)], axis=1).astype(np.float32)
    cm = np.zeros((128, 4, 128), np.float32)
    for q4 in range(4):
        cm[:, q4, 32 * q4:32 * q4 + 32] = 1.0
    return {"s5_rmask": rmask, "s5_emask": emask, "s5_cmask": cm.reshape(128, 512)}


def s5_mixer(k, io, li):
    nc, P = k.nc, k.P
    from contextlib import ExitStack
    R = lambda n: Res(n)
    PI = math.pi
    with ExitStack() as es:
        sb = lambda name, shape, dt: es.enter_context(nc.sbuf_tensor(k.uname(name), shape, dt))
        ps = lambda name, shape, dt=F32: es.enter_context(nc.psum_tensor(k.uname(name), shape, dt))
        R_t = R("s5tab")
        R_uTd = R("s5_uTd")
        with ExitStack() as es0:
            sb0 = lambda name, shape, dt: es0.enter_context(nc.sbuf_tensor(k.uname(name), shape, dt))
            ps0 = lambda name, shape, dt=F32: es0.enter_context(nc.psum_tensor(k.uname(name), shape, dt))
            modnorm = mk_modnorm(k, io, sb0, ps0)
            uT = sb0("s5_uT", [128, 8, 512], BF16)
            R_uT = R("s5_uT")
            for ti in range(len(TILES)):
                t0, N, _ = TILES[ti]
                modnorm(ti, li, 1, lambda j: uT[:, j, 0:N], R_uT)
                k.dma("sp", io["uT_d"][:, t0:t0 + N].rearrange("(c p) t -> p c t", p=128), uT[:, :, 0:N], [R_uT], [R_uTd])
        P.barrier()
        pT = ps("s5_pT", [128, 512])
        R_pT = R("s5_pT")
        rmask = sb("s5_rmask", [128, 4], F32)
        emask = sb("s5_emask", [128, 2], F32)
        cmask = sb("s5_cmask", [128, 4, 128], F32)
        k.dma("sp", rmask[:, :], io["s5_rmask"], [], [R_t])
        k.dma("sp", emask[:, :], io["s5_emask"], [], [R_t])
        k.dma("sp", cmask[:, :, :], io["s5_cmask"].rearrange("p (a b) -> p a b", b=128), [], [R_t])
        lst = sb("s5_lst", [64, 2, 128], F32)
        k.dma("sp", lst[:, 0, :], io["s5_lam_re"].rearrange("d (q e) p -> (d q) (e p)", e=2), [], [R_t])
        k.dma("sp", lst[:, 1, :], io["s5_lam_im"].rearrange("d (q e) p -> (d q) (e p)", e=2), [], [R_t])
        lam = sb("s5_lam", [128, 2, 64], F32)
        for w in range(2):
            k.tr(pT[:, 0:64], lst[:, w, :], k.ident[0:64, 0:64], [R_t, k.R_const], [R_pT])
            k.copy("dve", lam[:, w, :], pT[:, 0:64], [R_pT], [R_t])
        ones1 = sb("s5_ones1", [1, 128], F32)
        ldt1 = sb("s5_ldt1", [1, 128], F32)
        k.memset("dve", ones1[:, :], 1.0, [R_t])
        k.dma("sp", ldt1[:, :], io["s5_log_dt"].rearrange("(o d) g -> o (d g)", o=1), [], [R_t])
        k.mm(pT[:, 0:128], ones1[0:1, :], ldt1[0:1, :], True, True, [R_t], [R_pT])
        dt = sb("s5_dt", [128, 64], F32)
        bc = pT[:, 0:128].rearrange("p (d q e) -> p d q e", d=2, e=2)
        dt3 = dt[:, :].rearrange("p (d q) -> p d q", d=2)
        k.act(dt3[0:64], bc[0:64, :, :, 0], AF.Exp, [R_pT], [R_t])
        k.act(dt3[64:128], bc[64:128, :, :, 1], AF.Exp, [R_pT], [R_t])
        sm = lambda n: sb("s5_" + n, [128, 64], F32)
        mag, ang, ar, ai, tmpa, tmpb, sg_, den, cfr, cfi = [sm(n) for n in
                                                             "mag ang ar ai tmpa tmpb sg den cfr cfi".split()]
        k.tt("dve", mag[:, :], lam[:, 0, :], dt[:, :], ALU.mult, [R_t], [R_t])
        k.act(mag[:, :], mag[:, :], AF.Exp, [R_t], [R_t])
        k.tt("dve", ang[:, :], lam[:, 1, :], dt[:, :], ALU.mult, [R_t], [R_t])

        def sin_of(dst, shift):
            k.ts("dve", tmpa[:, :], ang[:, :], shift - 4 * PI, None, ALU.add, None, [R_t], [R_t])
            for thr in (PI, 3 * PI, 5 * PI, 7 * PI):
                k.ts("dve", tmpb[:, :], ang[:, :], shift - thr, None, ALU.add, None, [R_t], [R_t])
                k.act(sg_[:, :], tmpb[:, :], AF.Sign, [R_t], [R_t])
                k.stt("dve", tmpa[:, :], sg_[:, :], -PI, tmpa[:, :], ALU.mult, ALU.add, [R_t], [R_t])
            k.act(dst, tmpa[:, :], AF.Sin, [R_t], [R_t])

        sin_of(ai[:, :], 0.0)
        sin_of(ar[:, :], PI / 2)
        k.tt("dve", ar[:, :], ar[:, :], mag[:, :], ALU.mult, [R_t], [R_t])
        k.tt("dve", ai[:, :], ai[:, :], mag[:, :], ALU.mult, [R_t], [R_t])
        k.tt("dve", den[:, :], lam[:, 0, :], lam[:, 0, :], ALU.mult, [R_t], [R_t])
        k.tt("dve", tmpa[:, :], lam[:, 1, :], lam[:, 1, :], ALU.mult, [R_t], [R_t])
        k.tt("dve", den[:, :], den[:, :], tmpa[:, :], ALU.add, [R_t], [R_t])
        P.op("dve", lambda e: e.reciprocal(out=den[:, :], in_=den[:, :]), [R_t], [R_t])
        k.ts("dve", tmpb[:, :], ar[:, :], -1.0, None, ALU.add, None, [R_t], [R_t])
        k.tt("dve", cfr[:, :], tmpb[:, :], lam[:, 0, :], ALU.mult, [R_t], [R_t])
        k.tt("dve", tmpa[:, :], ai[:, :], lam[:, 1, :], ALU.mult, [R_t], [R_t])
        k.tt("dve", cfr[:, :], cfr[:, :], tmpa[:, :], ALU.add, [R_t], [R_t])
        k.tt("dve", cfr[:, :], cfr[:, :], den[:, :], ALU.mult, [R_t], [R_t])
        k.tt("dve", cfi[:, :], ai[:, :], lam[:, 0, :], ALU.mult, [R_t], [R_t])
        k.tt("dve", tmpa[:, :], tmpb[:, :], lam[:, 1, :], ALU.mult, [R_t], [R_t])
        k.tt("dve", cfi[:, :], cfi[:, :], tmpa[:, :], ALU.subtract, [R_t], [R_t])
        k.tt("dve", cfi[:, :], cfi[:, :], den[:, :], ALU.mult, [R_t], [R_t])
        Wt = sb("s5_Wt", [128, 2, 32, 2, 128], BF16)
        Ct = sb("s5_Ct", [128, 2, 32, 2, 128], BF16)
        with ExitStack() as es2:
            sb2 = lambda name, shape, dt_: es2.enter_context(nc.sbuf_tensor(k.uname(name), shape, dt_))
            Bn = sb2("s5_Bn", [128, 2, 2, 32, 16], F32)
            Bb = sb2("s5_Bb", [128, 2, 2, 32, 16], F32)
            for w, nm in enumerate(("s5_b_re", "s5_b_im")):
                for d in range(2):
                    k.dma("sp", Bn[:, w, d, :, :], io[nm][d].rearrange("(q e) p c -> (e p) q c", e=2), [], [R_t])
            t16 = sb2("s5_t16", [128, 64], F32)
            cf3r, cf3i = cfr[:, :], cfi[:, :]
            for ci in range(16):
                bre = Bn[:, 0, :, :, ci].rearrange("p d q -> p (d q)")
                bim = Bn[:, 1, :, :, ci].rearrange("p d q -> p (d q)")
                ore = Bb[:, 0, :, :, ci].rearrange("p d q -> p (d q)")
                oim = Bb[:, 1, :, :, ci].rearrange("p d q -> p (d q)")
                k.tt("dve", ore, cf3r, bre, ALU.mult, [R_t], [R_t])
                k.tt("dve", t16[:, :], cf3i, bim, ALU.mult, [R_t], [R_t])
                k.tt("dve", ore, ore, t16[:, :], ALU.subtract, [R_t], [R_t])
                k.tt("dve", oim, cf3r, bim, ALU.mult, [R_t], [R_t])
                k.tt("dve", t16[:, :], cf3i, bre, ALU.mult, [R_t], [R_t])
                k.tt("dve", oim, oim, t16[:, :], ALU.add, [R_t], [R_t])
            Nn = sb2("s5_Nn", [128, 4, 2, 16], F32)
            k.memset("dve", Nn[:, :, :, :], 0.0, [R_t])
            for d in range(2):
                for w in range(2):
                    for j in range(8):
                        k.copy("dve", Nn[0:64, :, 0, :], Bb[0:64, w, d, 4 * j:4 * j + 4, :], [R_t, R_pT], [R_t])
                        k.copy("dve", Nn[64:128, :, 1, :], Bb[64:128, w, d, 4 * j:4 * j + 4, :], [R_t], [R_t])
                        k.tr(pT[:, 0:128], Nn[:, :, :, :].rearrange("p a b c -> p (a b c)"), k.ident[:, :], [R_t, k.R_const], [R_pT])
                        for q4 in range(4):
                            k.ts("dve", Wt[:, d, 4 * j + q4, w, :], pT[:, 0:128], rmask[:, q4:q4 + 1], None, ALU.mult, None,
                                 [R_pT, R_t], [R_t])
            Cn = sb2("s5_Cn", [128, 2, 2, 8, 64], F32)
            for w, nm in enumerate(("s5_c_re", "s5_c_im")):
                for d in range(2):
                    k.dma("sp", Cn[:, w, d, :, :], io[nm][d].rearrange("(j g8) co p -> (g8 co) j p", g8=8), [], [R_t])
            Cexp = sb2("s5_Cexp", [128, 128], F32)
            for d in range(2):
                for w in range(2):
                    for j in range(8):
                        for e in range(2):
                            k.ts("dve", Cexp[:, e * 64:(e + 1) * 64], Cn[:, w, d, j, :], emask[:, e:e + 1], None, ALU.mult, None,
                                 [R_t, R_pT], [R_t])
                        k.tr(pT[:, 0:128], Cexp[:, :], k.ident[:, :], [R_t, k.R_const], [R_pT])
                        for q4 in range(4):
                            k.stt("dve", Ct[:, d, 4 * j + q4, w, :], pT[:, 0:128], (1.0 if w == 0 else -1.0), cmask[:, q4, :],
                                  ALU.mult, ALU.mult, [R_pT, R_t], [R_t])
        Ctab = sb("s5_Ctab", [128, 2, 32, TB], F32)
        Stab = sb("s5_Stab", [128, 2, 32, TB], F32)
        rt = sb("s5_rt", [128, 2, 32, TB], F32)
        w64 = sb("s5_w64", [128, 2, 64], F32)
        cs1, sn1, er, ei_, e2r, e2i = [sm(n) for n in "cs1 sn1 er ei e2r e2i".split()]
        P.op("dve", lambda e: e.reciprocal(out=tmpa[:, :], in_=mag[:, :]), [R_t], [R_t])
        k.tt("dve", cs1[:, :], ar[:, :], tmpa[:, :], ALU.mult, [R_t], [R_t])
        k.tt("dve", sn1[:, :], ai[:, :], tmpa[:, :], ALU.mult, [R_t], [R_t])
        k.memset("dve", er[:, :], 1.0, [R_t])
        k.memset("dve", ei_[:, :], 0.0, [R_t])
        dq = lambda t2d: t2d.rearrange("p (d q) -> p d q", d=2)
        for tp in range(TB):
            for d in range(2):
                col = tp if d == 0 else TB - 1 - tp
                k.copy("pool", Ctab[:, d, :, col], dq(er[:, :])[:, d, :], [R_t], [R_t])
                k.copy("pool", Stab[:, d, :, col], dq(ei_[:, :])[:, d, :], [R_t], [R_t])
                if tp == 0:
                    k.memset("pool", rt[:, d, :, col], 0.0, [R_t])
                else:
                    k.copy("pool", rt[:, d, :, col], dq(mag[:, :])[:, d, :], [R_t], [R_t])
            k.tt("dve", e2r[:, :], er[:, :], cs1[:, :], ALU.mult, [R_t], [R_t])
            k.tt("dve", tmpa[:, :], ei_[:, :], sn1[:, :], ALU.mult, [R_t], [R_t])
            k.tt("dve", e2r[:, :], e2r[:, :], tmpa[:, :], ALU.subtract, [R_t], [R_t])
            k.tt("dve", e2i[:, :], er[:, :], sn1[:, :], ALU.mult, [R_t], [R_t])
            k.tt("dve", tmpa[:, :], ei_[:, :], cs1[:, :], ALU.mult, [R_t], [R_t])
            k.tt("dve", ei_[:, :], e2i[:, :], tmpa[:, :], ALU.add, [R_t], [R_t])
            k.copy("dve", er[:, :], e2r[:, :], [R_t], [R_t])
        k.tt("dve", w64[:, 0, :], er[:, :], mag[:, :], ALU.mult, [R_t], [R_t])
        k.tt("dve", w64[:, 1, :], ei_[:, :], mag[:, :], ALU.mult, [R_t], [R_t])
        fl = lambda t3: t3.rearrange("p q t -> p (q t)")
        fl4 = lambda t4: t4.rearrange("p d q t -> p (d q t)")
        ub = [sb("s5_ub%d" % d, [128, 8, TB], BF16) for d in range(2)]
        Vr = sb("s5_Vr", [128, 2, 32, TB], F32)
        Vi = sb("s5_Vi", [128, 2, 32, TB], F32)
        T1 = sb("s5_T1", [128, 2, 32, TB], F32)
        T2 = sb("s5_T2", [128, 2, 32, TB], F32)
        Xb = sb("s5_Xb", [128, 2, 32, TB], BF16)
        zc = sb("s5_zc", [128, 2, 2, 32], F32)
        cw = sb("s5_cw", [128, 4, 2, 32], F32)
        ysb = sb("s5_ysb", [128, 8, TB], F32)
        pv = [[ps("s5_pv%d_%d" % (d, i), [128, 512]) for i in range(2)] for d in range(2)]
        pyy = [ps("s5_py%d" % d, [128, 512]) for d in range(2)]
        R_ub = [R("ub0"), R("ub1")]
        R_Vr, R_Vi, R_T1, R_T2, R_Xb, R_zc, R_cw, R_ys = [R(n) for n in "Vr Vi T1 T2 Xb zc cw ys".split()]
        R_pv = [[R("pv%d%d" % (d, i)) for i in range(2)] for d in range(2)]
        R_pyy = [R("pyy0"), R("pyy1")]
        R_yd = [R("yd0"), R("yd1")]
        k.memset("dve", zc[:, :, :, :], 0.0, [R_zc])
        order = [list(range(NBLK)), [3, 2, 1, 0] + list(range(NBLK - 1, 3, -1))]
        C4, S4 = fl4(Ctab[:, :, :, :]), fl4(Stab[:, :, :, :])
        vr, vi, t1, t2 = fl4(Vr[:, :, :, :]), fl4(Vi[:, :, :, :]), fl4(T1[:, :, :, :]), fl4(T2[:, :, :, :])
        w4 = w64[:, :, :].rearrange("p r (d q) -> p r d q", d=2)
        FIRST = (0, TB - 1)
        LAST = (TB - 1, 0)
        for it in range(NBLK):
            for d in range(2):
                b = order[d][it]
                c0 = b * TB
                k.dma("sp", ub[d][:, :, :], io["uT_d"][:, c0:c0 + TB].rearrange("(c p) t -> p c t", p=128), [R_uTd], [R_ub[d]])
                nev = 0
                for w, (Vd, Rv) in enumerate(((Vr, R_Vr), (Vi, R_Vi))):
                    for q8 in range(4):
                        bk = nev % 2
                        nev += 1
                        for qq in range(8):
                            q = q8 * 8 + qq
                            k.mm(pv[d][bk][:, qq * TB:(qq + 1) * TB], Wt[:, d, q, w, :], ub[d][:, q // 4, :], True, True,
                                 [R_t, R_ub[d]], [R_pv[d][bk]], signal=(qq == 7))
                        k.copy("act", Vd[:, d, q8 * 8:(q8 + 1) * 8, :],
                               pv[d][bk][:, :].rearrange("p (a b) -> p a b", b=TB), [R_pv[d][bk]], [Rv])
            k.tt("dve", t1, vr, C4, ALU.mult, [R_Vr, R_t], [R_T1])
            k.tt("dve", t2, vi, S4, ALU.mult, [R_Vi, R_t], [R_T2])
            k.tt("dve", vr, vr, S4, ALU.mult, [R_Vr, R_t], [R_Vr])
            k.tt("dve", vi, vi, C4, ALU.mult, [R_Vi, R_t], [R_Vi])
            k.tt("dve", t1, t1, t2, ALU.add, [R_T1, R_T2], [R_T1])
            k.tt("dve", vi, vi, vr, ALU.subtract, [R_Vi, R_Vr], [R_Vi])
            k.tt("dve", cw[:, 0, :, :], w4[:, 0, :, :], zc[:, 0, :, :], ALU.mult, [R_zc, R_t], [R_cw])
            k.tt("dve", cw[:, 1, :, :], w4[:, 1, :, :], zc[:, 1, :, :], ALU.mult, [R_zc, R_t], [R_cw])
            k.tt("dve", cw[:, 2, :, :], w4[:, 1, :, :], zc[:, 0, :, :], ALU.mult, [R_zc, R_t], [R_cw])
            k.tt("dve", cw[:, 3, :, :], w4[:, 0, :, :], zc[:, 1, :, :], ALU.mult, [R_zc, R_t], [R_cw])
            k.tt("dve", cw[:, 0, :, :], cw[:, 0, :, :], cw[:, 1, :, :], ALU.subtract, [R_cw], [R_cw])
            k.tt("dve", cw[:, 2, :, :], cw[:, 2, :, :], cw[:, 3, :, :], ALU.add, [R_cw], [R_cw])
            for d in range(2):
                k.tt("dve", T1[:, d, :, FIRST[d]], T1[:, d, :, FIRST[d]], cw[:, 0, d, :], ALU.add, [R_cw, R_T1], [R_T1])
                k.tt("dve", Vi[:, d, :, FIRST[d]], Vi[:, d, :, FIRST[d]], cw[:, 2, d, :], ALU.add, [R_cw, R_Vi], [R_Vi])
            for d in range(2):
                rv = (lambda a_: a_) if d == 0 else (lambda a_: a_[:, ::-1])
                rt2 = fl(rt[:, d, :, :])
                for (o_, i_, Ro, Ri) in ((T2, T1, R_T2, R_T1), (Vr, Vi, R_Vr, R_Vi)):
                    P.op("dve", (lambda o, a0, a1: (lambda e: e.tensor_tensor_scan(out=o, data0=a0, data1=a1, initial=0.0,
                                                                                     op0=ALU.mult, op1=ALU.add)))(
                        rv(fl(o_[:, d, :, :])), rv(rt2), rv(fl(i_[:, d, :, :]))), [Ri, R_t], [Ro])
            for d in range(2):
                k.copy("dve", zc[:, 0, d, :], T2[:, d, :, LAST[d]], [R_T2], [R_zc])
                k.copy("dve", zc[:, 1, d, :], Vr[:, d, :, LAST[d]], [R_Vr], [R_zc])
            k.tt("dve", t1, t2, C4, ALU.mult, [R_T2, R_t], [R_T1])
            k.tt("dve", vi, vr, S4, ALU.mult, [R_Vr, R_t], [R_Vi])
            k.tt("dve", t2, t2, S4, ALU.mult, [R_T2, R_t], [R_T2])
            k.tt("dve", vr, vr, C4, ALU.mult, [R_Vr, R_t], [R_Vr])
            for d in range(2):
                c0 = order[d][it] * TB
                k.tt("dve", Xb[:, 0, :, :], T1[:, d, :, :], Vi[:, d, :, :], ALU.subtract, [R_T1, R_Vi], [R_Xb])
                k.tt("dve", Xb[:, 1, :, :], Vr[:, d, :, :], T2[:, d, :, :], ALU.add, [R_Vr, R_T2], [R_Xb])
                for j in range(8):
                    n_ = 0
                    for q4 in range(4):
                        for w in range(2):
                            k.mm(pyy[d][:, j * TB:(j + 1) * TB], Ct[:, d, 4 * j + q4, w, :], Xb[:, w, 4 * j + q4, :], n_ == 0, n_ == 7,
                                 [R_t, R_Xb], [R_pyy[d]], signal=(n_ == 7))
                            n_ += 1
                k.copy("act", ysb[:, :, :], pyy[d][:, :].rearrange("p (a b) -> p a b", b=TB), [R_pyy[d]], [R_ys])
                k.dma("sp", io["yf_d"][d, :, c0:c0 + TB].rearrange("(c p) t -> p c t", p=128), ysb[:, :, :], [R_ys], [R_yd[d]])
    P.barrier()
    with ExitStack() as es:
        sb = lambda name, shape, dt: es.enter_context(nc.sbuf_tensor(k.uname(name), shape, dt))
        ps = lambda name, shape, dt=F32: es.enter_context(nc.psum_tensor(k.uname(name), shape, dt))
        resadd = mk_resadd(k, io, sb, li)
        wgl = sb("s5_wgl", [128, 8, 2 * D], BF16)
        R_wgl = R("wgl")
        for hh in range(4):
            k.dma("pool", wgl[:, :, hh * 512:(hh + 1) * 512],
                  io["s5_w_glu"].rearrange("(c p) n -> p c n", p=128)[:, :, hh * 512:(hh + 1) * 512], [], [R_wgl])
        yf = sb("s5_yf", [128, 8, 512], F32)
        yb = sb("s5_yb", [128, 8, 512], F32)
        uu = sb("s5_uu", [128, 8, 512], BF16)
        ge = sb("s5_ge", [128, 8, 512], BF16)
        w1 = sb("s5_w1", [128, 512], F32)
        w2 = sb("s5_w2", [128, 512], F32)
        sig = sb("s5_sig", [128, 512], F32)
        oo = sb("s5_oo", [128, 512], F32)
        R_yf, R_yb, R_uu, R_ge, R_w1, R_w2, R_sig, R_oo = [R(n) for n in "yf yb uu ge w1 w2 sig oo".split()]
        pa = ps("s5_pa", [128, 512])
        pb = ps("s5_pb", [128, 512])
        R_pa, R_pb = R("s5pa"), R("s5pb")
        for ti in range(len(TILES)):
            t0, N, isctx = TILES[ti]
            k.dma("sp", yf[:, :, 0:N], io["yf_d"][0, :, t0:t0 + N].rearrange("(c p) t -> p c t", p=128), [], [R_yf])
            k.dma("sp", yb[:, :, 0:N], io["yf_d"][1, :, t0:t0 + N].rearrange("(c p) t -> p c t", p=128), [], [R_yb])
            k.dma("sp", uu[:, :, 0:N], io["uT_d"][:, t0:t0 + N].rearrange("(c p) t -> p c t", p=128), [], [R_uu])
            for j in range(8):
                k.tt("dve", w1[:, 0:N], yf[:, j, 0:N], yb[:, j, 0:N], ALU.add, [R_yf, R_yb], [R_w1])
                k.stt("dve", w1[:, 0:N], uu[:, j, 0:N], k.vecT[:, VC_S5D + j:VC_S5D + j + 1], w1[:, 0:N], ALU.mult, ALU.add,
                      [R_uu, R_w1, k.R_vec], [R_w1])
                k.tt("dve", w2[:, 0:N], w1[:, 0:N], w1[:, 0:N], ALU.mult, [R_w1], [R_w2])
                k.ts("dve", w2[:, 0:N], w2[:, 0:N], 0.044715, 1.0, ALU.mult, ALU.add, [R_w2], [R_w2])
                k.tt("dve", w2[:, 0:N], w2[:, 0:N], w1[:, 0:N], ALU.mult, [R_w2, R_w1], [R_w2])
                k.act(w2[:, 0:N], w2[:, 0:N], AF.Sigmoid, [R_w2], [R_w2], scale=1.5957691216057308)
                k.tt("dve", ge[:, j, 0:N], w2[:, 0:N], w1[:, 0:N], ALU.mult, [R_w2, R_w1], [R_ge])
            for oc in range(8):
                for kc in range(8):
                    k.mm(pa[:, 0:N], wgl[:, kc, oc * 128:(oc + 1) * 128], ge[:, kc, 0:N], kc == 0, kc == 7, [R_wgl, R_ge], [R_pa],
                         signal=(kc == 7))
                for kc in range(8):
                    k.mm(pb[:, 0:N], wgl[:, kc, D + oc * 128:D + (oc + 1) * 128], ge[:, kc, 0:N], kc == 0, kc == 7, [R_wgl, R_ge],
                         [R_pb], signal=(kc == 7))
                k.act(sig[:, 0:N], pb[:, 0:N], AF.Sigmoid, [R_pb], [R_sig])
                k.tt("dve", oo[:, 0:N], pa[:, 0:N], sig[:, 0:N], ALU.mult, [R_pa, R_sig], [R_oo])
                resadd(ti, oc, 0, N, oo[:, 0:N], R_oo)
    P.barrier()


MIXERS[2] = s5_mixer
```

```python
import math
import numpy as np
import concourse.bass as bass
import concourse.mybir as mybir
from concourse.bass_utils import run_bass_kernel_spmd

F32 = mybir.dt.float32
BF16 = mybir.dt.bfloat16
AF = mybir.ActivationFunctionType
ALU = mybir.AluOpType
AX = mybir.AxisListType


class Res:
    __slots__ = ("name", "w", "r")

    def __init__(self, name):
        self.name = name
        self.w = None
        self.r = {}


class Prog:
    ENG = ("pe", "act", "dve", "pool", "sp")

    def __init__(self, nc, n_dma=40, same_sync=True):
        self.nc = nc
        self.items = {e: [] for e in self.ENG}
        self.cnt = {e: 0 for e in self.ENG}
        self.sems = {}
        for e in ("pe", "act", "dve", "pool"):
            self.sems[e] = nc.alloc_semaphore(name="s_" + e)
        self.n_dma = n_dma
        for i in range(n_dma):
            self.sems[("d", i)] = nc.alloc_semaphore(name="d%d" % i)
        self.dma_cum = [0] * n_dma
        q = n_dma // 5
        self.dma_pool = {"sp": (0, 2 * q), "pool": (2 * q, 4 * q), "act": (4 * q, n_dma)}
        self.dma_pi = {"sp": 0, "pool": 0, "act": 0}
        self.waited = {e: {} for e in self.ENG}
        self.same_sync = same_sync
        self.nops = 0

    def _need(self, eng, tok):
        if tok is None:
            return
        key, val = tok
        if val <= 0:
            return
        if key == eng and (eng == "pe" or (not self.same_sync and eng != "pool")):
            return
        if key == eng and val > self.cnt[eng]:
            return
        if self.waited[eng].get(key, 0) >= val:
            return
        self.waited[eng][key] = val
        self.items[eng].append(("wait", key, val))

    def _deps(self, eng, reads, writes):
        for r in reads:
            self._need(eng, r.w)
        for w in writes:
            self._need(eng, w.w)
            for k, v in w.r.items():
                self._need(eng, (k, v))

    def _mark(self, tok, reads, writes):
        k, v = tok
        for r in reads:
            if r.r.get(k, 0) < v:
                r.r[k] = v
        for w in writes:
            w.w = tok
            w.r = {}

    def op(self, eng, fn, reads=(), writes=(), signal=True):
        self._deps(eng, reads, writes)
        tok = (eng, self.cnt[eng] + 1)
        if signal:
            self.cnt[eng] += 1
            self.items[eng].append(("op", fn, eng, 1))
        else:
            self.items[eng].append(("op", fn, None, 0))
        self._mark(tok, reads, writes)
        self.nops += 1

    def dma(self, eng, out, in_, reads=(), writes=(), **kw):
        lo, hi = self.dma_pool[eng]
        s = lo + self.dma_pi[eng] % (hi - lo)
        self.dma_pi[eng] += 1
        key = ("d", s)
        self._need(eng, (key, self.dma_cum[s]))
        self._deps(eng, reads, writes)
        self.dma_cum[s] += 16
        tok = (key, self.dma_cum[s])
        self.items[eng].append(("op", lambda e: e.dma_start(out=out, in_=in_, **kw), key, 16))
        self._mark(tok, reads, writes)
        self.nops += 1

    def finish(self, eng="sp"):
        for s in range(self.n_dma):
            self._need(eng, (("d", s), self.dma_cum[s]))
        for e in ("pe", "act", "dve", "pool"):
            self._need(eng, (e, self.cnt[e]))

    def emit(self):
        nc = self.nc
        sems = self.sems

        def replay(name):
            def f(eng):
                for it in self.items[name]:
                    if it[0] == "wait":
                        eng.wait_ge(sems[it[1]], it[2])
                    else:
                        ins = it[1](eng)
                        if it[2] is not None:
                            ins.then_inc(sems[it[2]], it[3])
            return f

        with nc.Block() as block:
            block.tensor(replay("pe"))
            block.scalar(replay("act"))
            block.vector(replay("dve"))
            block.gpsimd(replay("pool"))
            block.sync(replay("sp"))


D = 1024
NCH = 8
SEQ = 4096
CTX = 256
T = SEQ + CTX
DEPTH = 4
NMOD = 9
DFF = 2816
FCH = DFF // 128
EPS = 1e-6
TILES = [(0, 256, True)] + [(256 + 512 * i, 512, False) for i in range(8)]


class K:
    def __init__(self, nc, P):
        self.nc = nc
        self.P = P
        self._uid = 0

    def uname(self, name):
        self._uid += 1
        return "%s_%d" % (name, self._uid)

    def mm(self, out, lhsT, rhs, start, stop, reads, writes, signal=True):
        self.P.op("pe", lambda e: e.matmul(out, lhsT, rhs, start=start, stop=stop), reads, writes, signal)

    def tr(self, out, in_, ident, reads, writes):
        self.P.op("pe", lambda e: e.transpose(out, in_, ident), reads, writes)

    def act(self, out, in_, func, reads, writes, bias=None, scale=None):
        kw = {}
        if bias is not None:
            kw["bias"] = bias
        if scale is not None:
            kw["scale"] = scale
        self.P.op("act", lambda e: e.activation(out=out, in_=in_, func=func, **kw), reads, writes)

    def tt(self, eng, out, in0, in1, op, reads, writes):
        self.P.op(eng, lambda e: e.tensor_tensor(out=out, in0=in0, in1=in1, op=op), reads, writes)

    def ts(self, eng, out, in0, s1, s2, op0, op1, reads, writes):
        if op1 is None:
            self.P.op(eng, lambda e: e.tensor_single_scalar(out=out, in_=in0, scalar=s1, op=op0), reads, writes)
        else:
            self.P.op(eng, lambda e: e.tensor_scalar(out=out, in0=in0, scalar1=s1, scalar2=s2, op0=op0, op1=op1),
                      reads, writes)

    def stt(self, eng, out, in0, scalar, in1, op0, op1, reads, writes):
        self.P.op(eng, lambda e: e.scalar_tensor_tensor(out=out, in0=in0, scalar=scalar, in1=in1, op0=op0, op1=op1),
                  reads, writes)

    def copy(self, eng, out, in_, reads, writes):
        if eng == "act":
            self.P.op(eng, lambda e: e.copy(out=out, in_=in_), reads, writes)
        else:
            self.P.op(eng, lambda e: e.tensor_copy(out=out, in_=in_), reads, writes)

    def memset(self, eng, ap, val, writes):
        self.P.op(eng, lambda e: e.memset(ap, val), (), writes)

    def dma(self, eng, out, in_, reads, writes, **kw):
        self.P.dma(eng, out, in_, reads, writes, **kw)


def _barrier(P):
    for e in P.ENG:
        for s in range(P.n_dma):
            P._need(e, (("d", s), P.dma_cum[s]))
        for e2 in ("pe", "act", "dve", "pool"):
            if e2 != e:
                P._need(e, (e2, P.cnt[e2]))


Prog.barrier = _barrier

VC_C, VC_CC, VC_FNG, VC_S5D, VC_NG, VC_BM = 0, 8, 16, 24, 32, 128
NVEC = 416


def declare_io(nc):
    io = {}

    def inp(name, shape):
        io[name] = nc.dram_tensor(name, list(shape), F32, kind="ExternalInput").ap()

    inp("x", [SEQ, D])
    inp("c", [8, 128])
    inp("ctx", [CTX, D])
    inp("c_ctx", [8, 128])
    inp("w_mod", [DEPTH, D, NMOD * D])
    inp("b_mod", [288, 128])
    inp("norm_g", [96, 128])
    inp("ffn_w_in", [DEPTH, 2, D, 2 * DFF])
    inp("ffn_w_out", [DEPTH, 2, DFF, D])
    inp("fnet_w_out", [2, D, D])
    inp("ret_w_in", [D, 6144])
    inp("ret_w_out", [2048, D])
    inp("ret_decay_logit", [2, 4])
    inp("s5_lam_re", [2, 64, 64])
    inp("s5_lam_im", [2, 64, 64])
    inp("s5_log_dt", [2, 64])
    inp("s5_b_re", [2, 64, 64, 16])
    inp("s5_b_im", [2, 64, 64, 16])
    inp("s5_c_re", [2, 64, 16, 64])
    inp("s5_c_im", [2, 64, 16, 64])
    inp("s5_d", [8, 128])
    inp("s5_w_glu", [D, 2 * D])
    inp("final_norm_g", [8, 128])
    io["out"] = nc.dram_tensor("out", [SEQ, D], F32, kind="ExternalOutput").ap()
    io["hT"] = nc.dram_tensor("hT", [D, T], F32).ap()
    declare_fnet_consts(nc, io)
    declare_ret_consts(nc, io)
    declare_s5_consts(nc, io)
    io["uT_d"] = nc.dram_tensor("uT_d", [D, T], BF16).ap()
    return io


def alloc_consts(k, es):
    nc = k.nc
    sb = lambda name, shape, dt: es.enter_context(nc.sbuf_tensor(k.uname(name), shape, dt))
    k.ident = sb("ident", [128, 128], F32)
    k.ones_b = sb("ones_b", [128, 128], BF16)
    k.vecT = sb("vecT", [128, NVEC], F32)
    k.sc = sb("sc", [128, 8, 2], BF16)
    k.mod = sb("mod", [128, DEPTH, 72, 2], F32)
    k.A = sb("Asc", [128, DEPTH, 3, 8, 2], F32)
    k.B = sb("Bsh", [128, DEPTH, 3, 8, 2], F32)
    k.G = sb("Ggt", [128, DEPTH, 3, 8, 2], F32)
    k.R_const = Res("const")
    k.R_vec = Res("vecT")
    k.R_mod = Res("mod")
    k.R_hT = [[Res("hT%d_%d" % (i, j)) for j in range(NCH)] for i in range(len(TILES))]
    P = k.P
    k.memset("pool", k.ident[:, :], 0.0, [k.R_const])
    P.op("pool", lambda e: e.affine_select(out=k.ident[:, :], in_=k.ident[:, :], pattern=[[-1, 128]],
                                            compare_op=ALU.not_equal, fill=1.0, base=0, channel_multiplier=1),
         [k.R_const], [k.R_const])
    k.memset("dve", k.ones_b[:, :], 1.0, [k.R_const])


def prologue(k, io):
    nc, P = k.nc, k.P
    from contextlib import ExitStack
    with ExitStack() as es:
        sb = lambda name, shape, dt: es.enter_context(nc.sbuf_tensor(k.uname(name), shape, dt))
        ps = lambda name, shape: es.enter_context(nc.psum_tensor(k.uname(name), shape, F32))
        stg = [sb("vstg%d" % i, [128, 128], F32) for i in range(4)]
        R_stg = [Res("vstg%d" % i) for i in range(4)]
        tp = [ps("tp%d" % i, [128, 512]) for i in range(2)]
        R_tp = [Res("tp%d" % i) for i in range(2)]
        rows0 = [(io["c"], 8, 0), (io["c_ctx"], 8, 8), (io["final_norm_g"], 8, 16), (io["s5_d"], 8, 24),
                 (io["norm_g"], 96, 32)]
        for ap, n, r0 in rows0:
            k.dma("sp", stg[0][r0:r0 + n, :], ap, [], [R_stg[0]])
        for b, (r0, n) in enumerate([(0, 128), (128, 128), (256, 32)]):
            k.dma("sp", stg[b + 1][0:n, :], io["b_mod"][r0:r0 + n, :], [], [R_stg[b + 1]])
        col = 0
        for b, n in enumerate([128, 128, 128, 32]):
            k.tr(tp[b % 2][:, 0:n], stg[b][0:n, :], k.ident[0:n, 0:n], [R_stg[b], k.R_const], [R_tp[b % 2]])
            k.copy("dve", k.vecT[:, col:col + n], tp[b % 2][:, 0:n], [R_tp[b % 2]], [k.R_vec])
            col += n
        k.act(k.sc[:, :, 0], k.vecT[:, VC_C:VC_C + 8], AF.Silu, [k.R_vec], [k.R_mod])
        k.act(k.sc[:, :, 1], k.vecT[:, VC_CC:VC_CC + 8], AF.Silu, [k.R_vec], [k.R_mod])

        wblk = [sb("wmodblk%d" % i, [128, 8, 1024], BF16) for i in range(2)]
        R_wblk = [Res("wmodblk%d" % i) for i in range(2)]
        prep = s5_prep_gen(k, io, sb, ps)
        mps = [ps("modps%d" % i, [128, 72, 2]) for i in range(2)]
        R_mps = [Res("modps%d" % i) for i in range(2)]
        nb = 0
        for li in range(DEPTH):
            wv = io["w_mod"][li].rearrange("(c p) n -> p c n", p=128)
            for kk in range(NMOD):
                b = nb % 2
                nb += 1
                for _ in range(5):
                    next(prep, None)
                for h2 in range(2):
                    k.dma("pool", wblk[b][:, h2 * 4:(h2 + 1) * 4, :], wv[:, h2 * 4:(h2 + 1) * 4, kk * 1024:(kk + 1) * 1024],
                          [], [R_wblk[b]])
                for j in range(8):
                    for kc in range(8):
                        k.mm(mps[li % 2][:, kk * 8 + j, :], wblk[b][:, kc, j * 128:(j + 1) * 128], k.sc[:, kc, :],
                             kc == 0, kc == 7, [R_wblk[b], k.R_mod], [R_mps[li % 2]], signal=(kc == 7))
            for r in range(2):
                k.tt("dve", k.mod[:, li, :, r], mps[li % 2][:, :, r], k.vecT[:, VC_BM + li * 72:VC_BM + (li + 1) * 72],
                     ALU.add, [R_mps[li % 2], k.R_vec], [k.R_mod])
            for s in range(3):
                gcol = VC_NG + (li * 3 + s) * 8
                for r in range(2):
                    sh = k.mod[:, li, (3 * s) * 8:(3 * s) * 8 + 8, r]
                    scl = k.mod[:, li, (3 * s + 1) * 8:(3 * s + 1) * 8 + 8, r]
                    gt = k.mod[:, li, (3 * s + 2) * 8:(3 * s + 2) * 8 + 8, r]
                    k.stt("dve", k.A[:, li, s, :, r], scl, 1.0, k.vecT[:, gcol:gcol + 8], ALU.add, ALU.mult,
                          [k.R_mod, k.R_vec], [k.R_mod])
                    k.copy("dve", k.B[:, li, s, :, r], sh, [k.R_mod], [k.R_mod])
                    if s == 1:
                        k.copy("dve", k.G[:, li, s, :, r], gt, [k.R_mod], [k.R_mod])
                    else:
                        k.ts("dve", k.G[:, li, s, :, r], gt, 0.5, None, ALU.mult, None, [k.R_mod], [k.R_mod])

        for _ in prep:
            pass
    P.barrier()
    with ExitStack() as es:
        sb = lambda name, shape, dt: es.enter_context(nc.sbuf_tensor(k.uname(name), shape, dt))
        ps = lambda name, shape: es.enter_context(nc.psum_tensor(k.uname(name), shape, F32))
        tp = [ps("tpx%d" % i, [128, 512]) for i in range(2)]
        R_tp = [Res("tpx%d" % i) for i in range(2)]
        xs = [sb("xs%d" % i, [128, 4, D], F32) for i in range(2)]
        R_xs = [Res("xs%d" % i) for i in range(2)]
        hst = [sb("hst%d" % i, [128, 8, 512], F32) for i in range(2)]
        R_hst = [Res("hst%d" % i) for i in range(2)]
        ncp = 0
        for ti, (t0, N, isctx) in enumerate(TILES):
            b = ti % 2
            ns = N // 128
            src = io["ctx"] if isctx else io["x"][t0 - CTX:t0 - CTX + N, :]
            k.dma("sp", xs[b][:, 0:ns, :], src.rearrange("(s p) d -> p s d", p=128), [], [R_xs[b]])
            for j in range(8):
                pb = ncp % 2
                for s in range(ns):
                    k.tr(tp[pb][:, s * 128:(s + 1) * 128], xs[b][:, s, j * 128:(j + 1) * 128], k.ident[:, :],
                         [R_xs[b], k.R_const], [R_tp[pb]])
                k.copy("act" if ncp % 2 else "dve", hst[b][:, j, 0:N], tp[pb][:, 0:N], [R_tp[pb]], [R_hst[b]])
                ncp += 1
            k.dma("sp", io["hT"][:, t0:t0 + N].rearrange("(c p) t -> p c t", p=128), hst[b][:, :, 0:N],
                  [R_hst[b]], k.R_hT[ti])
    P.barrier()


def ffn_phase(k, io, li, half):
    nc, P = k.nc, k.P
    s = 0 if half == 0 else 2
    from contextlib import ExitStack
    with ExitStack() as es:
        sb = lambda name, shape, dt: es.enter_context(nc.sbuf_tensor(k.uname(name), shape, dt))
        ps = lambda name, shape: es.enter_context(nc.psum_tensor(k.uname(name), shape, F32))
        w_in = sb("w_in", [128, 8, 2 * DFF], BF16)
        w_out = sb("w_out", [128, FCH, D], BF16)
        hbuf = sb("hbuf", [128, 8, 512], F32)
        uT = sb("uT", [128, 8, 512], BF16)
        gT = sb("gT", [128, FCH, 512], BF16)
        rstd = sb("rstd", [128, 512], F32)
        sqb = [sb("sqb%d" % i, [128, 512], BF16) for i in range(2)]
        tmp = [sb("tmp%d" % i, [128, 512], F32) for i in range(2)]
        sg = [sb("sg%d" % i, [128, 512], F32) for i in range(2)]
        hres = [sb("hres%d" % i, [128, 512], F32) for i in range(3)]
        pa = [ps("pa%d" % i, [128, 512]) for i in range(2)]
        pb = [ps("pb%d" % i, [128, 512]) for i in range(2)]
        py = [ps("py%d" % i, [128, 512]) for i in range(2)]
        pss = ps("pss", [128, 512])
        NWB = 11
        R_win = [Res("w_in%d" % i) for i in range(2 * NWB)]
        R_wout = [Res("w_out%d" % i) for i in range(2)]
        R_hbuf, R_uT, R_rstd, R_pss = Res("hbuf"), Res("uT"), Res("rstd"), Res("pss")
        R_gT = [Res("gT%d" % i) for i in range(FCH)]
        R_sqb = [Res("sqb%d" % i) for i in range(2)]
        R_tmp = [Res("tmp%d" % i) for i in range(2)]
        R_sg = [Res("sg%d" % i) for i in range(2)]
        R_hres = [Res("hres%d" % i) for i in range(3)]
        R_pa = [Res("pa%d" % i) for i in range(2)]
        R_pb = [Res("pb%d" % i) for i in range(2)]
        R_py = [Res("py%d" % i) for i in range(2)]

        wi = io["ffn_w_in"][li, half].rearrange("(c p) n -> p c n", p=128)
        wo = io["ffn_w_out"][li, half].rearrange("(c p) n -> p c n", p=128)
        for blk in range(NWB):
            for ab in range(2):
                c0 = ab * DFF + blk * 256
                k.dma("pool", w_in[:, :, c0:c0 + 256], wi[:, :, c0:c0 + 256], [], [R_win[ab * NWB + blk]])
        for hh in range(2):
            k.dma("pool", w_out[:, hh * 11:(hh + 1) * 11, :], wo[:, hh * 11:(hh + 1) * 11, :], [], [R_wout[hh]])

        def A1(ti):
            t0, N, isctx = TILES[ti]
            k.dma("sp", hbuf[:, :, 0:N], io["hT"][:, t0:t0 + N].rearrange("(c p) t -> p c t", p=128),
                  k.R_hT[ti], [R_hbuf])
            for j in range(8):
                k.act(sqb[j % 2][:, 0:N], hbuf[:, j, 0:N], AF.Square, [R_hbuf], [R_sqb[j % 2]])
                k.mm(pss[:, 0:N], k.ones_b[:, :], sqb[j % 2][:, 0:N], j == 0, j == 7, [R_sqb[j % 2], k.R_const], [R_pss])
            k.ts("dve", rstd[:, 0:N], pss[:, 0:N], 1.0 / D, EPS, ALU.mult, ALU.add, [R_pss], [R_rstd])
            k.act(rstd[:, 0:N], rstd[:, 0:N], AF.Sqrt, [R_rstd], [R_rstd])
            P.op("dve", (lambda o, i: (lambda e: e.reciprocal(out=o, in_=i)))(rstd[:, 0:N], rstd[:, 0:N]), [R_rstd], [R_rstd])

        def A2(ti):
            t0, N, isctx = TILES[ti]
            r = 1 if isctx else 0
            for j in range(8):
                k.stt("dve", tmp[j % 2][:, 0:N], hbuf[:, j, 0:N], k.A[:, li, s, j, r:r + 1], rstd[:, 0:N], ALU.mult, ALU.mult,
                      [R_hbuf, R_rstd, k.R_mod], [R_tmp[j % 2]])
                k.act(uT[:, j, 0:N], tmp[j % 2][:, 0:N], AF.Identity, [R_tmp[j % 2], k.R_mod], [R_uT],
                      bias=k.B[:, li, s, j, r:r + 1])

        def Bst(ti, c_lo, c_hi):
            t0, N, isctx = TILES[ti]
            for c in range(c_lo, c_hi):
                bk = c % 2
                for ab, pp, Rp in ((0, pa, R_pa), (1, pb, R_pb)):
                    col = ab * DFF + c * 128
                    Rw = R_win[ab * NWB + c // 2]
                    for kc in range(8):
                        k.mm(pp[bk][:, 0:N], w_in[:, kc, col:col + 128], uT[:, kc, 0:N], kc == 0, kc == 7,
                             [Rw, R_uT], [Rp[bk]], signal=(kc == 7))
                k.act(sg[bk][:, 0:N], pa[bk][:, 0:N], AF.Silu, [R_pa[bk]], [R_sg[bk]])
                k.tt("dve", gT[:, c, 0:N], sg[bk][:, 0:N], pb[bk][:, 0:N], ALU.mult, [R_sg[bk], R_pb[bk]], [R_gT[c]])

        def Cst(ti):
            t0, N, isctx = TILES[ti]
            r = 1 if isctx else 0
            for j in range(8):
                bk = j % 2
                hb = j % 3
                k.dma("sp", hres[hb][:, 0:N], io["hT"][j * 128:(j + 1) * 128, t0:t0 + N], [k.R_hT[ti][j]], [R_hres[hb]])
                for c in range(FCH):
                    k.mm(py[bk][:, 0:N], w_out[:, c, j * 128:(j + 1) * 128], gT[:, c, 0:N], c == 0, c == FCH - 1,
                         [R_wout[c // 11], R_gT[c]], [R_py[bk]], signal=(c == FCH - 1))
                k.stt("dve", hres[hb][:, 0:N], py[bk][:, 0:N], k.G[:, li, s, j, r:r + 1], hres[hb][:, 0:N], ALU.mult, ALU.add,
                      [R_py[bk], R_hres[hb], k.R_mod], [R_hres[hb]])
                k.dma("sp", io["hT"][j * 128:(j + 1) * 128, t0:t0 + N], hres[hb][:, 0:N], [R_hres[hb]], [k.R_hT[ti][j]])

        tiles = list(range(len(TILES)))
        if k.skip_ctx(li):
            tiles = tiles[1:]
        stop = k.dbg.get("ffn_stop", "C")
        tiles = tiles[:k.dbg.get("ffn_tiles", 99)]
        if stop != "w":
            A1(tiles[0])
            A2(tiles[0])
        for n_, ti in enumerate(tiles):
            if stop in ("w", "A"):
                break
            Bst(ti, 0, 11)
            if n_ + 1 < len(tiles):
                A1(tiles[n_ + 1])
            Bst(ti, 11, FCH)
            if n_ + 1 < len(tiles):
                A2(tiles[n_ + 1])
            if stop == "B":
                continue
            Cst(ti)
    P.barrier()


def final_phase(k, io):
    nc, P = k.nc, k.P
    from contextlib import ExitStack
    with ExitStack() as es:
        sb = lambda name, shape, dt: es.enter_context(nc.sbuf_tensor(k.uname(name), shape, dt))
        ps = lambda name, shape: es.enter_context(nc.psum_tensor(k.uname(name), shape, F32))
        hbuf = [sb("fhbuf%d" % i, [128, 8, 512], F32) for i in range(2)]
        R_hbuf = [Res("fhbuf%d" % i) for i in range(2)]
        sqb = [sb("fsqb%d" % i, [128, 512], BF16) for i in range(2)]
        R_sqb = [Res("fsqb%d" % i) for i in range(2)]
        rstd = sb("frstd", [128, 512], F32)
        R_rstd = Res("frstd")
        xn = [sb("fxn%d" % i, [128, 512], F32) for i in range(2)]
        R_xn = [Res("fxn%d" % i) for i in range(2)]
        ost = [sb("fost%d" % i, [128, 4, D], F32) for i in range(2)]
        R_ost = [Res("fost%d" % i) for i in range(2)]
        pss = ps("fpss", [128, 512])
        R_pss = Res("fpss")
        tp = [ps("ftp%d" % i, [128, 512]) for i in range(2)]
        R_tp = [Res("ftp%d" % i) for i in range(2)]
        R_out = Res("out")
        ncp = 0
        for n_, ti in enumerate(range(1, len(TILES))):
            t0, N, _ = TILES[ti]
            b = n_ % 2
            k.dma("sp", hbuf[b][:, :, 0:N], io["hT"][:, t0:t0 + N].rearrange("(c p) t -> p c t", p=128),
                  k.R_hT[ti], [R_hbuf[b]])
            for j in range(8):
                k.act(sqb[j % 2][:, 0:N], hbuf[b][:, j, 0:N], AF.Square, [R_hbuf[b]], [R_sqb[j % 2]])
                k.mm(pss[:, 0:N], k.ones_b[:, :], sqb[j % 2][:, 0:N], j == 0, j == 7, [R_sqb[j % 2], k.R_const], [R_pss])
            k.ts("dve", rstd[:, 0:N], pss[:, 0:N], 1.0 / D, EPS, ALU.mult, ALU.add, [R_pss], [R_rstd])
            k.act(rstd[:, 0:N], rstd[:, 0:N], AF.Sqrt, [R_rstd], [R_rstd])
            P.op("dve", (lambda o, i: (lambda e: e.reciprocal(out=o, in_=i)))(rstd[:, 0:N], rstd[:, 0:N]), [R_rstd], [R_rstd])
            for j in range(8):
                k.stt("dve", xn[j % 2][:, 0:N], hbuf[b][:, j, 0:N], k.vecT[:, VC_FNG + j:VC_FNG + j + 1], rstd[:, 0:N],
                      ALU.mult, ALU.mult, [R_hbuf[b], R_rstd, k.R_vec], [R_xn[j % 2]])
                pbk = ncp % 2
                for s_ in range(4):
                    k.tr(tp[pbk][:, s_ * 128:(s_ + 1) * 128], xn[j % 2][:, s_ * 128:(s_ + 1) * 128], k.ident[:, :],
                         [R_xn[j % 2], k.R_const], [R_tp[pbk]])
                k.copy("act" if ncp % 2 else "dve",
                       ost[b][:, :, j * 128:(j + 1) * 128], tp[pbk][:, :].rearrange("p (s f) -> p s f", f=128),
                       [R_tp[pbk]], [R_ost[b]])
                ncp += 1
            k.dma("sp", io["out"][t0 - CTX:t0 - CTX + N, :].rearrange("(s p) d -> p s d", p=128), ost[b][:, :, :],
                  [R_ost[b]], [R_out])
    P.barrier()


def _skip_ctx(self, li):
    return li == DEPTH - 1


K.skip_ctx = _skip_ctx

MIXERS = {}


def build_program(layers=range(DEPTH), mixers=True, do_final=True, ffn=True, dbg=None):
    from contextlib import ExitStack
    nc = bass.Bass("TRN2", target_bir_lowering=False)
    io = declare_io(nc)
    P = Prog(nc, n_dma=40, same_sync=bool((dbg or {}).get("same_sync", True)))
    k = K(nc, P)
    with ExitStack() as es:
        alloc_consts(k, es)
        k.dbg = dbg or {}
        prologue(k, io)
        for li in layers:
            if ffn:
                ffn_phase(k, io, li, 0)
            if mixers and (li % 3) in MIXERS:
                MIXERS[li % 3](k, io, li)
            if ffn:
                ffn_phase(k, io, li, 1)
        if do_final:
            final_phase(k, io)
        P.finish("sp")
        P.emit()
    return nc


def make_in_maps(inputs, cores=range(8)):
    f = lambda a: np.ascontiguousarray(np.asarray(a, dtype=np.float32))
    shared = {
        "c_ctx": f(inputs["c_ctx"]).reshape(8, 128),
        "w_mod": f(inputs["w_mod"]),
        "b_mod": f(inputs["b_mod"]).reshape(288, 128),
        "norm_g": f(inputs["norm_g"]).reshape(96, 128),
        "ffn_w_in": f(inputs["ffn_w_in"]),
        "ffn_w_out": f(inputs["ffn_w_out"]),
        "fnet_w_out": f(inputs["fnet_w_out"]),
        "ret_w_in": f(inputs["ret_w_in"]).reshape(D, 6144),
        "ret_w_out": f(inputs["ret_w_out"]).reshape(2048, D),
        "ret_decay_logit": f(inputs["ret_decay_logit"]).reshape(2, 4),
        "s5_lam_re": f(inputs["s5_lam_re"]).reshape(2, 64, 64),
        "s5_lam_im": f(inputs["s5_lam_im"]).reshape(2, 64, 64),
        "s5_log_dt": f(inputs["s5_log_dt"]).reshape(2, 64),
        "s5_b_re": f(inputs["s5_b_re"]).reshape(2, 64, 64, 16),
        "s5_b_im": f(inputs["s5_b_im"]).reshape(2, 64, 64, 16),
        "s5_c_re": f(inputs["s5_c_re"]).reshape(2, 64, 16, 64),
        "s5_c_im": f(inputs["s5_c_im"]).reshape(2, 64, 16, 64),
        "s5_d": f(inputs["s5_d"]).reshape(8, 128),
        "s5_w_glu": f(inputs["s5_w_glu"]).reshape(D, 2 * D),
        "final_norm_g": f(inputs["final_norm_g"]).reshape(8, 128),
    }
    shared.update(fnet_host_consts())
    shared.update(ret_host_consts())
    shared.update(s5_host_consts())
    x, c, ctx = f(inputs["x"]), f(inputs["c"]), f(inputs["ctx"])
    maps = []
    for b in cores:
        m = dict(shared)
        m["x"] = x[b]
        m["c"] = c[b].reshape(8, 128)
        m["ctx"] = ctx[b]
        maps.append(m)
    return maps


def kernel(**inputs):
    nc = build_program()
    maps = make_in_maps(inputs)
    res = run_bass_kernel_spmd(nc, maps, core_ids=list(range(8)))
    return np.stack([np.asarray(r["out"], dtype=np.float32) for r in res.results], axis=0)


def mk_modnorm(k, io, sb, ps):
    P = k.P
    hbuf = sb("mn_hbuf", [128, 8, 512], F32)
    sqb = [sb("mn_sqb%d" % i, [128, 512], BF16) for i in range(2)]
    tmp = [sb("mn_tmp%d" % i, [128, 512], F32) for i in range(2)]
    rstd = sb("mn_rstd", [128, 512], F32)
    pss = ps("mn_pss", [128, 512])
    R_hbuf, R_rstd, R_pss = Res("mn_hbuf"), Res("mn_rstd"), Res("mn_pss")
    R_sqb = [Res("mn_sqb%d" % i) for i in range(2)]
    R_tmp = [Res("mn_tmp%d" % i) for i in range(2)]

    def do(ti, li, s, uT_of, R_uT, chunks=range(8), raw_of=None):
        t0, N, isctx = TILES[ti]
        r = 1 if isctx else 0
        k.dma("sp", hbuf[:, :, 0:N], io["hT"][:, t0:t0 + N].rearrange("(c p) t -> p c t", p=128),
              k.R_hT[ti], [R_hbuf])
        for j in range(8):
            k.act(sqb[j % 2][:, 0:N], hbuf[:, j, 0:N], AF.Square, [R_hbuf], [R_sqb[j % 2]])
            k.mm(pss[:, 0:N], k.ones_b[:, :], sqb[j % 2][:, 0:N], j == 0, j == 7, [R_sqb[j % 2], k.R_const], [R_pss])
        k.ts("dve", rstd[:, 0:N], pss[:, 0:N], 1.0 / D, EPS, ALU.mult, ALU.add, [R_pss], [R_rstd])
        k.act(rstd[:, 0:N], rstd[:, 0:N], AF.Sqrt, [R_rstd], [R_rstd])
        P.op("dve", (lambda o, i: (lambda e: e.reciprocal(out=o, in_=i)))(rstd[:, 0:N], rstd[:, 0:N]), [R_rstd], [R_rstd])
        for n_, j in enumerate(chunks):
            k.stt("dve", tmp[n_ % 2][:, 0:N], hbuf[:, j, 0:N], k.A[:, li, s, j, r:r + 1], rstd[:, 0:N], ALU.mult, ALU.mult,
                  [R_hbuf, R_rstd, k.R_mod], [R_tmp[n_ % 2]])
            k.act(uT_of(j), tmp[n_ % 2][:, 0:N], AF.Identity, [R_tmp[n_ % 2], k.R_mod], [R_uT],
                  bias=k.B[:, li, s, j, r:r + 1])

    return do


def mk_resadd(k, io, sb, li):
    hres = [sb("ra_hres%d" % i, [128, 512], F32) for i in range(3)]
    R_hres = [Res("ra_hres%d" % i) for i in range(3)]
    cnt = [0]

    def do(ti, j, col0, N, y_ap, R_y, pre_reads=(), **dkw):
        t0, _, isctx = TILES[ti]
        r = 1 if isctx else 0
        hb = cnt[0] % 3
        cnt[0] += 1
        k.dma("sp", hres[hb][:, 0:N], io["hT"][j * 128:(j + 1) * 128, t0 + col0:t0 + col0 + N], [k.R_hT[ti][j]], [R_hres[hb]], **dkw)
        k.stt("dve", hres[hb][:, 0:N], y_ap, k.G[:, li, 1, j, r:r + 1], hres[hb][:, 0:N], ALU.mult, ALU.add,
              [R_y, R_hres[hb], k.R_mod] + list(pre_reads), [R_hres[hb]])
        k.dma("sp", io["hT"][j * 128:(j + 1) * 128, t0 + col0:t0 + col0 + N], hres[hb][:, 0:N], [R_hres[hb]], [k.R_hT[ti][j]], **dkw)

    return do


def declare_fnet_consts(nc, io):
    io["dft_c"] = nc.dram_tensor("dft_c", [128, 256], BF16, kind="ExternalInput").ap()
    io["dft_cosN"] = nc.dram_tensor("dft_cosN", [SEQ, SEQ], BF16, kind="ExternalInput").ap()
    io["dft_sinN"] = nc.dram_tensor("dft_sinN", [SEQ, SEQ], BF16, kind="ExternalInput").ap()
    io["dft_cosC"] = nc.dram_tensor("dft_cosC", [CTX, CTX], BF16, kind="ExternalInput").ap()
    io["dft_sinC"] = nc.dram_tensor("dft_sinC", [CTX, CTX], BF16, kind="ExternalInput").ap()
    io["fn_perm"] = nc.dram_tensor("fn_perm", [128, 4 * 128], BF16, kind="ExternalInput").ap()


def fnet_host_consts():
    import ml_dtypes
    bf = ml_dtypes.bfloat16
    c = np.arange(128)
    ang = 2.0 * np.pi * ((c[:, None] * c[None, :]) % 128) / 128.0
    dft_c = np.concatenate([np.cos(ang), -np.sin(ang)], axis=1).astype(np.float32).astype(bf)

    def mats(n):
        i = np.arange(n, dtype=np.int64)
        idx = (i[:, None] * i[None, :]) % n
        a = 2.0 * np.pi * np.arange(n, dtype=np.float64) / n
        return np.cos(a).astype(np.float32)[idx].astype(bf), np.sin(a).astype(np.float32)[idx].astype(bf)

    cN, sN = mats(SEQ)
    cC, sC = mats(CTX)
    J = np.zeros((128, 128), np.float32)
    for p in range(1, 128):
        J[128 - p, p] = 1.0
    E = np.zeros((128, 128), np.float32)
    E[0, 0] = 1.0
    perm = np.concatenate([J, -J, E, -E], axis=1).astype(bf)
    return {"dft_c": dft_c, "dft_cosN": cN, "dft_sinN": sN, "dft_cosC": cC, "dft_sinC": sC, "fn_perm": perm}


def fnet_mixer(k, io, li):
    nc, P = k.nc, k.P
    jw = li // 3
    from contextlib import ExitStack
    with ExitStack() as es:
        sb = lambda name, shape, dt: es.enter_context(nc.sbuf_tensor(k.uname(name), shape, dt))
        ps = lambda name, shape: es.enter_context(nc.psum_tensor(k.uname(name), shape, F32))
        modnorm = mk_modnorm(k, io, sb, ps)
        resadd = mk_resadd(k, io, sb, li)
        dftc = sb("dftc", [128, 256], BF16)
        w_out = sb("fw_out", [128, 8, D], BF16)
        X = sb("fX", [128, 32, 4, 256], BF16)
        cosb = sb("fcos", [128, 32, 512], BF16)
        sinb = sb("fsin", [128, 32, 512], BF16)
        uT = sb("fuT", [128, 8, 512], BF16)
        uTg = sb("fuTg", [128, 4, 512], BF16)
        R_uTg = Res("fuTg")
        R_uTd = [Res("uTd%d" % i) for i in range(len(TILES))]
        fT = [sb("ffT%d" % i, [128, 512], BF16) for i in range(8)]
        px = [ps("fpx%d" % i, [128, 512]) for i in range(2)]
        pf = [ps("fpf%d" % i, [128, 512]) for i in range(2)]
        py = [ps("fpy%d" % i, [128, 512]) for i in range(2)]
        R_dftc, R_wout, R_uT = Res("dftc"), Res("fw_out"), Res("fuT")
        R_X = [Res("fX%d" % i) for i in range(32)]
        R_cos = [Res("fcos%d" % i) for i in range(8)]
        R_sin = [Res("fsin%d" % i) for i in range(8)]
        R_fT = [Res("ffT%d" % i) for i in range(8)]
        R_px = [Res("fpx%d" % i) for i in range(2)]
        R_pf = [Res("fpf%d" % i) for i in range(2)]
        R_py = [Res("fpy%d" % i) for i in range(2)]
        k.dma("sp", dftc[:, :], io["dft_c"], [], [R_dftc])
        k.dma("pool", w_out[:, :, :], io["fnet_w_out"][jw].rearrange("(c p) n -> p c n", p=128), [], [R_wout])
        cnt = {"x": 0, "f": 0, "y": 0}

        def run(tiles, nch, groups_list, cosN, sinN, W, norm, sweep_only=False):
            for ti in tiles:
                t0, N, isctx = TILES[ti]
                modnorm(ti, li, 1, lambda j: uT[:, j, 0:N], R_uT)
                k.dma("sp", io["uT_d"][:, t0:t0 + N].rearrange("(c p) t -> p c t", p=128), uT[:, :, 0:N], [R_uT], [R_uTd[ti]])
            if sweep_only:
                return
            for groups in groups_list:
                ng = len(groups)
                for ti in tiles:
                    t0, N, isctx = TILES[ti]
                    g0 = groups[0]
                    k.dma("sp", uTg[:, 0:ng, 0:N],
                          io["uT_d"][g0 * 128:(g0 + ng) * 128, t0:t0 + N].rearrange("(c p) t -> p c t", p=128),
                          [R_uTd[ti]], [R_uTg])
                    c0 = (t0 - TILES[tiles[0]][0]) // 128
                    for sbi in range(N // 128):
                        for g2 in range(0, ng, 2):
                            bk = cnt["x"] % 2
                            cnt["x"] += 1
                            for u_ in range(2):
                                k.mm(px[bk][:, u_ * 256:(u_ + 1) * 256], uTg[:, g2 + u_, sbi * 128:(sbi + 1) * 128], dftc[:, :],
                                     True, True, [R_uTg, R_dftc], [R_px[bk]])
                            k.copy("act" if cnt["x"] % 2 else "dve",
                                   X[:, c0 + sbi, g2:g2 + 2, :], px[bk][:, :].rearrange("p (a b) -> p a b", b=256),
                                   [R_px[bk]], [R_X[c0 + sbi]])
                cq = max(1, nch // 8)
                for kt in range((nch * 128) // W):
                    for q in range(nch // cq):
                        k.dma("sp", cosb[:, q * cq:(q + 1) * cq, 0:W],
                              cosN.rearrange("(c p) k -> p c k", p=128)[:, q * cq:(q + 1) * cq, kt * W:(kt + 1) * W],
                              [], [R_cos[q]])
                        k.dma("act", sinb[:, q * cq:(q + 1) * cq, 0:W],
                              sinN.rearrange("(c p) k -> p c k", p=128)[:, q * cq:(q + 1) * cq, kt * W:(kt + 1) * W],
                              [], [R_sin[q]])
                    pf4 = [pf[0], pf[1], px[0], px[1]]
                    R_pf4 = [R_pf[0], R_pf[1], R_px[0], R_px[1]]
                    for c in range(nch):
                        for gi, g in enumerate(groups):
                            last = (c == nch - 1)
                            k.mm(pf4[gi][:, 0:W], X[:, c, gi, 0:128], cosb[:, c, 0:W], c == 0, False,
                                 [R_X[c], R_cos[c // cq]], [R_pf4[gi]], signal=False)
                            k.mm(pf4[gi][:, 0:W], X[:, c, gi, 128:256], sinb[:, c, 0:W], False, last,
                                 [R_X[c], R_sin[c // cq]], [R_pf4[gi]], signal=(last or (c % cq == cq - 1 and gi == ng - 1)))
                    for gi, g in enumerate(groups):
                        if gi % 2:
                            k.act(fT[gi][:, 0:W], pf4[gi][:, 0:W], AF.Copy, [R_pf4[gi]], [R_fT[gi]], scale=norm)
                        else:
                            k.ts("dve", fT[gi][:, 0:W], pf4[gi][:, 0:W], norm, None, ALU.mult, None, [R_pf4[gi]], [R_fT[gi]])
                    ti = tiles[0] + (kt * W) // 512 if W == 512 else tiles[0]
                    for j in range(8):
                        bk = cnt["y"] % 2
                        cnt["y"] += 1
                        for gi, g in enumerate(groups):
                            k.mm(py[bk][:, 0:W], w_out[:, g, j * 128:(j + 1) * 128], fT[gi][:, 0:W], gi == 0, gi == ng - 1,
                                 [R_wout, R_fT[gi]], [R_py[bk]], signal=(gi == ng - 1))
                        resadd(ti, j, 0, W, py[bk][:, 0:W], R_py[bk])

        if not k.skip_ctx(li):
            run([0], 2, [list(range(4)), list(range(4, 8))], io["dft_cosC"], io["dft_sinC"], 256, 1.0 / math.sqrt(CTX * 128))
        if k.dbg.get("fnet_old"):
            run(list(range(1, 9)), 32, [list(range(4)), list(range(4, 8))], io["dft_cosN"], io["dft_sinN"], 512,
                1.0 / math.sqrt(SEQ * 128))
        else:
            run(list(range(1, 9)), 32, None, None, None, 512, None, sweep_only=True)
    P.barrier()
    if not k.dbg.get("fnet_old"):
        fnet_latent_folded(k, io, li)


def fnet_latent_folded(k, io, li):
    nc, P = k.nc, k.P
    jw = li // 3
    from contextlib import ExitStack
    R = lambda n: Res(n)
    norm = 1.0 / math.sqrt(SEQ * 128)
    NF = 17
    with ExitStack() as es:
        sb = lambda name, shape, dt: es.enter_context(nc.sbuf_tensor(k.uname(name), shape, dt))
        ps = lambda name, shape: es.enter_context(nc.psum_tensor(k.uname(name), shape, F32))
        resadd = mk_resadd(k, io, sb, li)
        dftc = sb("g_dftc", [128, 256], BF16)
        perm = sb("g_perm", [128, 4, 128], BF16)
        w_out = sb("g_wout", [128, 8, D], BF16)
        Xf = sb("g_Xf", [128, NF, 8, 256], BF16)
        px = [ps("g_px%d" % i, [128, 512]) for i in range(2)]
        pf = [ps("g_pf%d" % i, [128, 512]) for i in range(2)]
        py = [ps("g_py%d" % i, [128, 512]) for i in range(2)]
        R_c, R_wout = R("g_c"), R("g_wout")
        R_Xf = [R("g_Xf%d" % i) for i in range(NF)]
        R_px = [R("g_px0"), R("g_px1")]
        R_pf = [R("g_pf0"), R("g_pf1")]
        R_py = [R("g_py0"), R("g_py1")]
        k.dma("sp", dftc[:, :], io["dft_c"], [], [R_c])
        k.dma("sp", perm[:, :, :], io["fn_perm"].rearrange("p (a b) -> p a b", b=128), [], [R_c])
        k.dma("pool", w_out[:, :, :], io["fnet_w_out"][jw].rearrange("(c p) n -> p c n", p=128), [], [R_wout])
        with ExitStack() as es1:
            sb1 = lambda name, shape, dt: es1.enter_context(nc.sbuf_tensor(k.uname(name), shape, dt))
            Xup = sb1("g_Xup", [128, 16, 8, 256], BF16)
            uTg = sb1("g_uTg", [128, 8, 512], BF16)
            R_uTg = R("g_uTg")
            R_Xup = [R("g_Xup%d" % i) for i in range(16)]
            nx = [0]

            def load_u(ti):
                t0, N, _ = TILES[ti]
                k.dma("sp", uTg[:, :, 0:N], io["uT_d"][:, t0:t0 + N].rearrange("(c p) t -> p c t", p=128), [], [R_uTg])

            def evac(dst, src, Rs, Rd):
                nx[0] += 1
                k.copy("act" if nx[0] % 2 else "dve", dst, src, [Rs], [Rd])

            for ti in range(5, 9):
                t0 = TILES[ti][0]
                load_u(ti)
                for sbi in range(4):
                    cu = (t0 - CTX) // 128 + sbi - 16
                    for g2 in range(0, 8, 2):
                        bk = nx[0] % 2
                        for u_ in range(2):
                            k.mm(px[bk][:, u_ * 256:(u_ + 1) * 256], uTg[:, g2 + u_, sbi * 128:(sbi + 1) * 128], dftc[:, :],
                                 True, True, [R_uTg, R_c], [R_px[bk]])
                        evac(Xup[:, cu, g2:g2 + 2, :], px[bk][:, :].rearrange("p (a b) -> p a b", b=256), R_px[bk], R_Xup[cu])
            for ti in range(1, 5):
                t0 = TILES[ti][0]
                load_u(ti)
                for sbi in range(4):
                    c = (t0 - CTX) // 128 + sbi
                    for g2 in range(0, 8, 2):
                        bk = nx[0] % 2
                        for u_ in range(2):
                            g = g2 + u_
                            for w in range(2):
                                o_ = px[bk][:, u_ * 256 + w * 128:u_ * 256 + (w + 1) * 128]
                                rd = [R_uTg, R_c, R_Xup[15 - c]] + ([R_Xup[16 - c]] if c >= 1 else [])
                                k.mm(o_, uTg[:, g, sbi * 128:(sbi + 1) * 128], dftc[:, w * 128:(w + 1) * 128], True, False,
                                     rd, [R_px[bk]], signal=False)
                                k.mm(o_, perm[:, w, :], Xup[:, 15 - c, g, w * 128:(w + 1) * 128], False, c == 0,
                                     rd, [R_px[bk]], signal=(c == 0))
                                if c >= 1:
                                    k.mm(o_, perm[:, 2 + w, :], Xup[:, 16 - c, g, w * 128:(w + 1) * 128], False, True,
                                         rd, [R_px[bk]], signal=True)
                        evac(Xf[:, c, g2:g2 + 2, :], px[bk][:, :].rearrange("p (a b) -> p a b", b=256), R_px[bk], R_Xf[c])
            k.memset("dve", Xf[:, 16, :, :], 0.0, [R_Xf[16]])
            for g2 in range(0, 8, 2):
                bk = nx[0] % 2
                for u_ in range(2):
                    k.mm(px[bk][:, u_ * 256:u_ * 256 + 128], perm[:, 2, :], Xup[:, 0, g2 + u_, 0:128], True, True,
                         [R_c, R_Xup[0]], [R_px[bk]])
                evac(Xf[:, 16, g2:g2 + 2, 0:128], px[bk][:, :].rearrange("p (a b) -> p a b", b=256)[:, :, 0:128], R_px[bk], R_Xf[16])
        P.barrier()
        with ExitStack() as es2:
            sb2 = lambda name, shape, dt: es2.enter_context(nc.sbuf_tensor(k.uname(name), shape, dt))
            cosb = [sb2("g_cos%d" % i, [128, NF, 512], BF16) for i in range(2)]
            sinb = [sb2("g_sin%d" % i, [128, NF, 512], BF16) for i in range(2)]
            fT = [sb2("g_fT%d" % i, [128, 512], BF16) for i in range(8)]
            SUBS = [(0, 4), (4, 8), (8, 12), (12, NF)]
            sub_of = [0] * 4 + [1] * 4 + [2] * 4 + [3] * 5
            R_cos = [[R("g_cos%d_%d" % (i, q)) for q in range(4)] for i in range(2)]
            R_sin = [[R("g_sin%d_%d" % (i, q)) for q in range(4)] for i in range(2)]
            R_fT = [R("g_fT%d" % i) for i in range(8)]
            cv = io["dft_cosN"][0:NF * 128, :].rearrange("(c p) k -> p c k", p=128)
            sv = io["dft_sinN"][0:NF * 128, :].rearrange("(c p) k -> p c k", p=128)
            pf4 = [pf[0], pf[1], px[0], px[1]]
            R_pf4 = [R_pf[0], R_pf[1], R_px[0], R_px[1]]
            ny = 0
            fTm = [sb2("g_fTm%d" % i, [128, 512], BF16) for i in range(8)]
            R_fTm = [R("g_fTm%d" % i) for i in range(8)]
            Bs = [sb2("g_Bs%d" % i, [128, 512], F32) for i in range(2)]
            R_Bs = [R("g_Bs0"), R("g_Bs1")]
            fT0 = sb2("g_fT0", [128, 8], BF16)
            R_fT0 = R("g_fT0")
            for g in range(8):
                for c in range(NF):
                    k.mm(pf[0][:, g:g + 1], Xf[:, c, g, 0:128], k.ones_b[:, 0:1], c == 0, c == NF - 1,
                         [R_Xf[c], k.R_const], [R_pf[0]], signal=(c == NF - 1))
            k.ts("dve", fT0[:, :], pf[0][:, 0:8], norm, None, ALU.mult, None, [R_pf[0]], [R_fT0])
            for j in range(8):
                bk = ny % 2
                ny += 1
                for g in range(8):
                    k.mm(py[bk][:, 0:1], w_out[:, g, j * 128:(j + 1) * 128], fT0[:, g:g + 1], g == 0, g == 7,
                         [R_wout, R_fT0], [R_py[bk]], signal=(g == 7))
                resadd(1, j, 0, 1, py[bk][:, 0:1], R_py[bk], allow_slow_non_contiguous=True)
            for jw in range(4):
                par = jw % 2
                k0 = 512 * jw + 1
                for qi, (a, b) in enumerate(SUBS):
                    k.dma("sp", cosb[par][:, a:b, :], cv[:, a:b, k0:k0 + 512], [], [R_cos[par][qi]])
                    k.dma("act", sinb[par][:, a:b, :], sv[:, a:b, k0:k0 + 512], [], [R_sin[par][qi]])
                for gp in range(4):
                    for c in range(NF):
                        for u in range(2):
                            g = 2 * gp + u
                            k.mm(pf4[2 * u][:, :], Xf[:, c, g, 0:128], cosb[par][:, c, :], c == 0, c == NF - 1,
                                 [R_Xf[c], R_cos[par][sub_of[c]]], [R_pf4[2 * u]], signal=(c == NF - 1))
                            k.mm(pf4[2 * u + 1][:, :], Xf[:, c, g, 128:256], sinb[par][:, c, :], c == 0, c == NF - 1,
                                 [R_Xf[c], R_sin[par][sub_of[c]]], [R_pf4[2 * u + 1]], signal=(c == NF - 1))
                    for u in range(2):
                        g = 2 * gp + u
                        k.act(Bs[u][:, :], pf4[2 * u + 1][:, :], AF.Copy, [R_pf4[2 * u + 1]], [R_Bs[u]], scale=norm)
                        k.stt("dve", fT[g][:, :], pf4[2 * u][:, :], norm, Bs[u][:, :], ALU.mult, ALU.add,
                              [R_pf4[2 * u], R_Bs[u]], [R_fT[g]])
                        k.stt("dve", fTm[g][:, ::-1], pf4[2 * u][:, :], norm, Bs[u][:, :], ALU.mult, ALU.subtract,
                              [R_pf4[2 * u], R_Bs[u]], [R_fTm[g]])
                pm = SEQ - k0 - 511
                for (fTs, R_fs, pos, c_lo) in ((fT, R_fT, k0, 0), (fTm, R_fTm, pm, 1 if jw == 3 else 0)):
                    Nw = 512 - c_lo
                    for j in range(8):
                        bk = ny % 2
                        ny += 1
                        for g in range(8):
                            k.mm(py[bk][:, 0:Nw], w_out[:, g, j * 128:(j + 1) * 128], fTs[g][:, c_lo:512], g == 0, g == 7,
                                 [R_wout, R_fs[g]], [R_py[bk]], signal=(g == 7))
                        resadd(1, j, pos + c_lo, Nw, py[bk][:, 0:Nw], R_py[bk])
    P.barrier()


MIXERS[0] = fnet_mixer


NCHK = T // 128


def declare_ret_consts(nc, io):
    io["ret_expo"] = nc.dram_tensor("ret_expo", [128, 4 * 128], F32, kind="ExternalInput").ap()
    io["ret_m01"] = nc.dram_tensor("ret_m01", [128, 2 * 128], F32, kind="ExternalInput").ap()
    io["ret_ramp"] = nc.dram_tensor("ret_ramp", [128, NCHK], F32, kind="ExternalInput").ap()
    io["ropeC"] = nc.dram_tensor("ropeC", [256, SEQ], F32, kind="ExternalInput").ap()
    io["ropeS"] = nc.dram_tensor("ropeS", [256, SEQ], F32, kind="ExternalInput").ap()
    io["zT_d"] = nc.dram_tensor("zT_d", [2048, T], BF16).ap()


def ret_host_consts():
    m = np.arange(128, dtype=np.float32)[:, None]
    l = np.arange(128, dtype=np.float32)[None, :]
    expo = np.concatenate([np.maximum(l - m, 0), l - m + 128, np.maximum(m - l, 0), m - l + 128], axis=1).astype(np.float32)
    m01 = np.concatenate([(m <= l), (m > l)], axis=1).astype(np.float32)
    ramp = np.broadcast_to(128.0 * np.arange(NCHK, dtype=np.float32)[None, :], (128, NCHK)).copy()
    t = np.arange(SEQ)
    inv_freq = np.exp(np.float32(-math.log(10000.0)) * np.arange(64, dtype=np.float32) / np.float32(64)).astype(np.float32)
    ang_r = ((t // 64).astype(np.float32)[:, None] * inv_freq[None, :]).astype(np.float32)
    ang_c = ((t % 64).astype(np.float32)[:, None] * inv_freq[None, :]).astype(np.float32)
    C = np.zeros((256, SEQ), np.float32)
    S = np.zeros((256, SEQ), np.float32)
    for d, ang in enumerate((ang_r, ang_c)):
        c, s = np.cos(ang).T.astype(np.float32), np.sin(ang).T.astype(np.float32)
        C[d * 128:d * 128 + 64] = c
        C[d * 128 + 64:d * 128 + 128] = c
        S[d * 128:d * 128 + 64] = -s
        S[d * 128 + 64:d * 128 + 128] = s
    return {"ret_expo": expo, "ret_m01": m01, "ret_ramp": ramp, "ropeC": C, "ropeS": S}


def ret_mixer(k, io, li):
    nc, P = k.nc, k.P
    from contextlib import ExitStack
    with ExitStack() as es:
        sb = lambda name, shape, dt: es.enter_context(nc.sbuf_tensor(k.uname(name), shape, dt))
        ps = lambda name, shape, dt=F32: es.enter_context(nc.psum_tensor(k.uname(name), shape, dt))
        modnorm = mk_modnorm(k, io, sb, ps)
        resadd = mk_resadd(k, io, sb, li)
        R = lambda n: Res(n)
        expo = sb("r_expo", [128, 4, 128], F32)
        m01 = sb("r_m01", [128, 2, 128], F32)
        ramp = sb("r_ramp", [128, NCHK], F32)
        ones1 = sb("r_ones1", [1, 128], F32)
        lg1 = sb("r_lg1", [1, 8], F32)
        lgam = sb("r_lgam", [128, 8], F32)
        Etab = sb("r_Etab", [128, 8, 2, 128], F32)
        Wdiag = sb("r_Wdiag", [128, 4, 128], F32)
        apow = sb("r_apow", [128, 8, NCHK], F32)
        R_tab = R("r_tab")
        pA = ps("r_pA", [128, 512])
        pB = ps("r_pB", [128, 512])
        psc_t = [ps("r_psc%d" % i, [128, 512]) for i in range(2)]
        NSC = 4
        psc = [psc_t[0], psc_t[1], pA, pB]
        po = ps("r_po", [128, 512])
        pg = ps("r_pg", [128, 512])
        pt = ps("r_pt", [128, 512], BF16)
        R_pA, R_pB, R_po, R_pg, R_pt = R("pA"), R("pB"), R("po"), R("pg"), R("pt")
        R_psc = [R("psc0"), R("psc1"), R_pA, R_pB]
        k.dma("sp", expo[:, :, :], io["ret_expo"].rearrange("p (a b) -> p a b", b=128), [], [R_tab])
        k.dma("sp", m01[:, :, :], io["ret_m01"].rearrange("p (a b) -> p a b", b=128), [], [R_tab])
        k.dma("sp", ramp[:, :], io["ret_ramp"], [], [R_tab])
        k.dma("sp", lg1[:, :], io["ret_decay_logit"].rearrange("(o a) b -> o (a b)", o=1), [], [R_tab])
        k.memset("dve", ones1[:, :], 1.0, [R_tab])
        k.mm(pA[:, 0:8], ones1[0:1, :], lg1[0:1, :], True, True, [R_tab], [R_pA])
        k.act(lgam[:, :], pA[:, 0:8], AF.Exp, [R_pA], [R_tab], scale=-1.0)
        k.ts("dve", lgam[:, :], lgam[:, :], 1.0, None, ALU.add, None, [R_tab], [R_tab])
        k.act(lgam[:, :], lgam[:, :], AF.Ln, [R_tab], [R_tab])
        k.ts("dve", lgam[:, :], lgam[:, :], -1.0, None, ALU.mult, None, [R_tab], [R_tab])
        for d in range(2):
            for h in range(4):
                col = d * 4 + h
                for w in range(2):
                    k.act(Etab[:, col, w, :], expo[:, d * 2 + w, :], AF.Exp, [R_tab], [R_tab], scale=lgam[:, col:col + 1])
                k.tt("dve", Etab[:, col, 0, :], Etab[:, col, 0, :], m01[:, d, :], ALU.mult, [R_tab], [R_tab])
                k.act(apow[:, col, :], ramp[:, :], AF.Exp, [R_tab], [R_tab], scale=lgam[:, col:col + 1])
        for h in range(4):
            k.tt("dve", Wdiag[:, h, :], Etab[:, h, 0, :], Etab[:, 4 + h, 0, :], ALU.add, [R_tab], [R_tab])

        uT = sb("r_uT", [128, 8, 512], BF16)
        R_uT = R("r_uT")
        R_uTd = [R("uTd%d" % i) for i in range(len(TILES))]
        for ti in range(len(TILES)):
            t0, N, _ = TILES[ti]
            modnorm(ti, li, 1, lambda j: uT[:, j, 0:N], R_uT)
            k.dma("sp", io["uT_d"][:, t0:t0 + N].rearrange("(c p) t -> p c t", p=128), uT[:, :, 0:N], [R_uT], [R_uTd[ti]])

        wq = sb("r_wq", [128, 8, 256], BF16)
        wk = sb("r_wk", [128, 8, 256], BF16)
        wqs = sb("r_wqs", [128, 8, 256], BF16)
        wks = sb("r_wks", [128, 8, 256], BF16)
        wv = sb("r_wv", [128, 8, 512], BF16)
        wg = sb("r_wg", [128, 8, 512], BF16)
        R_w = R("r_w")
        qT = sb("r_qT", [128, 2, T], BF16)
        kT = sb("r_kT", [128, 2, T], BF16)
        vtm = sb("r_vtm", [128, NCHK, 512], BF16)
        R_qT, R_kT = R("qT"), R("kT")
        R_v = [R("v%d" % i) for i in range(NCHK)]
        rc = sb("r_rc", [128, 2, 512], F32)
        rs = sb("r_rs", [128, 2, 512], F32)
        R_rope = R("rope")
        t1 = [sb("r_t1_%d" % i, [128, 512], F32) for i in range(2)]
        t2 = [sb("r_t2_%d" % i, [128, 512], F32) for i in range(2)]
        R_t1 = [R("t1a"), R("t1b")]
        R_t2 = [R("t2a"), R("t2b")]
        NPB = 24
        Pb = [sb("r_Pb%d" % i, [128, 128], BF16) for i in range(NPB)]
        R_Pb = [R("Pb%d" % i) for i in range(NPB)]
        Pt = [sb("r_Pt%d" % i, [128, 128], F32) for i in range(2)]
        R_Pt = [R("Pt0"), R("Pt1")]
        uTc = sb("r_uTc", [128, 8, 128], BF16)
        R_uTc = R("uTc")
        osb = sb("r_osb", [128, 512], F32)
        sgt = sb("r_sgt", [128, 512], F32)
        sqj = sb("r_sqj", [128, 512], F32)
        zb = sb("r_zb", [128, 512], BF16)
        zb2 = sb("r_zb2", [128, 512], BF16)
        zbs = [zb, zb2]
        R_zbs = [R("zb0"), R("zb1")]
        zTc = sb("r_zTc", [128, 4, 128], BF16)
        st = sb("r_st", [128, 4], F32)
        identb = sb("r_identb", [128, 128], BF16)
        R_osb, R_sgt, R_zb, R_zTc, R_st, R_sqj = R("osb"), R("sgt"), R("zb"), R("zTc"), R("st"), R("sqj")
        R_zTd = [R("zTd%d" % i) for i in range(NCHK)]
        k.copy("dve", identb[:, :], k.ident[:, :], [k.R_const], [R_tab])
        wi = io["ret_w_in"].rearrange("(c p) n -> p c n", p=128)

        def cb(c):
            return c - 2 if c >= 2 else 32 + c

        nblk = 0
        for h in range(4):
            for (dst, c0, n) in ((wq, h * 256, 256), (wk, 1024 + h * 256, 256), (wv, 2048 + h * 512, 512), (wg, 4096 + h * 512, 512)):
                k.dma("pool", dst[:, :, 0:n], wi[:, :, c0:c0 + n], [], [R_w])
            for (dst, c0) in ((wqs, h * 256), (wks, 1024 + h * 256)):
                for dk in range(2):
                    b0 = c0 + dk * 128
                    k.dma("pool", dst[:, :, dk * 128:dk * 128 + 64], wi[:, :, b0 + 64:b0 + 128], [], [R_w])
                    k.dma("pool", dst[:, :, dk * 128 + 64:dk * 128 + 128], wi[:, :, b0:b0 + 64], [], [R_w])
            for ti in range(len(TILES)):
                t0, N, isctx = TILES[ti]
                k.dma("sp", uT[:, :, 0:N], io["uT_d"][:, t0:t0 + N].rearrange("(c p) t -> p c t", p=128), [R_uTd[ti]], [R_uT])
                if not isctx:
                    p0 = t0 - CTX
                    k.dma("sp", rc[:, :, 0:N], io["ropeC"][:, p0:p0 + N].rearrange("(d p) t -> p d t", p=128), [], [R_rope])
                    k.dma("sp", rs[:, :, 0:N], io["ropeS"][:, p0:p0 + N].rearrange("(d p) t -> p d t", p=128), [], [R_rope])
                for (w_, ws_, dst, R_dst, scl) in ((wq, wqs, qT, R_qT, 1.0), (wk, wks, kT, R_kT, 0.0625)):
                    for dk in range(2):
                        for kc in range(8):
                            k.mm(pA[:, 0:N], w_[:, kc, dk * 128:(dk + 1) * 128], uT[:, kc, 0:N], kc == 0, kc == 7,
                                 [R_w, R_uT], [R_pA], signal=(kc == 7))
                        if isctx:
                            k.ts("dve", dst[:, dk, t0:t0 + N], pA[:, 0:N], scl, None, ALU.mult, None, [R_pA], [R_dst])
                            continue
                        for kc in range(8):
                            k.mm(pB[:, 0:N], ws_[:, kc, dk * 128:(dk + 1) * 128], uT[:, kc, 0:N], kc == 0, kc == 7,
                                 [R_w, R_uT], [R_pB], signal=(kc == 7))
                        b = dk
                        k.stt("dve", t1[b][:, 0:N], pA[:, 0:N], scl, rc[:, dk, 0:N], ALU.mult, ALU.mult, [R_pA, R_rope], [R_t1[b]])
                        k.stt("dve", t2[b][:, 0:N], pB[:, 0:N], scl, rs[:, dk, 0:N], ALU.mult, ALU.mult, [R_pB, R_rope], [R_t2[b]])
                        k.tt("pool", dst[:, dk, t0:t0 + N], t1[b][:, 0:N], t2[b][:, 0:N], ALU.add, [R_t1[b], R_t2[b]], [R_dst])
                for sbi in range(N // 128):
                    c = t0 // 128 + sbi
                    for kc in range(8):
                        k.mm(pg[:, :], uT[:, kc, sbi * 128:(sbi + 1) * 128], wv[:, kc, :], kc == 0, kc == 7,
                             [R_w, R_uT], [R_pg], signal=(kc == 7))
                    k.copy("act", vtm[:, c, :], pg[:, :], [R_pg], [R_v[c]])
            tasks = []
            for lc in range(NCHK):
                blocks = []
                for mc in range(NCHK):
                    terms = []
                    if mc == lc:
                        terms = ["diag"]
                    else:
                        if mc < lc:
                            terms.append((h, lc - mc - 1))
                        if cb(mc) > cb(lc):
                            terms.append((4 + h, cb(mc) - cb(lc) - 1))
                    if terms:
                        blocks.append((mc, terms))
                for bi, (mc, terms) in enumerate(blocks):
                    tasks.append((lc, bi, len(blocks), mc, terms))

            GB = 4
            groups_ = [tasks[i:i + GB] for i in range(0, len(tasks), GB)]

            def emit_scores(grp, gslot):
                sk = gslot % NSC
                for i_, (lc, bi, nb, mc, terms) in enumerate(grp):
                    for kc in range(2):
                        k.mm(psc[sk][:, i_ * 128:(i_ + 1) * 128], kT[:, kc, mc * 128:(mc + 1) * 128], qT[:, kc, lc * 128:(lc + 1) * 128],
                             kc == 0, kc == 1, [R_kT, R_qT], [R_psc[sk]], signal=(kc == 1 and i_ == len(grp) - 1))
                for i_, (lc, bi, nb, mc, terms) in enumerate(grp):
                    pk = (gslot * GB + i_) % NPB
                    sc_ap = psc[sk][:, i_ * 128:(i_ + 1) * 128]
                    if terms[0] == "diag":
                        k.tt("dve", Pb[pk][:, :], sc_ap, Wdiag[:, h, :], ALU.mult, [R_psc[sk], R_tab], [R_Pb[pk]])
                    elif len(terms) == 1:
                        col, n = terms[0]
                        k.stt("dve", Pb[pk][:, :], sc_ap, apow[:, col, n:n + 1], Etab[:, col, 1, :], ALU.mult, ALU.mult,
                              [R_psc[sk], R_tab], [R_Pb[pk]])
                    else:
                        for q_, (col, n) in enumerate(terms):
                            k.stt("dve", Pt[q_][:, :], sc_ap, apow[:, col, n:n + 1], Etab[:, col, 1, :], ALU.mult, ALU.mult,
                                  [R_psc[sk], R_tab], [R_Pt[q_]])
                        k.tt("dve", Pb[pk][:, :], Pt[0][:, :], Pt[1][:, :], ALU.add, [R_Pt[0], R_Pt[1]], [R_Pb[pk]])

            DLOOK = 3
            pending_fin = []

            def flush_fin(now):
                while pending_fin and pending_fin[0][0] <= now:
                    _, lc_ = pending_fin.pop(0)
                    zb_ = zbs[lc_ % 2]
                    for ec in range(4):
                        k.tr(pt[:, ec * 128:(ec + 1) * 128], zb_[:, ec * 128:(ec + 1) * 128], identb[:, :], [R_zbs[lc_ % 2], R_tab], [R_pt])
                    k.copy("act", zTc[:, :, :], pt[:, :].rearrange("p (a b) -> p a b", b=128), [R_pt], [R_zTc])
                    k.dma("sp", io["zT_d"][h * 512:(h + 1) * 512, lc_ * 128:(lc_ + 1) * 128].rearrange("(c p) t -> p c t", p=128),
                          zTc[:, :, :], [R_zTc], [R_zTd[lc_]])

            gbase = nblk
            pv_list = []
            for gi_ in range(len(groups_) + DLOOK):
                if gi_ < len(groups_):
                    emit_scores(groups_[gi_], gbase + gi_)
                if gi_ - DLOOK >= 0:
                    for i_, tk in enumerate(groups_[gi_ - DLOOK]):
                        pv_list.append((tk, ((gbase + gi_ - DLOOK) * GB + i_) % NPB, gi_))
                while pv_list:
                    (lc, bi, nb, mc, terms), pk, idx = pv_list.pop(0)
                    flush_fin(idx)
                    k.mm(po[:, :], Pb[pk][:, :], vtm[:, mc, :], bi == 0, bi == nb - 1, [R_Pb[pk], R_v[mc]], [R_po])
                    if bi != nb - 1:
                        continue
                    k.dma("sp", uTc[:, :, :], io["uT_d"][:, lc * 128:(lc + 1) * 128].rearrange("(c p) t -> p c t", p=128),
                          [R_uTd[0 if lc < 2 else 1 + (lc - 2) // 4]], [R_uTc])
                    for kc in range(8):
                        k.mm(pg[:, :], uTc[:, kc, :], wg[:, kc, :], kc == 0, kc == 7, [R_w, R_uTc], [R_pg], signal=(kc == 7))
                    k.act(sgt[:, :], pg[:, :], AF.Silu, [R_pg], [R_sgt])
                    k.copy("act", osb[:, :], po[:, :], [R_po], [R_osb])
                    P.op("dve", lambda e: e.reduce_sum(out=st[:, 0:1], in_=osb[:, :], axis=AX.X), [R_osb], [R_st])
                    k.ts("dve", st[:, 0:1], st[:, 0:1], 1.0 / 512, None, ALU.mult, None, [R_st], [R_st])
                    k.ts("dve", osb[:, :], osb[:, :], st[:, 0:1], None, ALU.subtract, None, [R_st, R_osb], [R_osb])
                    k.tt("dve", sqj[:, :], osb[:, :], osb[:, :], ALU.mult, [R_osb], [R_sqj])
                    P.op("dve", lambda e: e.reduce_sum(out=st[:, 1:2], in_=sqj[:, :], axis=AX.X), [R_sqj], [R_st])
                    k.ts("dve", st[:, 1:2], st[:, 1:2], 1.0 / 512, EPS, ALU.mult, ALU.add, [R_st], [R_st])
                    k.act(st[:, 1:2], st[:, 1:2], AF.Sqrt, [R_st], [R_st])
                    P.op("dve", lambda e: e.reciprocal(out=st[:, 2:3], in_=st[:, 1:2]), [R_st], [R_st])
                    k.stt("dve", zbs[lc % 2][:, :], osb[:, :], st[:, 2:3], sgt[:, :], ALU.mult, ALU.mult, [R_osb, R_st, R_sgt], [R_zbs[lc % 2]])
                    pending_fin.append((idx + 3, lc))
            flush_fin(10 ** 9)
            nblk += len(groups_) + DLOOK
    P.barrier()
    with ExitStack() as es:
        sb = lambda name, shape, dt: es.enter_context(nc.sbuf_tensor(k.uname(name), shape, dt))
        ps = lambda name, shape, dt=F32: es.enter_context(nc.psum_tensor(k.uname(name), shape, dt))
        resadd = mk_resadd(k, io, sb, li)
        pA = ps("r_pA2", [128, 512])
        pB = ps("r_pB2", [128, 512])
        R_pA, R_pB = R("pA2"), R("pB2")
        wo = sb("r_wo", [128, 16, D], BF16)
        zt = sb("r_zt", [128, 16, 512], BF16)
        R_wo, R_zt = R("r_wo"), R("r_zt")
        k.dma("pool", wo[:, :, :], io["ret_w_out"].rearrange("(c p) n -> p c n", p=128), [], [R_wo])
        for ti in range(len(TILES)):
            t0, N, isctx = TILES[ti]
            k.dma("sp", zt[:, :, 0:N], io["zT_d"][:, t0:t0 + N].rearrange("(c p) t -> p c t", p=128),
                  R_zTd[t0 // 128:(t0 + N) // 128], [R_zt])
            for j in range(8):
                pp, Rp = (pA, R_pA) if j % 2 == 0 else (pB, R_pB)
                for ec in range(16):
                    k.mm(pp[:, 0:N], wo[:, ec, j * 128:(j + 1) * 128], zt[:, ec, 0:N], ec == 0, ec == 15, [R_wo, R_zt], [Rp],
                         signal=(ec == 15))
                resadd(ti, j, 0, N, pp[:, 0:N], Rp)
    P.barrier()


MIXERS[1] = ret_mixer


TB = 64
NBLK = T // TB


def declare_s5_consts(nc, io):
    io["s5_rmask"] = nc.dram_tensor("s5_rmask", [128, 4], F32, kind="ExternalInput").ap()
    io["s5_emask"] = nc.dram_tensor("s5_emask", [128, 2], F32, kind="ExternalInput").ap()
    io["s5_cmask"] = nc.dram_tensor("s5_cmask", [128, 4 * 128], F32, kind="ExternalInput").ap()
    io["yf_d"] = nc.dram_tensor("yf_d", [2, D, T], F32).ap()
    io["s5p_Wt"] = nc.dram_tensor("s5p_Wt", [128, 2 * 32 * 2 * 128], BF16).ap()
    io["s5p_Ct"] = nc.dram_tensor("s5p_Ct", [128, 2 * 32 * 2 * 128], BF16).ap()
    io["s5p_Ctab"] = nc.dram_tensor("s5p_Ctab", [128, 2 * 32 * 64], F32).ap()
    io["s5p_Stab"] = nc.dram_tensor("s5p_Stab", [128, 2 * 32 * 64], F32).ap()
    io["s5p_rt"] = nc.dram_tensor("s5p_rt", [128, 2 * 32 * 64], F32).ap()
    io["s5p_w64"] = nc.dram_tensor("s5p_w64", [128, 128], F32).ap()


def s5_host_consts():
    r = np.arange(128)
    rmask = np.stack([(r // 32 == q4) for q4 in range(4)], axis=1).astype(np.float32)
    emask = np.stack([((r // 16) % 2 == e) for e in range(2)], axis=1).astype(np.float32)
    cm = np.zeros((128, 4, 128), np.float32)
    for q4 in range(4):
        cm[:, q4, 32 * q4:32 * q4 + 32] = 1.0
    return {"s5_rmask": rmask, "s5_emask": emask, "s5_cmask": cm.reshape(128, 512)}


def s5_prep_gen(k, io, sb, ps):
    nc, P = k.nc, k.P
    from contextlib import ExitStack
    R = lambda n: Res(n)
    PI = math.pi
    R_t = R("s5tab")
    pT = ps("s5_pT", [128, 512])
    R_pT = R("s5_pT")
    rmask = sb("s5_rmask", [128, 4], F32)
    emask = sb("s5_emask", [128, 2], F32)
    cmask = sb("s5_cmask", [128, 4, 128], F32)
    k.dma("sp", rmask[:, :], io["s5_rmask"], [], [R_t])
    k.dma("sp", emask[:, :], io["s5_emask"], [], [R_t])
    k.dma("sp", cmask[:, :, :], io["s5_cmask"].rearrange("p (a b) -> p a b", b=128), [], [R_t])
    lst = sb("s5_lst", [64, 2, 128], F32)
    k.dma("sp", lst[:, 0, :], io["s5_lam_re"].rearrange("d (q e) p -> (d q) (e p)", e=2), [], [R_t])
    k.dma("sp", lst[:, 1, :], io["s5_lam_im"].rearrange("d (q e) p -> (d q) (e p)", e=2), [], [R_t])
    lam = sb("s5_lam", [128, 2, 64], F32)
    for w in range(2):
        k.tr(pT[:, 0:64], lst[:, w, :], k.ident[0:64, 0:64], [R_t, k.R_const], [R_pT])
        k.copy("dve", lam[:, w, :], pT[:, 0:64], [R_pT], [R_t])
    ones1 = sb("s5_ones1", [1, 128], F32)
    ldt1 = sb("s5_ldt1", [1, 128], F32)
    k.memset("dve", ones1[:, :], 1.0, [R_t])
    k.dma("sp", ldt1[:, :], io["s5_log_dt"].rearrange("(o d) g -> o (d g)", o=1), [], [R_t])
    k.mm(pT[:, 0:128], ones1[0:1, :], ldt1[0:1, :], True, True, [R_t], [R_pT])
    dt = sb("s5_dt", [128, 64], F32)
    bc = pT[:, 0:128].rearrange("p (d q e) -> p d q e", d=2, e=2)
    dt3 = dt[:, :].rearrange("p (d q) -> p d q", d=2)
    k.act(dt3[0:64], bc[0:64, :, :, 0], AF.Exp, [R_pT], [R_t])
    k.act(dt3[64:128], bc[64:128, :, :, 1], AF.Exp, [R_pT], [R_t])
    sm = lambda n: sb("s5_" + n, [128, 64], F32)
    mag, ang, ar, ai, tmpa, tmpb, sg_, den, cfr, cfi = [sm(n) for n in
                                                         "mag ang ar ai tmpa tmpb sg den cfr cfi".split()]
    k.tt("dve", mag[:, :], lam[:, 0, :], dt[:, :], ALU.mult, [R_t], [R_t])
    k.act(mag[:, :], mag[:, :], AF.Exp, [R_t], [R_t])
    k.tt("dve", ang[:, :], lam[:, 1, :], dt[:, :], ALU.mult, [R_t], [R_t])

    def sin_of(dst, shift):
        k.ts("dve", tmpa[:, :], ang[:, :], shift - 4 * PI, None, ALU.add, None, [R_t], [R_t])
        for thr in (PI, 3 * PI, 5 * PI, 7 * PI):
            k.ts("dve", tmpb[:, :], ang[:, :], shift - thr, None, ALU.add, None, [R_t], [R_t])
            k.act(sg_[:, :], tmpb[:, :], AF.Sign, [R_t], [R_t])
            k.stt("dve", tmpa[:, :], sg_[:, :], -PI, tmpa[:, :], ALU.mult, ALU.add, [R_t], [R_t])
        k.act(dst, tmpa[:, :], AF.Sin, [R_t], [R_t])

    sin_of(ai[:, :], 0.0)
    sin_of(ar[:, :], PI / 2)
    k.tt("dve", ar[:, :], ar[:, :], mag[:, :], ALU.mult, [R_t], [R_t])
    k.tt("dve", ai[:, :], ai[:, :], mag[:, :], ALU.mult, [R_t], [R_t])
    k.tt("dve", den[:, :], lam[:, 0, :], lam[:, 0, :], ALU.mult, [R_t], [R_t])
    k.tt("dve", tmpa[:, :], lam[:, 1, :], lam[:, 1, :], ALU.mult, [R_t], [R_t])
    k.tt("dve", den[:, :], den[:, :], tmpa[:, :], ALU.add, [R_t], [R_t])
    P.op("dve", lambda e: e.reciprocal(out=den[:, :], in_=den[:, :]), [R_t], [R_t])
    k.ts("dve", tmpb[:, :], ar[:, :], -1.0, None, ALU.add, None, [R_t], [R_t])
    k.tt("dve", cfr[:, :], tmpb[:, :], lam[:, 0, :], ALU.mult, [R_t], [R_t])
    k.tt("dve", tmpa[:, :], ai[:, :], lam[:, 1, :], ALU.mult, [R_t], [R_t])
    k.tt("dve", cfr[:, :], cfr[:, :], tmpa[:, :], ALU.add, [R_t], [R_t])
    k.tt("dve", cfr[:, :], cfr[:, :], den[:, :], ALU.mult, [R_t], [R_t])
    k.tt("dve", cfi[:, :], ai[:, :], lam[:, 0, :], ALU.mult, [R_t], [R_t])
    k.tt("dve", tmpa[:, :], tmpb[:, :], lam[:, 1, :], ALU.mult, [R_t], [R_t])
    k.tt("dve", cfi[:, :], cfi[:, :], tmpa[:, :], ALU.subtract, [R_t], [R_t])
    k.tt("dve", cfi[:, :], cfi[:, :], den[:, :], ALU.mult, [R_t], [R_t])
    Wt = sb("s5_Wt", [128, 2, 32, 2, 128], BF16)
    Ct = sb("s5_Ct", [128, 2, 32, 2, 128], BF16)
    with ExitStack() as es2:
        sb2 = lambda name, shape, dt_: es2.enter_context(nc.sbuf_tensor(k.uname(name), shape, dt_))
        Bn = sb2("s5_Bn", [128, 2, 2, 32, 16], F32)
        Bb = sb2("s5_Bb", [128, 2, 2, 32, 16], F32)
        for w, nm in enumerate(("s5_b_re", "s5_b_im")):
            for d in range(2):
                k.dma("sp", Bn[:, w, d, :, :], io[nm][d].rearrange("(q e) p c -> (e p) q c", e=2), [], [R_t])
        t16 = sb2("s5_t16", [128, 64], F32)
        cf3r, cf3i = cfr[:, :], cfi[:, :]
        for ci in range(16):
            yield
            bre = Bn[:, 0, :, :, ci].rearrange("p d q -> p (d q)")
            bim = Bn[:, 1, :, :, ci].rearrange("p d q -> p (d q)")
            ore = Bb[:, 0, :, :, ci].rearrange("p d q -> p (d q)")
            oim = Bb[:, 1, :, :, ci].rearrange("p d q -> p (d q)")
            k.tt("dve", ore, cf3r, bre, ALU.mult, [R_t], [R_t])
            k.tt("dve", t16[:, :], cf3i, bim, ALU.mult, [R_t], [R_t])
            k.tt("dve", ore, ore, t16[:, :], ALU.subtract, [R_t], [R_t])
            k.tt("dve", oim, cf3r, bim, ALU.mult, [R_t], [R_t])
            k.tt("dve", t16[:, :], cf3i, bre, ALU.mult, [R_t], [R_t])
            k.tt("dve", oim, oim, t16[:, :], ALU.add, [R_t], [R_t])
        Nn = sb2("s5_Nn", [128, 4, 2, 16], F32)
        k.memset("dve", Nn[:, :, :, :], 0.0, [R_t])
        for d in range(2):
            for w in range(2):
                for j in range(8):
                    yield
                    k.copy("dve", Nn[0:64, :, 0, :], Bb[0:64, w, d, 4 * j:4 * j + 4, :], [R_t, R_pT], [R_t])
                    k.copy("dve", Nn[64:128, :, 1, :], Bb[64:128, w, d, 4 * j:4 * j + 4, :], [R_t], [R_t])
                    k.tr(pT[:, 0:128], Nn[:, :, :, :].rearrange("p a b c -> p (a b c)"), k.ident[:, :], [R_t, k.R_const], [R_pT])
                    for q4 in range(4):
                        k.ts("dve", Wt[:, d, 4 * j + q4, w, :], pT[:, 0:128], rmask[:, q4:q4 + 1], None, ALU.mult, None,
                             [R_pT, R_t], [R_t])
        Cn = sb2("s5_Cn", [128, 2, 2, 8, 64], F32)
        for w, nm in enumerate(("s5_c_re", "s5_c_im")):
            for d in range(2):
                k.dma("sp", Cn[:, w, d, :, :], io[nm][d].rearrange("(j g8) co p -> (g8 co) j p", g8=8), [], [R_t])
        Cexp = sb2("s5_Cexp", [128, 128], F32)
        for d in range(2):
            for w in range(2):
                for j in range(8):
                    yield
                    for e in range(2):
                        k.ts("dve", Cexp[:, e * 64:(e + 1) * 64], Cn[:, w, d, j, :], emask[:, e:e + 1], None, ALU.mult, None,
                             [R_t, R_pT], [R_t])
                    k.tr(pT[:, 0:128], Cexp[:, :], k.ident[:, :], [R_t, k.R_const], [R_pT])
                    for q4 in range(4):
                        k.stt("dve", Ct[:, d, 4 * j + q4, w, :], pT[:, 0:128], (1.0 if w == 0 else -1.0), cmask[:, q4, :],
                              ALU.mult, ALU.mult, [R_pT, R_t], [R_t])
    Ctab = sb("s5_Ctab", [128, 2, 32, TB], F32)
    Stab = sb("s5_Stab", [128, 2, 32, TB], F32)
    rt = sb("s5_rt", [128, 2, 32, TB], F32)
    w64 = sb("s5_w64", [128, 2, 64], F32)
    cs1, sn1, er, ei_, e2r, e2i = [sm(n) for n in "cs1 sn1 er ei e2r e2i".split()]
    P.op("dve", lambda e: e.reciprocal(out=tmpa[:, :], in_=mag[:, :]), [R_t], [R_t])
    k.tt("dve", cs1[:, :], ar[:, :], tmpa[:, :], ALU.mult, [R_t], [R_t])
    k.tt("dve", sn1[:, :], ai[:, :], tmpa[:, :], ALU.mult, [R_t], [R_t])
    k.memset("dve", er[:, :], 1.0, [R_t])
    k.memset("dve", ei_[:, :], 0.0, [R_t])
    dq = lambda t2d: t2d.rearrange("p (d q) -> p d q", d=2)
    for tp in range(TB):
        yield
        for d in range(2):
            col = tp if d == 0 else TB - 1 - tp
            k.copy("pool", Ctab[:, d, :, col], dq(er[:, :])[:, d, :], [R_t], [R_t])
            k.copy("pool", Stab[:, d, :, col], dq(ei_[:, :])[:, d, :], [R_t], [R_t])
            if tp == 0:
                k.memset("pool", rt[:, d, :, col], 0.0, [R_t])
            else:
                k.copy("pool", rt[:, d, :, col], dq(mag[:, :])[:, d, :], [R_t], [R_t])
        k.tt("dve", e2r[:, :], er[:, :], cs1[:, :], ALU.mult, [R_t], [R_t])
        k.tt("dve", tmpa[:, :], ei_[:, :], sn1[:, :], ALU.mult, [R_t], [R_t])
        k.tt("dve", e2r[:, :], e2r[:, :], tmpa[:, :], ALU.subtract, [R_t], [R_t])
        k.tt("dve", e2i[:, :], er[:, :], sn1[:, :], ALU.mult, [R_t], [R_t])
        k.tt("dve", tmpa[:, :], ei_[:, :], cs1[:, :], ALU.mult, [R_t], [R_t])
        k.tt("dve", ei_[:, :], e2i[:, :], tmpa[:, :], ALU.add, [R_t], [R_t])
        k.copy("dve", er[:, :], e2r[:, :], [R_t], [R_t])
    k.tt("dve", w64[:, 0, :], er[:, :], mag[:, :], ALU.mult, [R_t], [R_t])
    k.tt("dve", w64[:, 1, :], ei_[:, :], mag[:, :], ALU.mult, [R_t], [R_t])
    yield
    R_o = R("s5p_out")
    k.dma("sp", io["s5p_Wt"], Wt[:, :, :, :, :].rearrange("p a b c d -> p (a b c d)"), [R_t], [R_o])
    k.dma("sp", io["s5p_Ct"], Ct[:, :, :, :, :].rearrange("p a b c d -> p (a b c d)"), [R_t], [R_o])
    k.dma("sp", io["s5p_Ctab"], Ctab[:, :, :, :].rearrange("p a b c -> p (a b c)"), [R_t], [R_o])
    k.dma("sp", io["s5p_Stab"], Stab[:, :, :, :].rearrange("p a b c -> p (a b c)"), [R_t], [R_o])
    k.dma("sp", io["s5p_rt"], rt[:, :, :, :].rearrange("p a b c -> p (a b c)"), [R_t], [R_o])
    k.dma("sp", io["s5p_w64"], w64[:, :, :].rearrange("p a b -> p (a b)"), [R_t], [R_o])
    yield

def s5_mixer(k, io, li):
    nc, P = k.nc, k.P
    from contextlib import ExitStack
    R = lambda n: Res(n)
    PI = math.pi
    with ExitStack() as es:
        sb = lambda name, shape, dt: es.enter_context(nc.sbuf_tensor(k.uname(name), shape, dt))
        ps = lambda name, shape, dt=F32: es.enter_context(nc.psum_tensor(k.uname(name), shape, dt))
        R_t = R("s5tab")
        R_uTd = R("s5_uTd")
        with ExitStack() as es0:
            sb0 = lambda name, shape, dt: es0.enter_context(nc.sbuf_tensor(k.uname(name), shape, dt))
            ps0 = lambda name, shape, dt=F32: es0.enter_context(nc.psum_tensor(k.uname(name), shape, dt))
            modnorm = mk_modnorm(k, io, sb0, ps0)
            uT = sb0("s5_uT", [128, 8, 512], BF16)
            R_uT = R("s5_uT")
            for ti in range(len(TILES)):
                t0, N, _ = TILES[ti]
                modnorm(ti, li, 1, lambda j: uT[:, j, 0:N], R_uT)
                k.dma("sp", io["uT_d"][:, t0:t0 + N].rearrange("(c p) t -> p c t", p=128), uT[:, :, 0:N], [R_uT], [R_uTd])
        P.barrier()
        Wt = sb("s5_Wt", [128, 2, 32, 2, 128], BF16)
        Ct = sb("s5_Ct", [128, 2, 32, 2, 128], BF16)
        Ctab = sb("s5_Ctab", [128, 2, 32, TB], F32)
        Stab = sb("s5_Stab", [128, 2, 32, TB], F32)
        rt = sb("s5_rt", [128, 2, 32, TB], F32)
        w64 = sb("s5_w64", [128, 2, 64], F32)
        k.dma("sp", Wt[:, :, :, :, :].rearrange("p a b c d -> p (a b c d)"), io["s5p_Wt"], [], [R_t])
        k.dma("act", Ct[:, :, :, :, :].rearrange("p a b c d -> p (a b c d)"), io["s5p_Ct"], [], [R_t])
        k.dma("sp", Ctab[:, :, :, :].rearrange("p a b c -> p (a b c)"), io["s5p_Ctab"], [], [R_t])
        k.dma("act", Stab[:, :, :, :].rearrange("p a b c -> p (a b c)"), io["s5p_Stab"], [], [R_t])
        k.dma("sp", rt[:, :, :, :].rearrange("p a b c -> p (a b c)"), io["s5p_rt"], [], [R_t])
        k.dma("sp", w64[:, :, :].rearrange("p a b -> p (a b)"), io["s5p_w64"], [], [R_t])
        dq = lambda t2d: t2d.rearrange("p (d q) -> p d q", d=2)
        fl = lambda t3: t3.rearrange("p q t -> p (q t)")
        fl4 = lambda t4: t4.rearrange("p d q t -> p (d q t)")
        ub = [sb("s5_ub%d" % d, [128, 8, TB], BF16) for d in range(2)]
        Vr = sb("s5_Vr", [128, 2, 32, TB], F32)
        Vi = sb("s5_Vi", [128, 2, 32, TB], F32)
        T1 = sb("s5_T1", [128, 2, 32, TB], F32)
        T2 = sb("s5_T2", [128, 2, 32, TB], F32)
        Xb = sb("s5_Xb", [128, 2, 32, TB], BF16)
        zc = sb("s5_zc", [128, 2, 2, 32], F32)
        cw = sb("s5_cw", [128, 4, 2, 32], F32)
        ysb = sb("s5_ysb", [128, 8, TB], F32)
        pv = [[ps("s5_pv%d_%d" % (d, i), [128, 512]) for i in range(2)] for d in range(2)]
        pyy = [ps("s5_py%d" % d, [128, 512]) for d in range(2)]
        R_ub = [R("ub0"), R("ub1")]
        R_Vr, R_Vi, R_T1, R_T2, R_Xb, R_zc, R_cw, R_ys = [R(n) for n in "Vr Vi T1 T2 Xb zc cw ys".split()]
        R_pv = [[R("pv%d%d" % (d, i)) for i in range(2)] for d in range(2)]
        R_pyy = [R("pyy0"), R("pyy1")]
        R_yd = [R("yd0"), R("yd1")]
        k.memset("dve", zc[:, :, :, :], 0.0, [R_zc])
        order = [list(range(NBLK)), [3, 2, 1, 0] + list(range(NBLK - 1, 3, -1))]
        C4, S4 = fl4(Ctab[:, :, :, :]), fl4(Stab[:, :, :, :])
        vr, vi, t1, t2 = fl4(Vr[:, :, :, :]), fl4(Vi[:, :, :, :]), fl4(T1[:, :, :, :]), fl4(T2[:, :, :, :])
        w4 = w64[:, :, :].rearrange("p r (d q) -> p r d q", d=2)
        FIRST = (0, TB - 1)
        LAST = (TB - 1, 0)
        for it in range(NBLK):
            for d in range(2):
                b = order[d][it]
                c0 = b * TB
                k.dma("sp", ub[d][:, :, :], io["uT_d"][:, c0:c0 + TB].rearrange("(c p) t -> p c t", p=128), [R_uTd], [R_ub[d]])
                nev = 0
                for w, (Vd, Rv) in enumerate(((Vr, R_Vr), (Vi, R_Vi))):
                    for q8 in range(4):
                        bk = nev % 2
                        nev += 1
                        for qq in range(8):
                            q = q8 * 8 + qq
                            k.mm(pv[d][bk][:, qq * TB:(qq + 1) * TB], Wt[:, d, q, w, :], ub[d][:, q // 4, :], True, True,
                                 [R_t, R_ub[d]], [R_pv[d][bk]], signal=(qq == 7))
                        k.copy("act", Vd[:, d, q8 * 8:(q8 + 1) * 8, :],
                               pv[d][bk][:, :].rearrange("p (a b) -> p a b", b=TB), [R_pv[d][bk]], [Rv])
            k.tt("dve", t1, vr, C4, ALU.mult, [R_Vr, R_t], [R_T1])
            k.tt("dve", t2, vi, S4, ALU.mult, [R_Vi, R_t], [R_T2])
            k.tt("dve", vr, vr, S4, ALU.mult, [R_Vr, R_t], [R_Vr])
            k.tt("dve", vi, vi, C4, ALU.mult, [R_Vi, R_t], [R_Vi])
            k.tt("dve", t1, t1, t2, ALU.add, [R_T1, R_T2], [R_T1])
            k.tt("dve", vi, vi, vr, ALU.subtract, [R_Vi, R_Vr], [R_Vi])
            k.tt("dve", cw[:, 0, :, :], w4[:, 0, :, :], zc[:, 0, :, :], ALU.mult, [R_zc, R_t], [R_cw])
            k.tt("dve", cw[:, 1, :, :], w4[:, 1, :, :], zc[:, 1, :, :], ALU.mult, [R_zc, R_t], [R_cw])
            k.tt("dve", cw[:, 2, :, :], w4[:, 1, :, :], zc[:, 0, :, :], ALU.mult, [R_zc, R_t], [R_cw])
            k.tt("dve", cw[:, 3, :, :], w4[:, 0, :, :], zc[:, 1, :, :], ALU.mult, [R_zc, R_t], [R_cw])
            k.tt("dve", cw[:, 0, :, :], cw[:, 0, :, :], cw[:, 1, :, :], ALU.subtract, [R_cw], [R_cw])
            k.tt("dve", cw[:, 2, :, :], cw[:, 2, :, :], cw[:, 3, :, :], ALU.add, [R_cw], [R_cw])
            for d in range(2):
                k.tt("dve", T1[:, d, :, FIRST[d]], T1[:, d, :, FIRST[d]], cw[:, 0, d, :], ALU.add, [R_cw, R_T1], [R_T1])
                k.tt("dve", Vi[:, d, :, FIRST[d]], Vi[:, d, :, FIRST[d]], cw[:, 2, d, :], ALU.add, [R_cw, R_Vi], [R_Vi])
            for d in range(2):
                rv = (lambda a_: a_) if d == 0 else (lambda a_: a_[:, ::-1])
                rt2 = fl(rt[:, d, :, :])
                for (o_, i_, Ro, Ri) in ((T2, T1, R_T2, R_T1), (Vr, Vi, R_Vr, R_Vi)):
                    P.op("dve", (lambda o, a0, a1: (lambda e: e.tensor_tensor_scan(out=o, data0=a0, data1=a1, initial=0.0,
                                                                                     op0=ALU.mult, op1=ALU.add)))(
                        rv(fl(o_[:, d, :, :])), rv(rt2), rv(fl(i_[:, d, :, :]))), [Ri, R_t], [Ro])
            for d in range(2):
                k.copy("dve", zc[:, 0, d, :], T2[:, d, :, LAST[d]], [R_T2], [R_zc])
                k.copy("dve", zc[:, 1, d, :], Vr[:, d, :, LAST[d]], [R_Vr], [R_zc])
            k.tt("dve", t1, t2, C4, ALU.mult, [R_T2, R_t], [R_T1])
            k.tt("dve", vi, vr, S4, ALU.mult, [R_Vr, R_t], [R_Vi])
            k.tt("dve", t2, t2, S4, ALU.mult, [R_T2, R_t], [R_T2])
            k.tt("dve", vr, vr, C4, ALU.mult, [R_Vr, R_t], [R_Vr])
            for d in range(2):
                c0 = order[d][it] * TB
                k.tt("dve", Xb[:, 0, :, :], T1[:, d, :, :], Vi[:, d, :, :], ALU.subtract, [R_T1, R_Vi], [R_Xb])
                k.tt("dve", Xb[:, 1, :, :], Vr[:, d, :, :], T2[:, d, :, :], ALU.add, [R_Vr, R_T2], [R_Xb])
                for j in range(8):
                    n_ = 0
                    for q4 in range(4):
                        for w in range(2):
                            k.mm(pyy[d][:, j * TB:(j + 1) * TB], Ct[:, d, 4 * j + q4, w, :], Xb[:, w, 4 * j + q4, :], n_ == 0, n_ == 7,
                                 [R_t, R_Xb], [R_pyy[d]], signal=(n_ == 7))
                            n_ += 1
                k.copy("act", ysb[:, :, :], pyy[d][:, :].rearrange("p (a b) -> p a b", b=TB), [R_pyy[d]], [R_ys])
                k.dma("sp", io["yf_d"][d, :, c0:c0 + TB].rearrange("(c p) t -> p c t", p=128), ysb[:, :, :], [R_ys], [R_yd[d]])
    P.barrier()
    with ExitStack() as es:
        sb = lambda name, shape, dt: es.enter_context(nc.sbuf_tensor(k.uname(name), shape, dt))
        ps = lambda name, shape, dt=F32: es.enter_context(nc.psum_tensor(k.uname(name), shape, dt))
        resadd = mk_resadd(k, io, sb, li)
        wgl = sb("s5_wgl", [128, 8, 2 * D], BF16)
        R_wgl = R("wgl")
        for hh in range(4):
            k.dma("pool", wgl[:, :, hh * 512:(hh + 1) * 512],
                  io["s5_w_glu"].rearrange("(c p) n -> p c n", p=128)[:, :, hh * 512:(hh + 1) * 512], [], [R_wgl])
        yf = sb("s5_yf", [128, 8, 512], F32)
        yb = sb("s5_yb", [128, 8, 512], F32)
        uu = sb("s5_uu", [128, 8, 512], BF16)
        ge = sb("s5_ge", [128, 8, 512], BF16)
        w1 = sb("s5_w1", [128, 512], F32)
        w2 = sb("s5_w2", [128, 512], F32)
        sig = sb("s5_sig", [128, 512], F32)
        oo = sb("s5_oo", [128, 512], F32)
        R_yf, R_yb, R_uu, R_ge, R_w1, R_w2, R_sig, R_oo = [R(n) for n in "yf yb uu ge w1 w2 sig oo".split()]
        pa = ps("s5_pa", [128, 512])
        pb = ps("s5_pb", [128, 512])
        R_pa, R_pb = R("s5pa"), R("s5pb")
        for ti in range(len(TILES)):
            t0, N, isctx = TILES[ti]
            k.dma("sp", yf[:, :, 0:N], io["yf_d"][0, :, t0:t0 + N].rearrange("(c p) t -> p c t", p=128), [], [R_yf])
            k.dma("sp", yb[:, :, 0:N], io["yf_d"][1, :, t0:t0 + N].rearrange("(c p) t -> p c t", p=128), [], [R_yb])
            k.dma("sp", uu[:, :, 0:N], io["uT_d"][:, t0:t0 + N].rearrange("(c p) t -> p c t", p=128), [], [R_uu])
            for j in range(8):
                k.tt("dve", w1[:, 0:N], yf[:, j, 0:N], yb[:, j, 0:N], ALU.add, [R_yf, R_yb], [R_w1])
                k.stt("dve", w1[:, 0:N], uu[:, j, 0:N], k.vecT[:, VC_S5D + j:VC_S5D + j + 1], w1[:, 0:N], ALU.mult, ALU.add,
                      [R_uu, R_w1, k.R_vec], [R_w1])
                k.tt("dve", w2[:, 0:N], w1[:, 0:N], w1[:, 0:N], ALU.mult, [R_w1], [R_w2])
                k.ts("dve", w2[:, 0:N], w2[:, 0:N], 0.044715, 1.0, ALU.mult, ALU.add, [R_w2], [R_w2])
                k.tt("dve", w2[:, 0:N], w2[:, 0:N], w1[:, 0:N], ALU.mult, [R_w2, R_w1], [R_w2])
                k.act(w2[:, 0:N], w2[:, 0:N], AF.Sigmoid, [R_w2], [R_w2], scale=1.5957691216057308)
                k.tt("dve", ge[:, j, 0:N], w2[:, 0:N], w1[:, 0:N], ALU.mult, [R_w2, R_w1], [R_ge])
            for oc in range(8):
                for kc in range(8):
                    k.mm(pa[:, 0:N], wgl[:, kc, oc * 128:(oc + 1) * 128], ge[:, kc, 0:N], kc == 0, kc == 7, [R_wgl, R_ge], [R_pa],
                         signal=(kc == 7))
                for kc in range(8):
                    k.mm(pb[:, 0:N], wgl[:, kc, D + oc * 128:D + (oc + 1) * 128], ge[:, kc, 0:N], kc == 0, kc == 7, [R_wgl, R_ge],
                         [R_pb], signal=(kc == 7))
                k.act(sig[:, 0:N], pb[:, 0:N], AF.Sigmoid, [R_pb], [R_sig])
                k.tt("dve", oo[:, 0:N], pa[:, 0:N], sig[:, 0:N], ALU.mult, [R_pa, R_sig], [R_oo])
                resadd(ti, oc, 0, N, oo[:, 0:N], R_oo)
    P.barrier()


MIXERS[2] = s5_mixer
```

```python
import math
import numpy as np
import concourse.bass as bass
import concourse.mybir as mybir
from concourse.bass_utils import run_bass_kernel_spmd

F32 = mybir.dt.float32
BF16 = mybir.dt.bfloat16
AF = mybir.ActivationFunctionType
ALU = mybir.AluOpType
AX = mybir.AxisListType


class Res:
    __slots__ = ("name", "w", "r")

    def __init__(self, name):
        self.name = name
        self.w = None
        self.r = {}


class Prog:
    ENG = ("pe", "act", "dve", "pool", "sp")

    def __init__(self, nc, n_dma=40, same_sync=True):
        self.nc = nc
        self.items = {e: [] for e in self.ENG}
        self.cnt = {e: 0 for e in self.ENG}
        self.sems = {}
        for e in ("pe", "act", "dve", "pool"):
            self.sems[e] = nc.alloc_semaphore(name="s_" + e)
        self.n_dma = n_dma
        for i in range(n_dma):
            self.sems[("d", i)] = nc.alloc_semaphore(name="d%d" % i)
        self.dma_cum = [0] * n_dma
        q = n_dma // 5
        self.dma_pool = {"sp": (0, 2 * q), "pool": (2 * q, 4 * q), "act": (4 * q, n_dma)}
        self.dma_pi = {"sp": 0, "pool": 0, "act": 0}
        self.waited = {e: {} for e in self.ENG}
        self.same_sync = same_sync
        self.nops = 0

    def _need(self, eng, tok):
        if tok is None:
            return
        key, val = tok
        if val <= 0:
            return
        if key == eng and (eng == "pe" or (not self.same_sync and eng != "pool")):
            return
        if key == eng and val > self.cnt[eng]:
            return
        if self.waited[eng].get(key, 0) >= val:
            return
        self.waited[eng][key] = val
        self.items[eng].append(("wait", key, val))

    def _deps(self, eng, reads, writes):
        for r in reads:
            self._need(eng, r.w)
        for w in writes:
            self._need(eng, w.w)
            for k, v in w.r.items():
                self._need(eng, (k, v))

    def _mark(self, tok, reads, writes):
        k, v = tok
        for r in reads:
            if r.r.get(k, 0) < v:
                r.r[k] = v
        for w in writes:
            w.w = tok
            w.r = {}

    def op(self, eng, fn, reads=(), writes=(), signal=True):
        self._deps(eng, reads, writes)
        tok = (eng, self.cnt[eng] + 1)
        if signal:
            self.cnt[eng] += 1
            self.items[eng].append(("op", fn, eng, 1))
        else:
            self.items[eng].append(("op", fn, None, 0))
        self._mark(tok, reads, writes)
        self.nops += 1

    def dma(self, eng, out, in_, reads=(), writes=(), **kw):
        lo, hi = self.dma_pool[eng]
        s = lo + self.dma_pi[eng] % (hi - lo)
        self.dma_pi[eng] += 1
        key = ("d", s)
        self._need(eng, (key, self.dma_cum[s]))
        self._deps(eng, reads, writes)
        self.dma_cum[s] += 16
        tok = (key, self.dma_cum[s])
        self.items[eng].append(("op", lambda e: e.dma_start(out=out, in_=in_, **kw), key, 16))
        self._mark(tok, reads, writes)
        self.nops += 1

    def finish(self, eng="sp"):
        for s in range(self.n_dma):
            self._need(eng, (("d", s), self.dma_cum[s]))
        for e in ("pe", "act", "dve", "pool"):
            self._need(eng, (e, self.cnt[e]))

    def emit(self):
        nc = self.nc
        sems = self.sems

        def replay(name):
            def f(eng):
                for it in self.items[name]:
                    if it[0] == "wait":
                        eng.wait_ge(sems[it[1]], it[2])
                    else:
                        ins = it[1](eng)
                        if it[2] is not None:
                            ins.then_inc(sems[it[2]], it[3])
            return f

        with nc.Block() as block:
            block.tensor(replay("pe"))
            block.scalar(replay("act"))
            block.vector(replay("dve"))
            block.gpsimd(replay("pool"))
            block.sync(replay("sp"))


D = 1024
NCH = 8
SEQ = 4096
CTX = 256
T = SEQ + CTX
DEPTH = 4
NMOD = 9
DFF = 2816
FCH = DFF // 128
EPS = 1e-6
TILES = [(0, 256, True)] + [(256 + 512 * i, 512, False) for i in range(8)]


class K:
    def __init__(self, nc, P):
        self.nc = nc
        self.P = P
        self._uid = 0

    def uname(self, name):
        self._uid += 1
        return "%s_%d" % (name, self._uid)

    def mm(self, out, lhsT, rhs, start, stop, reads, writes, signal=True):
        self.P.op("pe", lambda e: e.matmul(out, lhsT, rhs, start=start, stop=stop), reads, writes, signal)

    def tr(self, out, in_, ident, reads, writes):
        self.P.op("pe", lambda e: e.transpose(out, in_, ident), reads, writes)

    def act(self, out, in_, func, reads, writes, bias=None, scale=None):
        kw = {}
        if bias is not None:
            kw["bias"] = bias
        if scale is not None:
            kw["scale"] = scale
        self.P.op("act", lambda e: e.activation(out=out, in_=in_, func=func, **kw), reads, writes)

    def tt(self, eng, out, in0, in1, op, reads, writes):
        self.P.op(eng, lambda e: e.tensor_tensor(out=out, in0=in0, in1=in1, op=op), reads, writes)

    def ts(self, eng, out, in0, s1, s2, op0, op1, reads, writes):
        if op1 is None:
            self.P.op(eng, lambda e: e.tensor_single_scalar(out=out, in_=in0, scalar=s1, op=op0), reads, writes)
        else:
            self.P.op(eng, lambda e: e.tensor_scalar(out=out, in0=in0, scalar1=s1, scalar2=s2, op0=op0, op1=op1),
                      reads, writes)

    def stt(self, eng, out, in0, scalar, in1, op0, op1, reads, writes):
        self.P.op(eng, lambda e: e.scalar_tensor_tensor(out=out, in0=in0, scalar=scalar, in1=in1, op0=op0, op1=op1),
                  reads, writes)

    def copy(self, eng, out, in_, reads, writes):
        if eng == "act":
            self.P.op(eng, lambda e: e.copy(out=out, in_=in_), reads, writes)
        else:
            self.P.op(eng, lambda e: e.tensor_copy(out=out, in_=in_), reads, writes)

    def memset(self, eng, ap, val, writes):
        self.P.op(eng, lambda e: e.memset(ap, val), (), writes)

    def dma(self, eng, out, in_, reads, writes, **kw):
        self.P.dma(eng, out, in_, reads, writes, **kw)


def _barrier(P):
    for e in P.ENG:
        for s in range(P.n_dma):
            P._need(e, (("d", s), P.dma_cum[s]))
        for e2 in ("pe", "act", "dve", "pool"):
            if e2 != e:
                P._need(e, (e2, P.cnt[e2]))


Prog.barrier = _barrier

VC_C, VC_CC, VC_FNG, VC_S5D, VC_NG, VC_BM = 0, 8, 16, 24, 32, 128
NVEC = 416


def declare_io(nc):
    io = {}

    def inp(name, shape):
        io[name] = nc.dram_tensor(name, list(shape), F32, kind="ExternalInput").ap()

    inp("x", [SEQ, D])
    inp("c", [8, 128])
    inp("ctx", [CTX, D])
    inp("c_ctx", [8, 128])
    inp("w_mod", [DEPTH, D, NMOD * D])
    inp("b_mod", [288, 128])
    inp("norm_g", [96, 128])
    inp("ffn_w_in", [DEPTH, 2, D, 2 * DFF])
    inp("ffn_w_out", [DEPTH, 2, DFF, D])
    inp("fnet_w_out", [2, D, D])
    inp("ret_w_in", [D, 6144])
    inp("ret_w_out", [2048, D])
    inp("ret_decay_logit", [2, 4])
    inp("s5_lam_re", [2, 64, 64])
    inp("s5_lam_im", [2, 64, 64])
    inp("s5_log_dt", [2, 64])
    inp("s5_b_re", [2, 64, 64, 16])
    inp("s5_b_im", [2, 64, 64, 16])
    inp("s5_c_re", [2, 64, 16, 64])
    inp("s5_c_im", [2, 64, 16, 64])
    inp("s5_d", [8, 128])
    inp("s5_w_glu", [D, 2 * D])
    inp("final_norm_g", [8, 128])
    io["out"] = nc.dram_tensor("out", [SEQ, D], F32, kind="ExternalOutput").ap()
    io["hT"] = nc.dram_tensor("hT", [D, T], F32).ap()
    declare_fnet_consts(nc, io)
    declare_ret_consts(nc, io)
    declare_s5_consts(nc, io)
    io["uT_d"] = nc.dram_tensor("uT_d", [D, T], BF16).ap()
    return io


def alloc_consts(k, es):
    nc = k.nc
    sb = lambda name, shape, dt: es.enter_context(nc.sbuf_tensor(k.uname(name), shape, dt))
    k.ident = sb("ident", [128, 128], F32)
    k.ones_b = sb("ones_b", [128, 128], BF16)
    k.vecT = sb("vecT", [128, NVEC], F32)
    k.sc = sb("sc", [128, 8, 2], BF16)
    k.mod = sb("mod", [128, DEPTH, 72, 2], F32)
    k.A = sb("Asc", [128, DEPTH, 3, 8, 2], F32)
    k.B = sb("Bsh", [128, DEPTH, 3, 8, 2], F32)
    k.G = sb("Ggt", [128, DEPTH, 3, 8, 2], F32)
    k.R_const = Res("const")
    k.R_vec = Res("vecT")
    k.R_mod = Res("mod")
    k.R_hT = [[Res("hT%d_%d" % (i, j)) for j in range(NCH)] for i in range(len(TILES))]
    P = k.P
    k.memset("pool", k.ident[:, :], 0.0, [k.R_const])
    P.op("pool", lambda e: e.affine_select(out=k.ident[:, :], in_=k.ident[:, :], pattern=[[-1, 128]],
                                            compare_op=ALU.not_equal, fill=1.0, base=0, channel_multiplier=1),
         [k.R_const], [k.R_const])
    k.memset("dve", k.ones_b[:, :], 1.0, [k.R_const])


def prologue(k, io):
    nc, P = k.nc, k.P
    from contextlib import ExitStack
    with ExitStack() as es:
        sb = lambda name, shape, dt: es.enter_context(nc.sbuf_tensor(k.uname(name), shape, dt))
        ps = lambda name, shape: es.enter_context(nc.psum_tensor(k.uname(name), shape, F32))
        stg = [sb("vstg%d" % i, [128, 128], F32) for i in range(4)]
        R_stg = [Res("vstg%d" % i) for i in range(4)]
        tp = [ps("tp%d" % i, [128, 512]) for i in range(2)]
        R_tp = [Res("tp%d" % i) for i in range(2)]
        rows0 = [(io["c"], 8, 0), (io["c_ctx"], 8, 8), (io["final_norm_g"], 8, 16), (io["s5_d"], 8, 24),
                 (io["norm_g"], 96, 32)]
        for ap, n, r0 in rows0:
            k.dma("sp", stg[0][r0:r0 + n, :], ap, [], [R_stg[0]])
        for b, (r0, n) in enumerate([(0, 128), (128, 128), (256, 32)]):
            k.dma("sp", stg[b + 1][0:n, :], io["b_mod"][r0:r0 + n, :], [], [R_stg[b + 1]])
        col = 0
        for b, n in enumerate([128, 128, 128, 32]):
            k.tr(tp[b % 2][:, 0:n], stg[b][0:n, :], k.ident[0:n, 0:n], [R_stg[b], k.R_const], [R_tp[b % 2]])
            k.copy("dve", k.vecT[:, col:col + n], tp[b % 2][:, 0:n], [R_tp[b % 2]], [k.R_vec])
            col += n
        k.act(k.sc[:, :, 0], k.vecT[:, VC_C:VC_C + 8], AF.Silu, [k.R_vec], [k.R_mod])
        k.act(k.sc[:, :, 1], k.vecT[:, VC_CC:VC_CC + 8], AF.Silu, [k.R_vec], [k.R_mod])

        wblk = [sb("wmodblk%d" % i, [128, 8, 1024], BF16) for i in range(2)]
        R_wblk = [Res("wmodblk%d" % i) for i in range(2)]
        prep = s5_prep_gen(k, io, sb, ps)
        mps = [ps("modps%d" % i, [128, 72, 2]) for i in range(2)]
        R_mps = [Res("modps%d" % i) for i in range(2)]
        nb = 0
        for li in range(DEPTH):
            wv = io["w_mod"][li].rearrange("(c p) n -> p c n", p=128)
            for kk in range(NMOD):
                b = nb % 2
                nb += 1
                for _ in range(5):
                    next(prep, None)
                for h2 in range(2):
                    k.dma("pool", wblk[b][:, h2 * 4:(h2 + 1) * 4, :], wv[:, h2 * 4:(h2 + 1) * 4, kk * 1024:(kk + 1) * 1024],
                          [], [R_wblk[b]])
                for j in range(8):
                    for kc in range(8):
                        k.mm(mps[li % 2][:, kk * 8 + j, :], wblk[b][:, kc, j * 128:(j + 1) * 128], k.sc[:, kc, :],
                             kc == 0, kc == 7, [R_wblk[b], k.R_mod], [R_mps[li % 2]], signal=(kc == 7))
            for r in range(2):
                k.tt("dve", k.mod[:, li, :, r], mps[li % 2][:, :, r], k.vecT[:, VC_BM + li * 72:VC_BM + (li + 1) * 72],
                     ALU.add, [R_mps[li % 2], k.R_vec], [k.R_mod])
            for s in range(3):
                gcol = VC_NG + (li * 3 + s) * 8
                for r in range(2):
                    sh = k.mod[:, li, (3 * s) * 8:(3 * s) * 8 + 8, r]
                    scl = k.mod[:, li, (3 * s + 1) * 8:(3 * s + 1) * 8 + 8, r]
                    gt = k.mod[:, li, (3 * s + 2) * 8:(3 * s + 2) * 8 + 8, r]
                    k.stt("dve", k.A[:, li, s, :, r], scl, 1.0, k.vecT[:, gcol:gcol + 8], ALU.add, ALU.mult,
                          [k.R_mod, k.R_vec], [k.R_mod])
                    k.copy("dve", k.B[:, li, s, :, r], sh, [k.R_mod], [k.R_mod])
                    if s == 1:
                        k.copy("dve", k.G[:, li, s, :, r], gt, [k.R_mod], [k.R_mod])
                    else:
                        k.ts("dve", k.G[:, li, s, :, r], gt, 0.5, None, ALU.mult, None, [k.R_mod], [k.R_mod])

        for _ in prep:
            pass
    P.barrier()
    with ExitStack() as es:
        sb = lambda name, shape, dt: es.enter_context(nc.sbuf_tensor(k.uname(name), shape, dt))
        ps = lambda name, shape: es.enter_context(nc.psum_tensor(k.uname(name), shape, F32))
        tp = [ps("tpx%d" % i, [128, 512]) for i in range(2)]
        R_tp = [Res("tpx%d" % i) for i in range(2)]
        xs = [sb("xs%d" % i, [128, 4, D], F32) for i in range(2)]
        R_xs = [Res("xs%d" % i) for i in range(2)]
        hst = [sb("hst%d" % i, [128, 8, 512], F32) for i in range(2)]
        R_hst = [Res("hst%d" % i) for i in range(2)]
        ncp = 0
        for ti, (t0, N, isctx) in enumerate(TILES):
            b = ti % 2
            ns = N // 128
            src = io["ctx"] if isctx else io["x"][t0 - CTX:t0 - CTX + N, :]
            k.dma("sp", xs[b][:, 0:ns, :], src.rearrange("(s p) d -> p s d", p=128), [], [R_xs[b]])
            for j in range(8):
                pb = ncp % 2
                for s in range(ns):
                    k.tr(tp[pb][:, s * 128:(s + 1) * 128], xs[b][:, s, j * 128:(j + 1) * 128], k.ident[:, :],
                         [R_xs[b], k.R_const], [R_tp[pb]])
                k.copy("act" if ncp % 2 else "dve", hst[b][:, j, 0:N], tp[pb][:, 0:N], [R_tp[pb]], [R_hst[b]])
                ncp += 1
            k.dma("sp", io["hT"][:, t0:t0 + N].rearrange("(c p) t -> p c t", p=128), hst[b][:, :, 0:N],
                  [R_hst[b]], k.R_hT[ti])
    P.barrier()


def ffn_phase(k, io, li, half):
    nc, P = k.nc, k.P
    s = 0 if half == 0 else 2
    from contextlib import ExitStack
    with ExitStack() as es:
        sb = lambda name, shape, dt: es.enter_context(nc.sbuf_tensor(k.uname(name), shape, dt))
        ps = lambda name, shape: es.enter_context(nc.psum_tensor(k.uname(name), shape, F32))
        w_in = sb("w_in", [128, 8, 2 * DFF], BF16)
        w_out = sb("w_out", [128, FCH, D], BF16)
        hbuf = sb("hbuf", [128, 8, 512], F32)
        uT = sb("uT", [128, 8, 512], BF16)
        gT = sb("gT", [128, FCH, 512], BF16)
        rstd = sb("rstd", [128, 512], F32)
        sqb = [sb("sqb%d" % i, [128, 512], BF16) for i in range(2)]
        tmp = [sb("tmp%d" % i, [128, 512], F32) for i in range(2)]
        sg = [sb("sg%d" % i, [128, 512], F32) for i in range(2)]
        hres = [sb("hres%d" % i, [128, 512], F32) for i in range(3)]
        pa = [ps("pa%d" % i, [128, 512]) for i in range(2)]
        pb = [ps("pb%d" % i, [128, 512]) for i in range(2)]
        py = [ps("py%d" % i, [128, 512]) for i in range(2)]
        pss = ps("pss", [128, 512])
        NWB = 11
        R_win = [Res("w_in%d" % i) for i in range(2 * NWB)]
        R_wout = [Res("w_out%d" % i) for i in range(2)]
        R_hbuf, R_uT, R_rstd, R_pss = Res("hbuf"), Res("uT"), Res("rstd"), Res("pss")
        R_gT = [Res("gT%d" % i) for i in range(FCH)]
        R_sqb = [Res("sqb%d" % i) for i in range(2)]
        R_tmp = [Res("tmp%d" % i) for i in range(2)]
        R_sg = [Res("sg%d" % i) for i in range(2)]
        R_hres = [Res("hres%d" % i) for i in range(3)]
        R_pa = [Res("pa%d" % i) for i in range(2)]
        R_pb = [Res("pb%d" % i) for i in range(2)]
        R_py = [Res("py%d" % i) for i in range(2)]

        wi = io["ffn_w_in"][li, half].rearrange("(c p) n -> p c n", p=128)
        wo = io["ffn_w_out"][li, half].rearrange("(c p) n -> p c n", p=128)
        for blk in range(NWB):
            for ab in range(2):
                c0 = ab * DFF + blk * 256
                k.dma("pool", w_in[:, :, c0:c0 + 256], wi[:, :, c0:c0 + 256], [], [R_win[ab * NWB + blk]])
        for hh in range(2):
            k.dma("pool", w_out[:, hh * 11:(hh + 1) * 11, :], wo[:, hh * 11:(hh + 1) * 11, :], [], [R_wout[hh]])

        def A1(ti):
            t0, N, isctx = TILES[ti]
            k.dma("sp", hbuf[:, :, 0:N], io["hT"][:, t0:t0 + N].rearrange("(c p) t -> p c t", p=128),
                  k.R_hT[ti], [R_hbuf])
            for j in range(8):
                k.act(sqb[j % 2][:, 0:N], hbuf[:, j, 0:N], AF.Square, [R_hbuf], [R_sqb[j % 2]])
                k.mm(pss[:, 0:N], k.ones_b[:, :], sqb[j % 2][:, 0:N], j == 0, j == 7, [R_sqb[j % 2], k.R_const], [R_pss])
            k.ts("dve", rstd[:, 0:N], pss[:, 0:N], 1.0 / D, EPS, ALU.mult, ALU.add, [R_pss], [R_rstd])
            k.act(rstd[:, 0:N], rstd[:, 0:N], AF.Sqrt, [R_rstd], [R_rstd])
            P.op("dve", (lambda o, i: (lambda e: e.reciprocal(out=o, in_=i)))(rstd[:, 0:N], rstd[:, 0:N]), [R_rstd], [R_rstd])

        def A2(ti):
            t0, N, isctx = TILES[ti]
            r = 1 if isctx else 0
            for j in range(8):
                k.stt("dve", tmp[j % 2][:, 0:N], hbuf[:, j, 0:N], k.A[:, li, s, j, r:r + 1], rstd[:, 0:N], ALU.mult, ALU.mult,
                      [R_hbuf, R_rstd, k.R_mod], [R_tmp[j % 2]])
                k.act(uT[:, j, 0:N], tmp[j % 2][:, 0:N], AF.Identity, [R_tmp[j % 2], k.R_mod], [R_uT],
                      bias=k.B[:, li, s, j, r:r + 1])

        def Bst(ti, c_lo, c_hi):
            t0, N, isctx = TILES[ti]
            for c in range(c_lo, c_hi):
                bk = c % 2
                for ab, pp, Rp in ((0, pa, R_pa), (1, pb, R_pb)):
                    col = ab * DFF + c * 128
                    Rw = R_win[ab * NWB + c // 2]
                    for kc in range(8):
                        k.mm(pp[bk][:, 0:N], w_in[:, kc, col:col + 128], uT[:, kc, 0:N], kc == 0, kc == 7,
                             [Rw, R_uT], [Rp[bk]], signal=(kc == 7))
                k.act(sg[bk][:, 0:N], pa[bk][:, 0:N], AF.Silu, [R_pa[bk]], [R_sg[bk]])
                k.tt("dve", gT[:, c, 0:N], sg[bk][:, 0:N], pb[bk][:, 0:N], ALU.mult, [R_sg[bk], R_pb[bk]], [R_gT[c]])

        def Cst(ti):
            t0, N, isctx = TILES[ti]
            r = 1 if isctx else 0
            for j in range(8):
                bk = j % 2
                hb = j % 3
                k.dma("sp", hres[hb][:, 0:N], io["hT"][j * 128:(j + 1) * 128, t0:t0 + N], [k.R_hT[ti][j]], [R_hres[hb]])
                for c in range(FCH):
                    k.mm(py[bk][:, 0:N], w_out[:, c, j * 128:(j + 1) * 128], gT[:, c, 0:N], c == 0, c == FCH - 1,
                         [R_wout[c // 11], R_gT[c]], [R_py[bk]], signal=(c == FCH - 1))
                k.stt("dve", hres[hb][:, 0:N], py[bk][:, 0:N], k.G[:, li, s, j, r:r + 1], hres[hb][:, 0:N], ALU.mult, ALU.add,
                      [R_py[bk], R_hres[hb], k.R_mod], [R_hres[hb]])
                k.dma("sp", io["hT"][j * 128:(j + 1) * 128, t0:t0 + N], hres[hb][:, 0:N], [R_hres[hb]], [k.R_hT[ti][j]])

        tiles = list(range(len(TILES)))
        if k.skip_ctx(li):
            tiles = tiles[1:]
        stop = k.dbg.get("ffn_stop", "C")
        tiles = tiles[:k.dbg.get("ffn_tiles", 99)]
        if stop != "w":
            A1(tiles[0])
            A2(tiles[0])
        for n_, ti in enumerate(tiles):
            if stop in ("w", "A"):
                break
            Bst(ti, 0, 11)
            if n_ + 1 < len(tiles):
                A1(tiles[n_ + 1])
            Bst(ti, 11, FCH)
            if n_ + 1 < len(tiles):
                A2(tiles[n_ + 1])
            if stop == "B":
                continue
            Cst(ti)
    P.barrier()


def final_phase(k, io):
    nc, P = k.nc, k.P
    from contextlib import ExitStack
    with ExitStack() as es:
        sb = lambda name, shape, dt: es.enter_context(nc.sbuf_tensor(k.uname(name), shape, dt))
        ps = lambda name, shape: es.enter_context(nc.psum_tensor(k.uname(name), shape, F32))
        hbuf = [sb("fhbuf%d" % i, [128, 8, 512], F32) for i in range(2)]
        R_hbuf = [Res("fhbuf%d" % i) for i in range(2)]
        sqb = [sb("fsqb%d" % i, [128, 512], BF16) for i in range(2)]
        R_sqb = [Res("fsqb%d" % i) for i in range(2)]
        rstd = sb("frstd", [128, 512], F32)
        R_rstd = Res("frstd")
        xn = [sb("fxn%d" % i, [128, 512], F32) for i in range(2)]
        R_xn = [Res("fxn%d" % i) for i in range(2)]
        ost = [sb("fost%d" % i, [128, 4, D], F32) for i in range(2)]
        R_ost = [Res("fost%d" % i) for i in range(2)]
        pss = ps("fpss", [128, 512])
        R_pss = Res("fpss")
        tp = [ps("ftp%d" % i, [128, 512]) for i in range(2)]
        R_tp = [Res("ftp%d" % i) for i in range(2)]
        R_out = Res("out")
        ncp = 0
        for n_, ti in enumerate(range(1, len(TILES))):
            t0, N, _ = TILES[ti]
            b = n_ % 2
            k.dma("sp", hbuf[b][:, :, 0:N], io["hT"][:, t0:t0 + N].rearrange("(c p) t -> p c t", p=128),
                  k.R_hT[ti], [R_hbuf[b]])
            for j in range(8):
                k.act(sqb[j % 2][:, 0:N], hbuf[b][:, j, 0:N], AF.Square, [R_hbuf[b]], [R_sqb[j % 2]])
                k.mm(pss[:, 0:N], k.ones_b[:, :], sqb[j % 2][:, 0:N], j == 0, j == 7, [R_sqb[j % 2], k.R_const], [R_pss])
            k.ts("dve", rstd[:, 0:N], pss[:, 0:N], 1.0 / D, EPS, ALU.mult, ALU.add, [R_pss], [R_rstd])
            k.act(rstd[:, 0:N], rstd[:, 0:N], AF.Sqrt, [R_rstd], [R_rstd])
            P.op("dve", (lambda o, i: (lambda e: e.reciprocal(out=o, in_=i)))(rstd[:, 0:N], rstd[:, 0:N]), [R_rstd], [R_rstd])
            for j in range(8):
                k.stt("dve", xn[j % 2][:, 0:N], hbuf[b][:, j, 0:N], k.vecT[:, VC_FNG + j:VC_FNG + j + 1], rstd[:, 0:N],
                      ALU.mult, ALU.mult, [R_hbuf[b], R_rstd, k.R_vec], [R_xn[j % 2]])
                pbk = ncp % 2
                for s_ in range(4):
                    k.tr(tp[pbk][:, s_ * 128:(s_ + 1) * 128], xn[j % 2][:, s_ * 128:(s_ + 1) * 128], k.ident[:, :],
                         [R_xn[j % 2], k.R_const], [R_tp[pbk]])
                k.copy("act" if ncp % 2 else "dve",
                       ost[b][:, :, j * 128:(j + 1) * 128], tp[pbk][:, :].rearrange("p (s f) -> p s f", f=128),
                       [R_tp[pbk]], [R_ost[b]])
                ncp += 1
            k.dma("sp", io["out"][t0 - CTX:t0 - CTX + N, :].rearrange("(s p) d -> p s d", p=128), ost[b][:, :, :],
                  [R_ost[b]], [R_out])
    P.barrier()


def _skip_ctx(self, li):
    return li == DEPTH - 1


K.skip_ctx = _skip_ctx

MIXERS = {}


def build_program(layers=range(DEPTH), mixers=True, do_final=True, ffn=True, dbg=None):
    from contextlib import ExitStack
    nc = bass.Bass("TRN2", target_bir_lowering=False)
    io = declare_io(nc)
    P = Prog(nc, n_dma=40, same_sync=bool((dbg or {}).get("same_sync", True)))
    k = K(nc, P)
    with ExitStack() as es:
        alloc_consts(k, es)
        k.dbg = dbg or {}
        prologue(k, io)
        for li in layers:
            if ffn:
                ffn_phase(k, io, li, 0)
            if mixers and (li % 3) in MIXERS:
                MIXERS[li % 3](k, io, li)
            if ffn:
                ffn_phase(k, io, li, 1)
        if do_final:
            final_phase(k, io)
        P.finish("sp")
        P.emit()
    return nc


def make_in_maps(inputs, cores=range(8)):
    f = lambda a: np.ascontiguousarray(np.asarray(a, dtype=np.float32))
    shared = {
        "c_ctx": f(inputs["c_ctx"]).reshape(8, 128),
        "w_mod": f(inputs["w_mod"]),
        "b_mod": f(inputs["b_mod"]).reshape(288, 128),
        "norm_g": f(inputs["norm_g"]).reshape(96, 128),
        "ffn_w_in": f(inputs["ffn_w_in"]),
        "ffn_w_out": f(inputs["ffn_w_out"]),
        "fnet_w_out": f(inputs["fnet_w_out"]),
        "ret_w_in": f(inputs["ret_w_in"]).reshape(D, 6144),
        "ret_w_out": f(inputs["ret_w_out"]).reshape(2048, D),
        "ret_decay_logit": f(inputs["ret_decay_logit"]).reshape(2, 4),
        "s5_lam_re": f(inputs["s5_lam_re"]).reshape(2, 64, 64),
        "s5_lam_im": f(inputs["s5_lam_im"]).reshape(2, 64, 64),
        "s5_log_dt": f(inputs["s5_log_dt"]).reshape(2, 64),
        "s5_b_re": f(inputs["s5_b_re"]).reshape(2, 64, 64, 16),
        "s5_b_im": f(inputs["s5_b_im"]).reshape(2, 64, 64, 16),
        "s5_c_re": f(inputs["s5_c_re"]).reshape(2, 64, 16, 64),
        "s5_c_im": f(inputs["s5_c_im"]).reshape(2, 64, 16, 64),
        "s5_d": f(inputs["s5_d"]).reshape(8, 128),
        "s5_w_glu": f(inputs["s5_w_glu"]).reshape(D, 2 * D),
        "final_norm_g": f(inputs["final_norm_g"]).reshape(8, 128),
    }
    shared.update(fnet_host_consts())
    shared.update(ret_host_consts())
    shared.update(s5_host_consts())
    x, c, ctx = f(inputs["x"]), f(inputs["c"]), f(inputs["ctx"])
    maps = []
    for b in cores:
        m = dict(shared)
        m["x"] = x[b]
        m["c"] = c[b].reshape(8, 128)
        m["ctx"] = ctx[b]
        maps.append(m)
    return maps


def kernel(**inputs):
    nc = build_program()
    maps = make_in_maps(inputs)
    res = run_bass_kernel_spmd(nc, maps, core_ids=list(range(8)))
    return np.stack([np.asarray(r["out"], dtype=np.float32) for r in res.results], axis=0)


def mk_modnorm(k, io, sb, ps):
    P = k.P
    hbuf = sb("mn_hbuf", [128, 8, 512], F32)
    sqb = [sb("mn_sqb%d" % i, [128, 512], BF16) for i in range(2)]
    tmp = [sb("mn_tmp%d" % i, [128, 512], F32) for i in range(2)]
    rstd = sb("mn_rstd", [128, 512], F32)
    pss = ps("mn_pss", [128, 512])
    R_hbuf, R_rstd, R_pss = Res("mn_hbuf"), Res("mn_rstd"), Res("mn_pss")
    R_sqb = [Res("mn_sqb%d" % i) for i in range(2)]
    R_tmp = [Res("mn_tmp%d" % i) for i in range(2)]

    def do(ti, li, s, uT_of, R_uT, chunks=range(8), raw_of=None):
        t0, N, isctx = TILES[ti]
        r = 1 if isctx else 0
        k.dma("sp", hbuf[:, :, 0:N], io["hT"][:, t0:t0 + N].rearrange("(c p) t -> p c t", p=128),
              k.R_hT[ti], [R_hbuf])
        for j in range(8):
            k.act(sqb[j % 2][:, 0:N], hbuf[:, j, 0:N], AF.Square, [R_hbuf], [R_sqb[j % 2]])
            k.mm(pss[:, 0:N], k.ones_b[:, :], sqb[j % 2][:, 0:N], j == 0, j == 7, [R_sqb[j % 2], k.R_const], [R_pss])
        k.ts("dve", rstd[:, 0:N], pss[:, 0:N], 1.0 / D, EPS, ALU.mult, ALU.add, [R_pss], [R_rstd])
        k.act(rstd[:, 0:N], rstd[:, 0:N], AF.Sqrt, [R_rstd], [R_rstd])
        P.op("dve", (lambda o, i: (lambda e: e.reciprocal(out=o, in_=i)))(rstd[:, 0:N], rstd[:, 0:N]), [R_rstd], [R_rstd])
        for n_, j in enumerate(chunks):
            k.stt("dve", tmp[n_ % 2][:, 0:N], hbuf[:, j, 0:N], k.A[:, li, s, j, r:r + 1], rstd[:, 0:N], ALU.mult, ALU.mult,
                  [R_hbuf, R_rstd, k.R_mod], [R_tmp[n_ % 2]])
            k.act(uT_of(j), tmp[n_ % 2][:, 0:N], AF.Identity, [R_tmp[n_ % 2], k.R_mod], [R_uT],
                  bias=k.B[:, li, s, j, r:r + 1])

    return do


def mk_resadd(k, io, sb, li):
    hres = [sb("ra_hres%d" % i, [128, 512], F32) for i in range(3)]
    R_hres = [Res("ra_hres%d" % i) for i in range(3)]
    cnt = [0]

    def do(ti, j, col0, N, y_ap, R_y, pre_reads=(), **dkw):
        t0, _, isctx = TILES[ti]
        r = 1 if isctx else 0
        hb = cnt[0] % 3
        cnt[0] += 1
        k.dma("sp", hres[hb][:, 0:N], io["hT"][j * 128:(j + 1) * 128, t0 + col0:t0 + col0 + N], [k.R_hT[ti][j]], [R_hres[hb]], **dkw)
        k.stt("dve", hres[hb][:, 0:N], y_ap, k.G[:, li, 1, j, r:r + 1], hres[hb][:, 0:N], ALU.mult, ALU.add,
              [R_y, R_hres[hb], k.R_mod] + list(pre_reads), [R_hres[hb]])
        k.dma("sp", io["hT"][j * 128:(j + 1) * 128, t0 + col0:t0 + col0 + N], hres[hb][:, 0:N], [R_hres[hb]], [k.R_hT[ti][j]], **dkw)

    return do


def declare_fnet_consts(nc, io):
    io["dft_c"] = nc.dram_tensor("dft_c", [128, 256], BF16, kind="ExternalInput").ap()
    io["dft_cosN"] = nc.dram_tensor("dft_cosN", [SEQ, SEQ], BF16, kind="ExternalInput").ap()
    io["dft_sinN"] = nc.dram_tensor("dft_sinN", [SEQ, SEQ], BF16, kind="ExternalInput").ap()
    io["dft_cosC"] = nc.dram_tensor("dft_cosC", [CTX, CTX], BF16, kind="ExternalInput").ap()
    io["dft_sinC"] = nc.dram_tensor("dft_sinC", [CTX, CTX], BF16, kind="ExternalInput").ap()
    io["fn_perm"] = nc.dram_tensor("fn_perm", [128, 4 * 128], BF16, kind="ExternalInput").ap()


def fnet_host_consts():
    import ml_dtypes
    bf = ml_dtypes.bfloat16
    c = np.arange(128)
    ang = 2.0 * np.pi * ((c[:, None] * c[None, :]) % 128) / 128.0
    dft_c = np.concatenate([np.cos(ang), -np.sin(ang)], axis=1).astype(np.float32).astype(bf)

    def mats(n):
        i = np.arange(n, dtype=np.int64)
        idx = (i[:, None] * i[None, :]) % n
        a = 2.0 * np.pi * np.arange(n, dtype=np.float64) / n
        return np.cos(a).astype(np.float32)[idx].astype(bf), np.sin(a).astype(np.float32)[idx].astype(bf)

    cN, sN = mats(SEQ)
    cC, sC = mats(CTX)
    J = np.zeros((128, 128), np.float32)
    for p in range(1, 128):
        J[128 - p, p] = 1.0
    E = np.zeros((128, 128), np.float32)
    E[0, 0] = 1.0
    perm = np.concatenate([J, -J, E, -E], axis=1).astype(bf)
    return {"dft_c": dft_c, "dft_cosN": cN, "dft_sinN": sN, "dft_cosC": cC, "dft_sinC": sC, "fn_perm": perm}


def fnet_mixer(k, io, li):
    nc, P = k.nc, k.P
    jw = li // 3
    from contextlib import ExitStack
    with ExitStack() as es:
        sb = lambda name, shape, dt: es.enter_context(nc.sbuf_tensor(k.uname(name), shape, dt))
        ps = lambda name, shape: es.enter_context(nc.psum_tensor(k.uname(name), shape, F32))
        modnorm = mk_modnorm(k, io, sb, ps)
        resadd = mk_resadd(k, io, sb, li)
        dftc = sb("dftc", [128, 256], BF16)
        w_out = sb("fw_out", [128, 8, D], BF16)
        X = sb("fX", [128, 32, 4, 256], BF16)
        cosb = sb("fcos", [128, 32, 512], BF16)
        sinb = sb("fsin", [128, 32, 512], BF16)
        uT = sb("fuT", [128, 8, 512], BF16)
        uTg = sb("fuTg", [128, 4, 512], BF16)
        R_uTg = Res("fuTg")
        R_uTd = [Res("uTd%d" % i) for i in range(len(TILES))]
        fT = [sb("ffT%d" % i, [128, 512], BF16) for i in range(8)]
        px = [ps("fpx%d" % i, [128, 512]) for i in range(2)]
        pf = [ps("fpf%d" % i, [128, 512]) for i in range(2)]
        py = [ps("fpy%d" % i, [128, 512]) for i in range(2)]
        R_dftc, R_wout, R_uT = Res("dftc"), Res("fw_out"), Res("fuT")
        R_X = [Res("fX%d" % i) for i in range(32)]
        R_cos = [Res("fcos%d" % i) for i in range(8)]
        R_sin = [Res("fsin%d" % i) for i in range(8)]
        R_fT = [Res("ffT%d" % i) for i in range(8)]
        R_px = [Res("fpx%d" % i) for i in range(2)]
        R_pf = [Res("fpf%d" % i) for i in range(2)]
        R_py = [Res("fpy%d" % i) for i in range(2)]
        k.dma("sp", dftc[:, :], io["dft_c"], [], [R_dftc])
        k.dma("pool", w_out[:, :, :], io["fnet_w_out"][jw].rearrange("(c p) n -> p c n", p=128), [], [R_wout])
        cnt = {"x": 0, "f": 0, "y": 0}

        def run(tiles, nch, groups_list, cosN, sinN, W, norm, sweep_only=False):
            for ti in tiles:
                t0, N, isctx = TILES[ti]
                modnorm(ti, li, 1, lambda j: uT[:, j, 0:N], R_uT)
                k.dma("sp", io["uT_d"][:, t0:t0 + N].rearrange("(c p) t -> p c t", p=128), uT[:, :, 0:N], [R_uT], [R_uTd[ti]])
            if sweep_only:
                return
            for groups in groups_list:
                ng = len(groups)
                for ti in tiles:
                    t0, N, isctx = TILES[ti]
                    g0 = groups[0]
                    k.dma("sp", uTg[:, 0:ng, 0:N],
                          io["uT_d"][g0 * 128:(g0 + ng) * 128, t0:t0 + N].rearrange("(c p) t -> p c t", p=128),
                          [R_uTd[ti]], [R_uTg])
                    c0 = (t0 - TILES[tiles[0]][0]) // 128
                    for sbi in range(N // 128):
                        for g2 in range(0, ng, 2):
                            bk = cnt["x"] % 2
                            cnt["x"] += 1
                            for u_ in range(2):
                                k.mm(px[bk][:, u_ * 256:(u_ + 1) * 256], uTg[:, g2 + u_, sbi * 128:(sbi + 1) * 128], dftc[:, :],
                                     True, True, [R_uTg, R_dftc], [R_px[bk]])
                            k.copy("act" if cnt["x"] % 2 else "dve",
                                   X[:, c0 + sbi, g2:g2 + 2, :], px[bk][:, :].rearrange("p (a b) -> p a b", b=256),
                                   [R_px[bk]], [R_X[c0 + sbi]])
                cq = max(1, nch // 8)
                for kt in range((nch * 128) // W):
                    for q in range(nch // cq):
                        k.dma("sp", cosb[:, q * cq:(q + 1) * cq, 0:W],
                              cosN.rearrange("(c p) k -> p c k", p=128)[:, q * cq:(q + 1) * cq, kt * W:(kt + 1) * W],
                              [], [R_cos[q]])
                        k.dma("act", sinb[:, q * cq:(q + 1) * cq, 0:W],
                              sinN.rearrange("(c p) k -> p c k", p=128)[:, q * cq:(q + 1) * cq, kt * W:(kt + 1) * W],
                              [], [R_sin[q]])
                    pf4 = [pf[0], pf[1], px[0], px[1]]
                    R_pf4 = [R_pf[0], R_pf[1], R_px[0], R_px[1]]
                    for c in range(nch):
                        for gi, g in enumerate(groups):
                            last = (c == nch - 1)
                            k.mm(pf4[gi][:, 0:W], X[:, c, gi, 0:128], cosb[:, c, 0:W], c == 0, False,
                                 [R_X[c], R_cos[c // cq]], [R_pf4[gi]], signal=False)
                            k.mm(pf4[gi][:, 0:W], X[:, c, gi, 128:256], sinb[:, c, 0:W], False, last,
                                 [R_X[c], R_sin[c // cq]], [R_pf4[gi]], signal=(last or (c % cq == cq - 1 and gi == ng - 1)))
                    for gi, g in enumerate(groups):
                        if gi % 2:
                            k.act(fT[gi][:, 0:W], pf4[gi][:, 0:W], AF.Copy, [R_pf4[gi]], [R_fT[gi]], scale=norm)
                        else:
                            k.ts("dve", fT[gi][:, 0:W], pf4[gi][:, 0:W], norm, None, ALU.mult, None, [R_pf4[gi]], [R_fT[gi]])
                    ti = tiles[0] + (kt * W) // 512 if W == 512 else tiles[0]
                    for j in range(8):
                        bk = cnt["y"] % 2
                        cnt["y"] += 1
                        for gi, g in enumerate(groups):
                            k.mm(py[bk][:, 0:W], w_out[:, g, j * 128:(j + 1) * 128], fT[gi][:, 0:W], gi == 0, gi == ng - 1,
                                 [R_wout, R_fT[gi]], [R_py[bk]], signal=(gi == ng - 1))
                        resadd(ti, j, 0, W, py[bk][:, 0:W], R_py[bk])

        if not k.skip_ctx(li):
            run([0], 2, [list(range(4)), list(range(4, 8))], io["dft_cosC"], io["dft_sinC"], 256, 1.0 / math.sqrt(CTX * 128))
        if k.dbg.get("fnet_old"):
            run(list(range(1, 9)), 32, [list(range(4)), list(range(4, 8))], io["dft_cosN"], io["dft_sinN"], 512,
                1.0 / math.sqrt(SEQ * 128))
        else:
            run(list(range(1, 9)), 32, None, None, None, 512, None, sweep_only=True)
    P.barrier()
    if not k.dbg.get("fnet_old"):
        fnet_latent_folded(k, io, li)


def fnet_latent_folded(k, io, li):
    nc, P = k.nc, k.P
    jw = li // 3
    from contextlib import ExitStack
    R = lambda n: Res(n)
    norm = 1.0 / math.sqrt(SEQ * 128)
    NF = 17
    with ExitStack() as es:
        sb = lambda name, shape, dt: es.enter_context(nc.sbuf_tensor(k.uname(name), shape, dt))
        ps = lambda name, shape: es.enter_context(nc.psum_tensor(k.uname(name), shape, F32))
        resadd = mk_resadd(k, io, sb, li)
        dftc = sb("g_dftc", [128, 256], BF16)
        perm = sb("g_perm", [128, 4, 128], BF16)
        w_out = sb("g_wout", [128, 8, D], BF16)
        Xf = sb("g_Xf", [128, NF, 8, 256], BF16)
        px = [ps("g_px%d" % i, [128, 512]) for i in range(2)]
        pf = [ps("g_pf%d" % i, [128, 512]) for i in range(2)]
        py = [ps("g_py%d" % i, [128, 512]) for i in range(2)]
        R_c, R_wout = R("g_c"), R("g_wout")
        R_Xf = [R("g_Xf%d" % i) for i in range(NF)]
        R_px = [R("g_px0"), R("g_px1")]
        R_pf = [R("g_pf0"), R("g_pf1")]
        R_py = [R("g_py0"), R("g_py1")]
        k.dma("sp", dftc[:, :], io["dft_c"], [], [R_c])
        k.dma("sp", perm[:, :, :], io["fn_perm"].rearrange("p (a b) -> p a b", b=128), [], [R_c])
        k.dma("pool", w_out[:, :, :], io["fnet_w_out"][jw].rearrange("(c p) n -> p c n", p=128), [], [R_wout])
        with ExitStack() as es1:
            sb1 = lambda name, shape, dt: es1.enter_context(nc.sbuf_tensor(k.uname(name), shape, dt))
            Xup = sb1("g_Xup", [128, 16, 8, 256], BF16)
            uTg = sb1("g_uTg", [128, 8, 512], BF16)
            R_uTg = R("g_uTg")
            R_Xup = [R("g_Xup%d" % i) for i in range(16)]
            nx = [0]

            def load_u(ti):
                t0, N, _ = TILES[ti]
                k.dma("sp", uTg[:, :, 0:N], io["uT_d"][:, t0:t0 + N].rearrange("(c p) t -> p c t", p=128), [], [R_uTg])

            def evac(dst, src, Rs, Rd):
                nx[0] += 1
                k.copy("act" if nx[0] % 2 else "dve", dst, src, [Rs], [Rd])

            for ti in range(5, 9):
                t0 = TILES[ti][0]
                load_u(ti)
                for sbi in range(4):
                    cu = (t0 - CTX) // 128 + sbi - 16
                    for g2 in range(0, 8, 2):
                        bk = nx[0] % 2
                        for u_ in range(2):
                            k.mm(px[bk][:, u_ * 256:(u_ + 1) * 256], uTg[:, g2 + u_, sbi * 128:(sbi + 1) * 128], dftc[:, :],
                                 True, True, [R_uTg, R_c], [R_px[bk]])
                        evac(Xup[:, cu, g2:g2 + 2, :], px[bk][:, :].rearrange("p (a b) -> p a b", b=256), R_px[bk], R_Xup[cu])
            for ti in range(1, 5):
                t0 = TILES[ti][0]
                load_u(ti)
                for sbi in range(4):
                    c = (t0 - CTX) // 128 + sbi
                    for g2 in range(0, 8, 2):
                        bk = nx[0] % 2
                        for u_ in range(2):
                            g = g2 + u_
                            for w in range(2):
                                o_ = px[bk][:, u_ * 256 + w * 128:u_ * 256 + (w + 1) * 128]
                                rd = [R_uTg, R_c, R_Xup[15 - c]] + ([R_Xup[16 - c]] if c >= 1 else [])
                                k.mm(o_, uTg[:, g, sbi * 128:(sbi + 1) * 128], dftc[:, w * 128:(w + 1) * 128], True, False,
                                     rd, [R_px[bk]], signal=False)
                                k.mm(o_, perm[:, w, :], Xup[:, 15 - c, g, w * 128:(w + 1) * 128], False, c == 0,
                                     rd, [R_px[bk]], signal=(c == 0))
                                if c >= 1:
                                    k.mm(o_, perm[:, 2 + w, :], Xup[:, 16 - c, g, w * 128:(w + 1) * 128], False, True,
                                         rd, [R_px[bk]], signal=True)
                        evac(Xf[:, c, g2:g2 + 2, :], px[bk][:, :].rearrange("p (a b) -> p a b", b=256), R_px[bk], R_Xf[c])
            k.memset("dve", Xf[:, 16, :, :], 0.0, [R_Xf[16]])
            for g2 in range(0, 8, 2):
                bk = nx[0] % 2
                for u_ in range(2):
                    k.mm(px[bk][:, u_ * 256:u_ * 256 + 128], perm[:, 2, :], Xup[:, 0, g2 + u_, 0:128], True, True,
                         [R_c, R_Xup[0]], [R_px[bk]])
                evac(Xf[:, 16, g2:g2 + 2, 0:128], px[bk][:, :].rearrange("p (a b) -> p a b", b=256)[:, :, 0:128], R_px[bk], R_Xf[16])
        P.barrier()
        with ExitStack() as es2:
            sb2 = lambda name, shape, dt: es2.enter_context(nc.sbuf_tensor(k.uname(name), shape, dt))
            cosb = [sb2("g_cos%d" % i, [128, NF, 512], BF16) for i in range(2)]
            sinb = [sb2("g_sin%d" % i, [128, NF, 512], BF16) for i in range(2)]
            fT = [sb2("g_fT%d" % i, [128, 512], BF16) for i in range(8)]
            SUBS = [(0, 4), (4, 8), (8, 12), (12, NF)]
            sub_of = [0] * 4 + [1] * 4 + [2] * 4 + [3] * 5
            R_cos = [[R("g_cos%d_%d" % (i, q)) for q in range(4)] for i in range(2)]
            R_sin = [[R("g_sin%d_%d" % (i, q)) for q in range(4)] for i in range(2)]
            R_fT = [R("g_fT%d" % i) for i in range(8)]
            cv = io["dft_cosN"][0:NF * 128, :].rearrange("(c p) k -> p c k", p=128)
            sv = io["dft_sinN"][0:NF * 128, :].rearrange("(c p) k -> p c k", p=128)
            pf4 = [pf[0], pf[1], px[0], px[1]]
            R_pf4 = [R_pf[0], R_pf[1], R_px[0], R_px[1]]
            ny = 0
            fTm = [sb2("g_fTm%d" % i, [128, 512], BF16) for i in range(8)]
            R_fTm = [R("g_fTm%d" % i) for i in range(8)]
            Bs = [sb2("g_Bs%d" % i, [128, 512], F32) for i in range(2)]
            R_Bs = [R("g_Bs0"), R("g_Bs1")]
            fT0 = sb2("g_fT0", [128, 8], BF16)
            R_fT0 = R("g_fT0")
            for g in range(8):
                for c in range(NF):
                    k.mm(pf[0][:, g:g + 1], Xf[:, c, g, 0:128], k.ones_b[:, 0:1], c == 0, c == NF - 1,
                         [R_Xf[c], k.R_const], [R_pf[0]], signal=(c == NF - 1))
            k.ts("dve", fT0[:, :], pf[0][:, 0:8], norm, None, ALU.mult, None, [R_pf[0]], [R_fT0])
            for j in range(8):
                bk = ny % 2
                ny += 1
                for g in range(8):
                    k.mm(py[bk][:, 0:1], w_out[:, g, j * 128:(j + 1) * 128], fT0[:, g:g + 1], g == 0, g == 7,
                         [R_wout, R_fT0], [R_py[bk]], signal=(g == 7))
                resadd(1, j, 0, 1, py[bk][:, 0:1], R_py[bk], allow_slow_non_contiguous=True)
            for jw in range(4):
                par = jw % 2
                k0 = 512 * jw + 1
                for qi, (a, b) in enumerate(SUBS):
                    k.dma("sp", cosb[par][:, a:b, :], cv[:, a:b, k0:k0 + 512], [], [R_cos[par][qi]])
                    k.dma("act", sinb[par][:, a:b, :], sv[:, a:b, k0:k0 + 512], [], [R_sin[par][qi]])
                for gp in range(4):
                    for c in range(NF):
                        for u in range(2):
                            g = 2 * gp + u
                            k.mm(pf4[2 * u][:, :], Xf[:, c, g, 0:128], cosb[par][:, c, :], c == 0, c == NF - 1,
                                 [R_Xf[c], R_cos[par][sub_of[c]]], [R_pf4[2 * u]], signal=(c == NF - 1))
                            k.mm(pf4[2 * u + 1][:, :], Xf[:, c, g, 128:256], sinb[par][:, c, :], c == 0, c == NF - 1,
                                 [R_Xf[c], R_sin[par][sub_of[c]]], [R_pf4[2 * u + 1]], signal=(c == NF - 1))
                    for u in range(2):
                        g = 2 * gp + u
                        k.act(Bs[u][:, :], pf4[2 * u + 1][:, :], AF.Copy, [R_pf4[2 * u + 1]], [R_Bs[u]], scale=norm)
                        k.stt("dve", fT[g][:, :], pf4[2 * u][:, :], norm, Bs[u][:, :], ALU.mult, ALU.add,
                              [R_pf4[2 * u], R_Bs[u]], [R_fT[g]])
                        k.stt("dve", fTm[g][:, ::-1], pf4[2 * u][:, :], norm, Bs[u][:, :], ALU.mult, ALU.subtract,
                              [R_pf4[2 * u], R_Bs[u]], [R_fTm[g]])
                pm = SEQ - k0 - 511
                for (fTs, R_fs, pos, c_lo) in ((fT, R_fT, k0, 0), (fTm, R_fTm, pm, 1 if jw == 3 else 0)):
                    Nw = 512 - c_lo
                    for j in range(8):
                        bk = ny % 2
                        ny += 1
                        for g in range(8):
                            k.mm(py[bk][:, 0:Nw], w_out[:, g, j * 128:(j + 1) * 128], fTs[g][:, c_lo:512], g == 0, g == 7,
                                 [R_wout, R_fs[g]], [R_py[bk]], signal=(g == 7))
                        resadd(1, j, pos + c_lo, Nw, py[bk][:, 0:Nw], R_py[bk])
    P.barrier()


MIXERS[0] = fnet_mixer


NCHK = T // 128


def declare_ret_consts(nc, io):
    io["ret_expo"] = nc.dram_tensor("ret_expo", [128, 4 * 128], F32, kind="ExternalInput").ap()
    io["ret_m01"] = nc.dram_tensor("ret_m01", [128, 2 * 128], F32, kind="ExternalInput").ap()
    io["ret_ramp"] = nc.dram_tensor("ret_ramp", [128, NCHK], F32, kind="ExternalInput").ap()
    io["ropeC"] = nc.dram_tensor("ropeC", [256, SEQ], F32, kind="ExternalInput").ap()
    io["ropeS"] = nc.dram_tensor("ropeS", [256, SEQ], F32, kind="ExternalInput").ap()
    io["zT_d"] = nc.dram_tensor("zT_d", [2048, T], BF16).ap()


def ret_host_consts():
    m = np.arange(128, dtype=np.float32)[:, None]
    l = np.arange(128, dtype=np.float32)[None, :]
    expo = np.concatenate([np.maximum(l - m, 0), l - m + 128, np.maximum(m - l, 0), m - l + 128], axis=1).astype(np.float32)
    m01 = np.concatenate([(m <= l), (m > l)], axis=1).astype(np.float32)
    ramp = np.broadcast_to(128.0 * np.arange(NCHK, dtype=np.float32)[None, :], (128, NCHK)).copy()
    t = np.arange(SEQ)
    inv_freq = np.exp(np.float32(-math.log(10000.0)) * np.arange(64, dtype=np.float32) / np.float32(64)).astype(np.float32)
    ang_r = ((t // 64).astype(np.float32)[:, None] * inv_freq[None, :]).astype(np.float32)
    ang_c = ((t % 64).astype(np.float32)[:, None] * inv_freq[None, :]).astype(np.float32)
    C = np.zeros((256, SEQ), np.float32)
    S = np.zeros((256, SEQ), np.float32)
    for d, ang in enumerate((ang_r, ang_c)):
        c, s = np.cos(ang).T.astype(np.float32), np.sin(ang).T.astype(np.float32)
        C[d * 128:d * 128 + 64] = c
        C[d * 128 + 64:d * 128 + 128] = c
        S[d * 128:d * 128 + 64] = -s
        S[d * 128 + 64:d * 128 + 128] = s
    return {"ret_expo": expo, "ret_m01": m01, "ret_ramp": ramp, "ropeC": C, "ropeS": S}


def ret_mixer(k, io, li):
    nc, P = k.nc, k.P
    from contextlib import ExitStack
    with ExitStack() as es:
        sb = lambda name, shape, dt: es.enter_context(nc.sbuf_tensor(k.uname(name), shape, dt))
        ps = lambda name, shape, dt=F32: es.enter_context(nc.psum_tensor(k.uname(name), shape, dt))
        modnorm = mk_modnorm(k, io, sb, ps)
        resadd = mk_resadd(k, io, sb, li)
        R = lambda n: Res(n)
        expo = sb("r_expo", [128, 4, 128], F32)
        m01 = sb("r_m01", [128, 2, 128], F32)
        ramp = sb("r_ramp", [128, NCHK], F32)
        ones1 = sb("r_ones1", [1, 128], F32)
        lg1 = sb("r_lg1", [1, 8], F32)
        lgam = sb("r_lgam", [128, 8], F32)
        Etab = sb("r_Etab", [128, 8, 2, 128], F32)
        Wdiag = sb("r_Wdiag", [128, 4, 128], F32)
        apow = sb("r_apow", [128, 8, NCHK], F32)
        R_tab = R("r_tab")
        pA = ps("r_pA", [128, 512])
        pB = ps("r_pB", [128, 512])
        psc_t = [ps("r_psc%d" % i, [128, 512]) for i in range(2)]
        NSC = 4
        psc = [psc_t[0], psc_t[1], pA, pB]
        po = ps("r_po", [128, 512])
        pg = ps("r_pg", [128, 512])
        pt = ps("r_pt", [128, 512], BF16)
        R_pA, R_pB, R_po, R_pg, R_pt = R("pA"), R("pB"), R("po"), R("pg"), R("pt")
        R_psc = [R("psc0"), R("psc1"), R_pA, R_pB]
        k.dma("sp", expo[:, :, :], io["ret_expo"].rearrange("p (a b) -> p a b", b=128), [], [R_tab])
        k.dma("sp", m01[:, :, :], io["ret_m01"].rearrange("p (a b) -> p a b", b=128), [], [R_tab])
        k.dma("sp", ramp[:, :], io["ret_ramp"], [], [R_tab])
        k.dma("sp", lg1[:, :], io["ret_decay_logit"].rearrange("(o a) b -> o (a b)", o=1), [], [R_tab])
        k.memset("dve", ones1[:, :], 1.0, [R_tab])
        k.mm(pA[:, 0:8], ones1[0:1, :], lg1[0:1, :], True, True, [R_tab], [R_pA])
        k.act(lgam[:, :], pA[:, 0:8], AF.Exp, [R_pA], [R_tab], scale=-1.0)
        k.ts("dve", lgam[:, :], lgam[:, :], 1.0, None, ALU.add, None, [R_tab], [R_tab])
        k.act(lgam[:, :], lgam[:, :], AF.Ln, [R_tab], [R_tab])
        k.ts("dve", lgam[:, :], lgam[:, :], -1.0, None, ALU.mult, None, [R_tab], [R_tab])
        for d in range(2):
            for h in range(4):
                col = d * 4 + h
                for w in range(2):
                    k.act(Etab[:, col, w, :], expo[:, d * 2 + w, :], AF.Exp, [R_tab], [R_tab], scale=lgam[:, col:col + 1])
                k.tt("dve", Etab[:, col, 0, :], Etab[:, col, 0, :], m01[:, d, :], ALU.mult, [R_tab], [R_tab])
                k.act(apow[:, col, :], ramp[:, :], AF.Exp, [R_tab], [R_tab], scale=lgam[:, col:col + 1])
        for h in range(4):
            k.tt("dve", Wdiag[:, h, :], Etab[:, h, 0, :], Etab[:, 4 + h, 0, :], ALU.add, [R_tab], [R_tab])

        uT = sb("r_uT", [128, 8, 512], BF16)
        R_uT = R("r_uT")
        R_uTd = [R("uTd%d" % i) for i in range(len(TILES))]
        for ti in range(len(TILES)):
            t0, N, _ = TILES[ti]
            modnorm(ti, li, 1, lambda j: uT[:, j, 0:N], R_uT)
            k.dma("sp", io["uT_d"][:, t0:t0 + N].rearrange("(c p) t -> p c t", p=128), uT[:, :, 0:N], [R_uT], [R_uTd[ti]])

        wq = sb("r_wq", [128, 8, 256], BF16)
        wk = sb("r_wk", [128, 8, 256], BF16)
        wqs = sb("r_wqs", [128, 8, 256], BF16)
        wks = sb("r_wks", [128, 8, 256], BF16)
        wv = sb("r_wv", [128, 8, 512], BF16)
        wg = sb("r_wg", [128, 8, 512], BF16)
        R_w = R("r_w")
        qT = sb("r_qT", [128, 2, T], BF16)
        kT = sb("r_kT", [128, 2, T], BF16)
        vtm = sb("r_vtm", [128, NCHK, 512], BF16)
        R_qT, R_kT = R("qT"), R("kT")
        R_v = [R("v%d" % i) for i in range(NCHK)]
        rc = sb("r_rc", [128, 2, 512], F32)
        rs = sb("r_rs", [128, 2, 512], F32)
        R_rope = R("rope")
        t1 = [sb("r_t1_%d" % i, [128, 512], F32) for i in range(2)]
        t2 = [sb("r_t2_%d" % i, [128, 512], F32) for i in range(2)]
        R_t1 = [R("t1a"), R("t1b")]
        R_t2 = [R("t2a"), R("t2b")]
        NPB = 24
        Pb = [sb("r_Pb%d" % i, [128, 128], BF16) for i in range(NPB)]
        R_Pb = [R("Pb%d" % i) for i in range(NPB)]
        Pt = [sb("r_Pt%d" % i, [128, 128], F32) for i in range(2)]
        R_Pt = [R("Pt0"), R("Pt1")]
        uTc = sb("r_uTc", [128, 8, 128], BF16)
        R_uTc = R("uTc")
        osb = sb("r_osb", [128, 512], F32)
        sgt = sb("r_sgt", [128, 512], F32)
        sqj = sb("r_sqj", [128, 512], F32)
        zb = sb("r_zb", [128, 512], BF16)
        zb2 = sb("r_zb2", [128, 512], BF16)
        zbs = [zb, zb2]
        R_zbs = [R("zb0"), R("zb1")]
        zTc = sb("r_zTc", [128, 4, 128], BF16)
        st = sb("r_st", [128, 4], F32)
        identb = sb("r_identb", [128, 128], BF16)
        R_osb, R_sgt, R_zb, R_zTc, R_st, R_sqj = R("osb"), R("sgt"), R("zb"), R("zTc"), R("st"), R("sqj")
        R_zTd = [R("zTd%d" % i) for i in range(NCHK)]
        k.copy("dve", identb[:, :], k.ident[:, :], [k.R_const], [R_tab])
        wi = io["ret_w_in"].rearrange("(c p) n -> p c n", p=128)

        def cb(c):
            return c - 2 if c >= 2 else 32 + c

        nblk = 0
        for h in range(4):
            for (dst, c0, n) in ((wq, h * 256, 256), (wk, 1024 + h * 256, 256), (wv, 2048 + h * 512, 512), (wg, 4096 + h * 512, 512)):
                k.dma("pool", dst[:, :, 0:n], wi[:, :, c0:c0 + n], [], [R_w])
            for (dst, c0) in ((wqs, h * 256), (wks, 1024 + h * 256)):
                for dk in range(2):
                    b0 = c0 + dk * 128
                    k.dma("pool", dst[:, :, dk * 128:dk * 128 + 64], wi[:, :, b0 + 64:b0 + 128], [], [R_w])
                    k.dma("pool", dst[:, :, dk * 128 + 64:dk * 128 + 128], wi[:, :, b0:b0 + 64], [], [R_w])
            for ti in range(len(TILES)):
                t0, N, isctx = TILES[ti]
                k.dma("sp", uT[:, :, 0:N], io["uT_d"][:, t0:t0 + N].rearrange("(c p) t -> p c t", p=128), [R_uTd[ti]], [R_uT])
                if not isctx:
                    p0 = t0 - CTX
                    k.dma("sp", rc[:, :, 0:N], io["ropeC"][:, p0:p0 + N].rearrange("(d p) t -> p d t", p=128), [], [R_rope])
                    k.dma("sp", rs[:, :, 0:N], io["ropeS"][:, p0:p0 + N].rearrange("(d p) t -> p d t", p=128), [], [R_rope])
                for (w_, ws_, dst, R_dst, scl) in ((wq, wqs, qT, R_qT, 1.0), (wk, wks, kT, R_kT, 0.0625)):
                    for dk in range(2):
                        for kc in range(8):
                            k.mm(pA[:, 0:N], w_[:, kc, dk * 128:(dk + 1) * 128], uT[:, kc, 0:N], kc == 0, kc == 7,
                                 [R_w, R_uT], [R_pA], signal=(kc == 7))
                        if isctx:
                            k.ts("dve", dst[:, dk, t0:t0 + N], pA[:, 0:N], scl, None, ALU.mult, None, [R_pA], [R_dst])
                            continue
                        for kc in range(8):
                            k.mm(pB[:, 0:N], ws_[:, kc, dk * 128:(dk + 1) * 128], uT[:, kc, 0:N], kc == 0, kc == 7,
                                 [R_w, R_uT], [R_pB], signal=(kc == 7))
                        b = dk
                        k.stt("dve", t1[b][:, 0:N], pA[:, 0:N], scl, rc[:, dk, 0:N], ALU.mult, ALU.mult, [R_pA, R_rope], [R_t1[b]])
                        k.stt("dve", t2[b][:, 0:N], pB[:, 0:N], scl, rs[:, dk, 0:N], ALU.mult, ALU.mult, [R_pB, R_rope], [R_t2[b]])
                        k.tt("pool", dst[:, dk, t0:t0 + N], t1[b][:, 0:N], t2[b][:, 0:N], ALU.add, [R_t1[b], R_t2[b]], [R_dst])
                for sbi in range(N // 128):
                    c = t0 // 128 + sbi
                    for kc in range(8):
                        k.mm(pg[:, :], uT[:, kc, sbi * 128:(sbi + 1) * 128], wv[:, kc, :], kc == 0, kc == 7,
                             [R_w, R_uT], [R_pg], signal=(kc == 7))
                    k.copy("act", vtm[:, c, :], pg[:, :], [R_pg], [R_v[c]])
            tasks = []
            for lc in range(NCHK):
                blocks = []
                for mc in range(NCHK):
                    terms = []
                    if mc == lc:
                        terms = ["diag"]
                    else:
                        if mc < lc:
                            terms.append((h, lc - mc - 1))
                        if cb(mc) > cb(lc):
                            terms.append((4 + h, cb(mc) - cb(lc) - 1))
                    if terms:
                        blocks.append((mc, terms))
                for bi, (mc, terms) in enumerate(blocks):
                    tasks.append((lc, bi, len(blocks), mc, terms))

            GB = 4
            groups_ = [tasks[i:i + GB] for i in range(0, len(tasks), GB)]

            def emit_scores(grp, gslot):
                sk = gslot % NSC
                for i_, (lc, bi, nb, mc, terms) in enumerate(grp):
                    for kc in range(2):
                        k.mm(psc[sk][:, i_ * 128:(i_ + 1) * 128], kT[:, kc, mc * 128:(mc + 1) * 128], qT[:, kc, lc * 128:(lc + 1) * 128],
                             kc == 0, kc == 1, [R_kT, R_qT], [R_psc[sk]], signal=(kc == 1 and i_ == len(grp) - 1))
                for i_, (lc, bi, nb, mc, terms) in enumerate(grp):
                    pk = (gslot * GB + i_) % NPB
                    sc_ap = psc[sk][:, i_ * 128:(i_ + 1) * 128]
                    if terms[0] == "diag":
                        k.tt("dve", Pb[pk][:, :], sc_ap, Wdiag[:, h, :], ALU.mult, [R_psc[sk], R_tab], [R_Pb[pk]])
                    elif len(terms) == 1:
                        col, n = terms[0]
                        k.stt("dve", Pb[pk][:, :], sc_ap, apow[:, col, n:n + 1], Etab[:, col, 1, :], ALU.mult, ALU.mult,
                              [R_psc[sk], R_tab], [R_Pb[pk]])
                    else:
                        for q_, (col, n) in enumerate(terms):
                            k.stt("dve", Pt[q_][:, :], sc_ap, apow[:, col, n:n + 1], Etab[:, col, 1, :], ALU.mult, ALU.mult,
                                  [R_psc[sk], R_tab], [R_Pt[q_]])
                        k.tt("dve", Pb[pk][:, :], Pt[0][:, :], Pt[1][:, :], ALU.add, [R_Pt[0], R_Pt[1]], [R_Pb[pk]])

            DLOOK = 3
            pending_fin = []

            def flush_fin(now):
                while pending_fin and pending_fin[0][0] <= now:
                    _, lc_ = pending_fin.pop(0)
                    zb_ = zbs[lc_ % 2]
                    for ec in range(4):
                        k.tr(pt[:, ec * 128:(ec + 1) * 128], zb_[:, ec * 128:(ec + 1) * 128], identb[:, :], [R_zbs[lc_ % 2], R_tab], [R_pt])
                    k.copy("act", zTc[:, :, :], pt[:, :].rearrange("p (a b) -> p a b", b=128), [R_pt], [R_zTc])
                    k.dma("sp", io["zT_d"][h * 512:(h + 1) * 512, lc_ * 128:(lc_ + 1) * 128].rearrange("(c p) t -> p c t", p=128),
                          zTc[:, :, :], [R_zTc], [R_zTd[lc_]])

            gbase = nblk
            pv_list = []
            for gi_ in range(len(groups_) + DLOOK):
                if gi_ < len(groups_):
                    emit_scores(groups_[gi_], gbase + gi_)
                if gi_ - DLOOK >= 0:
                    for i_, tk in enumerate(groups_[gi_ - DLOOK]):
                        pv_list.append((tk, ((gbase + gi_ - DLOOK) * GB + i_) % NPB, gi_))
                while pv_list:
                    (lc, bi, nb, mc, terms), pk, idx = pv_list.pop(0)
                    flush_fin(idx)
                    k.mm(po[:, :], Pb[pk][:, :], vtm[:, mc, :], bi == 0, bi == nb - 1, [R_Pb[pk], R_v[mc]], [R_po])
                    if bi != nb - 1:
                        continue
                    k.dma("sp", uTc[:, :, :], io["uT_d"][:, lc * 128:(lc + 1) * 128].rearrange("(c p) t -> p c t", p=128),
                          [R_uTd[0 if lc < 2 else 1 + (lc - 2) // 4]], [R_uTc])
                    for kc in range(8):
                        k.mm(pg[:, :], uTc[:, kc, :], wg[:, kc, :], kc == 0, kc == 7, [R_w, R_uTc], [R_pg], signal=(kc == 7))
                    k.act(sgt[:, :], pg[:, :], AF.Silu, [R_pg], [R_sgt])
                    k.copy("act", osb[:, :], po[:, :], [R_po], [R_osb])
                    P.op("dve", lambda e: e.reduce_sum(out=st[:, 0:1], in_=osb[:, :], axis=AX.X), [R_osb], [R_st])
                    k.ts("dve", st[:, 0:1], st[:, 0:1], 1.0 / 512, None, ALU.mult, None, [R_st], [R_st])
                    k.ts("dve", osb[:, :], osb[:, :], st[:, 0:1], None, ALU.subtract, None, [R_st, R_osb], [R_osb])
                    k.tt("dve", sqj[:, :], osb[:, :], osb[:, :], ALU.mult, [R_osb], [R_sqj])
                    P.op("dve", lambda e: e.reduce_sum(out=st[:, 1:2], in_=sqj[:, :], axis=AX.X), [R_sqj], [R_st])
                    k.ts("dve", st[:, 1:2], st[:, 1:2], 1.0 / 512, EPS, ALU.mult, ALU.add, [R_st], [R_st])
                    k.act(st[:, 1:2], st[:, 1:2], AF.Sqrt, [R_st], [R_st])
                    P.op("dve", lambda e: e.reciprocal(out=st[:, 2:3], in_=st[:, 1:2]), [R_st], [R_st])
                    k.stt("dve", zbs[lc % 2][:, :], osb[:, :], st[:, 2:3], sgt[:, :], ALU.mult, ALU.mult, [R_osb, R_st, R_sgt], [R_zbs[lc % 2]])
                    pending_fin.append((idx + 3, lc))
            flush_fin(10 ** 9)
            nblk += len(groups_) + DLOOK
    P.barrier()
    with ExitStack() as es:
        sb = lambda name, shape, dt: es.enter_context(nc.sbuf_tensor(k.uname(name), shape, dt))
        ps = lambda name, shape, dt=F32: es.enter_context(nc.psum_tensor(k.uname(name), shape, dt))
        resadd = mk_resadd(k, io, sb, li)
        pA = ps("r_pA2", [128, 512])
        pB = ps("r_pB2", [128, 512])
        R_pA, R_pB = R("pA2"), R("pB2")
        wo = sb("r_wo", [128, 16, D], BF16)
        zt = sb("r_zt", [128, 16, 512], BF16)
        R_wo, R_zt = R("r_wo"), R("r_zt")
        k.dma("pool", wo[:, :, :], io["ret_w_out"].rearrange("(c p) n -> p c n", p=128), [], [R_wo])
        for ti in range(len(TILES)):
            t0, N, isctx = TILES[ti]
            k.dma("sp", zt[:, :, 0:N], io["zT_d"][:, t0:t0 + N].rearrange("(c p) t -> p c t", p=128),
                  R_zTd[t0 // 128:(t0 + N) // 128], [R_zt])
            for j in range(8):
                pp, Rp = (pA, R_pA) if j % 2 == 0 else (pB, R_pB)
                for ec in range(16):
                    k.mm(pp[:, 0:N], wo[:, ec, j * 128:(j + 1) * 128], zt[:, ec, 0:N], ec == 0, ec == 15, [R_wo, R_zt], [Rp],
                         signal=(ec == 15))
                resadd(ti, j, 0, N, pp[:, 0:N], Rp)
    P.barrier()


MIXERS[1] = ret_mixer


TB = 64
NBLK = T // TB


def declare_s5_consts(nc, io):
    io["s5_rmask"] = nc.dram_tensor("s5_rmask", [128, 4], F32, kind="ExternalInput").ap()
    io["s5_emask"] = nc.dram_tensor("s5_emask", [128, 2], F32, kind="ExternalInput").ap()
    io["s5_cmask"] = nc.dram_tensor("s5_cmask", [128, 4 * 128], F32, kind="ExternalInput").ap()
    io["yf_d"] = nc.dram_tensor("yf_d", [2, D, T], F32).ap()
    io["s5p_Wt"] = nc.dram_tensor("s5p_Wt", [128, 2 * 32 * 2 * 128], BF16).ap()
    io["s5p_Ct"] = nc.dram_tensor("s5p_Ct", [128, 2 * 32 * 2 * 128], BF16).ap()
    io["s5p_Ctab"] = nc.dram_tensor("s5p_Ctab", [128, 2 * 32 * 64], F32).ap()
    io["s5p_Stab"] = nc.dram_tensor("s5p_Stab", [128, 2 * 32 * 64], F32).ap()
    io["s5p_rt"] = nc.dram_tensor("s5p_rt", [128, 2 * 32 * 64], F32).ap()
    io["s5p_w64"] = nc.dram_tensor("s5p_w64", [128, 128], F32).ap()


def s5_host_consts():
    r = np.arange(128)
    rmask = np.stack([(r // 32 == q4) for q4 in range(4)], axis=1).astype(np.float32)
    emask = np.stack([((r // 16) % 2 == e) for e in range(2)], axis=1).astype(np.float32)
    cm = np.zeros((128, 4, 128), np.float32)
    for q4 in range(4):
        cm[:, q4, 32 * q4:32 * q4 + 32] = 1.0
    return {"s5_rmask": rmask, "s5_emask": emask, "s5_cmask": cm.reshape(128, 512)}


def s5_prep_gen(k, io, sb, ps):
    nc, P = k.nc, k.P
    from contextlib import ExitStack
    R = lambda n: Res(n)
    PI = math.pi
    R_t = R("s5tab")
    pT = ps("s5_pT", [128, 512])
    R_pT = R("s5_pT")
    rmask = sb("s5_rmask", [128, 4], F32)
    emask = sb("s5_emask", [128, 2], F32)
    cmask = sb("s5_cmask", [128, 4, 128], F32)
    k.dma("sp", rmask[:, :], io["s5_rmask"], [], [R_t])
    k.dma("sp", emask[:, :], io["s5_emask"], [], [R_t])
    k.dma("sp", cmask[:, :, :], io["s5_cmask"].rearrange("p (a b) -> p a b", b=128), [], [R_t])
    lst = sb("s5_lst", [64, 2, 128], F32)
    k.dma("sp", lst[:, 0, :], io["s5_lam_re"].rearrange("d (q e) p -> (d q) (e p)", e=2), [], [R_t])
    k.dma("sp", lst[:, 1, :], io["s5_lam_im"].rearrange("d (q e) p -> (d q) (e p)", e=2), [], [R_t])
    lam = sb("s5_lam", [128, 2, 64], F32)
    for w in range(2):
        k.tr(pT[:, 0:64], lst[:, w, :], k.ident[0:64, 0:64], [R_t, k.R_const], [R_pT])
        k.copy("dve", lam[:, w, :], pT[:, 0:64], [R_pT], [R_t])
    ones1 = sb("s5_ones1", [1, 128], F32)
    ldt1 = sb("s5_ldt1", [1, 128], F32)
    k.memset("dve", ones1[:, :], 1.0, [R_t])
    k.dma("sp", ldt1[:, :], io["s5_log_dt"].rearrange("(o d) g -> o (d g)", o=1), [], [R_t])
    k.mm(pT[:, 0:128], ones1[0:1, :], ldt1[0:1, :], True, True, [R_t], [R_pT])
    dt = sb("s5_dt", [128, 64], F32)
    bc = pT[:, 0:128].rearrange("p (d q e) -> p d q e", d=2, e=2)
    dt3 = dt[:, :].rearrange("p (d q) -> p d q", d=2)
    k.act(dt3[0:64], bc[0:64, :, :, 0], AF.Exp, [R_pT], [R_t])
    k.act(dt3[64:128], bc[64:128, :, :, 1], AF.Exp, [R_pT], [R_t])
    sm = lambda n: sb("s5_" + n, [128, 64], F32)
    mag, ang, ar, ai, tmpa, tmpb, sg_, den, cfr, cfi = [sm(n) for n in
                                                         "mag ang ar ai tmpa tmpb sg den cfr cfi".split()]
    k.tt("dve", mag[:, :], lam[:, 0, :], dt[:, :], ALU.mult, [R_t], [R_t])
    k.act(mag[:, :], mag[:, :], AF.Exp, [R_t], [R_t])
    k.tt("dve", ang[:, :], lam[:, 1, :], dt[:, :], ALU.mult, [R_t], [R_t])

    def sin_of(dst, shift):
        k.ts("dve", tmpa[:, :], ang[:, :], shift - 4 * PI, None, ALU.add, None, [R_t], [R_t])
        for thr in (PI, 3 * PI, 5 * PI, 7 * PI):
            k.ts("dve", tmpb[:, :], ang[:, :], shift - thr, None, ALU.add, None, [R_t], [R_t])
            k.act(sg_[:, :], tmpb[:, :], AF.Sign, [R_t], [R_t])
            k.stt("dve", tmpa[:, :], sg_[:, :], -PI, tmpa[:, :], ALU.mult, ALU.add, [R_t], [R_t])
        k.act(dst, tmpa[:, :], AF.Sin, [R_t], [R_t])

    sin_of(ai[:, :], 0.0)
    sin_of(ar[:, :], PI / 2)
    k.tt("dve", ar[:, :], ar[:, :], mag[:, :], ALU.mult, [R_t], [R_t])
    k.tt("dve", ai[:, :], ai[:, :], mag[:, :], ALU.mult, [R_t], [R_t])
    k.tt("dve", den[:, :], lam[:, 0, :], lam[:, 0, :], ALU.mult, [R_t], [R_t])
    k.tt("dve", tmpa[:, :], lam[:, 1, :], lam[:, 1, :], ALU.mult, [R_t], [R_t])
    k.tt("dve", den[:, :], den[:, :], tmpa[:, :], ALU.add, [R_t], [R_t])
    P.op("dve", lambda e: e.reciprocal(out=den[:, :], in_=den[:, :]), [R_t], [R_t])
    k.ts("dve", tmpb[:, :], ar[:, :], -1.0, None, ALU.add, None, [R_t], [R_t])
    k.tt("dve", cfr[:, :], tmpb[:, :], lam[:, 0, :], ALU.mult, [R_t], [R_t])
    k.tt("dve", tmpa[:, :], ai[:, :], lam[:, 1, :], ALU.mult, [R_t], [R_t])
    k.tt("dve", cfr[:, :], cfr[:, :], tmpa[:, :], ALU.add, [R_t], [R_t])
    k.tt("dve", cfr[:, :], cfr[:, :], den[:, :], ALU.mult, [R_t], [R_t])
    k.tt("dve", cfi[:, :], ai[:, :], lam[:, 0, :], ALU.mult, [R_t], [R_t])
    k.tt("dve", tmpa[:, :], tmpb[:, :], lam[:, 1, :], ALU.mult, [R_t], [R_t])
    k.tt("dve", cfi[:, :], cfi[:, :], tmpa[:, :], ALU.subtract, [R_t], [R_t])
    k.tt("dve", cfi[:, :], cfi[:, :], den[:, :], ALU.mult, [R_t], [R_t])
    Wt = sb("s5_Wt", [128, 2, 32, 2, 128], BF16)
    Ct = sb("s5_Ct", [128, 2, 32, 2, 128], BF16)
    with ExitStack() as es2:
        sb2 = lambda name, shape, dt_: es2.enter_context(nc.sbuf_tensor(k.uname(name), shape, dt_))
        Bn = sb2("s5_Bn", [128, 2, 2, 32, 16], F32)
        Bb = sb2("s5_Bb", [128, 2, 2, 32, 16], F32)
        for w, nm in enumerate(("s5_b_re", "s5_b_im")):
            for d in range(2):
                k.dma("sp", Bn[:, w, d, :, :], io[nm][d].rearrange("(q e) p c -> (e p) q c", e=2), [], [R_t])
        t16 = sb2("s5_t16", [128, 64], F32)
        cf3r, cf3i = cfr[:, :], cfi[:, :]
        for ci in range(16):
            yield
            bre = Bn[:, 0, :, :, ci].rearrange("p d q -> p (d q)")
            bim = Bn[:, 1, :, :, ci].rearrange("p d q -> p (d q)")
            ore = Bb[:, 0, :, :, ci].rearrange("p d q -> p (d q)")
            oim = Bb[:, 1, :, :, ci].rearrange("p d q -> p (d q)")
            k.tt("dve", ore, cf3r, bre, ALU.mult, [R_t], [R_t])
            k.tt("dve", t16[:, :], cf3i, bim, ALU.mult, [R_t], [R_t])
            k.tt("dve", ore, ore, t16[:, :], ALU.subtract, [R_t], [R_t])
            k.tt("dve", oim, cf3r, bim, ALU.mult, [R_t], [R_t])
            k.tt("dve", t16[:, :], cf3i, bre, ALU.mult, [R_t], [R_t])
            k.tt("dve", oim, oim, t16[:, :], ALU.add, [R_t], [R_t])
        Nn = sb2("s5_Nn", [128, 4, 2, 16], F32)
        k.memset("dve", Nn[:, :, :, :], 0.0, [R_t])
        for d in range(2):
            for w in range(2):
                for j in range(8):
                    yield
                    k.copy("dve", Nn[0:64, :, 0, :], Bb[0:64, w, d, 4 * j:4 * j + 4, :], [R_t, R_pT], [R_t])
                    k.copy("dve", Nn[64:128, :, 1, :], Bb[64:128, w, d, 4 * j:4 * j + 4, :], [R_t], [R_t])
                    k.tr(pT[:, 0:128], Nn[:, :, :, :].rearrange("p a b c -> p (a b c)"), k.ident[:, :], [R_t, k.R_const], [R_pT])
                    for q4 in range(4):
                        k.ts("dve", Wt[:, d, 4 * j + q4, w, :], pT[:, 0:128], rmask[:, q4:q4 + 1], None, ALU.mult, None,
                             [R_pT, R_t], [R_t])
        Cn = sb2("s5_Cn", [128, 2, 2, 8, 64], F32)
        for w, nm in enumerate(("s5_c_re", "s5_c_im")):
            for d in range(2):
                k.dma("sp", Cn[:, w, d, :, :], io[nm][d].rearrange("(j g8) co p -> (g8 co) j p", g8=8), [], [R_t])
        Cexp = sb2("s5_Cexp", [128, 128], F32)
        for d in range(2):
            for w in range(2):
                for j in range(8):
                    yield
                    for e in range(2):
                        k.ts("dve", Cexp[:, e * 64:(e + 1) * 64], Cn[:, w, d, j, :], emask[:, e:e + 1], None, ALU.mult, None,
                             [R_t, R_pT], [R_t])
                    k.tr(pT[:, 0:128], Cexp[:, :], k.ident[:, :], [R_t, k.R_const], [R_pT])
                    for q4 in range(4):
                        k.stt("dve", Ct[:, d, 4 * j + q4, w, :], pT[:, 0:128], (1.0 if w == 0 else -1.0), cmask[:, q4, :],
                              ALU.mult, ALU.mult, [R_pT, R_t], [R_t])
    Ctab = sb("s5_Ctab", [128, 2, 32, TB], F32)
    Stab = sb("s5_Stab", [128, 2, 32, TB], F32)
    rt = sb("s5_rt", [128, 2, 32, TB], F32)
    w64 = sb("s5_w64", [128, 2, 64], F32)
    cs1, sn1, er, ei_, e2r, e2i = [sm(n) for n in "cs1 sn1 er ei e2r e2i".split()]
    P.op("dve", lambda e: e.reciprocal(out=tmpa[:, :], in_=mag[:, :]), [R_t], [R_t])
    k.tt("dve", cs1[:, :], ar[:, :], tmpa[:, :], ALU.mult, [R_t], [R_t])
    k.tt("dve", sn1[:, :], ai[:, :], tmpa[:, :], ALU.mult, [R_t], [R_t])
    k.memset("dve", er[:, :], 1.0, [R_t])
    k.memset("dve", ei_[:, :], 0.0, [R_t])
    dq = lambda t2d: t2d.rearrange("p (d q) -> p d q", d=2)
    for tp in range(TB):
        yield
        for d in range(2):
            col = tp if d == 0 else TB - 1 - tp
            k.copy("pool", Ctab[:, d, :, col], dq(er[:, :])[:, d, :], [R_t], [R_t])
            k.copy("pool", Stab[:, d, :, col], dq(ei_[:, :])[:, d, :], [R_t], [R_t])
            if tp == 0:
                k.memset("pool", rt[:, d, :, col], 0.0, [R_t])
            else:
                k.copy("pool", rt[:, d, :, col], dq(mag[:, :])[:, d, :], [R_t], [R_t])
        k.tt("dve", e2r[:, :], er[:, :], cs1[:, :], ALU.mult, [R_t], [R_t])
        k.tt("dve", tmpa[:, :], ei_[:, :], sn1[:, :], ALU.mult, [R_t], [R_t])
        k.tt("dve", e2r[:, :], e2r[:, :], tmpa[:, :], ALU.subtract, [R_t], [R_t])
        k.tt("dve", e2i[:, :], er[:, :], sn1[:, :], ALU.mult, [R_t], [R_t])
        k.tt("dve", tmpa[:, :], ei_[:, :], cs1[:, :], ALU.mult, [R_t], [R_t])
        k.tt("dve", ei_[:, :], e2i[:, :], tmpa[:, :], ALU.add, [R_t], [R_t])
        k.copy("dve", er[:, :], e2r[:, :], [R_t], [R_t])
    k.tt("dve", w64[:, 0, :], er[:, :], mag[:, :], ALU.mult, [R_t], [R_t])
    k.tt("dve", w64[:, 1, :], ei_[:, :], mag[:, :], ALU.mult, [R_t], [R_t])
    yield
    R_o = R("s5p_out")
    k.dma("sp", io["s5p_Wt"], Wt[:, :, :, :, :].rearrange("p a b c d -> p (a b c d)"), [R_t], [R_o])
    k.dma("sp", io["s5p_Ct"], Ct[:, :, :, :, :].rearrange("p a b c d -> p (a b c d)"), [R_t], [R_o])
    k.dma("sp", io["s5p_Ctab"], Ctab[:, :, :, :].rearrange("p a b c -> p (a b c)"), [R_t], [R_o])
    k.dma("sp", io["s5p_Stab"], Stab[:, :, :, :].rearrange("p a b c -> p (a b c)"), [R_t], [R_o])
    k.dma("sp", io["s5p_rt"], rt[:, :, :, :].rearrange("p a b c -> p (a b c)"), [R_t], [R_o])
    k.dma("sp", io["s5p_w64"], w64[:, :, :].rearrange("p a b -> p (a b)"), [R_t], [R_o])
    yield

def s5_mixer(k, io, li):
    nc, P = k.nc, k.P
    from contextlib import ExitStack
    R = lambda n: Res(n)
    PI = math.pi
    with ExitStack() as es:
        sb = lambda name, shape, dt: es.enter_context(nc.sbuf_tensor(k.uname(name), shape, dt))
        ps = lambda name, shape, dt=F32: es.enter_context(nc.psum_tensor(k.uname(name), shape, dt))
        R_t = R("s5tab")
        R_uTd = R("s5_uTd")
        with ExitStack() as es0:
            sb0 = lambda name, shape, dt: es0.enter_context(nc.sbuf_tensor(k.uname(name), shape, dt))
            ps0 = lambda name, shape, dt=F32: es0.enter_context(nc.psum_tensor(k.uname(name), shape, dt))
            modnorm = mk_modnorm(k, io, sb0, ps0)
            uT = sb0("s5_uT", [128, 8, 512], BF16)
            R_uT = R("s5_uT")
            for ti in range(len(TILES)):
                t0, N, _ = TILES[ti]
                modnorm(ti, li, 1, lambda j: uT[:, j, 0:N], R_uT)
                k.dma("sp", io["uT_d"][:, t0:t0 + N].rearrange("(c p) t -> p c t", p=128), uT[:, :, 0:N], [R_uT], [R_uTd])
        P.barrier()
        Wt = sb("s5_Wt", [128, 2, 32, 2, 128], BF16)
        Ct = sb("s5_Ct", [128, 2, 32, 2, 128], BF16)
        Ctab = sb("s5_Ctab", [128, 2, 32, TB], F32)
        Stab = sb("s5_Stab", [128, 2, 32, TB], F32)
        rt = sb("s5_rt", [128, 2, 32, TB], F32)
        w64 = sb("s5_w64", [128, 2, 64], F32)
        k.dma("sp", Wt[:, :, :, :, :].rearrange("p a b c d -> p (a b c d)"), io["s5p_Wt"], [], [R_t])
        k.dma("act", Ct[:, :, :, :, :].rearrange("p a b c d -> p (a b c d)"), io["s5p_Ct"], [], [R_t])
        k.dma("sp", Ctab[:, :, :, :].rearrange("p a b c -> p (a b c)"), io["s5p_Ctab"], [], [R_t])
        k.dma("act", Stab[:, :, :, :].rearrange("p a b c -> p (a b c)"), io["s5p_Stab"], [], [R_t])
        k.dma("sp", rt[:, :, :, :].rearrange("p a b c -> p (a b c)"), io["s5p_rt"], [], [R_t])
        k.dma("sp", w64[:, :, :].rearrange("p a b -> p (a b)"), io["s5p_w64"], [], [R_t])
        dq = lambda t2d: t2d.rearrange("p (d q) -> p d q", d=2)
        fl = lambda t3: t3.rearrange("p q t -> p (q t)")
        fl4 = lambda t4: t4.rearrange("p d q t -> p (d q t)")
        ub = [sb("s5_ub%d" % d, [128, 8, TB], BF16) for d in range(2)]
        Vr = sb("s5_Vr", [128, 2, 32, TB], F32)
        Vi = sb("s5_Vi", [128, 2, 32, TB], F32)
        T1 = sb("s5_T1", [128, 2, 32, TB], F32)
        T2 = sb("s5_T2", [128, 2, 32, TB], F32)
        Xb = sb("s5_Xb", [128, 2, 32, TB], BF16)
        zc = sb("s5_zc", [128, 2, 2, 32], F32)
        cw = sb("s5_cw", [128, 4, 2, 32], F32)
        ysb = sb("s5_ysb", [128, 8, TB], F32)
        pv = [[ps("s5_pv%d_%d" % (d, i), [128, 512]) for i in range(2)] for d in range(2)]
        pyy = [ps("s5_py%d" % d, [128, 512]) for d in range(2)]
        R_ub = [R("ub0"), R("ub1")]
        R_Xb, R_zc, R_cw, R_ys = [R(n) for n in "Xb zc cw ys".split()]
        R_Vr, R_Vi, R_T1, R_T2 = [[R(n + str(d)) for d in range(2)] for n in ("Vr", "Vi", "T1", "T2")]
        R_pv = [[R("pv%d%d" % (d, i)) for i in range(2)] for d in range(2)]
        R_pyy = [R("pyy0"), R("pyy1")]
        R_yd = [R("yd0"), R("yd1")]
        k.memset("dve", zc[:, :, :, :], 0.0, [R_zc])
        order = [list(range(NBLK)), [3, 2, 1, 0] + list(range(NBLK - 1, 3, -1))]
        C4, S4 = fl4(Ctab[:, :, :, :]), fl4(Stab[:, :, :, :])
        vr, vi, t1, t2 = fl4(Vr[:, :, :, :]), fl4(Vi[:, :, :, :]), fl4(T1[:, :, :, :]), fl4(T2[:, :, :, :])
        w4 = w64[:, :, :].rearrange("p r (d q) -> p r d q", d=2)
        FIRST = (0, TB - 1)
        LAST = (TB - 1, 0)
        nev = [0, 0]

        def emit_V(it, d):
            c0 = order[d][it] * TB
            k.dma("sp", ub[d][:, :, :], io["uT_d"][:, c0:c0 + TB].rearrange("(c p) t -> p c t", p=128), [R_uTd], [R_ub[d]])
            for w, (Vd, Rv) in enumerate(((Vr, R_Vr[d]), (Vi, R_Vi[d]))):
                for q8 in range(4):
                    bk = nev[d] % 2
                    nev[d] += 1
                    for qq in range(8):
                        q = q8 * 8 + qq
                        k.mm(pv[d][bk][:, qq * TB:(qq + 1) * TB], Wt[:, d, q, w, :], ub[d][:, q // 4, :], True, True,
                             [R_t, R_ub[d]], [R_pv[d][bk]], signal=(qq == 7))
                    k.copy("act", Vd[:, d, q8 * 8:(q8 + 1) * 8, :],
                           pv[d][bk][:, :].rearrange("p (a b) -> p a b", b=TB), [R_pv[d][bk]], [Rv])

        for d in range(2):
            emit_V(0, d)
        for it in range(NBLK):
            for d in range(2):
                Cd, Sd = Ctab[:, d, :, :], Stab[:, d, :, :]
                k.tt("dve", T1[:, d, :, :], Vr[:, d, :, :], Cd, ALU.mult, [R_Vr[d], R_t], [R_T1[d]])
                k.tt("dve", T2[:, d, :, :], Vi[:, d, :, :], Sd, ALU.mult, [R_Vi[d], R_t], [R_T2[d]])
                k.tt("dve", Vr[:, d, :, :], Vr[:, d, :, :], Sd, ALU.mult, [R_Vr[d], R_t], [R_Vr[d]])
                k.tt("dve", Vi[:, d, :, :], Vi[:, d, :, :], Cd, ALU.mult, [R_Vi[d], R_t], [R_Vi[d]])
                k.tt("dve", T1[:, d, :, :], T1[:, d, :, :], T2[:, d, :, :], ALU.add, [R_T1[d], R_T2[d]], [R_T1[d]])
                k.tt("dve", Vi[:, d, :, :], Vi[:, d, :, :], Vr[:, d, :, :], ALU.subtract, [R_Vi[d], R_Vr[d]], [R_Vi[d]])
            k.tt("dve", cw[:, 0, :, :], w4[:, 0, :, :], zc[:, 0, :, :], ALU.mult, [R_zc, R_t], [R_cw])
            k.tt("dve", cw[:, 1, :, :], w4[:, 1, :, :], zc[:, 1, :, :], ALU.mult, [R_zc, R_t], [R_cw])
            k.tt("dve", cw[:, 2, :, :], w4[:, 1, :, :], zc[:, 0, :, :], ALU.mult, [R_zc, R_t], [R_cw])
            k.tt("dve", cw[:, 3, :, :], w4[:, 0, :, :], zc[:, 1, :, :], ALU.mult, [R_zc, R_t], [R_cw])
            k.tt("dve", cw[:, 0, :, :], cw[:, 0, :, :], cw[:, 1, :, :], ALU.subtract, [R_cw], [R_cw])
            k.tt("dve", cw[:, 2, :, :], cw[:, 2, :, :], cw[:, 3, :, :], ALU.add, [R_cw], [R_cw])
            for d in range(2):
                k.tt("dve", T1[:, d, :, FIRST[d]], T1[:, d, :, FIRST[d]], cw[:, 0, d, :], ALU.add, [R_cw, R_T1[d]], [R_T1[d]])
                k.tt("dve", Vi[:, d, :, FIRST[d]], Vi[:, d, :, FIRST[d]], cw[:, 2, d, :], ALU.add, [R_cw, R_Vi[d]], [R_Vi[d]])
            for d in range(2):
                rv = (lambda a_: a_) if d == 0 else (lambda a_: a_[:, ::-1])
                rt2 = fl(rt[:, d, :, :])
                for (o_, i_, Ro, Ri) in ((T2, T1, R_T2[d], R_T1[d]), (Vr, Vi, R_Vr[d], R_Vi[d])):
                    P.op("dve", (lambda o, a0, a1: (lambda e: e.tensor_tensor_scan(out=o, data0=a0, data1=a1, initial=0.0,
                                                                                     op0=ALU.mult, op1=ALU.add)))(
                        rv(fl(o_[:, d, :, :])), rv(rt2), rv(fl(i_[:, d, :, :]))), [Ri, R_t], [Ro])
            for d in range(2):
                k.copy("dve", zc[:, 0, d, :], T2[:, d, :, LAST[d]], [R_T2[d]], [R_zc])
                k.copy("dve", zc[:, 1, d, :], Vr[:, d, :, LAST[d]], [R_Vr[d]], [R_zc])
            k.tt("dve", t1, t2, C4, ALU.mult, R_T2 + [R_t], R_T1)
            k.tt("dve", vi, vr, S4, ALU.mult, R_Vr + [R_t], R_Vi)
            k.tt("dve", t2, t2, S4, ALU.mult, R_T2 + [R_t], R_T2)
            k.tt("dve", vr, vr, C4, ALU.mult, R_Vr + [R_t], R_Vr)
            for d in range(2):
                c0 = order[d][it] * TB
                k.tt("dve", Xb[:, 0, :, :], T1[:, d, :, :], Vi[:, d, :, :], ALU.subtract, [R_T1[d], R_Vi[d]], [R_Xb])
                k.tt("dve", Xb[:, 1, :, :], Vr[:, d, :, :], T2[:, d, :, :], ALU.add, [R_Vr[d], R_T2[d]], [R_Xb])
                for j in range(8):
                    n_ = 0
                    for q4 in range(4):
                        for w in range(2):
                            k.mm(pyy[d][:, j * TB:(j + 1) * TB], Ct[:, d, 4 * j + q4, w, :], Xb[:, w, 4 * j + q4, :], n_ == 0, n_ == 7,
                                 [R_t, R_Xb], [R_pyy[d]], signal=(n_ == 7))
                            n_ += 1
                if it + 1 < NBLK:
                    emit_V(it + 1, d)
                k.copy("act", ysb[:, :, :], pyy[d][:, :].rearrange("p (a b) -> p a b", b=TB), [R_pyy[d]], [R_ys])
                k.dma("sp", io["yf_d"][d, :, c0:c0 + TB].rearrange("(c p) t -> p c t", p=128), ysb[:, :, :], [R_ys], [R_yd[d]])
    P.barrier()
    with ExitStack() as es:
        sb = lambda name, shape, dt: es.enter_context(nc.sbuf_tensor(k.uname(name), shape, dt))
        ps = lambda name, shape, dt=F32: es.enter_context(nc.psum_tensor(k.uname(name), shape, dt))
        resadd = mk_resadd(k, io, sb, li)
        wgl = sb("s5_wgl", [128, 8, 2 * D], BF16)
        R_wgl = R("wgl")
        for hh in range(4):
            k.dma("pool", wgl[:, :, hh * 512:(hh + 1) * 512],
                  io["s5_w_glu"].rearrange("(c p) n -> p c n", p=128)[:, :, hh * 512:(hh + 1) * 512], [], [R_wgl])
        yf = sb("s5_yf", [128, 8, 512], F32)
        yb = sb("s5_yb", [128, 8, 512], F32)
        uu = sb("s5_uu", [128, 8, 512], BF16)
        ge = sb("s5_ge", [128, 8, 512], BF16)
        w1 = sb("s5_w1", [128, 512], F32)
        w2 = sb("s5_w2", [128, 512], F32)
        sig = sb("s5_sig", [128, 512], F32)
        oo = sb("s5_oo", [128, 512], F32)
        R_yf, R_yb, R_uu, R_ge, R_w1, R_w2, R_sig, R_oo = [R(n) for n in "yf yb uu ge w1 w2 sig oo".split()]
        pa = ps("s5_pa", [128, 512])
        pb = ps("s5_pb", [128, 512])
        R_pa, R_pb = R("s5pa"), R("s5pb")
        for ti in range(len(TILES)):
            t0, N, isctx = TILES[ti]
            k.dma("sp", yf[:, :, 0:N], io["yf_d"][0, :, t0:t0 + N].rearrange("(c p) t -> p c t", p=128), [], [R_yf])
            k.dma("sp", yb[:, :, 0:N], io["yf_d"][1, :, t0:t0 + N].rearrange("(c p) t -> p c t", p=128), [], [R_yb])
            k.dma("sp", uu[:, :, 0:N], io["uT_d"][:, t0:t0 + N].rearrange("(c p) t -> p c t", p=128), [], [R_uu])
            for j in range(8):
                k.tt("dve", w1[:, 0:N], yf[:, j, 0:N], yb[:, j, 0:N], ALU.add, [R_yf, R_yb], [R_w1])
                k.stt("dve", w1[:, 0:N], uu[:, j, 0:N], k.vecT[:, VC_S5D + j:VC_S5D + j + 1], w1[:, 0:N], ALU.mult, ALU.add,
                      [R_uu, R_w1, k.R_vec], [R_w1])
                k.tt("dve", w2[:, 0:N], w1[:, 0:N], w1[:, 0:N], ALU.mult, [R_w1], [R_w2])
                k.ts("dve", w2[:, 0:N], w2[:, 0:N], 0.044715, 1.0, ALU.mult, ALU.add, [R_w2], [R_w2])
                k.tt("dve", w2[:, 0:N], w2[:, 0:N], w1[:, 0:N], ALU.mult, [R_w2, R_w1], [R_w2])
                k.act(w2[:, 0:N], w2[:, 0:N], AF.Sigmoid, [R_w2], [R_w2], scale=1.5957691216057308)
                k.tt("dve", ge[:, j, 0:N], w2[:, 0:N], w1[:, 0:N], ALU.mult, [R_w2, R_w1], [R_ge])
            for oc in range(8):
                for kc in range(8):
                    k.mm(pa[:, 0:N], wgl[:, kc, oc * 128:(oc + 1) * 128], ge[:, kc, 0:N], kc == 0, kc == 7, [R_wgl, R_ge], [R_pa],
                         signal=(kc == 7))
                for kc in range(8):
                    k.mm(pb[:, 0:N], wgl[:, kc, D + oc * 128:D + (oc + 1) * 128], ge[:, kc, 0:N], kc == 0, kc == 7, [R_wgl, R_ge],
                         [R_pb], signal=(kc == 7))
                k.act(sig[:, 0:N], pb[:, 0:N], AF.Sigmoid, [R_pb], [R_sig])
                k.tt("dve", oo[:, 0:N], pa[:, 0:N], sig[:, 0:N], ALU.mult, [R_pa, R_sig], [R_oo])
                resadd(ti, oc, 0, N, oo[:, 0:N], R_oo)
    P.barrier()


MIXERS[2] = s5_mixer
```

```python
import math
import numpy as np
import concourse.bass as bass
import concourse.mybir as mybir
from concourse.bass_utils import run_bass_kernel_spmd

F32 = mybir.dt.float32
BF16 = mybir.dt.bfloat16
AF = mybir.ActivationFunctionType
ALU = mybir.AluOpType
AX = mybir.AxisListType


class Res:
    __slots__ = ("name", "w", "r")

    def __init__(self, name):
        self.name = name
        self.w = None
        self.r = {}


class Prog:
    ENG = ("pe", "act", "dve", "pool", "sp")

    def __init__(self, nc, n_dma=40, same_sync=True):
        self.nc = nc
        self.items = {e: [] for e in self.ENG}
        self.cnt = {e: 0 for e in self.ENG}
        self.sems = {}
        for e in ("pe", "act", "dve", "pool"):
            self.sems[e] = nc.alloc_semaphore(name="s_" + e)
        self.n_dma = n_dma
        for i in range(n_dma):
            self.sems[("d", i)] = nc.alloc_semaphore(name="d%d" % i)
        self.dma_cum = [0] * n_dma
        q = n_dma // 5
        self.dma_pool = {"sp": (0, 2 * q), "pool": (2 * q, 4 * q), "act": (4 * q, n_dma)}
        self.dma_pi = {"sp": 0, "pool": 0, "act": 0}
        self.waited = {e: {} for e in self.ENG}
        self.same_sync = same_sync
        self.nops = 0

    def _need(self, eng, tok):
        if tok is None:
            return
        key, val = tok
        if val <= 0:
            return
        if key == eng and (eng == "pe" or (not self.same_sync and eng != "pool")):
            return
        if key == eng and val > self.cnt[eng]:
            return
        if self.waited[eng].get(key, 0) >= val:
            return
        self.waited[eng][key] = val
        self.items[eng].append(("wait", key, val))

    def _deps(self, eng, reads, writes):
        for r in reads:
            self._need(eng, r.w)
        for w in writes:
            self._need(eng, w.w)
            for k, v in w.r.items():
                self._need(eng, (k, v))

    def _mark(self, tok, reads, writes):
        k, v = tok
        for r in reads:
            if r.r.get(k, 0) < v:
                r.r[k] = v
        for w in writes:
            w.w = tok
            w.r = {}

    def op(self, eng, fn, reads=(), writes=(), signal=True):
        self._deps(eng, reads, writes)
        tok = (eng, self.cnt[eng] + 1)
        if signal:
            self.cnt[eng] += 1
            self.items[eng].append(("op", fn, eng, 1))
        else:
            self.items[eng].append(("op", fn, None, 0))
        self._mark(tok, reads, writes)
        self.nops += 1

    def dma(self, eng, out, in_, reads=(), writes=(), **kw):
        lo, hi = self.dma_pool[eng]
        s = lo + self.dma_pi[eng] % (hi - lo)
        self.dma_pi[eng] += 1
        key = ("d", s)
        self._need(eng, (key, self.dma_cum[s]))
        self._deps(eng, reads, writes)
        self.dma_cum[s] += 16
        tok = (key, self.dma_cum[s])
        self.items[eng].append(("op", lambda e: e.dma_start(out=out, in_=in_, **kw), key, 16))
        self._mark(tok, reads, writes)
        self.nops += 1

    def finish(self, eng="sp"):
        for s in range(self.n_dma):
            self._need(eng, (("d", s), self.dma_cum[s]))
        for e in ("pe", "act", "dve", "pool"):
            self._need(eng, (e, self.cnt[e]))

    def emit(self):
        nc = self.nc
        sems = self.sems

        def replay(name):
            def f(eng):
                for it in self.items[name]:
                    if it[0] == "wait":
                        eng.wait_ge(sems[it[1]], it[2])
                    else:
                        ins = it[1](eng)
                        if it[2] is not None:
                            ins.then_inc(sems[it[2]], it[3])
            return f

        with nc.Block() as block:
            block.tensor(replay("pe"))
            block.scalar(replay("act"))
            block.vector(replay("dve"))
            block.gpsimd(replay("pool"))
            block.sync(replay("sp"))


D = 1024
NCH = 8
SEQ = 4096
CTX = 256
T = SEQ + CTX
DEPTH = 4
NMOD = 9
DFF = 2816
FCH = DFF // 128
EPS = 1e-6
TILES = [(0, 256, True)] + [(256 + 512 * i, 512, False) for i in range(8)]


class K:
    def __init__(self, nc, P):
        self.nc = nc
        self.P = P
        self._uid = 0

    def uname(self, name):
        self._uid += 1
        return "%s_%d" % (name, self._uid)

    def mm(self, out, lhsT, rhs, start, stop, reads, writes, signal=True):
        self.P.op("pe", lambda e: e.matmul(out, lhsT, rhs, start=start, stop=stop), reads, writes, signal)

    def tr(self, out, in_, ident, reads, writes):
        self.P.op("pe", lambda e: e.transpose(out, in_, ident), reads, writes)

    def act(self, out, in_, func, reads, writes, bias=None, scale=None):
        kw = {}
        if bias is not None:
            kw["bias"] = bias
        if scale is not None:
            kw["scale"] = scale
        self.P.op("act", lambda e: e.activation(out=out, in_=in_, func=func, **kw), reads, writes)

    def tt(self, eng, out, in0, in1, op, reads, writes):
        self.P.op(eng, lambda e: e.tensor_tensor(out=out, in0=in0, in1=in1, op=op), reads, writes)

    def ts(self, eng, out, in0, s1, s2, op0, op1, reads, writes):
        if op1 is None:
            self.P.op(eng, lambda e: e.tensor_single_scalar(out=out, in_=in0, scalar=s1, op=op0), reads, writes)
        else:
            self.P.op(eng, lambda e: e.tensor_scalar(out=out, in0=in0, scalar1=s1, scalar2=s2, op0=op0, op1=op1),
                      reads, writes)

    def stt(self, eng, out, in0, scalar, in1, op0, op1, reads, writes):
        self.P.op(eng, lambda e: e.scalar_tensor_tensor(out=out, in0=in0, scalar=scalar, in1=in1, op0=op0, op1=op1),
                  reads, writes)

    def copy(self, eng, out, in_, reads, writes):
        if eng == "act":
            self.P.op(eng, lambda e: e.copy(out=out, in_=in_), reads, writes)
        else:
            self.P.op(eng, lambda e: e.tensor_copy(out=out, in_=in_), reads, writes)

    def memset(self, eng, ap, val, writes):
        self.P.op(eng, lambda e: e.memset(ap, val), (), writes)

    def dma(self, eng, out, in_, reads, writes, **kw):
        self.P.dma(eng, out, in_, reads, writes, **kw)


def _barrier(P):
    for e in P.ENG:
        for s in range(P.n_dma):
            P._need(e, (("d", s), P.dma_cum[s]))
        for e2 in ("pe", "act", "dve", "pool"):
            if e2 != e:
                P._need(e, (e2, P.cnt[e2]))


Prog.barrier = _barrier

VC_C, VC_CC, VC_FNG, VC_S5D, VC_NG, VC_BM = 0, 8, 16, 24, 32, 128
NVEC = 416


def declare_io(nc):
    io = {}

    def inp(name, shape):
        io[name] = nc.dram_tensor(name, list(shape), F32, kind="ExternalInput").ap()

    inp("x", [SEQ, D])
    inp("c", [8, 128])
    inp("ctx", [CTX, D])
    inp("c_ctx", [8, 128])
    inp("w_mod", [DEPTH, D, NMOD * D])
    inp("b_mod", [288, 128])
    inp("norm_g", [96, 128])
    inp("ffn_w_in", [DEPTH, 2, D, 2 * DFF])
    inp("ffn_w_out", [DEPTH, 2, DFF, D])
    inp("fnet_w_out", [2, D, D])
    inp("ret_w_in", [D, 6144])
    inp("ret_w_out", [2048, D])
    inp("ret_decay_logit", [2, 4])
    inp("s5_lam_re", [2, 64, 64])
    inp("s5_lam_im", [2, 64, 64])
    inp("s5_log_dt", [2, 64])
    inp("s5_b_re", [2, 64, 64, 16])
    inp("s5_b_im", [2, 64, 64, 16])
    inp("s5_c_re", [2, 64, 16, 64])
    inp("s5_c_im", [2, 64, 16, 64])
    inp("s5_d", [8, 128])
    inp("s5_w_glu", [D, 2 * D])
    inp("final_norm_g", [8, 128])
    io["out"] = nc.dram_tensor("out", [SEQ, D], F32, kind="ExternalOutput").ap()
    io["hT"] = nc.dram_tensor("hT", [D, T], F32).ap()
    declare_fnet_consts(nc, io)
    declare_ret_consts(nc, io)
    declare_s5_consts(nc, io)
    io["uT_d"] = nc.dram_tensor("uT_d", [D, T], BF16).ap()
    return io


def alloc_consts(k, es):
    nc = k.nc
    sb = lambda name, shape, dt: es.enter_context(nc.sbuf_tensor(k.uname(name), shape, dt))
    k.ident = sb("ident", [128, 128], F32)
    k.ones_b = sb("ones_b", [128, 128], BF16)
    k.vecT = sb("vecT", [128, NVEC], F32)
    k.sc = sb("sc", [128, 8, 2], BF16)
    k.mod = sb("mod", [128, DEPTH, 72, 2], F32)
    k.A = sb("Asc", [128, DEPTH, 3, 8, 2], F32)
    k.B = sb("Bsh", [128, DEPTH, 3, 8, 2], F32)
    k.G = sb("Ggt", [128, DEPTH, 3, 8, 2], F32)
    k.R_const = Res("const")
    k.R_vec = Res("vecT")
    k.R_mod = Res("mod")
    k.R_hT = [[Res("hT%d_%d" % (i, j)) for j in range(NCH)] for i in range(len(TILES))]
    P = k.P
    k.memset("pool", k.ident[:, :], 0.0, [k.R_const])
    P.op("pool", lambda e: e.affine_select(out=k.ident[:, :], in_=k.ident[:, :], pattern=[[-1, 128]],
                                            compare_op=ALU.not_equal, fill=1.0, base=0, channel_multiplier=1),
         [k.R_const], [k.R_const])
    k.memset("dve", k.ones_b[:, :], 1.0, [k.R_const])


def prologue(k, io):
    nc, P = k.nc, k.P
    from contextlib import ExitStack
    with ExitStack() as es:
        sb = lambda name, shape, dt: es.enter_context(nc.sbuf_tensor(k.uname(name), shape, dt))
        ps = lambda name, shape: es.enter_context(nc.psum_tensor(k.uname(name), shape, F32))
        stg = [sb("vstg%d" % i, [128, 128], F32) for i in range(4)]
        R_stg = [Res("vstg%d" % i) for i in range(4)]
        tp = [ps("tp%d" % i, [128, 512]) for i in range(2)]
        R_tp = [Res("tp%d" % i) for i in range(2)]
        rows0 = [(io["c"], 8, 0), (io["c_ctx"], 8, 8), (io["final_norm_g"], 8, 16), (io["s5_d"], 8, 24),
                 (io["norm_g"], 96, 32)]
        for ap, n, r0 in rows0:
            k.dma("sp", stg[0][r0:r0 + n, :], ap, [], [R_stg[0]])
        for b, (r0, n) in enumerate([(0, 128), (128, 128), (256, 32)]):
            k.dma("sp", stg[b + 1][0:n, :], io["b_mod"][r0:r0 + n, :], [], [R_stg[b + 1]])
        col = 0
        for b, n in enumerate([128, 128, 128, 32]):
            k.tr(tp[b % 2][:, 0:n], stg[b][0:n, :], k.ident[0:n, 0:n], [R_stg[b], k.R_const], [R_tp[b % 2]])
            k.copy("dve", k.vecT[:, col:col + n], tp[b % 2][:, 0:n], [R_tp[b % 2]], [k.R_vec])
            col += n
        k.act(k.sc[:, :, 0], k.vecT[:, VC_C:VC_C + 8], AF.Silu, [k.R_vec], [k.R_mod])
        k.act(k.sc[:, :, 1], k.vecT[:, VC_CC:VC_CC + 8], AF.Silu, [k.R_vec], [k.R_mod])

        wblk = [sb("wmodblk%d" % i, [128, 8, 1024], BF16) for i in range(2)]
        R_wblk = [Res("wmodblk%d" % i) for i in range(2)]
        prep = s5_prep_gen(k, io, sb, ps)
        mps = [ps("modps%d" % i, [128, 72, 2]) for i in range(2)]
        R_mps = [Res("modps%d" % i) for i in range(2)]
        nb = 0
        for li in range(DEPTH):
            wv = io["w_mod"][li].rearrange("(c p) n -> p c n", p=128)
            for kk in range(NMOD):
                b = nb % 2
                nb += 1
                for _ in range(5):
                    next(prep, None)
                for h2 in range(2):
                    k.dma("pool", wblk[b][:, h2 * 4:(h2 + 1) * 4, :], wv[:, h2 * 4:(h2 + 1) * 4, kk * 1024:(kk + 1) * 1024],
                          [], [R_wblk[b]])
                for j in range(8):
                    for kc in range(8):
                        k.mm(mps[li % 2][:, kk * 8 + j, :], wblk[b][:, kc, j * 128:(j + 1) * 128], k.sc[:, kc, :],
                             kc == 0, kc == 7, [R_wblk[b], k.R_mod], [R_mps[li % 2]], signal=(kc == 7))
            for r in range(2):
                k.tt("dve", k.mod[:, li, :, r], mps[li % 2][:, :, r], k.vecT[:, VC_BM + li * 72:VC_BM + (li + 1) * 72],
                     ALU.add, [R_mps[li % 2], k.R_vec], [k.R_mod])
            for s in range(3):
                gcol = VC_NG + (li * 3 + s) * 8
                for r in range(2):
                    sh = k.mod[:, li, (3 * s) * 8:(3 * s) * 8 + 8, r]
                    scl = k.mod[:, li, (3 * s + 1) * 8:(3 * s + 1) * 8 + 8, r]
                    gt = k.mod[:, li, (3 * s + 2) * 8:(3 * s + 2) * 8 + 8, r]
                    k.stt("dve", k.A[:, li, s, :, r], scl, 1.0, k.vecT[:, gcol:gcol + 8], ALU.add, ALU.mult,
                          [k.R_mod, k.R_vec], [k.R_mod])
                    k.copy("dve", k.B[:, li, s, :, r], sh, [k.R_mod], [k.R_mod])
                    if s == 1:
                        k.copy("dve", k.G[:, li, s, :, r], gt, [k.R_mod], [k.R_mod])
                    else:
                        k.ts("dve", k.G[:, li, s, :, r], gt, 0.5, None, ALU.mult, None, [k.R_mod], [k.R_mod])

        for _ in prep:
            pass
    P.barrier()
    with ExitStack() as es:
        sb = lambda name, shape, dt: es.enter_context(nc.sbuf_tensor(k.uname(name), shape, dt))
        ps = lambda name, shape: es.enter_context(nc.psum_tensor(k.uname(name), shape, F32))
        tp = [ps("tpx%d" % i, [128, 512]) for i in range(2)]
        R_tp = [Res("tpx%d" % i) for i in range(2)]
        xs = [sb("xs%d" % i, [128, 4, D], F32) for i in range(2)]
        R_xs = [Res("xs%d" % i) for i in range(2)]
        hst = [sb("hst%d" % i, [128, 8, 512], F32) for i in range(2)]
        R_hst = [Res("hst%d" % i) for i in range(2)]
        ncp = 0
        for ti, (t0, N, isctx) in enumerate(TILES):
            b = ti % 2
            ns = N // 128
            src = io["ctx"] if isctx else io["x"][t0 - CTX:t0 - CTX + N, :]
            k.dma("sp", xs[b][:, 0:ns, :], src.rearrange("(s p) d -> p s d", p=128), [], [R_xs[b]])
            for j in range(8):
                pb = ncp % 2
                for s in range(ns):
                    k.tr(tp[pb][:, s * 128:(s + 1) * 128], xs[b][:, s, j * 128:(j + 1) * 128], k.ident[:, :],
                         [R_xs[b], k.R_const], [R_tp[pb]])
                k.copy("act" if ncp % 2 else "dve", hst[b][:, j, 0:N], tp[pb][:, 0:N], [R_tp[pb]], [R_hst[b]])
                ncp += 1
            k.dma("sp", io["hT"][:, t0:t0 + N].rearrange("(c p) t -> p c t", p=128), hst[b][:, :, 0:N],
                  [R_hst[b]], k.R_hT[ti])
    P.barrier()


def ffn_phase(k, io, li, half):
    nc, P = k.nc, k.P
    s = 0 if half == 0 else 2
    from contextlib import ExitStack
    with ExitStack() as es:
        sb = lambda name, shape, dt: es.enter_context(nc.sbuf_tensor(k.uname(name), shape, dt))
        ps = lambda name, shape: es.enter_context(nc.psum_tensor(k.uname(name), shape, F32))
        w_in = sb("w_in", [128, 8, 2 * DFF], BF16)
        w_out = sb("w_out", [128, FCH, D], BF16)
        hbuf = sb("hbuf", [128, 8, 512], F32)
        uT = sb("uT", [128, 8, 512], BF16)
        gT = sb("gT", [128, FCH, 512], BF16)
        rstd = sb("rstd", [128, 512], F32)
        sqb = [sb("sqb%d" % i, [128, 512], BF16) for i in range(2)]
        tmp = [sb("tmp%d" % i, [128, 512], F32) for i in range(2)]
        sg = [sb("sg%d" % i, [128, 512], F32) for i in range(2)]
        hres = [sb("hres%d" % i, [128, 512], F32) for i in range(3)]
        pa = [ps("pa%d" % i, [128, 512]) for i in range(2)]
        pb = [ps("pb%d" % i, [128, 512]) for i in range(2)]
        py = [ps("py%d" % i, [128, 512]) for i in range(2)]
        pss = ps("pss", [128, 512])
        NWB = 11
        R_win = [Res("w_in%d" % i) for i in range(2 * NWB)]
        R_wout = [Res("w_out%d" % i) for i in range(2)]
        R_hbuf, R_uT, R_rstd, R_pss = Res("hbuf"), Res("uT"), Res("rstd"), Res("pss")
        R_gT = [Res("gT%d" % i) for i in range(FCH)]
        R_sqb = [Res("sqb%d" % i) for i in range(2)]
        R_tmp = [Res("tmp%d" % i) for i in range(2)]
        R_sg = [Res("sg%d" % i) for i in range(2)]
        R_hres = [Res("hres%d" % i) for i in range(3)]
        R_pa = [Res("pa%d" % i) for i in range(2)]
        R_pb = [Res("pb%d" % i) for i in range(2)]
        R_py = [Res("py%d" % i) for i in range(2)]

        wi = io["ffn_w_in"][li, half].rearrange("(c p) n -> p c n", p=128)
        wo = io["ffn_w_out"][li, half].rearrange("(c p) n -> p c n", p=128)
        for blk in range(NWB):
            for ab in range(2):
                c0 = ab * DFF + blk * 256
                k.dma("pool", w_in[:, :, c0:c0 + 256], wi[:, :, c0:c0 + 256], [], [R_win[ab * NWB + blk]])
        for hh in range(2):
            k.dma("pool", w_out[:, hh * 11:(hh + 1) * 11, :], wo[:, hh * 11:(hh + 1) * 11, :], [], [R_wout[hh]])

        def A1(ti):
            t0, N, isctx = TILES[ti]
            k.dma("sp", hbuf[:, :, 0:N], io["hT"][:, t0:t0 + N].rearrange("(c p) t -> p c t", p=128),
                  k.R_hT[ti], [R_hbuf])
            for j in range(8):
                k.act(sqb[j % 2][:, 0:N], hbuf[:, j, 0:N], AF.Square, [R_hbuf], [R_sqb[j % 2]])
                k.mm(pss[:, 0:N], k.ones_b[:, :], sqb[j % 2][:, 0:N], j == 0, j == 7, [R_sqb[j % 2], k.R_const], [R_pss])
            k.ts("dve", rstd[:, 0:N], pss[:, 0:N], 1.0 / D, EPS, ALU.mult, ALU.add, [R_pss], [R_rstd])
            k.act(rstd[:, 0:N], rstd[:, 0:N], AF.Sqrt, [R_rstd], [R_rstd])
            P.op("dve", (lambda o, i: (lambda e: e.reciprocal(out=o, in_=i)))(rstd[:, 0:N], rstd[:, 0:N]), [R_rstd], [R_rstd])

        def A2(ti):
            t0, N, isctx = TILES[ti]
            r = 1 if isctx else 0
            for j in range(8):
                k.stt("dve", tmp[j % 2][:, 0:N], hbuf[:, j, 0:N], k.A[:, li, s, j, r:r + 1], rstd[:, 0:N], ALU.mult, ALU.mult,
                      [R_hbuf, R_rstd, k.R_mod], [R_tmp[j % 2]])
                k.act(uT[:, j, 0:N], tmp[j % 2][:, 0:N], AF.Identity, [R_tmp[j % 2], k.R_mod], [R_uT],
                      bias=k.B[:, li, s, j, r:r + 1])

        def Bst(ti, c_lo, c_hi):
            t0, N, isctx = TILES[ti]
            for c in range(c_lo, c_hi):
                bk = c % 2
                for ab, pp, Rp in ((0, pa, R_pa), (1, pb, R_pb)):
                    col = ab * DFF + c * 128
                    Rw = R_win[ab * NWB + c // 2]
                    for kc in range(8):
                        k.mm(pp[bk][:, 0:N], w_in[:, kc, col:col + 128], uT[:, kc, 0:N], kc == 0, kc == 7,
                             [Rw, R_uT], [Rp[bk]], signal=(kc == 7))
                k.act(sg[bk][:, 0:N], pa[bk][:, 0:N], AF.Silu, [R_pa[bk]], [R_sg[bk]])
                k.tt("dve", gT[:, c, 0:N], sg[bk][:, 0:N], pb[bk][:, 0:N], ALU.mult, [R_sg[bk], R_pb[bk]], [R_gT[c]])

        def Cst(ti):
            t0, N, isctx = TILES[ti]
            r = 1 if isctx else 0
            for j in range(8):
                bk = j % 2
                hb = j % 3
                k.dma("sp", hres[hb][:, 0:N], io["hT"][j * 128:(j + 1) * 128, t0:t0 + N], [k.R_hT[ti][j]], [R_hres[hb]])
                for c in range(FCH):
                    k.mm(py[bk][:, 0:N], w_out[:, c, j * 128:(j + 1) * 128], gT[:, c, 0:N], c == 0, c == FCH - 1,
                         [R_wout[c // 11], R_gT[c]], [R_py[bk]], signal=(c == FCH - 1))
                k.stt("dve", hres[hb][:, 0:N], py[bk][:, 0:N], k.G[:, li, s, j, r:r + 1], hres[hb][:, 0:N], ALU.mult, ALU.add,
                      [R_py[bk], R_hres[hb], k.R_mod], [R_hres[hb]])
                k.dma("sp", io["hT"][j * 128:(j + 1) * 128, t0:t0 + N], hres[hb][:, 0:N], [R_hres[hb]], [k.R_hT[ti][j]])

        tiles = list(range(len(TILES)))
        if k.skip_ctx(li):
            tiles = tiles[1:]
        stop = k.dbg.get("ffn_stop", "C")
        tiles = tiles[:k.dbg.get("ffn_tiles", 99)]
        if stop != "w":
            A1(tiles[0])
            A2(tiles[0])
        for n_, ti in enumerate(tiles):
            if stop in ("w", "A"):
                break
            Bst(ti, 0, 11)
            if n_ + 1 < len(tiles):
                A1(tiles[n_ + 1])
            Bst(ti, 11, FCH)
            if n_ + 1 < len(tiles):
                A2(tiles[n_ + 1])
            if stop == "B":
                continue
            Cst(ti)
    P.barrier()


def final_phase(k, io):
    nc, P = k.nc, k.P
    from contextlib import ExitStack
    with ExitStack() as es:
        sb = lambda name, shape, dt: es.enter_context(nc.sbuf_tensor(k.uname(name), shape, dt))
        ps = lambda name, shape: es.enter_context(nc.psum_tensor(k.uname(name), shape, F32))
        hbuf = [sb("fhbuf%d" % i, [128, 8, 512], F32) for i in range(2)]
        R_hbuf = [Res("fhbuf%d" % i) for i in range(2)]
        sqb = [sb("fsqb%d" % i, [128, 512], BF16) for i in range(2)]
        R_sqb = [Res("fsqb%d" % i) for i in range(2)]
        rstd = sb("frstd", [128, 512], F32)
        R_rstd = Res("frstd")
        xn = [sb("fxn%d" % i, [128, 512], F32) for i in range(2)]
        R_xn = [Res("fxn%d" % i) for i in range(2)]
        ost = [sb("fost%d" % i, [128, 4, D], F32) for i in range(2)]
        R_ost = [Res("fost%d" % i) for i in range(2)]
        pss = ps("fpss", [128, 512])
        R_pss = Res("fpss")
        tp = [ps("ftp%d" % i, [128, 512]) for i in range(2)]
        R_tp = [Res("ftp%d" % i) for i in range(2)]
        R_out = Res("out")
        ncp = 0
        for n_, ti in enumerate(range(1, len(TILES))):
            t0, N, _ = TILES[ti]
            b = n_ % 2
            k.dma("sp", hbuf[b][:, :, 0:N], io["hT"][:, t0:t0 + N].rearrange("(c p) t -> p c t", p=128),
                  k.R_hT[ti], [R_hbuf[b]])
            for j in range(8):
                k.act(sqb[j % 2][:, 0:N], hbuf[b][:, j, 0:N], AF.Square, [R_hbuf[b]], [R_sqb[j % 2]])
                k.mm(pss[:, 0:N], k.ones_b[:, :], sqb[j % 2][:, 0:N], j == 0, j == 7, [R_sqb[j % 2], k.R_const], [R_pss])
            k.ts("dve", rstd[:, 0:N], pss[:, 0:N], 1.0 / D, EPS, ALU.mult, ALU.add, [R_pss], [R_rstd])
            k.act(rstd[:, 0:N], rstd[:, 0:N], AF.Sqrt, [R_rstd], [R_rstd])
            P.op("dve", (lambda o, i: (lambda e: e.reciprocal(out=o, in_=i)))(rstd[:, 0:N], rstd[:, 0:N]), [R_rstd], [R_rstd])
            for j in range(8):
                k.stt("dve", xn[j % 2][:, 0:N], hbuf[b][:, j, 0:N], k.vecT[:, VC_FNG + j:VC_FNG + j + 1], rstd[:, 0:N],
                      ALU.mult, ALU.mult, [R_hbuf[b], R_rstd, k.R_vec], [R_xn[j % 2]])
                pbk = ncp % 2
                for s_ in range(4):
                    k.tr(tp[pbk][:, s_ * 128:(s_ + 1) * 128], xn[j % 2][:, s_ * 128:(s_ + 1) * 128], k.ident[:, :],
                         [R_xn[j % 2], k.R_const], [R_tp[pbk]])
                k.copy("act" if ncp % 2 else "dve",
                       ost[b][:, :, j * 128:(j + 1) * 128], tp[pbk][:, :].rearrange("p (s f) -> p s f", f=128),
                       [R_tp[pbk]], [R_ost[b]])
                ncp += 1
            k.dma("sp", io["out"][t0 - CTX:t0 - CTX + N, :].rearrange("(s p) d -> p s d", p=128), ost[b][:, :, :],
                  [R_ost[b]], [R_out])
    P.barrier()


def _skip_ctx(self, li):
    return li == DEPTH - 1


K.skip_ctx = _skip_ctx

MIXERS = {}


def build_program(layers=range(DEPTH), mixers=True, do_final=True, ffn=True, dbg=None):
    from contextlib import ExitStack
    nc = bass.Bass("TRN2", target_bir_lowering=False)
    io = declare_io(nc)
    P = Prog(nc, n_dma=40, same_sync=bool((dbg or {}).get("same_sync", True)))
    k = K(nc, P)
    with ExitStack() as es:
        alloc_consts(k, es)
        k.dbg = dbg or {}
        prologue(k, io)
        for li in layers:
            if ffn:
                ffn_phase(k, io, li, 0)
            if mixers and (li % 3) in MIXERS:
                MIXERS[li % 3](k, io, li)
            if ffn:
                ffn_phase(k, io, li, 1)
        if do_final:
            final_phase(k, io)
        P.finish("sp")
        P.emit()
    return nc


def make_in_maps(inputs, cores=range(8)):
    f = lambda a: np.ascontiguousarray(np.asarray(a, dtype=np.float32))
    shared = {
        "c_ctx": f(inputs["c_ctx"]).reshape(8, 128),
        "w_mod": f(inputs["w_mod"]),
        "b_mod": f(inputs["b_mod"]).reshape(288, 128),
        "norm_g": f(inputs["norm_g"]).reshape(96, 128),
        "ffn_w_in": f(inputs["ffn_w_in"]),
        "ffn_w_out": f(inputs["ffn_w_out"]),
        "fnet_w_out": f(inputs["fnet_w_out"]),
        "ret_w_in": f(inputs["ret_w_in"]).reshape(D, 6144),
        "ret_w_out": f(inputs["ret_w_out"]).reshape(2048, D),
        "ret_decay_logit": f(inputs["ret_decay_logit"]).reshape(2, 4),
        "s5_lam_re": f(inputs["s5_lam_re"]).reshape(2, 64, 64),
        "s5_lam_im": f(inputs["s5_lam_im"]).reshape(2, 64, 64),
        "s5_log_dt": f(inputs["s5_log_dt"]).reshape(2, 64),
        "s5_b_re": f(inputs["s5_b_re"]).reshape(2, 64, 64, 16),
        "s5_b_im": f(inputs["s5_b_im"]).reshape(2, 64, 64, 16),
        "s5_c_re": f(inputs["s5_c_re"]).reshape(2, 64, 16, 64),
        "s5_c_im": f(inputs["s5_c_im"]).reshape(2, 64, 16, 64),
        "s5_d": f(inputs["s5_d"]).reshape(8, 128),
        "s5_w_glu": f(inputs["s5_w_glu"]).reshape(D, 2 * D),
        "final_norm_g": f(inputs["final_norm_g"]).reshape(8, 128),
    }
    shared.update(fnet_host_consts())
    shared.update(ret_host_consts())
    shared.update(s5_host_consts())
    x, c, ctx = f(inputs["x"]), f(inputs["c"]), f(inputs["ctx"])
    maps = []
    for b in cores:
        m = dict(shared)
        m["x"] = x[b]
        m["c"] = c[b].reshape(8, 128)
        m["ctx"] = ctx[b]
        maps.append(m)
    return maps


def kernel(**inputs):
    nc = build_program()
    maps = make_in_maps(inputs)
    res = run_bass_kernel_spmd(nc, maps, core_ids=list(range(8)))
    return np.stack([np.asarray(r["out"], dtype=np.float32) for r in res.results], axis=0)


def mk_modnorm(k, io, sb, ps):
    P = k.P
    hbuf = sb("mn_hbuf", [128, 8, 512], F32)
    sqb = [sb("mn_sqb%d" % i, [128, 512], BF16) for i in range(2)]
    tmp = [sb("mn_tmp%d" % i, [128, 512], F32) for i in range(2)]
    rstd = sb("mn_rstd", [128, 512], F32)
    pss = ps("mn_pss", [128, 512])
    R_hbuf, R_rstd, R_pss = Res("mn_hbuf"), Res("mn_rstd"), Res("mn_pss")
    R_sqb = [Res("mn_sqb%d" % i) for i in range(2)]
    R_tmp = [Res("mn_tmp%d" % i) for i in range(2)]

    def do(ti, li, s, uT_of, R_uT, chunks=range(8), raw_of=None):
        t0, N, isctx = TILES[ti]
        r = 1 if isctx else 0
        k.dma("sp", hbuf[:, :, 0:N], io["hT"][:, t0:t0 + N].rearrange("(c p) t -> p c t", p=128),
              k.R_hT[ti], [R_hbuf])
        for j in range(8):
            k.act(sqb[j % 2][:, 0:N], hbuf[:, j, 0:N], AF.Square, [R_hbuf], [R_sqb[j % 2]])
            k.mm(pss[:, 0:N], k.ones_b[:, :], sqb[j % 2][:, 0:N], j == 0, j == 7, [R_sqb[j % 2], k.R_const], [R_pss])
        k.ts("dve", rstd[:, 0:N], pss[:, 0:N], 1.0 / D, EPS, ALU.mult, ALU.add, [R_pss], [R_rstd])
        k.act(rstd[:, 0:N], rstd[:, 0:N], AF.Sqrt, [R_rstd], [R_rstd])
        P.op("dve", (lambda o, i: (lambda e: e.reciprocal(out=o, in_=i)))(rstd[:, 0:N], rstd[:, 0:N]), [R_rstd], [R_rstd])
        for n_, j in enumerate(chunks):
            k.stt("dve", tmp[n_ % 2][:, 0:N], hbuf[:, j, 0:N], k.A[:, li, s, j, r:r + 1], rstd[:, 0:N], ALU.mult, ALU.mult,
                  [R_hbuf, R_rstd, k.R_mod], [R_tmp[n_ % 2]])
            k.act(uT_of(j), tmp[n_ % 2][:, 0:N], AF.Identity, [R_tmp[n_ % 2], k.R_mod], [R_uT],
                  bias=k.B[:, li, s, j, r:r + 1])

    return do


def mk_resadd(k, io, sb, li):
    hres = [sb("ra_hres%d" % i, [128, 512], F32) for i in range(3)]
    R_hres = [Res("ra_hres%d" % i) for i in range(3)]
    cnt = [0]

    def do(ti, j, col0, N, y_ap, R_y, pre_reads=(), **dkw):
        t0, _, isctx = TILES[ti]
        r = 1 if isctx else 0
        hb = cnt[0] % 3
        cnt[0] += 1
        k.dma("sp", hres[hb][:, 0:N], io["hT"][j * 128:(j + 1) * 128, t0 + col0:t0 + col0 + N], [k.R_hT[ti][j]], [R_hres[hb]], **dkw)
        k.stt("dve", hres[hb][:, 0:N], y_ap, k.G[:, li, 1, j, r:r + 1], hres[hb][:, 0:N], ALU.mult, ALU.add,
              [R_y, R_hres[hb], k.R_mod] + list(pre_reads), [R_hres[hb]])
        k.dma("sp", io["hT"][j * 128:(j + 1) * 128, t0 + col0:t0 + col0 + N], hres[hb][:, 0:N], [R_hres[hb]], [k.R_hT[ti][j]], **dkw)

    return do


def declare_fnet_consts(nc, io):
    io["dft_c"] = nc.dram_tensor("dft_c", [128, 256], BF16, kind="ExternalInput").ap()
    io["dft_cosN"] = nc.dram_tensor("dft_cosN", [SEQ, SEQ], BF16, kind="ExternalInput").ap()
    io["dft_sinN"] = nc.dram_tensor("dft_sinN", [SEQ, SEQ], BF16, kind="ExternalInput").ap()
    io["dft_cosC"] = nc.dram_tensor("dft_cosC", [CTX, CTX], BF16, kind="ExternalInput").ap()
    io["dft_sinC"] = nc.dram_tensor("dft_sinC", [CTX, CTX], BF16, kind="ExternalInput").ap()
    io["fn_perm"] = nc.dram_tensor("fn_perm", [128, 4 * 128], BF16, kind="ExternalInput").ap()


def fnet_host_consts():
    import ml_dtypes
    bf = ml_dtypes.bfloat16
    c = np.arange(128)
    ang = 2.0 * np.pi * ((c[:, None] * c[None, :]) % 128) / 128.0
    dft_c = np.concatenate([np.cos(ang), -np.sin(ang)], axis=1).astype(np.float32).astype(bf)

    def mats(n):
        i = np.arange(n, dtype=np.int64)
        idx = (i[:, None] * i[None, :]) % n
        a = 2.0 * np.pi * np.arange(n, dtype=np.float64) / n
        return np.cos(a).astype(np.float32)[idx].astype(bf), np.sin(a).astype(np.float32)[idx].astype(bf)

    cN, sN = mats(SEQ)
    cC, sC = mats(CTX)
    J = np.zeros((128, 128), np.float32)
    for p in range(1, 128):
        J[128 - p, p] = 1.0
    E = np.zeros((128, 128), np.float32)
    E[0, 0] = 1.0
    perm = np.concatenate([J, -J, E, -E], axis=1).astype(bf)
    return {"dft_c": dft_c, "dft_cosN": cN, "dft_sinN": sN, "dft_cosC": cC, "dft_sinC": sC, "fn_perm": perm}


def fnet_mixer(k, io, li):
    nc, P = k.nc, k.P
    jw = li // 3
    from contextlib import ExitStack
    with ExitStack() as es:
        sb = lambda name, shape, dt: es.enter_context(nc.sbuf_tensor(k.uname(name), shape, dt))
        ps = lambda name, shape: es.enter_context(nc.psum_tensor(k.uname(name), shape, F32))
        modnorm = mk_modnorm(k, io, sb, ps)
        resadd = mk_resadd(k, io, sb, li)
        dftc = sb("dftc", [128, 256], BF16)
        w_out = sb("fw_out", [128, 8, D], BF16)
        X = sb("fX", [128, 32, 4, 256], BF16)
        cosb = sb("fcos", [128, 32, 512], BF16)
        sinb = sb("fsin", [128, 32, 512], BF16)
        uT = sb("fuT", [128, 8, 512], BF16)
        uTg = sb("fuTg", [128, 4, 512], BF16)
        R_uTg = Res("fuTg")
        R_uTd = [Res("uTd%d" % i) for i in range(len(TILES))]
        fT = [sb("ffT%d" % i, [128, 512], BF16) for i in range(8)]
        px = [ps("fpx%d" % i, [128, 512]) for i in range(2)]
        pf = [ps("fpf%d" % i, [128, 512]) for i in range(2)]
        py = [ps("fpy%d" % i, [128, 512]) for i in range(2)]
        R_dftc, R_wout, R_uT = Res("dftc"), Res("fw_out"), Res("fuT")
        R_X = [Res("fX%d" % i) for i in range(32)]
        R_cos = [Res("fcos%d" % i) for i in range(8)]
        R_sin = [Res("fsin%d" % i) for i in range(8)]
        R_fT = [Res("ffT%d" % i) for i in range(8)]
        R_px = [Res("fpx%d" % i) for i in range(2)]
        R_pf = [Res("fpf%d" % i) for i in range(2)]
        R_py = [Res("fpy%d" % i) for i in range(2)]
        k.dma("sp", dftc[:, :], io["dft_c"], [], [R_dftc])
        k.dma("pool", w_out[:, :, :], io["fnet_w_out"][jw].rearrange("(c p) n -> p c n", p=128), [], [R_wout])
        cnt = {"x": 0, "f": 0, "y": 0}

        def run(tiles, nch, groups_list, cosN, sinN, W, norm, sweep_only=False):
            for ti in tiles:
                t0, N, isctx = TILES[ti]
                modnorm(ti, li, 1, lambda j: uT[:, j, 0:N], R_uT)
                k.dma("sp", io["uT_d"][:, t0:t0 + N].rearrange("(c p) t -> p c t", p=128), uT[:, :, 0:N], [R_uT], [R_uTd[ti]])
            if sweep_only:
                return
            for groups in groups_list:
                ng = len(groups)
                for ti in tiles:
                    t0, N, isctx = TILES[ti]
                    g0 = groups[0]
                    k.dma("sp", uTg[:, 0:ng, 0:N],
                          io["uT_d"][g0 * 128:(g0 + ng) * 128, t0:t0 + N].rearrange("(c p) t -> p c t", p=128),
                          [R_uTd[ti]], [R_uTg])
                    c0 = (t0 - TILES[tiles[0]][0]) // 128
                    for sbi in range(N // 128):
                        for g2 in range(0, ng, 2):
                            bk = cnt["x"] % 2
                            cnt["x"] += 1
                            for u_ in range(2):
                                k.mm(px[bk][:, u_ * 256:(u_ + 1) * 256], uTg[:, g2 + u_, sbi * 128:(sbi + 1) * 128], dftc[:, :],
                                     True, True, [R_uTg, R_dftc], [R_px[bk]])
                            k.copy("act" if cnt["x"] % 2 else "dve",
                                   X[:, c0 + sbi, g2:g2 + 2, :], px[bk][:, :].rearrange("p (a b) -> p a b", b=256),
                                   [R_px[bk]], [R_X[c0 + sbi]])
                cq = max(1, nch // 8)
                for kt in range((nch * 128) // W):
                    for q in range(nch // cq):
                        k.dma("sp", cosb[:, q * cq:(q + 1) * cq, 0:W],
                              cosN.rearrange("(c p) k -> p c k", p=128)[:, q * cq:(q + 1) * cq, kt * W:(kt + 1) * W],
                              [], [R_cos[q]])
                        k.dma("act", sinb[:, q * cq:(q + 1) * cq, 0:W],
                              sinN.rearrange("(c p) k -> p c k", p=128)[:, q * cq:(q + 1) * cq, kt * W:(kt + 1) * W],
                              [], [R_sin[q]])
                    pf4 = [pf[0], pf[1], px[0], px[1]]
                    R_pf4 = [R_pf[0], R_pf[1], R_px[0], R_px[1]]
                    for c in range(nch):
                        for gi, g in enumerate(groups):
                            last = (c == nch - 1)
                            k.mm(pf4[gi][:, 0:W], X[:, c, gi, 0:128], cosb[:, c, 0:W], c == 0, False,
                                 [R_X[c], R_cos[c // cq]], [R_pf4[gi]], signal=False)
                            k.mm(pf4[gi][:, 0:W], X[:, c, gi, 128:256], sinb[:, c, 0:W], False, last,
                                 [R_X[c], R_sin[c // cq]], [R_pf4[gi]], signal=(last or (c % cq == cq - 1 and gi == ng - 1)))
                    for gi, g in enumerate(groups):
                        if gi % 2:
                            k.act(fT[gi][:, 0:W], pf4[gi][:, 0:W], AF.Copy, [R_pf4[gi]], [R_fT[gi]], scale=norm)
                        else:
                            k.ts("dve", fT[gi][:, 0:W], pf4[gi][:, 0:W], norm, None, ALU.mult, None, [R_pf4[gi]], [R_fT[gi]])
                    ti = tiles[0] + (kt * W) // 512 if W == 512 else tiles[0]
                    for j in range(8):
                        bk = cnt["y"] % 2
                        cnt["y"] += 1
                        for gi, g in enumerate(groups):
                            k.mm(py[bk][:, 0:W], w_out[:, g, j * 128:(j + 1) * 128], fT[gi][:, 0:W], gi == 0, gi == ng - 1,
                                 [R_wout, R_fT[gi]], [R_py[bk]], signal=(gi == ng - 1))
                        resadd(ti, j, 0, W, py[bk][:, 0:W], R_py[bk])

        if not k.skip_ctx(li):
            run([0], 2, [list(range(4)), list(range(4, 8))], io["dft_cosC"], io["dft_sinC"], 256, 1.0 / math.sqrt(CTX * 128))
        if k.dbg.get("fnet_old"):
            run(list(range(1, 9)), 32, [list(range(4)), list(range(4, 8))], io["dft_cosN"], io["dft_sinN"], 512,
                1.0 / math.sqrt(SEQ * 128))
        else:
            run(list(range(1, 9)), 32, None, None, None, 512, None, sweep_only=True)
    P.barrier()
    if not k.dbg.get("fnet_old"):
        fnet_latent_folded(k, io, li)


def fnet_latent_folded(k, io, li):
    nc, P = k.nc, k.P
    jw = li // 3
    from contextlib import ExitStack
    R = lambda n: Res(n)
    norm = 1.0 / math.sqrt(SEQ * 128)
    NF = 17
    with ExitStack() as es:
        sb = lambda name, shape, dt: es.enter_context(nc.sbuf_tensor(k.uname(name), shape, dt))
        ps = lambda name, shape: es.enter_context(nc.psum_tensor(k.uname(name), shape, F32))
        resadd = mk_resadd(k, io, sb, li)
        dftc = sb("g_dftc", [128, 256], BF16)
        perm = sb("g_perm", [128, 4, 128], BF16)
        w_out = sb("g_wout", [128, 8, D], BF16)
        Xf = sb("g_Xf", [128, NF, 8, 256], BF16)
        px = [ps("g_px%d" % i, [128, 512]) for i in range(2)]
        pf = [ps("g_pf%d" % i, [128, 512]) for i in range(2)]
        py = [ps("g_py%d" % i, [128, 512]) for i in range(2)]
        R_c, R_wout = R("g_c"), R("g_wout")
        R_Xf = [R("g_Xf%d" % i) for i in range(NF)]
        R_px = [R("g_px0"), R("g_px1")]
        R_pf = [R("g_pf0"), R("g_pf1")]
        R_py = [R("g_py0"), R("g_py1")]
        k.dma("sp", dftc[:, :], io["dft_c"], [], [R_c])
        k.dma("sp", perm[:, :, :], io["fn_perm"].rearrange("p (a b) -> p a b", b=128), [], [R_c])
        k.dma("pool", w_out[:, :, :], io["fnet_w_out"][jw].rearrange("(c p) n -> p c n", p=128), [], [R_wout])
        with ExitStack() as es1:
            sb1 = lambda name, shape, dt: es1.enter_context(nc.sbuf_tensor(k.uname(name), shape, dt))
            Xup = sb1("g_Xup", [128, 16, 8, 256], BF16)
            uTg = sb1("g_uTg", [128, 8, 512], BF16)
            R_uTg = R("g_uTg")
            R_Xup = [R("g_Xup%d" % i) for i in range(16)]
            nx = [0]

            def load_u(ti):
                t0, N, _ = TILES[ti]
                k.dma("sp", uTg[:, :, 0:N], io["uT_d"][:, t0:t0 + N].rearrange("(c p) t -> p c t", p=128), [], [R_uTg])

            def evac(dst, src, Rs, Rd):
                nx[0] += 1
                k.copy("act" if nx[0] % 2 else "dve", dst, src, [Rs], [Rd])

            for ti in range(5, 9):
                t0 = TILES[ti][0]
                load_u(ti)
                for sbi in range(4):
                    cu = (t0 - CTX) // 128 + sbi - 16
                    for g2 in range(0, 8, 2):
                        bk = nx[0] % 2
                        for u_ in range(2):
                            k.mm(px[bk][:, u_ * 256:(u_ + 1) * 256], uTg[:, g2 + u_, sbi * 128:(sbi + 1) * 128], dftc[:, :],
                                 True, True, [R_uTg, R_c], [R_px[bk]])
                        evac(Xup[:, cu, g2:g2 + 2, :], px[bk][:, :].rearrange("p (a b) -> p a b", b=256), R_px[bk], R_Xup[cu])
            for ti in range(1, 5):
                t0 = TILES[ti][0]
                load_u(ti)
                for sbi in range(4):
                    c = (t0 - CTX) // 128 + sbi
                    for g2 in range(0, 8, 2):
                        bk = nx[0] % 2
                        for u_ in range(2):
                            g = g2 + u_
                            for w in range(2):
                                o_ = px[bk][:, u_ * 256 + w * 128:u_ * 256 + (w + 1) * 128]
                                rd = [R_uTg, R_c, R_Xup[15 - c]] + ([R_Xup[16 - c]] if c >= 1 else [])
                                k.mm(o_, uTg[:, g, sbi * 128:(sbi + 1) * 128], dftc[:, w * 128:(w + 1) * 128], True, False,
                                     rd, [R_px[bk]], signal=False)
                                k.mm(o_, perm[:, w, :], Xup[:, 15 - c, g, w * 128:(w + 1) * 128], False, c == 0,
                                     rd, [R_px[bk]], signal=(c == 0))
                                if c >= 1:
                                    k.mm(o_, perm[:, 2 + w, :], Xup[:, 16 - c, g, w * 128:(w + 1) * 128], False, True,
                                         rd, [R_px[bk]], signal=True)
                        evac(Xf[:, c, g2:g2 + 2, :], px[bk][:, :].rearrange("p (a b) -> p a b", b=256), R_px[bk], R_Xf[c])
            k.memset("dve", Xf[:, 16, :, :], 0.0, [R_Xf[16]])
            for g2 in range(0, 8, 2):
                bk = nx[0] % 2
                for u_ in range(2):
                    k.mm(px[bk][:, u_ * 256:u_ * 256 + 128], perm[:, 2, :], Xup[:, 0, g2 + u_, 0:128], True, True,
                         [R_c, R_Xup[0]], [R_px[bk]])
                evac(Xf[:, 16, g2:g2 + 2, 0:128], px[bk][:, :].rearrange("p (a b) -> p a b", b=256)[:, :, 0:128], R_px[bk], R_Xf[16])
        P.barrier()
        with ExitStack() as es2:
            sb2 = lambda name, shape, dt: es2.enter_context(nc.sbuf_tensor(k.uname(name), shape, dt))
            cosb = [sb2("g_cos%d" % i, [128, NF, 512], BF16) for i in range(2)]
            sinb = [sb2("g_sin%d" % i, [128, NF, 512], BF16) for i in range(2)]
            fT = [sb2("g_fT%d" % i, [128, 512], BF16) for i in range(8)]
            SUBS = [(0, 4), (4, 8), (8, 12), (12, NF)]
            sub_of = [0] * 4 + [1] * 4 + [2] * 4 + [3] * 5
            R_cos = [[R("g_cos%d_%d" % (i, q)) for q in range(4)] for i in range(2)]
            R_sin = [[R("g_sin%d_%d" % (i, q)) for q in range(4)] for i in range(2)]
            R_fT = [R("g_fT%d" % i) for i in range(8)]
            cv = io["dft_cosN"][0:NF * 128, :].rearrange("(c p) k -> p c k", p=128)
            sv = io["dft_sinN"][0:NF * 128, :].rearrange("(c p) k -> p c k", p=128)
            pf4 = [pf[0], pf[1], px[0], px[1]]
            R_pf4 = [R_pf[0], R_pf[1], R_px[0], R_px[1]]
            ny = 0
            fTm = [sb2("g_fTm%d" % i, [128, 512], BF16) for i in range(8)]
            R_fTm = [R("g_fTm%d" % i) for i in range(8)]
            Bs = [sb2("g_Bs%d" % i, [128, 512], F32) for i in range(2)]
            R_Bs = [R("g_Bs0"), R("g_Bs1")]
            fT0 = sb2("g_fT0", [128, 8], BF16)
            R_fT0 = R("g_fT0")
            for g in range(8):
                for c in range(NF):
                    k.mm(pf[0][:, g:g + 1], Xf[:, c, g, 0:128], k.ones_b[:, 0:1], c == 0, c == NF - 1,
                         [R_Xf[c], k.R_const], [R_pf[0]], signal=(c == NF - 1))
            k.ts("dve", fT0[:, :], pf[0][:, 0:8], norm, None, ALU.mult, None, [R_pf[0]], [R_fT0])
            for j in range(8):
                bk = ny % 2
                ny += 1
                for g in range(8):
                    k.mm(py[bk][:, 0:1], w_out[:, g, j * 128:(j + 1) * 128], fT0[:, g:g + 1], g == 0, g == 7,
                         [R_wout, R_fT0], [R_py[bk]], signal=(g == 7))
                resadd(1, j, 0, 1, py[bk][:, 0:1], R_py[bk], allow_slow_non_contiguous=True)
            for jw in range(4):
                par = jw % 2
                k0 = 512 * jw + 1
                for qi, (a, b) in enumerate(SUBS):
                    k.dma("sp", cosb[par][:, a:b, :], cv[:, a:b, k0:k0 + 512], [], [R_cos[par][qi]])
                    k.dma("act", sinb[par][:, a:b, :], sv[:, a:b, k0:k0 + 512], [], [R_sin[par][qi]])
                for gp in range(4):
                    for c in range(NF):
                        for u in range(2):
                            g = 2 * gp + u
                            k.mm(pf4[2 * u][:, :], Xf[:, c, g, 0:128], cosb[par][:, c, :], c == 0, c == NF - 1,
                                 [R_Xf[c], R_cos[par][sub_of[c]]], [R_pf4[2 * u]], signal=(c == NF - 1))
                            k.mm(pf4[2 * u + 1][:, :], Xf[:, c, g, 128:256], sinb[par][:, c, :], c == 0, c == NF - 1,
                                 [R_Xf[c], R_sin[par][sub_of[c]]], [R_pf4[2 * u + 1]], signal=(c == NF - 1))
                    for u in range(2):
                        g = 2 * gp + u
                        k.act(Bs[u][:, :], pf4[2 * u + 1][:, :], AF.Copy, [R_pf4[2 * u + 1]], [R_Bs[u]], scale=norm)
                        k.stt("dve", fT[g][:, :], pf4[2 * u][:, :], norm, Bs[u][:, :], ALU.mult, ALU.add,
                              [R_pf4[2 * u], R_Bs[u]], [R_fT[g]])
                        k.stt("dve", fTm[g][:, ::-1], pf4[2 * u][:, :], norm, Bs[u][:, :], ALU.mult, ALU.subtract,
                              [R_pf4[2 * u], R_Bs[u]], [R_fTm[g]])
                pm = SEQ - k0 - 511
                for (fTs, R_fs, pos, c_lo) in ((fT, R_fT, k0, 0), (fTm, R_fTm, pm, 1 if jw == 3 else 0)):
                    Nw = 512 - c_lo
                    for j in range(8):
                        bk = ny % 2
                        ny += 1
                        for g in range(8):
                            k.mm(py[bk][:, 0:Nw], w_out[:, g, j * 128:(j + 1) * 128], fTs[g][:, c_lo:512], g == 0, g == 7,
                                 [R_wout, R_fs[g]], [R_py[bk]], signal=(g == 7))
                        resadd(1, j, pos + c_lo, Nw, py[bk][:, 0:Nw], R_py[bk])
    P.barrier()


MIXERS[0] = fnet_mixer


NCHK = T // 128


def declare_ret_consts(nc, io):
    io["ret_expo"] = nc.dram_tensor("ret_expo", [128, 4 * 128], F32, kind="ExternalInput").ap()
    io["ret_m01"] = nc.dram_tensor("ret_m01", [128, 2 * 128], F32, kind="ExternalInput").ap()
    io["ret_ramp"] = nc.dram_tensor("ret_ramp", [128, NCHK], F32, kind="ExternalInput").ap()
    io["ropeC"] = nc.dram_tensor("ropeC", [256, SEQ], F32, kind="ExternalInput").ap()
    io["ropeS"] = nc.dram_tensor("ropeS", [256, SEQ], F32, kind="ExternalInput").ap()
    io["zT_d"] = nc.dram_tensor("zT_d", [2048, T], BF16).ap()


def ret_host_consts():
    m = np.arange(128, dtype=np.float32)[:, None]
    l = np.arange(128, dtype=np.float32)[None, :]
    expo = np.concatenate([np.maximum(l - m, 0), l - m + 128, np.maximum(m - l, 0), m - l + 128], axis=1).astype(np.float32)
    m01 = np.concatenate([(m <= l), (m > l)], axis=1).astype(np.float32)
    ramp = np.broadcast_to(128.0 * np.arange(NCHK, dtype=np.float32)[None, :], (128, NCHK)).copy()
    t = np.arange(SEQ)
    inv_freq = np.exp(np.float32(-math.log(10000.0)) * np.arange(64, dtype=np.float32) / np.float32(64)).astype(np.float32)
    ang_r = ((t // 64).astype(np.float32)[:, None] * inv_freq[None, :]).astype(np.float32)
    ang_c = ((t % 64).astype(np.float32)[:, None] * inv_freq[None, :]).astype(np.float32)
    C = np.zeros((256, SEQ), np.float32)
    S = np.zeros((256, SEQ), np.float32)
    for d, ang in enumerate((ang_r, ang_c)):
        c, s = np.cos(ang).T.astype(np.float32), np.sin(ang).T.astype(np.float32)
        C[d * 128:d * 128 + 64] = c
        C[d * 128 + 64:d * 128 + 128] = c
        S[d * 128:d * 128 + 64] = -s
        S[d * 128 + 64:d * 128 + 128] = s
    return {"ret_expo": expo, "ret_m01": m01, "ret_ramp": ramp, "ropeC": C, "ropeS": S}


def ret_mixer(k, io, li):
    nc, P = k.nc, k.P
    from contextlib import ExitStack
    with ExitStack() as es:
        sb = lambda name, shape, dt: es.enter_context(nc.sbuf_tensor(k.uname(name), shape, dt))
        ps = lambda name, shape, dt=F32: es.enter_context(nc.psum_tensor(k.uname(name), shape, dt))
        modnorm = mk_modnorm(k, io, sb, ps)
        resadd = mk_resadd(k, io, sb, li)
        R = lambda n: Res(n)
        expo = sb("r_expo", [128, 4, 128], F32)
        m01 = sb("r_m01", [128, 2, 128], F32)
        ramp = sb("r_ramp", [128, NCHK], F32)
        ones1 = sb("r_ones1", [1, 128], F32)
        lg1 = sb("r_lg1", [1, 8], F32)
        lgam = sb("r_lgam", [128, 8], F32)
        Etab = sb("r_Etab", [128, 8, 2, 128], F32)
        Wdiag = sb("r_Wdiag", [128, 4, 128], F32)
        apow = sb("r_apow", [128, 8, NCHK], F32)
        R_tab = R("r_tab")
        pA = ps("r_pA", [128, 512])
        pB = ps("r_pB", [128, 512])
        psc_t = [ps("r_psc%d" % i, [128, 512]) for i in range(2)]
        NSC = 4
        psc = [psc_t[0], psc_t[1], pA, pB]
        po = ps("r_po", [128, 512])
        pg = ps("r_pg", [128, 512])
        pt = ps("r_pt", [128, 512], BF16)
        R_pA, R_pB, R_po, R_pg, R_pt = R("pA"), R("pB"), R("po"), R("pg"), R("pt")
        R_psc = [R("psc0"), R("psc1"), R_pA, R_pB]
        k.dma("sp", expo[:, :, :], io["ret_expo"].rearrange("p (a b) -> p a b", b=128), [], [R_tab])
        k.dma("sp", m01[:, :, :], io["ret_m01"].rearrange("p (a b) -> p a b", b=128), [], [R_tab])
        k.dma("sp", ramp[:, :], io["ret_ramp"], [], [R_tab])
        k.dma("sp", lg1[:, :], io["ret_decay_logit"].rearrange("(o a) b -> o (a b)", o=1), [], [R_tab])
        k.memset("dve", ones1[:, :], 1.0, [R_tab])
        k.mm(pA[:, 0:8], ones1[0:1, :], lg1[0:1, :], True, True, [R_tab], [R_pA])
        k.act(lgam[:, :], pA[:, 0:8], AF.Exp, [R_pA], [R_tab], scale=-1.0)
        k.ts("dve", lgam[:, :], lgam[:, :], 1.0, None, ALU.add, None, [R_tab], [R_tab])
        k.act(lgam[:, :], lgam[:, :], AF.Ln, [R_tab], [R_tab])
        k.ts("dve", lgam[:, :], lgam[:, :], -1.0, None, ALU.mult, None, [R_tab], [R_tab])
        for d in range(2):
            for h in range(4):
                col = d * 4 + h
                for w in range(2):
                    k.act(Etab[:, col, w, :], expo[:, d * 2 + w, :], AF.Exp, [R_tab], [R_tab], scale=lgam[:, col:col + 1])
                k.tt("dve", Etab[:, col, 0, :], Etab[:, col, 0, :], m01[:, d, :], ALU.mult, [R_tab], [R_tab])
                k.act(apow[:, col, :], ramp[:, :], AF.Exp, [R_tab], [R_tab], scale=lgam[:, col:col + 1])
        for h in range(4):
            k.tt("dve", Wdiag[:, h, :], Etab[:, h, 0, :], Etab[:, 4 + h, 0, :], ALU.add, [R_tab], [R_tab])

        uT = sb("r_uT", [128, 8, 512], BF16)
        R_uT = R("r_uT")
        R_uTd = [R("uTd%d" % i) for i in range(len(TILES))]
        for ti in range(len(TILES)):
            t0, N, _ = TILES[ti]
            modnorm(ti, li, 1, lambda j: uT[:, j, 0:N], R_uT)
            k.dma("sp", io["uT_d"][:, t0:t0 + N].rearrange("(c p) t -> p c t", p=128), uT[:, :, 0:N], [R_uT], [R_uTd[ti]])

        wq = sb("r_wq", [128, 8, 256], BF16)
        wk = sb("r_wk", [128, 8, 256], BF16)
        wqs = sb("r_wqs", [128, 8, 256], BF16)
        wks = sb("r_wks", [128, 8, 256], BF16)
        wv = sb("r_wv", [128, 8, 512], BF16)
        wg = sb("r_wg", [128, 8, 512], BF16)
        R_w = R("r_w")
        qT = sb("r_qT", [128, 2, T], BF16)
        kT = sb("r_kT", [128, 2, T], BF16)
        vtm = sb("r_vtm", [128, NCHK, 512], BF16)
        R_qT, R_kT = R("qT"), R("kT")
        R_v = [R("v%d" % i) for i in range(NCHK)]
        rc = sb("r_rc", [128, 2, 512], F32)
        rs = sb("r_rs", [128, 2, 512], F32)
        R_rope = R("rope")
        t1 = [sb("r_t1_%d" % i, [128, 512], F32) for i in range(2)]
        t2 = [sb("r_t2_%d" % i, [128, 512], F32) for i in range(2)]
        R_t1 = [R("t1a"), R("t1b")]
        R_t2 = [R("t2a"), R("t2b")]
        NPB = 24
        Pb = [sb("r_Pb%d" % i, [128, 128], BF16) for i in range(NPB)]
        R_Pb = [R("Pb%d" % i) for i in range(NPB)]
        Pt = [sb("r_Pt%d" % i, [128, 128], F32) for i in range(2)]
        R_Pt = [R("Pt0"), R("Pt1")]
        uTcs = [sb("r_uTc%d" % i, [128, 8, 128], BF16) for i in range(2)]
        R_uTcs = [R("uTc0"), R("uTc1")]

        def load_uTc(lc_):
            k.dma("sp", uTcs[lc_ % 2][:, :, :], io["uT_d"][:, lc_ * 128:(lc_ + 1) * 128].rearrange("(c p) t -> p c t", p=128),
                  [R_uTd[0 if lc_ < 2 else 1 + (lc_ - 2) // 4]], [R_uTcs[lc_ % 2]])
        osb = sb("r_osb", [128, 512], F32)
        sgt = sb("r_sgt", [128, 512], F32)
        sqj = sb("r_sqj", [128, 512], F32)
        zb = sb("r_zb", [128, 512], BF16)
        zb2 = sb("r_zb2", [128, 512], BF16)
        zbs = [zb, zb2]
        R_zbs = [R("zb0"), R("zb1")]
        zTc = sb("r_zTc", [128, 4, 128], BF16)
        st = sb("r_st", [128, 4], F32)
        identb = sb("r_identb", [128, 128], BF16)
        R_osb, R_sgt, R_zb, R_zTc, R_st, R_sqj = R("osb"), R("sgt"), R("zb"), R("zTc"), R("st"), R("sqj")
        R_zTd = [R("zTd%d" % i) for i in range(NCHK)]
        k.copy("dve", identb[:, :], k.ident[:, :], [k.R_const], [R_tab])
        wi = io["ret_w_in"].rearrange("(c p) n -> p c n", p=128)

        def cb(c):
            return c - 2 if c >= 2 else 32 + c

        nblk = 0
        for h in range(4):
            for (dst, c0, n) in ((wq, h * 256, 256), (wk, 1024 + h * 256, 256), (wv, 2048 + h * 512, 512), (wg, 4096 + h * 512, 512)):
                k.dma("pool", dst[:, :, 0:n], wi[:, :, c0:c0 + n], [], [R_w])
            for (dst, c0) in ((wqs, h * 256), (wks, 1024 + h * 256)):
                for dk in range(2):
                    b0 = c0 + dk * 128
                    k.dma("pool", dst[:, :, dk * 128:dk * 128 + 64], wi[:, :, b0 + 64:b0 + 128], [], [R_w])
                    k.dma("pool", dst[:, :, dk * 128 + 64:dk * 128 + 128], wi[:, :, b0:b0 + 64], [], [R_w])
            for ti in range(len(TILES)):
                t0, N, isctx = TILES[ti]
                k.dma("sp", uT[:, :, 0:N], io["uT_d"][:, t0:t0 + N].rearrange("(c p) t -> p c t", p=128), [R_uTd[ti]], [R_uT])
                if not isctx:
                    p0 = t0 - CTX
                    k.dma("sp", rc[:, :, 0:N], io["ropeC"][:, p0:p0 + N].rearrange("(d p) t -> p d t", p=128), [], [R_rope])
                    k.dma("sp", rs[:, :, 0:N], io["ropeS"][:, p0:p0 + N].rearrange("(d p) t -> p d t", p=128), [], [R_rope])
                for (w_, ws_, dst, R_dst, scl) in ((wq, wqs, qT, R_qT, 1.0), (wk, wks, kT, R_kT, 0.0625)):
                    for dk in range(2):
                        for kc in range(8):
                            k.mm(pA[:, 0:N], w_[:, kc, dk * 128:(dk + 1) * 128], uT[:, kc, 0:N], kc == 0, kc == 7,
                                 [R_w, R_uT], [R_pA], signal=(kc == 7))
                        if isctx:
                            k.ts("dve", dst[:, dk, t0:t0 + N], pA[:, 0:N], scl, None, ALU.mult, None, [R_pA], [R_dst])
                            continue
                        for kc in range(8):
                            k.mm(pB[:, 0:N], ws_[:, kc, dk * 128:(dk + 1) * 128], uT[:, kc, 0:N], kc == 0, kc == 7,
                                 [R_w, R_uT], [R_pB], signal=(kc == 7))
                        b = dk
                        k.stt("dve", t1[b][:, 0:N], pA[:, 0:N], scl, rc[:, dk, 0:N], ALU.mult, ALU.mult, [R_pA, R_rope], [R_t1[b]])
                        k.stt("dve", t2[b][:, 0:N], pB[:, 0:N], scl, rs[:, dk, 0:N], ALU.mult, ALU.mult, [R_pB, R_rope], [R_t2[b]])
                        k.tt("pool", dst[:, dk, t0:t0 + N], t1[b][:, 0:N], t2[b][:, 0:N], ALU.add, [R_t1[b], R_t2[b]], [R_dst])
                for sbi in range(N // 128):
                    c = t0 // 128 + sbi
                    for kc in range(8):
                        k.mm(pg[:, :], uT[:, kc, sbi * 128:(sbi + 1) * 128], wv[:, kc, :], kc == 0, kc == 7,
                             [R_w, R_uT], [R_pg], signal=(kc == 7))
                    k.copy("act", vtm[:, c, :], pg[:, :], [R_pg], [R_v[c]])
            tasks = []
            for lc in range(NCHK):
                blocks = []
                for mc in range(NCHK):
                    terms = []
                    if mc == lc:
                        terms = ["diag"]
                    else:
                        if mc < lc:
                            terms.append((h, lc - mc - 1))
                        if cb(mc) > cb(lc):
                            terms.append((4 + h, cb(mc) - cb(lc) - 1))
                    if terms:
                        blocks.append((mc, terms))
                for bi, (mc, terms) in enumerate(blocks):
                    tasks.append((lc, bi, len(blocks), mc, terms))

            GB = 4
            groups_ = [tasks[i:i + GB] for i in range(0, len(tasks), GB)]

            def emit_scores(grp, gslot):
                sk = gslot % NSC
                for i_, (lc, bi, nb, mc, terms) in enumerate(grp):
                    for kc in range(2):
                        k.mm(psc[sk][:, i_ * 128:(i_ + 1) * 128], kT[:, kc, mc * 128:(mc + 1) * 128], qT[:, kc, lc * 128:(lc + 1) * 128],
                             kc == 0, kc == 1, [R_kT, R_qT], [R_psc[sk]], signal=(kc == 1 and i_ == len(grp) - 1))
                for i_, (lc, bi, nb, mc, terms) in enumerate(grp):
                    pk = (gslot * GB + i_) % NPB
                    sc_ap = psc[sk][:, i_ * 128:(i_ + 1) * 128]
                    if terms[0] == "diag":
                        k.tt("dve", Pb[pk][:, :], sc_ap, Wdiag[:, h, :], ALU.mult, [R_psc[sk], R_tab], [R_Pb[pk]])
                    elif len(terms) == 1:
                        col, n = terms[0]
                        k.stt("dve", Pb[pk][:, :], sc_ap, apow[:, col, n:n + 1], Etab[:, col, 1, :], ALU.mult, ALU.mult,
                              [R_psc[sk], R_tab], [R_Pb[pk]])
                    else:
                        for q_, (col, n) in enumerate(terms):
                            k.stt("dve", Pt[q_][:, :], sc_ap, apow[:, col, n:n + 1], Etab[:, col, 1, :], ALU.mult, ALU.mult,
                                  [R_psc[sk], R_tab], [R_Pt[q_]])
                        k.tt("dve", Pb[pk][:, :], Pt[0][:, :], Pt[1][:, :], ALU.add, [R_Pt[0], R_Pt[1]], [R_Pb[pk]])
                    if fin_ops:
                        fin_ops.pop(0)()

            DLOOK = 3
            pending_fin = []
            fin_ops = []
            fin_clock = [0]
            load_uTc(0)

            def flush_fin(now):
                while pending_fin and pending_fin[0][0] <= now:
                    _, lc_ = pending_fin.pop(0)
                    zb_ = zbs[lc_ % 2]
                    for ec in range(4):
                        k.tr(pt[:, ec * 128:(ec + 1) * 128], zb_[:, ec * 128:(ec + 1) * 128], identb[:, :], [R_zbs[lc_ % 2], R_tab], [R_pt])
                    k.copy("act", zTc[:, :, :], pt[:, :].rearrange("p (a b) -> p a b", b=128), [R_pt], [R_zTc])
                    k.dma("sp", io["zT_d"][h * 512:(h + 1) * 512, lc_ * 128:(lc_ + 1) * 128].rearrange("(c p) t -> p c t", p=128),
                          zTc[:, :, :], [R_zTc], [R_zTd[lc_]])

            gbase = nblk
            pv_list = []
            for gi_ in range(len(groups_) + DLOOK):
                if gi_ < len(groups_):
                    emit_scores(groups_[gi_], gbase + gi_)
                if gi_ - DLOOK >= 0:
                    for i_, tk in enumerate(groups_[gi_ - DLOOK]):
                        pv_list.append((tk, ((gbase + gi_ - DLOOK) * GB + i_) % NPB, gi_))
                while pv_list:
                    (lc, bi, nb, mc, terms), pk, idx = pv_list.pop(0)
                    fin_clock[0] = idx
                    flush_fin(idx)
                    k.mm(po[:, :], Pb[pk][:, :], vtm[:, mc, :], bi == 0, bi == nb - 1, [R_Pb[pk], R_v[mc]], [R_po])
                    if bi != nb - 1:
                        continue
                    while fin_ops:
                        fin_ops.pop(0)()
                    uTc = uTcs[lc % 2]
                    for kc in range(8):
                        k.mm(pg[:, :], uTc[:, kc, :], wg[:, kc, :], kc == 0, kc == 7, [R_w, R_uTcs[lc % 2]], [R_pg], signal=(kc == 7))
                    if lc + 1 < NCHK:
                        load_uTc(lc + 1)
                    k.copy("act", osb[:, :], po[:, :], [R_po], [R_osb])
                    k.act(sgt[:, :], pg[:, :], AF.Silu, [R_pg], [R_sgt])
                    zb_, Rzb_ = zbs[lc % 2], R_zbs[lc % 2]
                    fin_ops.extend([
                        lambda: P.op("dve", lambda e: e.reduce_sum(out=st[:, 0:1], in_=osb[:, :], axis=AX.X), [R_osb], [R_st]),
                        lambda: k.ts("dve", st[:, 0:1], st[:, 0:1], 1.0 / 512, None, ALU.mult, None, [R_st], [R_st]),
                        lambda: k.ts("dve", osb[:, :], osb[:, :], st[:, 0:1], None, ALU.subtract, None, [R_st, R_osb], [R_osb]),
                        lambda: k.tt("dve", sqj[:, :], osb[:, :], osb[:, :], ALU.mult, [R_osb], [R_sqj]),
                        lambda: P.op("dve", lambda e: e.reduce_sum(out=st[:, 1:2], in_=sqj[:, :], axis=AX.X), [R_sqj], [R_st]),
                        lambda: k.ts("dve", st[:, 1:2], st[:, 1:2], 1.0 / 512, EPS, ALU.mult, ALU.add, [R_st], [R_st]),
                        lambda: k.act(st[:, 1:2], st[:, 1:2], AF.Sqrt, [R_st], [R_st]),
                        lambda: None,
                        lambda: None,
                        lambda: P.op("dve", lambda e: e.reciprocal(out=st[:, 2:3], in_=st[:, 1:2]), [R_st], [R_st]),
                        (lambda zb_=zb_, Rzb_=Rzb_: k.stt("dve", zb_[:, :], osb[:, :], st[:, 2:3], sgt[:, :], ALU.mult, ALU.mult,
                                                          [R_osb, R_st, R_sgt], [Rzb_])),
                        (lambda lc=lc: pending_fin.append((fin_clock[0] + 3, lc))),
                    ])
            while fin_ops:
                fin_ops.pop(0)()
            flush_fin(10 ** 9)
            nblk += len(groups_) + DLOOK
    P.barrier()
    with ExitStack() as es:
        sb = lambda name, shape, dt: es.enter_context(nc.sbuf_tensor(k.uname(name), shape, dt))
        ps = lambda name, shape, dt=F32: es.enter_context(nc.psum_tensor(k.uname(name), shape, dt))
        resadd = mk_resadd(k, io, sb, li)
        pA = ps("r_pA2", [128, 512])
        pB = ps("r_pB2", [128, 512])
        R_pA, R_pB = R("pA2"), R("pB2")
        wo = sb("r_wo", [128, 16, D], BF16)
        zt = sb("r_zt", [128, 16, 512], BF16)
        R_wo, R_zt = R("r_wo"), R("r_zt")
        k.dma("pool", wo[:, :, :], io["ret_w_out"].rearrange("(c p) n -> p c n", p=128), [], [R_wo])
        for ti in range(len(TILES)):
            t0, N, isctx = TILES[ti]
            k.dma("sp", zt[:, :, 0:N], io["zT_d"][:, t0:t0 + N].rearrange("(c p) t -> p c t", p=128),
                  R_zTd[t0 // 128:(t0 + N) // 128], [R_zt])
            for j in range(8):
                pp, Rp = (pA, R_pA) if j % 2 == 0 else (pB, R_pB)
                for ec in range(16):
                    k.mm(pp[:, 0:N], wo[:, ec, j * 128:(j + 1) * 128], zt[:, ec, 0:N], ec == 0, ec == 15, [R_wo, R_zt], [Rp],
                         signal=(ec == 15))
                resadd(ti, j, 0, N, pp[:, 0:N], Rp)
    P.barrier()


MIXERS[1] = ret_mixer


TB = 64
NBLK = T // TB


def declare_s5_consts(nc, io):
    io["s5_rmask"] = nc.dram_tensor("s5_rmask", [128, 4], F32, kind="ExternalInput").ap()
    io["s5_emask"] = nc.dram_tensor("s5_emask", [128, 2], F32, kind="ExternalInput").ap()
    io["s5_cmask"] = nc.dram_tensor("s5_cmask", [128, 4 * 128], F32, kind="ExternalInput").ap()
    io["yf_d"] = nc.dram_tensor("yf_d", [2, D, T], F32).ap()
    io["s5p_Wt"] = nc.dram_tensor("s5p_Wt", [128, 2 * 32 * 2 * 128], BF16).ap()
    io["s5p_Ct"] = nc.dram_tensor("s5p_Ct", [128, 2 * 32 * 2 * 128], BF16).ap()
    io["s5p_Ctab"] = nc.dram_tensor("s5p_Ctab", [128, 2 * 32 * 64], F32).ap()
    io["s5p_Stab"] = nc.dram_tensor("s5p_Stab", [128, 2 * 32 * 64], F32).ap()
    io["s5p_rt"] = nc.dram_tensor("s5p_rt", [128, 2 * 32 * 64], F32).ap()
    io["s5p_w64"] = nc.dram_tensor("s5p_w64", [128, 128], F32).ap()


def s5_host_consts():
    r = np.arange(128)
    rmask = np.stack([(r // 32 == q4) for q4 in range(4)], axis=1).astype(np.float32)
    emask = np.stack([((r // 16) % 2 == e) for e in range(2)], axis=1).astype(np.float32)
    cm = np.zeros((128, 4, 128), np.float32)
    for q4 in range(4):
        cm[:, q4, 32 * q4:32 * q4 + 32] = 1.0
    return {"s5_rmask": rmask, "s5_emask": emask, "s5_cmask": cm.reshape(128, 512)}


def s5_prep_gen(k, io, sb, ps):
    nc, P = k.nc, k.P
    from contextlib import ExitStack
    R = lambda n: Res(n)
    PI = math.pi
    R_t = R("s5tab")
    pT = ps("s5_pT", [128, 512])
    R_pT = R("s5_pT")
    rmask = sb("s5_rmask", [128, 4], F32)
    emask = sb("s5_emask", [128, 2], F32)
    cmask = sb("s5_cmask", [128, 4, 128], F32)
    k.dma("sp", rmask[:, :], io["s5_rmask"], [], [R_t])
    k.dma("sp", emask[:, :], io["s5_emask"], [], [R_t])
    k.dma("sp", cmask[:, :, :], io["s5_cmask"].rearrange("p (a b) -> p a b", b=128), [], [R_t])
    lst = sb("s5_lst", [64, 2, 128], F32)
    k.dma("sp", lst[:, 0, :], io["s5_lam_re"].rearrange("d (q e) p -> (d q) (e p)", e=2), [], [R_t])
    k.dma("sp", lst[:, 1, :], io["s5_lam_im"].rearrange("d (q e) p -> (d q) (e p)", e=2), [], [R_t])
    lam = sb("s5_lam", [128, 2, 64], F32)
    for w in range(2):
        k.tr(pT[:, 0:64], lst[:, w, :], k.ident[0:64, 0:64], [R_t, k.R_const], [R_pT])
        k.copy("dve", lam[:, w, :], pT[:, 0:64], [R_pT], [R_t])
    ones1 = sb("s5_ones1", [1, 128], F32)
    ldt1 = sb("s5_ldt1", [1, 128], F32)
    k.memset("dve", ones1[:, :], 1.0, [R_t])
    k.dma("sp", ldt1[:, :], io["s5_log_dt"].rearrange("(o d) g -> o (d g)", o=1), [], [R_t])
    k.mm(pT[:, 0:128], ones1[0:1, :], ldt1[0:1, :], True, True, [R_t], [R_pT])
    dt = sb("s5_dt", [128, 64], F32)
    bc = pT[:, 0:128].rearrange("p (d q e) -> p d q e", d=2, e=2)
    dt3 = dt[:, :].rearrange("p (d q) -> p d q", d=2)
    k.act(dt3[0:64], bc[0:64, :, :, 0], AF.Exp, [R_pT], [R_t])
    k.act(dt3[64:128], bc[64:128, :, :, 1], AF.Exp, [R_pT], [R_t])
    sm = lambda n: sb("s5_" + n, [128, 64], F32)
    mag, ang, ar, ai, tmpa, tmpb, sg_, den, cfr, cfi = [sm(n) for n in
                                                         "mag ang ar ai tmpa tmpb sg den cfr cfi".split()]
    k.tt("dve", mag[:, :], lam[:, 0, :], dt[:, :], ALU.mult, [R_t], [R_t])
    k.act(mag[:, :], mag[:, :], AF.Exp, [R_t], [R_t])
    k.tt("dve", ang[:, :], lam[:, 1, :], dt[:, :], ALU.mult, [R_t], [R_t])

    def sin_of(dst, shift):
        k.ts("dve", tmpa[:, :], ang[:, :], shift - 4 * PI, None, ALU.add, None, [R_t], [R_t])
        for thr in (PI, 3 * PI, 5 * PI, 7 * PI):
            k.ts("dve", tmpb[:, :], ang[:, :], shift - thr, None, ALU.add, None, [R_t], [R_t])
            k.act(sg_[:, :], tmpb[:, :], AF.Sign, [R_t], [R_t])
            k.stt("dve", tmpa[:, :], sg_[:, :], -PI, tmpa[:, :], ALU.mult, ALU.add, [R_t], [R_t])
        k.act(dst, tmpa[:, :], AF.Sin, [R_t], [R_t])

    sin_of(ai[:, :], 0.0)
    sin_of(ar[:, :], PI / 2)
    k.tt("dve", ar[:, :], ar[:, :], mag[:, :], ALU.mult, [R_t], [R_t])
    k.tt("dve", ai[:, :], ai[:, :], mag[:, :], ALU.mult, [R_t], [R_t])
    k.tt("dve", den[:, :], lam[:, 0, :], lam[:, 0, :], ALU.mult, [R_t], [R_t])
    k.tt("dve", tmpa[:, :], lam[:, 1, :], lam[:, 1, :], ALU.mult, [R_t], [R_t])
    k.tt("dve", den[:, :], den[:, :], tmpa[:, :], ALU.add, [R_t], [R_t])
    P.op("dve", lambda e: e.reciprocal(out=den[:, :], in_=den[:, :]), [R_t], [R_t])
    k.ts("dve", tmpb[:, :], ar[:, :], -1.0, None, ALU.add, None, [R_t], [R_t])
    k.tt("dve", cfr[:, :], tmpb[:, :], lam[:, 0, :], ALU.mult, [R_t], [R_t])
    k.tt("dve", tmpa[:, :], ai[:, :], lam[:, 1, :], ALU.mult, [R_t], [R_t])
    k.tt("dve", cfr[:, :], cfr[:, :], tmpa[:, :], ALU.add, [R_t], [R_t])
    k.tt("dve", cfr[:, :], cfr[:, :], den[:, :], ALU.mult, [R_t], [R_t])
    k.tt("dve", cfi[:, :], ai[:, :], lam[:, 0, :], ALU.mult, [R_t], [R_t])
    k.tt("dve", tmpa[:, :], tmpb[:, :], lam[:, 1, :], ALU.mult, [R_t], [R_t])
    k.tt("dve", cfi[:, :], cfi[:, :], tmpa[:, :], ALU.subtract, [R_t], [R_t])
    k.tt("dve", cfi[:, :], cfi[:, :], den[:, :], ALU.mult, [R_t], [R_t])
    Wt = sb("s5_Wt", [128, 2, 32, 2, 128], BF16)
    Ct = sb("s5_Ct", [128, 2, 32, 2, 128], BF16)
    with ExitStack() as es2:
        sb2 = lambda name, shape, dt_: es2.enter_context(nc.sbuf_tensor(k.uname(name), shape, dt_))
        Bn = sb2("s5_Bn", [128, 2, 2, 32, 16], F32)
        Bb = sb2("s5_Bb", [128, 2, 2, 32, 16], F32)
        for w, nm in enumerate(("s5_b_re", "s5_b_im")):
            for d in range(2):
                k.dma("sp", Bn[:, w, d, :, :], io[nm][d].rearrange("(q e) p c -> (e p) q c", e=2), [], [R_t])
        t16 = sb2("s5_t16", [128, 64], F32)
        cf3r, cf3i = cfr[:, :], cfi[:, :]
        for ci in range(16):
            yield
            bre = Bn[:, 0, :, :, ci].rearrange("p d q -> p (d q)")
            bim = Bn[:, 1, :, :, ci].rearrange("p d q -> p (d q)")
            ore = Bb[:, 0, :, :, ci].rearrange("p d q -> p (d q)")
            oim = Bb[:, 1, :, :, ci].rearrange("p d q -> p (d q)")
            k.tt("dve", ore, cf3r, bre, ALU.mult, [R_t], [R_t])
            k.tt("dve", t16[:, :], cf3i, bim, ALU.mult, [R_t], [R_t])
            k.tt("dve", ore, ore, t16[:, :], ALU.subtract, [R_t], [R_t])
            k.tt("dve", oim, cf3r, bim, ALU.mult, [R_t], [R_t])
            k.tt("dve", t16[:, :], cf3i, bre, ALU.mult, [R_t], [R_t])
            k.tt("dve", oim, oim, t16[:, :], ALU.add, [R_t], [R_t])
        Nn = sb2("s5_Nn", [128, 4, 2, 16], F32)
        k.memset("dve", Nn[:, :, :, :], 0.0, [R_t])
        for d in range(2):
            for w in range(2):
                for j in range(8):
                    yield
                    k.copy("dve", Nn[0:64, :, 0, :], Bb[0:64, w, d, 4 * j:4 * j + 4, :], [R_t, R_pT], [R_t])
                    k.copy("dve", Nn[64:128, :, 1, :], Bb[64:128, w, d, 4 * j:4 * j + 4, :], [R_t], [R_t])
                    k.tr(pT[:, 0:128], Nn[:, :, :, :].rearrange("p a b c -> p (a b c)"), k.ident[:, :], [R_t, k.R_const], [R_pT])
                    for q4 in range(4):
                        k.ts("dve", Wt[:, d, 4 * j + q4, w, :], pT[:, 0:128], rmask[:, q4:q4 + 1], None, ALU.mult, None,
                             [R_pT, R_t], [R_t])
        Cn = sb2("s5_Cn", [128, 2, 2, 8, 64], F32)
        for w, nm in enumerate(("s5_c_re", "s5_c_im")):
            for d in range(2):
                k.dma("sp", Cn[:, w, d, :, :], io[nm][d].rearrange("(j g8) co p -> (g8 co) j p", g8=8), [], [R_t])
        Cexp = sb2("s5_Cexp", [128, 128], F32)
        for d in range(2):
            for w in range(2):
                for j in range(8):
                    yield
                    for e in range(2):
                        k.ts("dve", Cexp[:, e * 64:(e + 1) * 64], Cn[:, w, d, j, :], emask[:, e:e + 1], None, ALU.mult, None,
                             [R_t, R_pT], [R_t])
                    k.tr(pT[:, 0:128], Cexp[:, :], k.ident[:, :], [R_t, k.R_const], [R_pT])
                    for q4 in range(4):
                        k.stt("dve", Ct[:, d, 4 * j + q4, w, :], pT[:, 0:128], (1.0 if w == 0 else -1.0), cmask[:, q4, :],
                              ALU.mult, ALU.mult, [R_pT, R_t], [R_t])
    Ctab = sb("s5_Ctab", [128, 2, 32, TB], F32)
    Stab = sb("s5_Stab", [128, 2, 32, TB], F32)
    rt = sb("s5_rt", [128, 2, 32, TB], F32)
    w64 = sb("s5_w64", [128, 2, 64], F32)
    cs1, sn1, er, ei_, e2r, e2i = [sm(n) for n in "cs1 sn1 er ei e2r e2i".split()]
    P.op("dve", lambda e: e.reciprocal(out=tmpa[:, :], in_=mag[:, :]), [R_t], [R_t])
    k.tt("dve", cs1[:, :], ar[:, :], tmpa[:, :], ALU.mult, [R_t], [R_t])
    k.tt("dve", sn1[:, :], ai[:, :], tmpa[:, :], ALU.mult, [R_t], [R_t])
    k.memset("dve", er[:, :], 1.0, [R_t])
    k.memset("dve", ei_[:, :], 0.0, [R_t])
    dq = lambda t2d: t2d.rearrange("p (d q) -> p d q", d=2)
    for tp in range(TB):
        yield
        for d in range(2):
            col = tp if d == 0 else TB - 1 - tp
            k.copy("pool", Ctab[:, d, :, col], dq(er[:, :])[:, d, :], [R_t], [R_t])
            k.copy("pool", Stab[:, d, :, col], dq(ei_[:, :])[:, d, :], [R_t], [R_t])
            if tp == 0:
                k.memset("pool", rt[:, d, :, col], 0.0, [R_t])
            else:
                k.copy("pool", rt[:, d, :, col], dq(mag[:, :])[:, d, :], [R_t], [R_t])
        k.tt("dve", e2r[:, :], er[:, :], cs1[:, :], ALU.mult, [R_t], [R_t])
        k.tt("dve", tmpa[:, :], ei_[:, :], sn1[:, :], ALU.mult, [R_t], [R_t])
        k.tt("dve", e2r[:, :], e2r[:, :], tmpa[:, :], ALU.subtract, [R_t], [R_t])
        k.tt("dve", e2i[:, :], er[:, :], sn1[:, :], ALU.mult, [R_t], [R_t])
        k.tt("dve", tmpa[:, :], ei_[:, :], cs1[:, :], ALU.mult, [R_t], [R_t])
        k.tt("dve", ei_[:, :], e2i[:, :], tmpa[:, :], ALU.add, [R_t], [R_t])
        k.copy("dve", er[:, :], e2r[:, :], [R_t], [R_t])
    k.tt("dve", w64[:, 0, :], er[:, :], mag[:, :], ALU.mult, [R_t], [R_t])
    k.tt("dve", w64[:, 1, :], ei_[:, :], mag[:, :], ALU.mult, [R_t], [R_t])
    yield
    R_o = R("s5p_out")
    k.dma("sp", io["s5p_Wt"], Wt[:, :, :, :, :].rearrange("p a b c d -> p (a b c d)"), [R_t], [R_o])
    k.dma("sp", io["s5p_Ct"], Ct[:, :, :, :, :].rearrange("p a b c d -> p (a b c d)"), [R_t], [R_o])
    k.dma("sp", io["s5p_Ctab"], Ctab[:, :, :, :].rearrange("p a b c -> p (a b c)"), [R_t], [R_o])
    k.dma("sp", io["s5p_Stab"], Stab[:, :, :, :].rearrange("p a b c -> p (a b c)"), [R_t], [R_o])
    k.dma("sp", io["s5p_rt"], rt[:, :, :, :].rearrange("p a b c -> p (a b c)"), [R_t], [R_o])
    k.dma("sp", io["s5p_w64"], w64[:, :, :].rearrange("p a b -> p (a b)"), [R_t], [R_o])
    yield

def s5_mixer(k, io, li):
    nc, P = k.nc, k.P
    from contextlib import ExitStack
    R = lambda n: Res(n)
    PI = math.pi
    with ExitStack() as es:
        sb = lambda name, shape, dt: es.enter_context(nc.sbuf_tensor(k.uname(name), shape, dt))
        ps = lambda name, shape, dt=F32: es.enter_context(nc.psum_tensor(k.uname(name), shape, dt))
        R_t = R("s5tab")
        R_uTd = R("s5_uTd")
        with ExitStack() as es0:
            sb0 = lambda name, shape, dt: es0.enter_context(nc.sbuf_tensor(k.uname(name), shape, dt))
            ps0 = lambda name, shape, dt=F32: es0.enter_context(nc.psum_tensor(k.uname(name), shape, dt))
            modnorm = mk_modnorm(k, io, sb0, ps0)
            uT = sb0("s5_uT", [128, 8, 512], BF16)
            R_uT = R("s5_uT")
            for ti in range(len(TILES)):
                t0, N, _ = TILES[ti]
                modnorm(ti, li, 1, lambda j: uT[:, j, 0:N], R_uT)
                k.dma("sp", io["uT_d"][:, t0:t0 + N].rearrange("(c p) t -> p c t", p=128), uT[:, :, 0:N], [R_uT], [R_uTd])
        P.barrier()
        Wt = sb("s5_Wt", [128, 2, 32, 2, 128], BF16)
        Ct = sb("s5_Ct", [128, 2, 32, 2, 128], BF16)
        Ctab = sb("s5_Ctab", [128, 2, 32, TB], F32)
        Stab = sb("s5_Stab", [128, 2, 32, TB], F32)
        rt = sb("s5_rt", [128, 2, 32, TB], F32)
        w64 = sb("s5_w64", [128, 2, 64], F32)
        k.dma("sp", Wt[:, :, :, :, :].rearrange("p a b c d -> p (a b c d)"), io["s5p_Wt"], [], [R_t])
        k.dma("act", Ct[:, :, :, :, :].rearrange("p a b c d -> p (a b c d)"), io["s5p_Ct"], [], [R_t])
        k.dma("sp", Ctab[:, :, :, :].rearrange("p a b c -> p (a b c)"), io["s5p_Ctab"], [], [R_t])
        k.dma("act", Stab[:, :, :, :].rearrange("p a b c -> p (a b c)"), io["s5p_Stab"], [], [R_t])
        k.dma("sp", rt[:, :, :, :].rearrange("p a b c -> p (a b c)"), io["s5p_rt"], [], [R_t])
        k.dma("sp", w64[:, :, :].rearrange("p a b -> p (a b)"), io["s5p_w64"], [], [R_t])
        dq = lambda t2d: t2d.rearrange("p (d q) -> p d q", d=2)
        fl = lambda t3: t3.rearrange("p q t -> p (q t)")
        fl4 = lambda t4: t4.rearrange("p d q t -> p (d q t)")
        ub = [sb("s5_ub%d" % d, [128, 8, TB], BF16) for d in range(2)]
        Vr = sb("s5_Vr", [128, 2, 32, TB], F32)
        Vi = sb("s5_Vi", [128, 2, 32, TB], F32)
        T1 = sb("s5_T1", [128, 2, 32, TB], F32)
        T2 = sb("s5_T2", [128, 2, 32, TB], F32)
        Xb = sb("s5_Xb", [128, 2, 32, TB], BF16)
        zc = sb("s5_zc", [128, 2, 2, 32], F32)
        cw = sb("s5_cw", [128, 4, 2, 32], F32)
        ysb = sb("s5_ysb", [128, 8, TB], F32)
        pv = [[ps("s5_pv%d_%d" % (d, i), [128, 512]) for i in range(2)] for d in range(2)]
        pyy = [ps("s5_py%d" % d, [128, 512]) for d in range(2)]
        R_ub = [R("ub0"), R("ub1")]
        R_Xb, R_zc, R_cw, R_ys = [R(n) for n in "Xb zc cw ys".split()]
        R_Vr, R_Vi, R_T1, R_T2 = [[R(n + str(d)) for d in range(2)] for n in ("Vr", "Vi", "T1", "T2")]
        R_pv = [[R("pv%d%d" % (d, i)) for i in range(2)] for d in range(2)]
        R_pyy = [R("pyy0"), R("pyy1")]
        R_yd = [R("yd0"), R("yd1")]
        k.memset("dve", zc[:, :, :, :], 0.0, [R_zc])
        order = [list(range(NBLK)), [3, 2, 1, 0] + list(range(NBLK - 1, 3, -1))]
        C4, S4 = fl4(Ctab[:, :, :, :]), fl4(Stab[:, :, :, :])
        vr, vi, t1, t2 = fl4(Vr[:, :, :, :]), fl4(Vi[:, :, :, :]), fl4(T1[:, :, :, :]), fl4(T2[:, :, :, :])
        w4 = w64[:, :, :].rearrange("p r (d q) -> p r d q", d=2)
        FIRST = (0, TB - 1)
        LAST = (TB - 1, 0)
        nev = [0, 0]

        def emit_V(it, d):
            c0 = order[d][it] * TB
            k.dma("sp", ub[d][:, :, :], io["uT_d"][:, c0:c0 + TB].rearrange("(c p) t -> p c t", p=128), [R_uTd], [R_ub[d]])
            for w, (Vd, Rv) in enumerate(((Vr, R_Vr[d]), (Vi, R_Vi[d]))):
                for q8 in range(4):
                    bk = nev[d] % 2
                    nev[d] += 1
                    for qq in range(8):
                        q = q8 * 8 + qq
                        k.mm(pv[d][bk][:, qq * TB:(qq + 1) * TB], Wt[:, d, q, w, :], ub[d][:, q // 4, :], True, True,
                             [R_t, R_ub[d]], [R_pv[d][bk]], signal=(qq == 7))
                    k.copy("act", Vd[:, d, q8 * 8:(q8 + 1) * 8, :],
                           pv[d][bk][:, :].rearrange("p (a b) -> p a b", b=TB), [R_pv[d][bk]], [Rv])

        for d in range(2):
            emit_V(0, d)
        for it in range(NBLK):
            for d in range(2):
                Cd, Sd = Ctab[:, d, :, :], Stab[:, d, :, :]
                k.tt("dve", T1[:, d, :, :], Vr[:, d, :, :], Cd, ALU.mult, [R_Vr[d], R_t], [R_T1[d]])
                k.tt("dve", T2[:, d, :, :], Vi[:, d, :, :], Sd, ALU.mult, [R_Vi[d], R_t], [R_T2[d]])
                k.tt("dve", Vr[:, d, :, :], Vr[:, d, :, :], Sd, ALU.mult, [R_Vr[d], R_t], [R_Vr[d]])
                k.tt("dve", Vi[:, d, :, :], Vi[:, d, :, :], Cd, ALU.mult, [R_Vi[d], R_t], [R_Vi[d]])
                k.tt("dve", T1[:, d, :, :], T1[:, d, :, :], T2[:, d, :, :], ALU.add, [R_T1[d], R_T2[d]], [R_T1[d]])
                k.tt("dve", Vi[:, d, :, :], Vi[:, d, :, :], Vr[:, d, :, :], ALU.subtract, [R_Vi[d], R_Vr[d]], [R_Vi[d]])
            k.tt("dve", cw[:, 0, :, :], w4[:, 0, :, :], zc[:, 0, :, :], ALU.mult, [R_zc, R_t], [R_cw])
            k.tt("dve", cw[:, 1, :, :], w4[:, 1, :, :], zc[:, 1, :, :], ALU.mult, [R_zc, R_t], [R_cw])
            k.tt("dve", cw[:, 2, :, :], w4[:, 1, :, :], zc[:, 0, :, :], ALU.mult, [R_zc, R_t], [R_cw])
            k.tt("dve", cw[:, 3, :, :], w4[:, 0, :, :], zc[:, 1, :, :], ALU.mult, [R_zc, R_t], [R_cw])
            k.tt("dve", cw[:, 0, :, :], cw[:, 0, :, :], cw[:, 1, :, :], ALU.subtract, [R_cw], [R_cw])
            k.tt("dve", cw[:, 2, :, :], cw[:, 2, :, :], cw[:, 3, :, :], ALU.add, [R_cw], [R_cw])
            for d in range(2):
                k.tt("dve", T1[:, d, :, FIRST[d]], T1[:, d, :, FIRST[d]], cw[:, 0, d, :], ALU.add, [R_cw, R_T1[d]], [R_T1[d]])
                k.tt("dve", Vi[:, d, :, FIRST[d]], Vi[:, d, :, FIRST[d]], cw[:, 2, d, :], ALU.add, [R_cw, R_Vi[d]], [R_Vi[d]])
            for d in range(2):
                rv = (lambda a_: a_) if d == 0 else (lambda a_: a_[:, ::-1])
                rt2 = fl(rt[:, d, :, :])
                for (o_, i_, Ro, Ri) in ((T2, T1, R_T2[d], R_T1[d]), (Vr, Vi, R_Vr[d], R_Vi[d])):
                    P.op("dve", (lambda o, a0, a1: (lambda e: e.tensor_tensor_scan(out=o, data0=a0, data1=a1, initial=0.0,
                                                                                     op0=ALU.mult, op1=ALU.add)))(
                        rv(fl(o_[:, d, :, :])), rv(rt2), rv(fl(i_[:, d, :, :]))), [Ri, R_t], [Ro])
            for d in range(2):
                k.copy("dve", zc[:, 0, d, :], T2[:, d, :, LAST[d]], [R_T2[d]], [R_zc])
                k.copy("dve", zc[:, 1, d, :], Vr[:, d, :, LAST[d]], [R_Vr[d]], [R_zc])
            k.tt("dve", t1, t2, C4, ALU.mult, R_T2 + [R_t], R_T1)
            k.tt("dve", vi, vr, S4, ALU.mult, R_Vr + [R_t], R_Vi)
            k.tt("dve", t2, t2, S4, ALU.mult, R_T2 + [R_t], R_T2)
            k.tt("dve", vr, vr, C4, ALU.mult, R_Vr + [R_t], R_Vr)
            for d in range(2):
                c0 = order[d][it] * TB
                k.tt("dve", Xb[:, 0, :, :], T1[:, d, :, :], Vi[:, d, :, :], ALU.subtract, [R_T1[d], R_Vi[d]], [R_Xb])
                k.tt("dve", Xb[:, 1, :, :], Vr[:, d, :, :], T2[:, d, :, :], ALU.add, [R_Vr[d], R_T2[d]], [R_Xb])
                for j in range(8):
                    n_ = 0
                    for q4 in range(4):
                        for w in range(2):
                            k.mm(pyy[d][:, j * TB:(j + 1) * TB], Ct[:, d, 4 * j + q4, w, :], Xb[:, w, 4 * j + q4, :], n_ == 0, n_ == 7,
                                 [R_t, R_Xb], [R_pyy[d]], signal=(n_ == 7))
                            n_ += 1
                if it + 1 < NBLK:
                    emit_V(it + 1, d)
                k.copy("act", ysb[:, :, :], pyy[d][:, :].rearrange("p (a b) -> p a b", b=TB), [R_pyy[d]], [R_ys])
                k.dma("sp", io["yf_d"][d, :, c0:c0 + TB].rearrange("(c p) t -> p c t", p=128), ysb[:, :, :], [R_ys], [R_yd[d]])
    P.barrier()
    with ExitStack() as es:
        sb = lambda name, shape, dt: es.enter_context(nc.sbuf_tensor(k.uname(name), shape, dt))
        ps = lambda name, shape, dt=F32: es.enter_context(nc.psum_tensor(k.uname(name), shape, dt))
        resadd = mk_resadd(k, io, sb, li)
        wgl = sb("s5_wgl", [128, 8, 2 * D], BF16)
        R_wgl = R("wgl")
        for hh in range(4):
            k.dma("pool", wgl[:, :, hh * 512:(hh + 1) * 512],
                  io["s5_w_glu"].rearrange("(c p) n -> p c n", p=128)[:, :, hh * 512:(hh + 1) * 512], [], [R_wgl])
        yf = sb("s5_yf", [128, 8, 512], F32)
        yb = sb("s5_yb", [128, 8, 512], F32)
        uu = sb("s5_uu", [128, 8, 512], BF16)
        ge = sb("s5_ge", [128, 8, 512], BF16)
        w1 = sb("s5_w1", [128, 512], F32)
        w2 = sb("s5_w2", [128, 512], F32)
        sig = sb("s5_sig", [128, 512], F32)
        oo = sb("s5_oo", [128, 512], F32)
        R_yf, R_yb, R_uu, R_ge, R_w1, R_w2, R_sig, R_oo = [R(n) for n in "yf yb uu ge w1 w2 sig oo".split()]
        pa = ps("s5_pa", [128, 512])
        pb = ps("s5_pb", [128, 512])
        R_pa, R_pb = R("s5pa"), R("s5pb")
        for ti in range(len(TILES)):
            t0, N, isctx = TILES[ti]
            k.dma("sp", yf[:, :, 0:N], io["yf_d"][0, :, t0:t0 + N].rearrange("(c p) t -> p c t", p=128), [], [R_yf])
            k.dma("sp", yb[:, :, 0:N], io["yf_d"][1, :, t0:t0 + N].rearrange("(c p) t -> p c t", p=128), [], [R_yb])
            k.dma("sp", uu[:, :, 0:N], io["uT_d"][:, t0:t0 + N].rearrange("(c p) t -> p c t", p=128), [], [R_uu])
            for j in range(8):
                k.tt("dve", w1[:, 0:N], yf[:, j, 0:N], yb[:, j, 0:N], ALU.add, [R_yf, R_yb], [R_w1])
                k.stt("dve", w1[:, 0:N], uu[:, j, 0:N], k.vecT[:, VC_S5D + j:VC_S5D + j + 1], w1[:, 0:N], ALU.mult, ALU.add,
                      [R_uu, R_w1, k.R_vec], [R_w1])
                k.tt("dve", w2[:, 0:N], w1[:, 0:N], w1[:, 0:N], ALU.mult, [R_w1], [R_w2])
                k.ts("dve", w2[:, 0:N], w2[:, 0:N], 0.044715, 1.0, ALU.mult, ALU.add, [R_w2], [R_w2])
                k.tt("dve", w2[:, 0:N], w2[:, 0:N], w1[:, 0:N], ALU.mult, [R_w2, R_w1], [R_w2])
                k.act(w2[:, 0:N], w2[:, 0:N], AF.Sigmoid, [R_w2], [R_w2], scale=1.5957691216057308)
                k.tt("dve", ge[:, j, 0:N], w2[:, 0:N], w1[:, 0:N], ALU.mult, [R_w2, R_w1], [R_ge])
            for oc in range(8):
                for kc in range(8):
                    k.mm(pa[:, 0:N], wgl[:, kc, oc * 128:(oc + 1) * 128], ge[:, kc, 0:N], kc == 0, kc == 7, [R_wgl, R_ge], [R_pa],
                         signal=(kc == 7))
                for kc in range(8):
                    k.mm(pb[:, 0:N], wgl[:, kc, D + oc * 128:D + (oc + 1) * 128], ge[:, kc, 0:N], kc == 0, kc == 7, [R_wgl, R_ge],
                         [R_pb], signal=(kc == 7))
                k.act(sig[:, 0:N], pb[:, 0:N], AF.Sigmoid, [R_pb], [R_sig])
                k.tt("dve", oo[:, 0:N], pa[:, 0:N], sig[:, 0:N], ALU.mult, [R_pa, R_sig], [R_oo])
                resadd(ti, oc, 0, N, oo[:, 0:N], R_oo)
    P.barrier()


MIXERS[2] = s5_mixer
```

```python
import math
import numpy as np
import concourse.bass as bass
import concourse.mybir as mybir
from concourse.bass_utils import run_bass_kernel_spmd

F32 = mybir.dt.float32
BF16 = mybir.dt.bfloat16
AF = mybir.ActivationFunctionType
ALU = mybir.AluOpType
AX = mybir.AxisListType


class Res:
    __slots__ = ("name", "w", "r")

    def __init__(self, name):
        self.name = name
        self.w = None
        self.r = {}


class Prog:
    ENG = ("pe", "act", "dve", "pool", "sp")

    def __init__(self, nc, n_dma=40, same_sync=True):
        self.nc = nc
        self.items = {e: [] for e in self.ENG}
        self.cnt = {e: 0 for e in self.ENG}
        self.sems = {}
        for e in ("pe", "act", "dve", "pool"):
            self.sems[e] = nc.alloc_semaphore(name="s_" + e)
        self.n_dma = n_dma
        for i in range(n_dma):
            self.sems[("d", i)] = nc.alloc_semaphore(name="d%d" % i)
        self.dma_cum = [0] * n_dma
        q = n_dma // 5
        self.dma_pool = {"sp": (0, 2 * q), "pool": (2 * q, 4 * q), "act": (4 * q, n_dma)}
        self.dma_pi = {"sp": 0, "pool": 0, "act": 0}
        self.waited = {e: {} for e in self.ENG}
        self.same_sync = same_sync
        self.nops = 0

    def _need(self, eng, tok):
        if tok is None:
            return
        key, val = tok
        if val <= 0:
            return
        if key == eng and (eng == "pe" or (not self.same_sync and eng != "pool")):
            return
        if key == eng and val > self.cnt[eng]:
            return
        if self.waited[eng].get(key, 0) >= val:
            return
        self.waited[eng][key] = val
        self.items[eng].append(("wait", key, val))

    def _deps(self, eng, reads, writes):
        for r in reads:
            self._need(eng, r.w)
        for w in writes:
            self._need(eng, w.w)
            for k, v in w.r.items():
                self._need(eng, (k, v))

    def _mark(self, tok, reads, writes):
        k, v = tok
        for r in reads:
            if r.r.get(k, 0) < v:
                r.r[k] = v
        for w in writes:
            w.w = tok
            w.r = {}

    def op(self, eng, fn, reads=(), writes=(), signal=True):
        self._deps(eng, reads, writes)
        tok = (eng, self.cnt[eng] + 1)
        if signal:
            self.cnt[eng] += 1
            self.items[eng].append(("op", fn, eng, 1))
        else:
            self.items[eng].append(("op", fn, None, 0))
        self._mark(tok, reads, writes)
        self.nops += 1

    def dma(self, eng, out, in_, reads=(), writes=(), **kw):
        lo, hi = self.dma_pool[eng]
        s = lo + self.dma_pi[eng] % (hi - lo)
        self.dma_pi[eng] += 1
        key = ("d", s)
        self._need(eng, (key, self.dma_cum[s]))
        self._deps(eng, reads, writes)
        self.dma_cum[s] += 16
        tok = (key, self.dma_cum[s])
        self.items[eng].append(("op", lambda e: e.dma_start(out=out, in_=in_, **kw), key, 16))
        self._mark(tok, reads, writes)
        self.nops += 1

    def finish(self, eng="sp"):
        for s in range(self.n_dma):
            self._need(eng, (("d", s), self.dma_cum[s]))
        for e in ("pe", "act", "dve", "pool"):
            self._need(eng, (e, self.cnt[e]))

    def emit(self):
        nc = self.nc
        sems = self.sems

        def replay(name):
            def f(eng):
                for it in self.items[name]:
                    if it[0] == "wait":
                        eng.wait_ge(sems[it[1]], it[2])
                    else:
                        ins = it[1](eng)
                        if it[2] is not None:
                            ins.then_inc(sems[it[2]], it[3])
            return f

        with nc.Block() as block:
            block.tensor(replay("pe"))
            block.scalar(replay("act"))
            block.vector(replay("dve"))
            block.gpsimd(replay("pool"))
            block.sync(replay("sp"))


D = 1024
NCH = 8
SEQ = 4096
CTX = 256
T = SEQ + CTX
DEPTH = 4
NMOD = 9
DFF = 2816
FCH = DFF // 128
EPS = 1e-6
TILES = [(0, 256, True)] + [(256 + 512 * i, 512, False) for i in range(8)]


class K:
    def __init__(self, nc, P):
        self.nc = nc
        self.P = P
        self._uid = 0

    def uname(self, name):
        self._uid += 1
        return "%s_%d" % (name, self._uid)

    def mm(self, out, lhsT, rhs, start, stop, reads, writes, signal=True):
        self.P.op("pe", lambda e: e.matmul(out, lhsT, rhs, start=start, stop=stop), reads, writes, signal)

    def tr(self, out, in_, ident, reads, writes):
        self.P.op("pe", lambda e: e.transpose(out, in_, ident), reads, writes)

    def act(self, out, in_, func, reads, writes, bias=None, scale=None):
        kw = {}
        if bias is not None:
            kw["bias"] = bias
        if scale is not None:
            kw["scale"] = scale
        self.P.op("act", lambda e: e.activation(out=out, in_=in_, func=func, **kw), reads, writes)

    def tt(self, eng, out, in0, in1, op, reads, writes):
        self.P.op(eng, lambda e: e.tensor_tensor(out=out, in0=in0, in1=in1, op=op), reads, writes)

    def ts(self, eng, out, in0, s1, s2, op0, op1, reads, writes):
        if op1 is None:
            self.P.op(eng, lambda e: e.tensor_single_scalar(out=out, in_=in0, scalar=s1, op=op0), reads, writes)
        else:
            self.P.op(eng, lambda e: e.tensor_scalar(out=out, in0=in0, scalar1=s1, scalar2=s2, op0=op0, op1=op1),
                      reads, writes)

    def stt(self, eng, out, in0, scalar, in1, op0, op1, reads, writes):
        self.P.op(eng, lambda e: e.scalar_tensor_tensor(out=out, in0=in0, scalar=scalar, in1=in1, op0=op0, op1=op1),
                  reads, writes)

    def copy(self, eng, out, in_, reads, writes):
        if eng == "act":
            self.P.op(eng, lambda e: e.copy(out=out, in_=in_), reads, writes)
        else:
            self.P.op(eng, lambda e: e.tensor_copy(out=out, in_=in_), reads, writes)

    def memset(self, eng, ap, val, writes):
        self.P.op(eng, lambda e: e.memset(ap, val), (), writes)

    def dma(self, eng, out, in_, reads, writes, **kw):
        self.P.dma(eng, out, in_, reads, writes, **kw)


def _barrier(P):
    for e in P.ENG:
        for s in range(P.n_dma):
            P._need(e, (("d", s), P.dma_cum[s]))
        for e2 in ("pe", "act", "dve", "pool"):
            if e2 != e:
                P._need(e, (e2, P.cnt[e2]))


Prog.barrier = _barrier

VC_C, VC_CC, VC_FNG, VC_S5D, VC_NG, VC_BM = 0, 8, 16, 24, 32, 128
NVEC = 416


def declare_io(nc):
    io = {}

    def inp(name, shape):
        io[name] = nc.dram_tensor(name, list(shape), F32, kind="ExternalInput").ap()

    inp("x", [SEQ, D])
    inp("c", [8, 128])
    inp("ctx", [CTX, D])
    inp("c_ctx", [8, 128])
    inp("w_mod", [DEPTH, D, NMOD * D])
    inp("b_mod", [288, 128])
    inp("norm_g", [96, 128])
    inp("ffn_w_in", [DEPTH, 2, D, 2 * DFF])
    inp("ffn_w_out", [DEPTH, 2, DFF, D])
    inp("fnet_w_out", [2, D, D])
    inp("ret_w_in", [D, 6144])
    inp("ret_w_out", [2048, D])
    inp("ret_decay_logit", [2, 4])
    inp("s5_lam_re", [2, 64, 64])
    inp("s5_lam_im", [2, 64, 64])
    inp("s5_log_dt", [2, 64])
    inp("s5_b_re", [2, 64, 64, 16])
    inp("s5_b_im", [2, 64, 64, 16])
    inp("s5_c_re", [2, 64, 16, 64])
    inp("s5_c_im", [2, 64, 16, 64])
    inp("s5_d", [8, 128])
    inp("s5_w_glu", [D, 2 * D])
    inp("final_norm_g", [8, 128])
    io["out"] = nc.dram_tensor("out", [SEQ, D], F32, kind="ExternalOutput").ap()
    io["hT"] = nc.dram_tensor("hT", [D, T], F32).ap()
    declare_fnet_consts(nc, io)
    declare_ret_consts(nc, io)
    declare_s5_consts(nc, io)
    io["uT_d"] = nc.dram_tensor("uT_d", [D, T], BF16).ap()
    return io


def alloc_consts(k, es):
    nc = k.nc
    sb = lambda name, shape, dt: es.enter_context(nc.sbuf_tensor(k.uname(name), shape, dt))
    k.ident = sb("ident", [128, 128], F32)
    k.ones_b = sb("ones_b", [128, 128], BF16)
    k.vecT = sb("vecT", [128, NVEC], F32)
    k.sc = sb("sc", [128, 8, 2], BF16)
    k.mod = sb("mod", [128, DEPTH, 72, 2], F32)
    k.A = sb("Asc", [128, DEPTH, 3, 8, 2], F32)
    k.B = sb("Bsh", [128, DEPTH, 3, 8, 2], F32)
    k.G = sb("Ggt", [128, DEPTH, 3, 8, 2], F32)
    k.R_const = Res("const")
    k.R_vec = Res("vecT")
    k.R_mod = Res("mod")
    k.R_hT = [[Res("hT%d_%d" % (i, j)) for j in range(NCH)] for i in range(len(TILES))]
    P = k.P
    k.memset("pool", k.ident[:, :], 0.0, [k.R_const])
    P.op("pool", lambda e: e.affine_select(out=k.ident[:, :], in_=k.ident[:, :], pattern=[[-1, 128]],
                                            compare_op=ALU.not_equal, fill=1.0, base=0, channel_multiplier=1),
         [k.R_const], [k.R_const])
    k.memset("dve", k.ones_b[:, :], 1.0, [k.R_const])


def prologue(k, io):
    nc, P = k.nc, k.P
    from contextlib import ExitStack
    with ExitStack() as es:
        sb = lambda name, shape, dt: es.enter_context(nc.sbuf_tensor(k.uname(name), shape, dt))
        ps = lambda name, shape: es.enter_context(nc.psum_tensor(k.uname(name), shape, F32))
        stg = [sb("vstg%d" % i, [128, 128], F32) for i in range(4)]
        R_stg = [Res("vstg%d" % i) for i in range(4)]
        tp = [ps("tp%d" % i, [128, 512]) for i in range(2)]
        R_tp = [Res("tp%d" % i) for i in range(2)]
        rows0 = [(io["c"], 8, 0), (io["c_ctx"], 8, 8), (io["final_norm_g"], 8, 16), (io["s5_d"], 8, 24),
                 (io["norm_g"], 96, 32)]
        for ap, n, r0 in rows0:
            k.dma("sp", stg[0][r0:r0 + n, :], ap, [], [R_stg[0]])
        for b, (r0, n) in enumerate([(0, 128), (128, 128), (256, 32)]):
            k.dma("sp", stg[b + 1][0:n, :], io["b_mod"][r0:r0 + n, :], [], [R_stg[b + 1]])
        col = 0
        for b, n in enumerate([128, 128, 128, 32]):
            k.tr(tp[b % 2][:, 0:n], stg[b][0:n, :], k.ident[0:n, 0:n], [R_stg[b], k.R_const], [R_tp[b % 2]])
            k.copy("dve", k.vecT[:, col:col + n], tp[b % 2][:, 0:n], [R_tp[b % 2]], [k.R_vec])
            col += n
        k.act(k.sc[:, :, 0], k.vecT[:, VC_C:VC_C + 8], AF.Silu, [k.R_vec], [k.R_mod])
        k.act(k.sc[:, :, 1], k.vecT[:, VC_CC:VC_CC + 8], AF.Silu, [k.R_vec], [k.R_mod])

        wblk = [sb("wmodblk%d" % i, [128, 8, 1024], BF16) for i in range(2)]
        R_wblk = [Res("wmodblk%d" % i) for i in range(2)]
        prep = s5_prep_gen(k, io, sb, ps)
        mps = [ps("modps%d" % i, [128, 72, 2]) for i in range(2)]
        R_mps = [Res("modps%d" % i) for i in range(2)]
        nb = 0
        for li in range(DEPTH):
            wv = io["w_mod"][li].rearrange("(c p) n -> p c n", p=128)
            for kk in range(NMOD):
                b = nb % 2
                nb += 1
                for _ in range(5):
                    next(prep, None)
                for h2 in range(2):
                    k.dma("pool", wblk[b][:, h2 * 4:(h2 + 1) * 4, :], wv[:, h2 * 4:(h2 + 1) * 4, kk * 1024:(kk + 1) * 1024],
                          [], [R_wblk[b]])
                for j in range(8):
                    for kc in range(8):
                        k.mm(mps[li % 2][:, kk * 8 + j, :], wblk[b][:, kc, j * 128:(j + 1) * 128], k.sc[:, kc, :],
                             kc == 0, kc == 7, [R_wblk[b], k.R_mod], [R_mps[li % 2]], signal=(kc == 7))
            for r in range(2):
                k.tt("dve", k.mod[:, li, :, r], mps[li % 2][:, :, r], k.vecT[:, VC_BM + li * 72:VC_BM + (li + 1) * 72],
                     ALU.add, [R_mps[li % 2], k.R_vec], [k.R_mod])
            for s in range(3):
                gcol = VC_NG + (li * 3 + s) * 8
                for r in range(2):
                    sh = k.mod[:, li, (3 * s) * 8:(3 * s) * 8 + 8, r]
                    scl = k.mod[:, li, (3 * s + 1) * 8:(3 * s + 1) * 8 + 8, r]
                    gt = k.mod[:, li, (3 * s + 2) * 8:(3 * s + 2) * 8 + 8, r]
                    k.stt("dve", k.A[:, li, s, :, r], scl, 1.0, k.vecT[:, gcol:gcol + 8], ALU.add, ALU.mult,
                          [k.R_mod, k.R_vec], [k.R_mod])
                    k.copy("dve", k.B[:, li, s, :, r], sh, [k.R_mod], [k.R_mod])
                    if s == 1:
                        k.copy("dve", k.G[:, li, s, :, r], gt, [k.R_mod], [k.R_mod])
                    else:
                        k.ts("dve", k.G[:, li, s, :, r], gt, 0.5, None, ALU.mult, None, [k.R_mod], [k.R_mod])

        for _ in prep:
            pass
    P.barrier()
    with ExitStack() as es:
        sb = lambda name, shape, dt: es.enter_context(nc.sbuf_tensor(k.uname(name), shape, dt))
        ps = lambda name, shape: es.enter_context(nc.psum_tensor(k.uname(name), shape, F32))
        tp = [ps("tpx%d" % i, [128, 512]) for i in range(2)]
        R_tp = [Res("tpx%d" % i) for i in range(2)]
        xs = [sb("xs%d" % i, [128, 4, D], F32) for i in range(2)]
        R_xs = [Res("xs%d" % i) for i in range(2)]
        hst = [sb("hst%d" % i, [128, 8, 512], F32) for i in range(2)]
        R_hst = [Res("hst%d" % i) for i in range(2)]
        ncp = 0
        for ti, (t0, N, isctx) in enumerate(TILES):
            b = ti % 2
            ns = N // 128
            src = io["ctx"] if isctx else io["x"][t0 - CTX:t0 - CTX + N, :]
            k.dma("sp", xs[b][:, 0:ns, :], src.rearrange("(s p) d -> p s d", p=128), [], [R_xs[b]])
            for j in range(8):
                pb = ncp % 2
                for s in range(ns):
                    k.tr(tp[pb][:, s * 128:(s + 1) * 128], xs[b][:, s, j * 128:(j + 1) * 128], k.ident[:, :],
                         [R_xs[b], k.R_const], [R_tp[pb]])
                k.copy("act" if ncp % 2 else "dve", hst[b][:, j, 0:N], tp[pb][:, 0:N], [R_tp[pb]], [R_hst[b]])
                ncp += 1
            k.dma("sp", io["hT"][:, t0:t0 + N].rearrange("(c p) t -> p c t", p=128), hst[b][:, :, 0:N],
                  [R_hst[b]], k.R_hT[ti])
    P.barrier()


def ffn_phase(k, io, li, half):
    nc, P = k.nc, k.P
    s = 0 if half == 0 else 2
    from contextlib import ExitStack
    with ExitStack() as es:
        sb = lambda name, shape, dt: es.enter_context(nc.sbuf_tensor(k.uname(name), shape, dt))
        ps = lambda name, shape: es.enter_context(nc.psum_tensor(k.uname(name), shape, F32))
        w_in = sb("w_in", [128, 8, 2 * DFF], BF16)
        w_out = sb("w_out", [128, FCH, D], BF16)
        hbuf = sb("hbuf", [128, 8, 512], F32)
        uT = sb("uT", [128, 8, 512], BF16)
        gT = sb("gT", [128, FCH, 512], BF16)
        rstd = sb("rstd", [128, 512], F32)
        sqb = [sb("sqb%d" % i, [128, 512], BF16) for i in range(2)]
        tmp = [sb("tmp%d" % i, [128, 512], F32) for i in range(2)]
        sg = [sb("sg%d" % i, [128, 512], F32) for i in range(2)]
        hres = [sb("hres%d" % i, [128, 512], F32) for i in range(3)]
        pa = [ps("pa%d" % i, [128, 512]) for i in range(2)]
        pb = [ps("pb%d" % i, [128, 512]) for i in range(2)]
        py = [ps("py%d" % i, [128, 512]) for i in range(2)]
        pss = ps("pss", [128, 512])
        NWB = 11
        R_win = [Res("w_in%d" % i) for i in range(2 * NWB)]
        R_wout = [Res("w_out%d" % i) for i in range(2)]
        R_hbuf, R_uT, R_rstd, R_pss = Res("hbuf"), Res("uT"), Res("rstd"), Res("pss")
        R_gT = [Res("gT%d" % i) for i in range(FCH)]
        R_sqb = [Res("sqb%d" % i) for i in range(2)]
        R_tmp = [Res("tmp%d" % i) for i in range(2)]
        R_sg = [Res("sg%d" % i) for i in range(2)]
        R_hres = [Res("hres%d" % i) for i in range(3)]
        R_pa = [Res("pa%d" % i) for i in range(2)]
        R_pb = [Res("pb%d" % i) for i in range(2)]
        R_py = [Res("py%d" % i) for i in range(2)]

        wi = io["ffn_w_in"][li, half].rearrange("(c p) n -> p c n", p=128)
        wo = io["ffn_w_out"][li, half].rearrange("(c p) n -> p c n", p=128)
        for blk in range(NWB):
            for ab in range(2):
                c0 = ab * DFF + blk * 256
                k.dma("pool", w_in[:, :, c0:c0 + 256], wi[:, :, c0:c0 + 256], [], [R_win[ab * NWB + blk]])
        for hh in range(2):
            k.dma("pool", w_out[:, hh * 11:(hh + 1) * 11, :], wo[:, hh * 11:(hh + 1) * 11, :], [], [R_wout[hh]])

        def A1(ti):
            t0, N, isctx = TILES[ti]
            k.dma("sp", hbuf[:, :, 0:N], io["hT"][:, t0:t0 + N].rearrange("(c p) t -> p c t", p=128),
                  k.R_hT[ti], [R_hbuf])
            for j in range(8):
                k.act(sqb[j % 2][:, 0:N], hbuf[:, j, 0:N], AF.Square, [R_hbuf], [R_sqb[j % 2]])
                k.mm(pss[:, 0:N], k.ones_b[:, :], sqb[j % 2][:, 0:N], j == 0, j == 7, [R_sqb[j % 2], k.R_const], [R_pss])
            k.ts("dve", rstd[:, 0:N], pss[:, 0:N], 1.0 / D, EPS, ALU.mult, ALU.add, [R_pss], [R_rstd])
            k.act(rstd[:, 0:N], rstd[:, 0:N], AF.Sqrt, [R_rstd], [R_rstd])
            P.op("dve", (lambda o, i: (lambda e: e.reciprocal(out=o, in_=i)))(rstd[:, 0:N], rstd[:, 0:N]), [R_rstd], [R_rstd])

        def A2(ti):
            t0, N, isctx = TILES[ti]
            r = 1 if isctx else 0
            for j in range(8):
                k.stt("dve", tmp[j % 2][:, 0:N], hbuf[:, j, 0:N], k.A[:, li, s, j, r:r + 1], rstd[:, 0:N], ALU.mult, ALU.mult,
                      [R_hbuf, R_rstd, k.R_mod], [R_tmp[j % 2]])
                k.act(uT[:, j, 0:N], tmp[j % 2][:, 0:N], AF.Identity, [R_tmp[j % 2], k.R_mod], [R_uT],
                      bias=k.B[:, li, s, j, r:r + 1])

        def Bst(ti, c_lo, c_hi):
            t0, N, isctx = TILES[ti]
            for c in range(c_lo, c_hi):
                bk = c % 2
                for ab, pp, Rp in ((0, pa, R_pa), (1, pb, R_pb)):
                    col = ab * DFF + c * 128
                    Rw = R_win[ab * NWB + c // 2]
                    for kc in range(8):
                        k.mm(pp[bk][:, 0:N], w_in[:, kc, col:col + 128], uT[:, kc, 0:N], kc == 0, kc == 7,
                             [Rw, R_uT], [Rp[bk]], signal=(kc == 7))
                k.act(sg[bk][:, 0:N], pa[bk][:, 0:N], AF.Silu, [R_pa[bk]], [R_sg[bk]])
                k.tt("dve", gT[:, c, 0:N], sg[bk][:, 0:N], pb[bk][:, 0:N], ALU.mult, [R_sg[bk], R_pb[bk]], [R_gT[c]])

        def Cst(ti):
            t0, N, isctx = TILES[ti]
            r = 1 if isctx else 0
            for j in range(8):
                bk = j % 2
                hb = j % 3
                k.dma("sp", hres[hb][:, 0:N], io["hT"][j * 128:(j + 1) * 128, t0:t0 + N], [k.R_hT[ti][j]], [R_hres[hb]])
                for c in range(FCH):
                    k.mm(py[bk][:, 0:N], w_out[:, c, j * 128:(j + 1) * 128], gT[:, c, 0:N], c == 0, c == FCH - 1,
                         [R_wout[c // 11], R_gT[c]], [R_py[bk]], signal=(c == FCH - 1))
                k.stt("dve", hres[hb][:, 0:N], py[bk][:, 0:N], k.G[:, li, s, j, r:r + 1], hres[hb][:, 0:N], ALU.mult, ALU.add,
                      [R_py[bk], R_hres[hb], k.R_mod], [R_hres[hb]])
                k.dma("sp", io["hT"][j * 128:(j + 1) * 128, t0:t0 + N], hres[hb][:, 0:N], [R_hres[hb]], [k.R_hT[ti][j]])

        tiles = list(range(len(TILES)))
        if k.skip_ctx(li):
            tiles = tiles[1:]
        stop = k.dbg.get("ffn_stop", "C")
        tiles = tiles[:k.dbg.get("ffn_tiles", 99)]
        if stop != "w":
            A1(tiles[0])
            A2(tiles[0])
        for n_, ti in enumerate(tiles):
            if stop in ("w", "A"):
                break
            Bst(ti, 0, 11)
            if n_ + 1 < len(tiles):
                A1(tiles[n_ + 1])
            Bst(ti, 11, FCH)
            if n_ + 1 < len(tiles):
                A2(tiles[n_ + 1])
            if stop == "B":
                continue
            Cst(ti)
    P.barrier()


def final_phase(k, io):
    nc, P = k.nc, k.P
    from contextlib import ExitStack
    with ExitStack() as es:
        sb = lambda name, shape, dt: es.enter_context(nc.sbuf_tensor(k.uname(name), shape, dt))
        ps = lambda name, shape: es.enter_context(nc.psum_tensor(k.uname(name), shape, F32))
        hbuf = [sb("fhbuf%d" % i, [128, 8, 512], F32) for i in range(2)]
        R_hbuf = [Res("fhbuf%d" % i) for i in range(2)]
        sqb = [sb("fsqb%d" % i, [128, 512], BF16) for i in range(2)]
        R_sqb = [Res("fsqb%d" % i) for i in range(2)]
        rstd = sb("frstd", [128, 512], F32)
        R_rstd = Res("frstd")
        xn = [sb("fxn%d" % i, [128, 512], F32) for i in range(2)]
        R_xn = [Res("fxn%d" % i) for i in range(2)]
        ost = [sb("fost%d" % i, [128, 4, D], F32) for i in range(2)]
        R_ost = [Res("fost%d" % i) for i in range(2)]
        pss = ps("fpss", [128, 512])
        R_pss = Res("fpss")
        tp = [ps("ftp%d" % i, [128, 512]) for i in range(2)]
        R_tp = [Res("ftp%d" % i) for i in range(2)]
        R_out = Res("out")
        ncp = 0
        for n_, ti in enumerate(range(1, len(TILES))):
            t0, N, _ = TILES[ti]
            b = n_ % 2
            k.dma("sp", hbuf[b][:, :, 0:N], io["hT"][:, t0:t0 + N].rearrange("(c p) t -> p c t", p=128),
                  k.R_hT[ti], [R_hbuf[b]])
            for j in range(8):
                k.act(sqb[j % 2][:, 0:N], hbuf[b][:, j, 0:N], AF.Square, [R_hbuf[b]], [R_sqb[j % 2]])
                k.mm(pss[:, 0:N], k.ones_b[:, :], sqb[j % 2][:, 0:N], j == 0, j == 7, [R_sqb[j % 2], k.R_const], [R_pss])
            k.ts("dve", rstd[:, 0:N], pss[:, 0:N], 1.0 / D, EPS, ALU.mult, ALU.add, [R_pss], [R_rstd])
            k.act(rstd[:, 0:N], rstd[:, 0:N], AF.Sqrt, [R_rstd], [R_rstd])
            P.op("dve", (lambda o, i: (lambda e: e.reciprocal(out=o, in_=i)))(rstd[:, 0:N], rstd[:, 0:N]), [R_rstd], [R_rstd])
            for j in range(8):
                k.stt("dve", xn[j % 2][:, 0:N], hbuf[b][:, j, 0:N], k.vecT[:, VC_FNG + j:VC_FNG + j + 1], rstd[:, 0:N],
                      ALU.mult, ALU.mult, [R_hbuf[b], R_rstd, k.R_vec], [R_xn[j % 2]])
                pbk = ncp % 2
                for s_ in range(4):
                    k.tr(tp[pbk][:, s_ * 128:(s_ + 1) * 128], xn[j % 2][:, s_ * 128:(s_ + 1) * 128], k.ident[:, :],
                         [R_xn[j % 2], k.R_const], [R_tp[pbk]])
                k.copy("act" if ncp % 2 else "dve",
                       ost[b][:, :, j * 128:(j + 1) * 128], tp[pbk][:, :].rearrange("p (s f) -> p s f", f=128),
                       [R_tp[pbk]], [R_ost[b]])
                ncp += 1
            k.dma("sp", io["out"][t0 - CTX:t0 - CTX + N, :].rearrange("(s p) d -> p s d", p=128), ost[b][:, :, :],
                  [R_ost[b]], [R_out])
    P.barrier()


def _skip_ctx(self, li):
    return li == DEPTH - 1


K.skip_ctx = _skip_ctx

MIXERS = {}


def build_program(layers=range(DEPTH), mixers=True, do_final=True, ffn=True, dbg=None):
    from contextlib import ExitStack
    nc = bass.Bass("TRN2", target_bir_lowering=False)
    io = declare_io(nc)
    P = Prog(nc, n_dma=40, same_sync=bool((dbg or {}).get("same_sync", True)))
    k = K(nc, P)
    with ExitStack() as es:
        alloc_consts(k, es)
        k.dbg = dbg or {}
        prologue(k, io)
        for li in layers:
            if ffn:
                ffn_phase(k, io, li, 0)
            if mixers and (li % 3) in MIXERS:
                MIXERS[li % 3](k, io, li)
            if ffn:
                ffn_phase(k, io, li, 1)
        if do_final:
            final_phase(k, io)
        P.finish("sp")
        P.emit()
    return nc


def make_in_maps(inputs, cores=range(8)):
    f = lambda a: np.ascontiguousarray(np.asarray(a, dtype=np.float32))
    shared = {
        "c_ctx": f(inputs["c_ctx"]).reshape(8, 128),
        "w_mod": f(inputs["w_mod"]),
        "b_mod": f(inputs["b_mod"]).reshape(288, 128),
        "norm_g": f(inputs["norm_g"]).reshape(96, 128),
        "ffn_w_in": f(inputs["ffn_w_in"]),
        "ffn_w_out": f(inputs["ffn_w_out"]),
        "fnet_w_out": f(inputs["fnet_w_out"]),
        "ret_w_in": f(inputs["ret_w_in"]).reshape(D, 6144),
        "ret_w_out": f(inputs["ret_w_out"]).reshape(2048, D),
        "ret_decay_logit": f(inputs["ret_decay_logit"]).reshape(2, 4),
        "s5_lam_re": f(inputs["s5_lam_re"]).reshape(2, 64, 64),
        "s5_lam_im": f(inputs["s5_lam_im"]).reshape(2, 64, 64),
        "s5_log_dt": f(inputs["s5_log_dt"]).reshape(2, 64),
        "s5_b_re": f(inputs["s5_b_re"]).reshape(2, 64, 64, 16),
        "s5_b_im": f(inputs["s5_b_im"]).reshape(2, 64, 64, 16),
        "s5_c_re": f(inputs["s5_c_re"]).reshape(2, 64, 16, 64),
        "s5_c_im": f(inputs["s5_c_im"]).reshape(2, 64, 16, 64),
        "s5_d": f(inputs["s5_d"]).reshape(8, 128),
        "s5_w_glu": f(inputs["s5_w_glu"]).reshape(D, 2 * D),
        "final_norm_g": f(inputs["final_norm_g"]).reshape(8, 128),
    }
    shared.update(fnet_host_consts())
    shared.update(ret_host_consts())
    shared.update(s5_host_consts())
    x, c, ctx = f(inputs["x"]), f(inputs["c"]), f(inputs["ctx"])
    maps = []
    for b in cores:
        m = dict(shared)
        m["x"] = x[b]
        m["c"] = c[b].reshape(8, 128)
        m["ctx"] = ctx[b]
        maps.append(m)
    return maps


def kernel(**inputs):
    nc = build_program()
    maps = make_in_maps(inputs)
    res = run_bass_kernel_spmd(nc, maps, core_ids=list(range(8)))
    return np.stack([np.asarray(r["out"], dtype=np.float32) for r in res.results], axis=0)


def mk_modnorm(k, io, sb, ps):
    P = k.P
    hbuf = sb("mn_hbuf", [128, 8, 512], F32)
    sqb = [sb("mn_sqb%d" % i, [128, 512], BF16) for i in range(2)]
    tmp = [sb("mn_tmp%d" % i, [128, 512], F32) for i in range(2)]
    rstd = sb("mn_rstd", [128, 512], F32)
    pss = ps("mn_pss", [128, 512])
    R_hbuf, R_rstd, R_pss = Res("mn_hbuf"), Res("mn_rstd"), Res("mn_pss")
    R_sqb = [Res("mn_sqb%d" % i) for i in range(2)]
    R_tmp = [Res("mn_tmp%d" % i) for i in range(2)]

    def do(ti, li, s, uT_of, R_uT, chunks=range(8), raw_of=None):
        t0, N, isctx = TILES[ti]
        r = 1 if isctx else 0
        k.dma("sp", hbuf[:, :, 0:N], io["hT"][:, t0:t0 + N].rearrange("(c p) t -> p c t", p=128),
              k.R_hT[ti], [R_hbuf])
        for j in range(8):
            k.act(sqb[j % 2][:, 0:N], hbuf[:, j, 0:N], AF.Square, [R_hbuf], [R_sqb[j % 2]])
            k.mm(pss[:, 0:N], k.ones_b[:, :], sqb[j % 2][:, 0:N], j == 0, j == 7, [R_sqb[j % 2], k.R_const], [R_pss])
        k.ts("dve", rstd[:, 0:N], pss[:, 0:N], 1.0 / D, EPS, ALU.mult, ALU.add, [R_pss], [R_rstd])
        k.act(rstd[:, 0:N], rstd[:, 0:N], AF.Sqrt, [R_rstd], [R_rstd])
        P.op("dve", (lambda o, i: (lambda e: e.reciprocal(out=o, in_=i)))(rstd[:, 0:N], rstd[:, 0:N]), [R_rstd], [R_rstd])
        for n_, j in enumerate(chunks):
            k.stt("dve", tmp[n_ % 2][:, 0:N], hbuf[:, j, 0:N], k.A[:, li, s, j, r:r + 1], rstd[:, 0:N], ALU.mult, ALU.mult,
                  [R_hbuf, R_rstd, k.R_mod], [R_tmp[n_ % 2]])
            k.act(uT_of(j), tmp[n_ % 2][:, 0:N], AF.Identity, [R_tmp[n_ % 2], k.R_mod], [R_uT],
                  bias=k.B[:, li, s, j, r:r + 1])

    return do


def mk_resadd(k, io, sb, li):
    hres = [sb("ra_hres%d" % i, [128, 512], F32) for i in range(3)]
    R_hres = [Res("ra_hres%d" % i) for i in range(3)]
    cnt = [0]

    def do(ti, j, col0, N, y_ap, R_y, pre_reads=(), **dkw):
        t0, _, isctx = TILES[ti]
        r = 1 if isctx else 0
        hb = cnt[0] % 3
        cnt[0] += 1
        k.dma("sp", hres[hb][:, 0:N], io["hT"][j * 128:(j + 1) * 128, t0 + col0:t0 + col0 + N], [k.R_hT[ti][j]], [R_hres[hb]], **dkw)
        k.stt("dve", hres[hb][:, 0:N], y_ap, k.G[:, li, 1, j, r:r + 1], hres[hb][:, 0:N], ALU.mult, ALU.add,
              [R_y, R_hres[hb], k.R_mod] + list(pre_reads), [R_hres[hb]])
        k.dma("sp", io["hT"][j * 128:(j + 1) * 128, t0 + col0:t0 + col0 + N], hres[hb][:, 0:N], [R_hres[hb]], [k.R_hT[ti][j]], **dkw)

    return do


def declare_fnet_consts(nc, io):
    io["dft_c"] = nc.dram_tensor("dft_c", [128, 256], BF16, kind="ExternalInput").ap()
    io["dft_cosN"] = nc.dram_tensor("dft_cosN", [SEQ, SEQ], BF16, kind="ExternalInput").ap()
    io["dft_sinN"] = nc.dram_tensor("dft_sinN", [SEQ, SEQ], BF16, kind="ExternalInput").ap()
    io["dft_cosC"] = nc.dram_tensor("dft_cosC", [CTX, CTX], BF16, kind="ExternalInput").ap()
    io["dft_sinC"] = nc.dram_tensor("dft_sinC", [CTX, CTX], BF16, kind="ExternalInput").ap()
    io["fn_perm"] = nc.dram_tensor("fn_perm", [128, 4 * 128], BF16, kind="ExternalInput").ap()


def fnet_host_consts():
    import ml_dtypes
    bf = ml_dtypes.bfloat16
    c = np.arange(128)
    ang = 2.0 * np.pi * ((c[:, None] * c[None, :]) % 128) / 128.0
    dft_c = np.concatenate([np.cos(ang), -np.sin(ang)], axis=1).astype(np.float32).astype(bf)

    def mats(n):
        i = np.arange(n, dtype=np.int64)
        idx = (i[:, None] * i[None, :]) % n
        a = 2.0 * np.pi * np.arange(n, dtype=np.float64) / n
        return np.cos(a).astype(np.float32)[idx].astype(bf), np.sin(a).astype(np.float32)[idx].astype(bf)

    cN, sN = mats(SEQ)
    cC, sC = mats(CTX)
    J = np.zeros((128, 128), np.float32)
    for p in range(1, 128):
        J[128 - p, p] = 1.0
    E = np.zeros((128, 128), np.float32)
    E[0, 0] = 1.0
    perm = np.concatenate([J, -J, E, -E], axis=1).astype(bf)
    return {"dft_c": dft_c, "dft_cosN": cN, "dft_sinN": sN, "dft_cosC": cC, "dft_sinC": sC, "fn_perm": perm}


def fnet_mixer(k, io, li):
    nc, P = k.nc, k.P
    jw = li // 3
    from contextlib import ExitStack
    with ExitStack() as es:
        sb = lambda name, shape, dt: es.enter_context(nc.sbuf_tensor(k.uname(name), shape, dt))
        ps = lambda name, shape: es.enter_context(nc.psum_tensor(k.uname(name), shape, F32))
        modnorm = mk_modnorm(k, io, sb, ps)
        resadd = mk_resadd(k, io, sb, li)
        dftc = sb("dftc", [128, 256], BF16)
        w_out = sb("fw_out", [128, 8, D], BF16)
        X = sb("fX", [128, 32, 4, 256], BF16)
        cosb = sb("fcos", [128, 32, 512], BF16)
        sinb = sb("fsin", [128, 32, 512], BF16)
        uT = sb("fuT", [128, 8, 512], BF16)
        uTg = sb("fuTg", [128, 4, 512], BF16)
        R_uTg = Res("fuTg")
        R_uTd = [Res("uTd%d" % i) for i in range(len(TILES))]
        fT = [sb("ffT%d" % i, [128, 512], BF16) for i in range(8)]
        px = [ps("fpx%d" % i, [128, 512]) for i in range(2)]
        pf = [ps("fpf%d" % i, [128, 512]) for i in range(2)]
        py = [ps("fpy%d" % i, [128, 512]) for i in range(2)]
        R_dftc, R_wout, R_uT = Res("dftc"), Res("fw_out"), Res("fuT")
        R_X = [Res("fX%d" % i) for i in range(32)]
        R_cos = [Res("fcos%d" % i) for i in range(8)]
        R_sin = [Res("fsin%d" % i) for i in range(8)]
        R_fT = [Res("ffT%d" % i) for i in range(8)]
        R_px = [Res("fpx%d" % i) for i in range(2)]
        R_pf = [Res("fpf%d" % i) for i in range(2)]
        R_py = [Res("fpy%d" % i) for i in range(2)]
        k.dma("sp", dftc[:, :], io["dft_c"], [], [R_dftc])
        k.dma("pool", w_out[:, :, :], io["fnet_w_out"][jw].rearrange("(c p) n -> p c n", p=128), [], [R_wout])
        cnt = {"x": 0, "f": 0, "y": 0}

        def run(tiles, nch, groups_list, cosN, sinN, W, norm, sweep_only=False):
            for ti in tiles:
                t0, N, isctx = TILES[ti]
                modnorm(ti, li, 1, lambda j: uT[:, j, 0:N], R_uT)
                k.dma("sp", io["uT_d"][:, t0:t0 + N].rearrange("(c p) t -> p c t", p=128), uT[:, :, 0:N], [R_uT], [R_uTd[ti]])
            if sweep_only:
                return
            for groups in groups_list:
                ng = len(groups)
                for ti in tiles:
                    t0, N, isctx = TILES[ti]
                    g0 = groups[0]
                    k.dma("sp", uTg[:, 0:ng, 0:N],
                          io["uT_d"][g0 * 128:(g0 + ng) * 128, t0:t0 + N].rearrange("(c p) t -> p c t", p=128),
                          [R_uTd[ti]], [R_uTg])
                    c0 = (t0 - TILES[tiles[0]][0]) // 128
                    for sbi in range(N // 128):
                        for g2 in range(0, ng, 2):
                            bk = cnt["x"] % 2
                            cnt["x"] += 1
                            for u_ in range(2):
                                k.mm(px[bk][:, u_ * 256:(u_ + 1) * 256], uTg[:, g2 + u_, sbi * 128:(sbi + 1) * 128], dftc[:, :],
                                     True, True, [R_uTg, R_dftc], [R_px[bk]])
                            k.copy("act" if cnt["x"] % 2 else "dve",
                                   X[:, c0 + sbi, g2:g2 + 2, :], px[bk][:, :].rearrange("p (a b) -> p a b", b=256),
                                   [R_px[bk]], [R_X[c0 + sbi]])
                cq = max(1, nch // 8)
                for kt in range((nch * 128) // W):
                    for q in range(nch // cq):
                        k.dma("sp", cosb[:, q * cq:(q + 1) * cq, 0:W],
                              cosN.rearrange("(c p) k -> p c k", p=128)[:, q * cq:(q + 1) * cq, kt * W:(kt + 1) * W],
                              [], [R_cos[q]])
                        k.dma("act", sinb[:, q * cq:(q + 1) * cq, 0:W],
                              sinN.rearrange("(c p) k -> p c k", p=128)[:, q * cq:(q + 1) * cq, kt * W:(kt + 1) * W],
                              [], [R_sin[q]])
                    pf4 = [pf[0], pf[1], px[0], px[1]]
                    R_pf4 = [R_pf[0], R_pf[1], R_px[0], R_px[1]]
                    for c in range(nch):
                        for gi, g in enumerate(groups):
                            last = (c == nch - 1)
                            k.mm(pf4[gi][:, 0:W], X[:, c, gi, 0:128], cosb[:, c, 0:W], c == 0, False,
                                 [R_X[c], R_cos[c // cq]], [R_pf4[gi]], signal=False)
                            k.mm(pf4[gi][:, 0:W], X[:, c, gi, 128:256], sinb[:, c, 0:W], False, last,
                                 [R_X[c], R_sin[c // cq]], [R_pf4[gi]], signal=(last or (c % cq == cq - 1 and gi == ng - 1)))
                    for gi, g in enumerate(groups):
                        if gi % 2:
                            k.act(fT[gi][:, 0:W], pf4[gi][:, 0:W], AF.Copy, [R_pf4[gi]], [R_fT[gi]], scale=norm)
                        else:
                            k.ts("dve", fT[gi][:, 0:W], pf4[gi][:, 0:W], norm, None, ALU.mult, None, [R_pf4[gi]], [R_fT[gi]])
                    ti = tiles[0] + (kt * W) // 512 if W == 512 else tiles[0]
                    for j in range(8):
                        bk = cnt["y"] % 2
                        cnt["y"] += 1
                        for gi, g in enumerate(groups):
                            k.mm(py[bk][:, 0:W], w_out[:, g, j * 128:(j + 1) * 128], fT[gi][:, 0:W], gi == 0, gi == ng - 1,
                                 [R_wout, R_fT[gi]], [R_py[bk]], signal=(gi == ng - 1))
                        resadd(ti, j, 0, W, py[bk][:, 0:W], R_py[bk])

        if not k.skip_ctx(li):
            run([0], 2, [list(range(4)), list(range(4, 8))], io["dft_cosC"], io["dft_sinC"], 256, 1.0 / math.sqrt(CTX * 128))
        if k.dbg.get("fnet_old"):
            run(list(range(1, 9)), 32, [list(range(4)), list(range(4, 8))], io["dft_cosN"], io["dft_sinN"], 512,
                1.0 / math.sqrt(SEQ * 128))
        else:
            run(list(range(1, 9)), 32, None, None, None, 512, None, sweep_only=True)
    P.barrier()
    if not k.dbg.get("fnet_old"):
        fnet_latent_folded(k, io, li)


def fnet_latent_folded(k, io, li):
    nc, P = k.nc, k.P
    jw = li // 3
    from contextlib import ExitStack
    R = lambda n: Res(n)
    norm = 1.0 / math.sqrt(SEQ * 128)
    NF = 17
    with ExitStack() as es:
        sb = lambda name, shape, dt: es.enter_context(nc.sbuf_tensor(k.uname(name), shape, dt))
        ps = lambda name, shape: es.enter_context(nc.psum_tensor(k.uname(name), shape, F32))
        resadd = mk_resadd(k, io, sb, li)
        dftc = sb("g_dftc", [128, 256], BF16)
        perm = sb("g_perm", [128, 4, 128], BF16)
        w_out = sb("g_wout", [128, 8, D], BF16)
        Xf = sb("g_Xf", [128, NF, 8, 256], BF16)
        px = [ps("g_px%d" % i, [128, 512]) for i in range(2)]
        pf = [ps("g_pf%d" % i, [128, 512]) for i in range(2)]
        py = [ps("g_py%d" % i, [128, 512]) for i in range(2)]
        R_c, R_wout = R("g_c"), R("g_wout")
        R_Xf = [R("g_Xf%d" % i) for i in range(NF)]
        R_px = [R("g_px0"), R("g_px1")]
        R_pf = [R("g_pf0"), R("g_pf1")]
        R_py = [R("g_py0"), R("g_py1")]
        k.dma("sp", dftc[:, :], io["dft_c"], [], [R_c])
        k.dma("sp", perm[:, :, :], io["fn_perm"].rearrange("p (a b) -> p a b", b=128), [], [R_c])
        k.dma("pool", w_out[:, :, :], io["fnet_w_out"][jw].rearrange("(c p) n -> p c n", p=128), [], [R_wout])
        with ExitStack() as es1:
            sb1 = lambda name, shape, dt: es1.enter_context(nc.sbuf_tensor(k.uname(name), shape, dt))
            Xup = sb1("g_Xup", [128, 16, 8, 256], BF16)
            uTg = sb1("g_uTg", [128, 8, 512], BF16)
            R_uTg = R("g_uTg")
            R_Xup = [R("g_Xup%d" % i) for i in range(16)]
            nx = [0]

            def load_u(ti):
                t0, N, _ = TILES[ti]
                k.dma("sp", uTg[:, :, 0:N], io["uT_d"][:, t0:t0 + N].rearrange("(c p) t -> p c t", p=128), [], [R_uTg])

            def evac(dst, src, Rs, Rd):
                nx[0] += 1
                k.copy("act" if nx[0] % 2 else "dve", dst, src, [Rs], [Rd])

            for ti in range(5, 9):
                t0 = TILES[ti][0]
                load_u(ti)
                for sbi in range(4):
                    cu = (t0 - CTX) // 128 + sbi - 16
                    for g2 in range(0, 8, 2):
                        bk = nx[0] % 2
                        for u_ in range(2):
                            k.mm(px[bk][:, u_ * 256:(u_ + 1) * 256], uTg[:, g2 + u_, sbi * 128:(sbi + 1) * 128], dftc[:, :],
                                 True, True, [R_uTg, R_c], [R_px[bk]])
                        evac(Xup[:, cu, g2:g2 + 2, :], px[bk][:, :].rearrange("p (a b) -> p a b", b=256), R_px[bk], R_Xup[cu])
            for ti in range(1, 5):
                t0 = TILES[ti][0]
                load_u(ti)
                for sbi in range(4):
                    c = (t0 - CTX) // 128 + sbi
                    for g2 in range(0, 8, 2):
                        bk = nx[0] % 2
                        for u_ in range(2):
                            g = g2 + u_
                            for w in range(2):
                                o_ = px[bk][:, u_ * 256 + w * 128:u_ * 256 + (w + 1) * 128]
                                rd = [R_uTg, R_c, R_Xup[15 - c]] + ([R_Xup[16 - c]] if c >= 1 else [])
                                k.mm(o_, uTg[:, g, sbi * 128:(sbi + 1) * 128], dftc[:, w * 128:(w + 1) * 128], True, False,
                                     rd, [R_px[bk]], signal=False)
                                k.mm(o_, perm[:, w, :], Xup[:, 15 - c, g, w * 128:(w + 1) * 128], False, c == 0,
                                     rd, [R_px[bk]], signal=(c == 0))
                                if c >= 1:
                                    k.mm(o_, perm[:, 2 + w, :], Xup[:, 16 - c, g, w * 128:(w + 1) * 128], False, True,
                                         rd, [R_px[bk]], signal=True)
                        evac(Xf[:, c, g2:g2 + 2, :], px[bk][:, :].rearrange("p (a b) -> p a b", b=256), R_px[bk], R_Xf[c])
            k.memset("dve", Xf[:, 16, :, :], 0.0, [R_Xf[16]])
            for g2 in range(0, 8, 2):
                bk = nx[0] % 2
                for u_ in range(2):
                    k.mm(px[bk][:, u_ * 256:u_ * 256 + 128], perm[:, 2, :], Xup[:, 0, g2 + u_, 0:128], True, True,
                         [R_c, R_Xup[0]], [R_px[bk]])
                evac(Xf[:, 16, g2:g2 + 2, 0:128], px[bk][:, :].rearrange("p (a b) -> p a b", b=256)[:, :, 0:128], R_px[bk], R_Xf[16])
        P.barrier()
        with ExitStack() as es2:
            sb2 = lambda name, shape, dt: es2.enter_context(nc.sbuf_tensor(k.uname(name), shape, dt))
            cosb = [sb2("g_cos%d" % i, [128, NF, 512], BF16) for i in range(2)]
            sinb = [sb2("g_sin%d" % i, [128, NF, 512], BF16) for i in range(2)]
            fT = [sb2("g_fT%d" % i, [128, 512], BF16) for i in range(8)]
            SUBS = [(0, 4), (4, 8), (8, 12), (12, NF)]
            sub_of = [0] * 4 + [1] * 4 + [2] * 4 + [3] * 5
            R_cos = [[R("g_cos%d_%d" % (i, q)) for q in range(4)] for i in range(2)]
            R_sin = [[R("g_sin%d_%d" % (i, q)) for q in range(4)] for i in range(2)]
            R_fT = [R("g_fT%d" % i) for i in range(8)]
            cv = io["dft_cosN"][0:NF * 128, :].rearrange("(c p) k -> p c k", p=128)
            sv = io["dft_sinN"][0:NF * 128, :].rearrange("(c p) k -> p c k", p=128)
            pf4 = [pf[0], pf[1], px[0], px[1]]
            R_pf4 = [R_pf[0], R_pf[1], R_px[0], R_px[1]]
            ny = 0
            fTm = [sb2("g_fTm%d" % i, [128, 512], BF16) for i in range(8)]
            R_fTm = [R("g_fTm%d" % i) for i in range(8)]
            Bs = [sb2("g_Bs%d" % i, [128, 512], F32) for i in range(2)]
            R_Bs = [R("g_Bs0"), R("g_Bs1")]
            fT0 = sb2("g_fT0", [128, 8], BF16)
            R_fT0 = R("g_fT0")
            for g in range(8):
                for c in range(NF):
                    k.mm(pf[0][:, g:g + 1], Xf[:, c, g, 0:128], k.ones_b[:, 0:1], c == 0, c == NF - 1,
                         [R_Xf[c], k.R_const], [R_pf[0]], signal=(c == NF - 1))
            k.ts("dve", fT0[:, :], pf[0][:, 0:8], norm, None, ALU.mult, None, [R_pf[0]], [R_fT0])
            for j in range(8):
                bk = ny % 2
                ny += 1
                for g in range(8):
                    k.mm(py[bk][:, 0:1], w_out[:, g, j * 128:(j + 1) * 128], fT0[:, g:g + 1], g == 0, g == 7,
                         [R_wout, R_fT0], [R_py[bk]], signal=(g == 7))
                resadd(1, j, 0, 1, py[bk][:, 0:1], R_py[bk], allow_slow_non_contiguous=True)
            for jw in range(4):
                par = jw % 2
                k0 = 512 * jw + 1
                for qi, (a, b) in enumerate(SUBS):
                    k.dma("sp", cosb[par][:, a:b, :], cv[:, a:b, k0:k0 + 512], [], [R_cos[par][qi]])
                    k.dma("act", sinb[par][:, a:b, :], sv[:, a:b, k0:k0 + 512], [], [R_sin[par][qi]])
                for gp in range(4):
                    for c in range(NF):
                        for u in range(2):
                            g = 2 * gp + u
                            k.mm(pf4[2 * u][:, :], Xf[:, c, g, 0:128], cosb[par][:, c, :], c == 0, c == NF - 1,
                                 [R_Xf[c], R_cos[par][sub_of[c]]], [R_pf4[2 * u]], signal=(c == NF - 1))
                            k.mm(pf4[2 * u + 1][:, :], Xf[:, c, g, 128:256], sinb[par][:, c, :], c == 0, c == NF - 1,
                                 [R_Xf[c], R_sin[par][sub_of[c]]], [R_pf4[2 * u + 1]], signal=(c == NF - 1))
                    for u in range(2):
                        g = 2 * gp + u
                        k.act(Bs[u][:, :], pf4[2 * u + 1][:, :], AF.Copy, [R_pf4[2 * u + 1]], [R_Bs[u]], scale=norm)
                        k.stt("dve", fT[g][:, :], pf4[2 * u][:, :], norm, Bs[u][:, :], ALU.mult, ALU.add,
                              [R_pf4[2 * u], R_Bs[u]], [R_fT[g]])
                        k.stt("dve", fTm[g][:, ::-1], pf4[2 * u][:, :], norm, Bs[u][:, :], ALU.mult, ALU.subtract,
                              [R_pf4[2 * u], R_Bs[u]], [R_fTm[g]])
                pm = SEQ - k0 - 511
                for (fTs, R_fs, pos, c_lo) in ((fT, R_fT, k0, 0), (fTm, R_fTm, pm, 1 if jw == 3 else 0)):
                    Nw = 512 - c_lo
                    for j in range(8):
                        bk = ny % 2
                        ny += 1
                        for g in range(8):
                            k.mm(py[bk][:, 0:Nw], w_out[:, g, j * 128:(j + 1) * 128], fTs[g][:, c_lo:512], g == 0, g == 7,
                                 [R_wout, R_fs[g]], [R_py[bk]], signal=(g == 7))
                        resadd(1, j, pos + c_lo, Nw, py[bk][:, 0:Nw], R_py[bk])
    P.barrier()


MIXERS[0] = fnet_mixer


NCHK = T // 128


def declare_ret_consts(nc, io):
    io["ret_expo"] = nc.dram_tensor("ret_expo", [128, 4 * 128], F32, kind="ExternalInput").ap()
    io["ret_m01"] = nc.dram_tensor("ret_m01", [128, 2 * 128], F32, kind="ExternalInput").ap()
    io["ret_ramp"] = nc.dram_tensor("ret_ramp", [128, NCHK], F32, kind="ExternalInput").ap()
    io["ropeC"] = nc.dram_tensor("ropeC", [256, SEQ], F32, kind="ExternalInput").ap()
    io["ropeS"] = nc.dram_tensor("ropeS", [256, SEQ], F32, kind="ExternalInput").ap()
    io["zT_d"] = nc.dram_tensor("zT_d", [2048, T], BF16).ap()


def ret_host_consts():
    m = np.arange(128, dtype=np.float32)[:, None]
    l = np.arange(128, dtype=np.float32)[None, :]
    expo = np.concatenate([np.maximum(l - m, 0), l - m + 128, np.maximum(m - l, 0), m - l + 128], axis=1).astype(np.float32)
    m01 = np.concatenate([(m <= l), (m > l)], axis=1).astype(np.float32)
    ramp = np.broadcast_to(128.0 * np.arange(NCHK, dtype=np.float32)[None, :], (128, NCHK)).copy()
    t = np.arange(SEQ)
    inv_freq = np.exp(np.float32(-math.log(10000.0)) * np.arange(64, dtype=np.float32) / np.float32(64)).astype(np.float32)
    ang_r = ((t // 64).astype(np.float32)[:, None] * inv_freq[None, :]).astype(np.float32)
    ang_c = ((t % 64).astype(np.float32)[:, None] * inv_freq[None, :]).astype(np.float32)
    C = np.zeros((256, SEQ), np.float32)
    S = np.zeros((256, SEQ), np.float32)
    for d, ang in enumerate((ang_r, ang_c)):
        c, s = np.cos(ang).T.astype(np.float32), np.sin(ang).T.astype(np.float32)
        C[d * 128:d * 128 + 64] = c
        C[d * 128 + 64:d * 128 + 128] = c
        S[d * 128:d * 128 + 64] = -s
        S[d * 128 + 64:d * 128 + 128] = s
    return {"ret_expo": expo, "ret_m01": m01, "ret_ramp": ramp, "ropeC": C, "ropeS": S}


def ret_mixer(k, io, li):
    nc, P = k.nc, k.P
    from contextlib import ExitStack
    with ExitStack() as es:
        sb = lambda name, shape, dt: es.enter_context(nc.sbuf_tensor(k.uname(name), shape, dt))
        ps = lambda name, shape, dt=F32: es.enter_context(nc.psum_tensor(k.uname(name), shape, dt))
        modnorm = mk_modnorm(k, io, sb, ps)
        resadd = mk_resadd(k, io, sb, li)
        R = lambda n: Res(n)
        expo = sb("r_expo", [128, 4, 128], F32)
        m01 = sb("r_m01", [128, 2, 128], F32)
        ramp = sb("r_ramp", [128, NCHK], F32)
        ones1 = sb("r_ones1", [1, 128], F32)
        lg1 = sb("r_lg1", [1, 8], F32)
        lgam = sb("r_lgam", [128, 8], F32)
        Etab = sb("r_Etab", [128, 8, 2, 128], F32)
        Wdiag = sb("r_Wdiag", [128, 4, 128], F32)
        apow = sb("r_apow", [128, 8, NCHK], F32)
        R_tab = R("r_tab")
        pA = ps("r_pA", [128, 512])
        pB = ps("r_pB", [128, 512])
        psc_t = [ps("r_psc%d" % i, [128, 512]) for i in range(2)]
        NSC = 4
        psc = [psc_t[0], psc_t[1], pA, pB]
        po = ps("r_po", [128, 512])
        pg = ps("r_pg", [128, 512])
        pt = ps("r_pt", [128, 512], BF16)
        R_pA, R_pB, R_po, R_pg, R_pt = R("pA"), R("pB"), R("po"), R("pg"), R("pt")
        R_psc = [R("psc0"), R("psc1"), R_pA, R_pB]
        k.dma("sp", expo[:, :, :], io["ret_expo"].rearrange("p (a b) -> p a b", b=128), [], [R_tab])
        k.dma("sp", m01[:, :, :], io["ret_m01"].rearrange("p (a b) -> p a b", b=128), [], [R_tab])
        k.dma("sp", ramp[:, :], io["ret_ramp"], [], [R_tab])
        k.dma("sp", lg1[:, :], io["ret_decay_logit"].rearrange("(o a) b -> o (a b)", o=1), [], [R_tab])
        k.memset("dve", ones1[:, :], 1.0, [R_tab])
        k.mm(pA[:, 0:8], ones1[0:1, :], lg1[0:1, :], True, True, [R_tab], [R_pA])
        k.act(lgam[:, :], pA[:, 0:8], AF.Exp, [R_pA], [R_tab], scale=-1.0)
        k.ts("dve", lgam[:, :], lgam[:, :], 1.0, None, ALU.add, None, [R_tab], [R_tab])
        k.act(lgam[:, :], lgam[:, :], AF.Ln, [R_tab], [R_tab])
        k.ts("dve", lgam[:, :], lgam[:, :], -1.0, None, ALU.mult, None, [R_tab], [R_tab])
        for d in range(2):
            for h in range(4):
                col = d * 4 + h
                for w in range(2):
                    k.act(Etab[:, col, w, :], expo[:, d * 2 + w, :], AF.Exp, [R_tab], [R_tab], scale=lgam[:, col:col + 1])
                k.tt("dve", Etab[:, col, 0, :], Etab[:, col, 0, :], m01[:, d, :], ALU.mult, [R_tab], [R_tab])
                k.act(apow[:, col, :], ramp[:, :], AF.Exp, [R_tab], [R_tab], scale=lgam[:, col:col + 1])
        for h in range(4):
            k.tt("dve", Wdiag[:, h, :], Etab[:, h, 0, :], Etab[:, 4 + h, 0, :], ALU.add, [R_tab], [R_tab])

        uT = sb("r_uT", [128, 8, 512], BF16)
        R_uT = R("r_uT")
        R_uTd = [R("uTd%d" % i) for i in range(len(TILES))]
        for ti in range(len(TILES)):
            t0, N, _ = TILES[ti]
            modnorm(ti, li, 1, lambda j: uT[:, j, 0:N], R_uT)
            k.dma("sp", io["uT_d"][:, t0:t0 + N].rearrange("(c p) t -> p c t", p=128), uT[:, :, 0:N], [R_uT], [R_uTd[ti]])

        wq = sb("r_wq", [128, 8, 256], BF16)
        wk = sb("r_wk", [128, 8, 256], BF16)
        wqs = sb("r_wqs", [128, 8, 256], BF16)
        wks = sb("r_wks", [128, 8, 256], BF16)
        wv = sb("r_wv", [128, 8, 512], BF16)
        wg = sb("r_wg", [128, 8, 512], BF16)
        R_w = R("r_w")
        qT = sb("r_qT", [128, 2, T], BF16)
        kT = sb("r_kT", [128, 2, T], BF16)
        vtm = sb("r_vtm", [128, NCHK, 512], BF16)
        R_qT, R_kT = R("qT"), R("kT")
        R_v = [R("v%d" % i) for i in range(NCHK)]
        rc = sb("r_rc", [128, 2, 512], F32)
        rs = sb("r_rs", [128, 2, 512], F32)
        R_rope = R("rope")
        t1 = [sb("r_t1_%d" % i, [128, 512], F32) for i in range(2)]
        t2 = [sb("r_t2_%d" % i, [128, 512], F32) for i in range(2)]
        R_t1 = [R("t1a"), R("t1b")]
        R_t2 = [R("t2a"), R("t2b")]
        NPB = 24
        Pb = [sb("r_Pb%d" % i, [128, 128], BF16) for i in range(NPB)]
        R_Pb = [R("Pb%d" % i) for i in range(NPB)]
        Pt = [sb("r_Pt%d" % i, [128, 128], F32) for i in range(2)]
        R_Pt = [R("Pt0"), R("Pt1")]
        uTcs = [sb("r_uTc%d" % i, [128, 8, 128], BF16) for i in range(2)]
        R_uTcs = [R("uTc0"), R("uTc1")]

        def load_uTc(lc_):
            k.dma("sp", uTcs[lc_ % 2][:, :, :], io["uT_d"][:, lc_ * 128:(lc_ + 1) * 128].rearrange("(c p) t -> p c t", p=128),
                  [R_uTd[0 if lc_ < 2 else 1 + (lc_ - 2) // 4]], [R_uTcs[lc_ % 2]])
        osb = sb("r_osb", [128, 512], F32)
        sgt = sb("r_sgt", [128, 512], F32)
        sqj = sb("r_sqj", [128, 512], F32)
        zb = sb("r_zb", [128, 512], BF16)
        zb2 = sb("r_zb2", [128, 512], BF16)
        zbs = [zb, zb2]
        R_zbs = [R("zb0"), R("zb1")]
        zTc = sb("r_zTc", [128, 4, 128], BF16)
        st = sb("r_st", [128, 4], F32)
        identb = sb("r_identb", [128, 128], BF16)
        R_osb, R_sgt, R_zb, R_zTc, R_st, R_sqj = R("osb"), R("sgt"), R("zb"), R("zTc"), R("st"), R("sqj")
        R_zTd = [R("zTd%d" % i) for i in range(NCHK)]
        k.copy("dve", identb[:, :], k.ident[:, :], [k.R_const], [R_tab])
        wi = io["ret_w_in"].rearrange("(c p) n -> p c n", p=128)

        def cb(c):
            return c - 2 if c >= 2 else 32 + c

        nblk = 0
        for h in range(4):
            for (dst, c0, n) in ((wq, h * 256, 256), (wk, 1024 + h * 256, 256), (wv, 2048 + h * 512, 512), (wg, 4096 + h * 512, 512)):
                k.dma("pool", dst[:, :, 0:n], wi[:, :, c0:c0 + n], [], [R_w])
            for (dst, c0) in ((wqs, h * 256), (wks, 1024 + h * 256)):
                for dk in range(2):
                    b0 = c0 + dk * 128
                    k.dma("pool", dst[:, :, dk * 128:dk * 128 + 64], wi[:, :, b0 + 64:b0 + 128], [], [R_w])
                    k.dma("pool", dst[:, :, dk * 128 + 64:dk * 128 + 128], wi[:, :, b0:b0 + 64], [], [R_w])
            for ti in range(len(TILES)):
                t0, N, isctx = TILES[ti]
                k.dma("sp", uT[:, :, 0:N], io["uT_d"][:, t0:t0 + N].rearrange("(c p) t -> p c t", p=128), [R_uTd[ti]], [R_uT])
                if not isctx:
                    p0 = t0 - CTX
                    k.dma("sp", rc[:, :, 0:N], io["ropeC"][:, p0:p0 + N].rearrange("(d p) t -> p d t", p=128), [], [R_rope])
                    k.dma("sp", rs[:, :, 0:N], io["ropeS"][:, p0:p0 + N].rearrange("(d p) t -> p d t", p=128), [], [R_rope])
                for (w_, ws_, dst, R_dst, scl) in ((wq, wqs, qT, R_qT, 1.0), (wk, wks, kT, R_kT, 0.0625)):
                    for dk in range(2):
                        for kc in range(8):
                            k.mm(pA[:, 0:N], w_[:, kc, dk * 128:(dk + 1) * 128], uT[:, kc, 0:N], kc == 0, kc == 7,
                                 [R_w, R_uT], [R_pA], signal=(kc == 7))
                        if isctx:
                            k.ts("dve", dst[:, dk, t0:t0 + N], pA[:, 0:N], scl, None, ALU.mult, None, [R_pA], [R_dst])
                            continue
                        for kc in range(8):
                            k.mm(pB[:, 0:N], ws_[:, kc, dk * 128:(dk + 1) * 128], uT[:, kc, 0:N], kc == 0, kc == 7,
                                 [R_w, R_uT], [R_pB], signal=(kc == 7))
                        b = dk
                        k.stt("dve", t1[b][:, 0:N], pA[:, 0:N], scl, rc[:, dk, 0:N], ALU.mult, ALU.mult, [R_pA, R_rope], [R_t1[b]])
                        k.stt("dve", t2[b][:, 0:N], pB[:, 0:N], scl, rs[:, dk, 0:N], ALU.mult, ALU.mult, [R_pB, R_rope], [R_t2[b]])
                        k.tt("pool", dst[:, dk, t0:t0 + N], t1[b][:, 0:N], t2[b][:, 0:N], ALU.add, [R_t1[b], R_t2[b]], [R_dst])
                for sbi in range(N // 128):
                    c = t0 // 128 + sbi
                    for kc in range(8):
                        k.mm(pg[:, :], uT[:, kc, sbi * 128:(sbi + 1) * 128], wv[:, kc, :], kc == 0, kc == 7,
                             [R_w, R_uT], [R_pg], signal=(kc == 7))
                    k.copy("act", vtm[:, c, :], pg[:, :], [R_pg], [R_v[c]])
            tasks = []
            for lc in range(NCHK):
                blocks = []
                for mc in range(NCHK):
                    terms = []
                    if mc == lc:
                        terms = ["diag"]
                    else:
                        if mc < lc:
                            terms.append((h, lc - mc - 1))
                        if cb(mc) > cb(lc):
                            terms.append((4 + h, cb(mc) - cb(lc) - 1))
                    if terms:
                        blocks.append((mc, terms))
                for bi, (mc, terms) in enumerate(blocks):
                    tasks.append((lc, bi, len(blocks), mc, terms))

            GB = 4
            groups_ = [tasks[i:i + GB] for i in range(0, len(tasks), GB)]

            def emit_scores(grp, gslot):
                sk = gslot % NSC
                for i_, (lc, bi, nb, mc, terms) in enumerate(grp):
                    for kc in range(2):
                        k.mm(psc[sk][:, i_ * 128:(i_ + 1) * 128], kT[:, kc, mc * 128:(mc + 1) * 128], qT[:, kc, lc * 128:(lc + 1) * 128],
                             kc == 0, kc == 1, [R_kT, R_qT], [R_psc[sk]], signal=(kc == 1 and i_ == len(grp) - 1))
                for i_, (lc, bi, nb, mc, terms) in enumerate(grp):
                    pk = (gslot * GB + i_) % NPB
                    sc_ap = psc[sk][:, i_ * 128:(i_ + 1) * 128]
                    if terms[0] == "diag":
                        k.tt("dve", Pb[pk][:, :], sc_ap, Wdiag[:, h, :], ALU.mult, [R_psc[sk], R_tab], [R_Pb[pk]])
                    elif len(terms) == 1:
                        col, n = terms[0]
                        k.stt("dve", Pb[pk][:, :], sc_ap, apow[:, col, n:n + 1], Etab[:, col, 1, :], ALU.mult, ALU.mult,
                              [R_psc[sk], R_tab], [R_Pb[pk]])
                    else:
                        for q_, (col, n) in enumerate(terms):
                            k.stt("dve", Pt[q_][:, :], sc_ap, apow[:, col, n:n + 1], Etab[:, col, 1, :], ALU.mult, ALU.mult,
                                  [R_psc[sk], R_tab], [R_Pt[q_]])
                        k.tt("dve", Pb[pk][:, :], Pt[0][:, :], Pt[1][:, :], ALU.add, [R_Pt[0], R_Pt[1]], [R_Pb[pk]])
                    if fin_ops:
                        fin_ops.pop(0)()

            DLOOK = 3
            pending_fin = []
            fin_ops = []
            fin_clock = [0]
            load_uTc(0)

            def flush_fin(now):
                while pending_fin and pending_fin[0][0] <= now:
                    _, lc_ = pending_fin.pop(0)
                    zb_ = zbs[lc_ % 2]
                    for ec in range(4):
                        k.tr(pt[:, ec * 128:(ec + 1) * 128], zb_[:, ec * 128:(ec + 1) * 128], identb[:, :], [R_zbs[lc_ % 2], R_tab], [R_pt])
                    k.copy("act", zTc[:, :, :], pt[:, :].rearrange("p (a b) -> p a b", b=128), [R_pt], [R_zTc])
                    k.dma("sp", io["zT_d"][h * 512:(h + 1) * 512, lc_ * 128:(lc_ + 1) * 128].rearrange("(c p) t -> p c t", p=128),
                          zTc[:, :, :], [R_zTc], [R_zTd[lc_]])

            gbase = nblk
            pv_list = []
            for gi_ in range(len(groups_) + DLOOK):
                if gi_ < len(groups_):
                    emit_scores(groups_[gi_], gbase + gi_)
                if gi_ - DLOOK >= 0:
                    for i_, tk in enumerate(groups_[gi_ - DLOOK]):
                        pv_list.append((tk, ((gbase + gi_ - DLOOK) * GB + i_) % NPB, gi_))
                while pv_list:
                    (lc, bi, nb, mc, terms), pk, idx = pv_list.pop(0)
                    fin_clock[0] = idx
                    flush_fin(idx)
                    k.mm(po[:, :], Pb[pk][:, :], vtm[:, mc, :], bi == 0, bi == nb - 1, [R_Pb[pk], R_v[mc]], [R_po])
                    if bi != nb - 1:
                        continue
                    while fin_ops:
                        fin_ops.pop(0)()
                    uTc = uTcs[lc % 2]
                    for kc in range(8):
                        k.mm(pg[:, :], uTc[:, kc, :], wg[:, kc, :], kc == 0, kc == 7, [R_w, R_uTcs[lc % 2]], [R_pg], signal=(kc == 7))
                    if lc + 1 < NCHK:
                        load_uTc(lc + 1)
                    k.copy("act", osb[:, :], po[:, :], [R_po], [R_osb])
                    k.act(sgt[:, :], pg[:, :], AF.Silu, [R_pg], [R_sgt])
                    zb_, Rzb_ = zbs[lc % 2], R_zbs[lc % 2]
                    fin_ops.extend([
                        lambda: P.op("dve", lambda e: e.reduce_sum(out=st[:, 0:1], in_=osb[:, :], axis=AX.X), [R_osb], [R_st]),
                        lambda: k.ts("dve", st[:, 0:1], st[:, 0:1], 1.0 / 512, None, ALU.mult, None, [R_st], [R_st]),
                        lambda: k.ts("dve", osb[:, :], osb[:, :], st[:, 0:1], None, ALU.subtract, None, [R_st, R_osb], [R_osb]),
                        lambda: k.tt("dve", sqj[:, :], osb[:, :], osb[:, :], ALU.mult, [R_osb], [R_sqj]),
                        lambda: P.op("dve", lambda e: e.reduce_sum(out=st[:, 1:2], in_=sqj[:, :], axis=AX.X), [R_sqj], [R_st]),
                        lambda: k.ts("dve", st[:, 1:2], st[:, 1:2], 1.0 / 512, EPS, ALU.mult, ALU.add, [R_st], [R_st]),
                        lambda: k.act(st[:, 1:2], st[:, 1:2], AF.Sqrt, [R_st], [R_st]),
                        lambda: None,
                        lambda: None,
                        lambda: P.op("dve", lambda e: e.reciprocal(out=st[:, 2:3], in_=st[:, 1:2]), [R_st], [R_st]),
                        (lambda zb_=zb_, Rzb_=Rzb_: k.stt("dve", zb_[:, :], osb[:, :], st[:, 2:3], sgt[:, :], ALU.mult, ALU.mult,
                                                          [R_osb, R_st, R_sgt], [Rzb_])),
                        (lambda lc=lc: pending_fin.append((fin_clock[0] + 3, lc))),
                    ])
            while fin_ops:
                fin_ops.pop(0)()
            flush_fin(10 ** 9)
            nblk += len(groups_) + DLOOK
    P.barrier()
    with ExitStack() as es:
        sb = lambda name, shape, dt: es.enter_context(nc.sbuf_tensor(k.uname(name), shape, dt))
        ps = lambda name, shape, dt=F32: es.enter_context(nc.psum_tensor(k.uname(name), shape, dt))
        resadd = mk_resadd(k, io, sb, li)
        pA = ps("r_pA2", [128, 512])
        pB = ps("r_pB2", [128, 512])
        R_pA, R_pB = R("pA2"), R("pB2")
        wo = sb("r_wo", [128, 16, D], BF16)
        zt = sb("r_zt", [128, 16, 512], BF16)
        R_wo, R_zt = R("r_wo"), R("r_zt")
        k.dma("pool", wo[:, :, :], io["ret_w_out"].rearrange("(c p) n -> p c n", p=128), [], [R_wo])
        for ti in range(len(TILES)):
            t0, N, isctx = TILES[ti]
            k.dma("sp", zt[:, :, 0:N], io["zT_d"][:, t0:t0 + N].rearrange("(c p) t -> p c t", p=128),
                  R_zTd[t0 // 128:(t0 + N) // 128], [R_zt])
            for j in range(8):
                pp, Rp = (pA, R_pA) if j % 2 == 0 else (pB, R_pB)
                for ec in range(16):
                    k.mm(pp[:, 0:N], wo[:, ec, j * 128:(j + 1) * 128], zt[:, ec, 0:N], ec == 0, ec == 15, [R_wo, R_zt], [Rp],
                         signal=(ec == 15))
                resadd(ti, j, 0, N, pp[:, 0:N], Rp)
    P.barrier()


MIXERS[1] = ret_mixer


TB = 64
NBLK = T // TB


def declare_s5_consts(nc, io):
    io["s5_rmask"] = nc.dram_tensor("s5_rmask", [128, 4], F32, kind="ExternalInput").ap()
    io["s5_emask"] = nc.dram_tensor("s5_emask", [128, 2], F32, kind="ExternalInput").ap()
    io["s5_cmask"] = nc.dram_tensor("s5_cmask", [128, 4 * 128], F32, kind="ExternalInput").ap()
    io["yf_d"] = nc.dram_tensor("yf_d", [2, D, T], F32).ap()
    io["s5p_Wt"] = nc.dram_tensor("s5p_Wt", [128, 2 * 32 * 2 * 128], BF16).ap()
    io["s5p_Ct"] = nc.dram_tensor("s5p_Ct", [128, 2 * 32 * 2 * 128], BF16).ap()
    io["s5p_Ctab"] = nc.dram_tensor("s5p_Ctab", [128, 2 * 32 * 64], F32).ap()
    io["s5p_Stab"] = nc.dram_tensor("s5p_Stab", [128, 2 * 32 * 64], F32).ap()
    io["s5p_rt"] = nc.dram_tensor("s5p_rt", [128, 2 * 32 * 64], F32).ap()
    io["s5p_w64"] = nc.dram_tensor("s5p_w64", [128, 128], F32).ap()


def s5_host_consts():
    r = np.arange(128)
    rmask = np.stack([(r // 32 == q4) for q4 in range(4)], axis=1).astype(np.float32)
    emask = np.stack([((r // 16) % 2 == e) for e in range(2)], axis=1).astype(np.float32)
    cm = np.zeros((128, 4, 128), np.float32)
    for q4 in range(4):
        cm[:, q4, 32 * q4:32 * q4 + 32] = 1.0
    return {"s5_rmask": rmask, "s5_emask": emask, "s5_cmask": cm.reshape(128, 512)}


def s5_prep_gen(k, io, sb, ps):
    nc, P = k.nc, k.P
    from contextlib import ExitStack
    R = lambda n: Res(n)
    PI = math.pi
    R_t = R("s5tab")
    pT = ps("s5_pT", [128, 512])
    R_pT = R("s5_pT")
    rmask = sb("s5_rmask", [128, 4], F32)
    emask = sb("s5_emask", [128, 2], F32)
    cmask = sb("s5_cmask", [128, 4, 128], F32)
    k.dma("sp", rmask[:, :], io["s5_rmask"], [], [R_t])
    k.dma("sp", emask[:, :], io["s5_emask"], [], [R_t])
    k.dma("sp", cmask[:, :, :], io["s5_cmask"].rearrange("p (a b) -> p a b", b=128), [], [R_t])
    lst = sb("s5_lst", [64, 2, 128], F32)
    k.dma("sp", lst[:, 0, :], io["s5_lam_re"].rearrange("d (q e) p -> (d q) (e p)", e=2), [], [R_t])
    k.dma("sp", lst[:, 1, :], io["s5_lam_im"].rearrange("d (q e) p -> (d q) (e p)", e=2), [], [R_t])
    lam = sb("s5_lam", [128, 2, 64], F32)
    for w in range(2):
        k.tr(pT[:, 0:64], lst[:, w, :], k.ident[0:64, 0:64], [R_t, k.R_const], [R_pT])
        k.copy("dve", lam[:, w, :], pT[:, 0:64], [R_pT], [R_t])
    ones1 = sb("s5_ones1", [1, 128], F32)
    ldt1 = sb("s5_ldt1", [1, 128], F32)
    k.memset("dve", ones1[:, :], 1.0, [R_t])
    k.dma("sp", ldt1[:, :], io["s5_log_dt"].rearrange("(o d) g -> o (d g)", o=1), [], [R_t])
    k.mm(pT[:, 0:128], ones1[0:1, :], ldt1[0:1, :], True, True, [R_t], [R_pT])
    dt = sb("s5_dt", [128, 64], F32)
    bc = pT[:, 0:128].rearrange("p (d q e) -> p d q e", d=2, e=2)
    dt3 = dt[:, :].rearrange("p (d q) -> p d q", d=2)
    k.act(dt3[0:64], bc[0:64, :, :, 0], AF.Exp, [R_pT], [R_t])
    k.act(dt3[64:128], bc[64:128, :, :, 1], AF.Exp, [R_pT], [R_t])
    sm = lambda n: sb("s5_" + n, [128, 64], F32)
    mag, ang, ar, ai, tmpa, tmpb, sg_, den, cfr, cfi = [sm(n) for n in
                                                         "mag ang ar ai tmpa tmpb sg den cfr cfi".split()]
    k.tt("dve", mag[:, :], lam[:, 0, :], dt[:, :], ALU.mult, [R_t], [R_t])
    k.act(mag[:, :], mag[:, :], AF.Exp, [R_t], [R_t])
    k.tt("dve", ang[:, :], lam[:, 1, :], dt[:, :], ALU.mult, [R_t], [R_t])

    def sin_of(dst, shift):
        k.ts("dve", tmpa[:, :], ang[:, :], shift - 4 * PI, None, ALU.add, None, [R_t], [R_t])
        for thr in (PI, 3 * PI, 5 * PI, 7 * PI):
            k.ts("dve", tmpb[:, :], ang[:, :], shift - thr, None, ALU.add, None, [R_t], [R_t])
            k.act(sg_[:, :], tmpb[:, :], AF.Sign, [R_t], [R_t])
            k.stt("dve", tmpa[:, :], sg_[:, :], -PI, tmpa[:, :], ALU.mult, ALU.add, [R_t], [R_t])
        k.act(dst, tmpa[:, :], AF.Sin, [R_t], [R_t])

    sin_of(ai[:, :], 0.0)
    sin_of(ar[:, :], PI / 2)
    k.tt("dve", ar[:, :], ar[:, :], mag[:, :], ALU.mult, [R_t], [R_t])
    k.tt("dve", ai[:, :], ai[:, :], mag[:, :], ALU.mult, [R_t], [R_t])
    k.tt("dve", den[:, :], lam[:, 0, :], lam[:, 0, :], ALU.mult, [R_t], [R_t])
    k.tt("dve", tmpa[:, :], lam[:, 1, :], lam[:, 1, :], ALU.mult, [R_t], [R_t])
    k.tt("dve", den[:, :], den[:, :], tmpa[:, :], ALU.add, [R_t], [R_t])
    P.op("dve", lambda e: e.reciprocal(out=den[:, :], in_=den[:, :]), [R_t], [R_t])
    k.ts("dve", tmpb[:, :], ar[:, :], -1.0, None, ALU.add, None, [R_t], [R_t])
    k.tt("dve", cfr[:, :], tmpb[:, :], lam[:, 0, :], ALU.mult, [R_t], [R_t])
    k.tt("dve", tmpa[:, :], ai[:, :], lam[:, 1, :], ALU.mult, [R_t], [R_t])
    k.tt("dve", cfr[:, :], cfr[:, :], tmpa[:, :], ALU.add, [R_t], [R_t])
    k.tt("dve", cfr[:, :], cfr[:, :], den[:, :], ALU.mult, [R_t], [R_t])
    k.tt("dve", cfi[:, :], ai[:, :], lam[:, 0, :], ALU.mult, [R_t], [R_t])
    k.tt("dve", tmpa[:, :], tmpb[:, :], lam[:, 1, :], ALU.mult, [R_t], [R_t])
    k.tt("dve", cfi[:, :], cfi[:, :], tmpa[:, :], ALU.subtract, [R_t], [R_t])
    k.tt("dve", cfi[:, :], cfi[:, :], den[:, :], ALU.mult, [R_t], [R_t])
    Wt = sb("s5_Wt", [128, 2, 32, 2, 128], BF16)
    Ct = sb("s5_Ct", [128, 2, 32, 2, 128], BF16)
    with ExitStack() as es2:
        sb2 = lambda name, shape, dt_: es2.enter_context(nc.sbuf_tensor(k.uname(name), shape, dt_))
        Bn = sb2("s5_Bn", [128, 2, 2, 32, 16], F32)
        Bb = sb2("s5_Bb", [128, 2, 2, 32, 16], F32)
        for w, nm in enumerate(("s5_b_re", "s5_b_im")):
            for d in range(2):
                k.dma("sp", Bn[:, w, d, :, :], io[nm][d].rearrange("(q e) p c -> (e p) q c", e=2), [], [R_t])
        t16 = sb2("s5_t16", [128, 64], F32)
        cf3r, cf3i = cfr[:, :], cfi[:, :]
        for ci in range(16):
            yield
            bre = Bn[:, 0, :, :, ci].rearrange("p d q -> p (d q)")
            bim = Bn[:, 1, :, :, ci].rearrange("p d q -> p (d q)")
            ore = Bb[:, 0, :, :, ci].rearrange("p d q -> p (d q)")
            oim = Bb[:, 1, :, :, ci].rearrange("p d q -> p (d q)")
            k.tt("dve", ore, cf3r, bre, ALU.mult, [R_t], [R_t])
            k.tt("dve", t16[:, :], cf3i, bim, ALU.mult, [R_t], [R_t])
            k.tt("dve", ore, ore, t16[:, :], ALU.subtract, [R_t], [R_t])
            k.tt("dve", oim, cf3r, bim, ALU.mult, [R_t], [R_t])
            k.tt("dve", t16[:, :], cf3i, bre, ALU.mult, [R_t], [R_t])
            k.tt("dve", oim, oim, t16[:, :], ALU.add, [R_t], [R_t])
        Nn = sb2("s5_Nn", [128, 4, 2, 16], F32)
        k.memset("dve", Nn[:, :, :, :], 0.0, [R_t])
        for d in range(2):
            for w in range(2):
                for j in range(8):
                    yield
                    k.copy("dve", Nn[0:64, :, 0, :], Bb[0:64, w, d, 4 * j:4 * j + 4, :], [R_t, R_pT], [R_t])
                    k.copy("dve", Nn[64:128, :, 1, :], Bb[64:128, w, d, 4 * j:4 * j + 4, :], [R_t], [R_t])
                    k.tr(pT[:, 0:128], Nn[:, :, :, :].rearrange("p a b c -> p (a b c)"), k.ident[:, :], [R_t, k.R_const], [R_pT])
                    for q4 in range(4):
                        k.ts("dve", Wt[:, d, 4 * j + q4, w, :], pT[:, 0:128], rmask[:, q4:q4 + 1], None, ALU.mult, None,
                             [R_pT, R_t], [R_t])
        Cn = sb2("s5_Cn", [128, 2, 2, 8, 64], F32)
        for w, nm in enumerate(("s5_c_re", "s5_c_im")):
            for d in range(2):
                k.dma("sp", Cn[:, w, d, :, :], io[nm][d].rearrange("(j g8) co p -> (g8 co) j p", g8=8), [], [R_t])
        Cexp = sb2("s5_Cexp", [128, 128], F32)
        for d in range(2):
            for w in range(2):
                for j in range(8):
                    yield
                    for e in range(2):
                        k.ts("dve", Cexp[:, e * 64:(e + 1) * 64], Cn[:, w, d, j, :], emask[:, e:e + 1], None, ALU.mult, None,
                             [R_t, R_pT], [R_t])
                    k.tr(pT[:, 0:128], Cexp[:, :], k.ident[:, :], [R_t, k.R_const], [R_pT])
                    for q4 in range(4):
                        k.stt("dve", Ct[:, d, 4 * j + q4, w, :], pT[:, 0:128], (1.0 if w == 0 else -1.0), cmask[:, q4, :],
                              ALU.mult, ALU.mult, [R_pT, R_t], [R_t])
    Ctab = sb("s5_Ctab", [128, 2, 32, TB], F32)
    Stab = sb("s5_Stab", [128, 2, 32, TB], F32)
    rt = sb("s5_rt", [128, 2, 32, TB], F32)
    w64 = sb("s5_w64", [128, 2, 64], F32)
    cs1, sn1, er, ei_, e2r, e2i = [sm(n) for n in "cs1 sn1 er ei e2r e2i".split()]
    P.op("dve", lambda e: e.reciprocal(out=tmpa[:, :], in_=mag[:, :]), [R_t], [R_t])
    k.tt("dve", cs1[:, :], ar[:, :], tmpa[:, :], ALU.mult, [R_t], [R_t])
    k.tt("dve", sn1[:, :], ai[:, :], tmpa[:, :], ALU.mult, [R_t], [R_t])
    k.memset("dve", er[:, :], 1.0, [R_t])
    k.memset("dve", ei_[:, :], 0.0, [R_t])
    dq = lambda t2d: t2d.rearrange("p (d q) -> p d q", d=2)
    for tp in range(TB):
        yield
        for d in range(2):
            col = tp if d == 0 else TB - 1 - tp
            k.copy("pool", Ctab[:, d, :, col], dq(er[:, :])[:, d, :], [R_t], [R_t])
            k.copy("pool", Stab[:, d, :, col], dq(ei_[:, :])[:, d, :], [R_t], [R_t])
            if tp == 0:
                k.memset("pool", rt[:, d, :, col], 0.0, [R_t])
            else:
                k.copy("pool", rt[:, d, :, col], dq(mag[:, :])[:, d, :], [R_t], [R_t])
        k.tt("dve", e2r[:, :], er[:, :], cs1[:, :], ALU.mult, [R_t], [R_t])
        k.tt("dve", tmpa[:, :], ei_[:, :], sn1[:, :], ALU.mult, [R_t], [R_t])
        k.tt("dve", e2r[:, :], e2r[:, :], tmpa[:, :], ALU.subtract, [R_t], [R_t])
        k.tt("dve", e2i[:, :], er[:, :], sn1[:, :], ALU.mult, [R_t], [R_t])
        k.tt("dve", tmpa[:, :], ei_[:, :], cs1[:, :], ALU.mult, [R_t], [R_t])
        k.tt("dve", ei_[:, :], e2i[:, :], tmpa[:, :], ALU.add, [R_t], [R_t])
        k.copy("dve", er[:, :], e2r[:, :], [R_t], [R_t])
    k.tt("dve", w64[:, 0, :], er[:, :], mag[:, :], ALU.mult, [R_t], [R_t])
    k.tt("dve", w64[:, 1, :], ei_[:, :], mag[:, :], ALU.mult, [R_t], [R_t])
    yield
    R_o = R("s5p_out")
    k.dma("sp", io["s5p_Wt"], Wt[:, :, :, :, :].rearrange("p a b c d -> p (a b c d)"), [R_t], [R_o])
    k.dma("sp", io["s5p_Ct"], Ct[:, :, :, :, :].rearrange("p a b c d -> p (a b c d)"), [R_t], [R_o])
    k.dma("sp", io["s5p_Ctab"], Ctab[:, :, :, :].rearrange("p a b c -> p (a b c)"), [R_t], [R_o])
    k.dma("sp", io["s5p_Stab"], Stab[:, :, :, :].rearrange("p a b c -> p (a b c)"), [R_t], [R_o])
    k.dma("sp", io["s5p_rt"], rt[:, :, :, :].rearrange("p a b c -> p (a b c)"), [R_t], [R_o])
    k.dma("sp", io["s5p_w64"], w64[:, :, :].rearrange("p a b -> p (a b)"), [R_t], [R_o])
    yield

def s5_mixer(k, io, li):
    nc, P = k.nc, k.P
    from contextlib import ExitStack
    R = lambda n: Res(n)
    PI = math.pi
    with ExitStack() as es:
        sb = lambda name, shape, dt: es.enter_context(nc.sbuf_tensor(k.uname(name), shape, dt))
        ps = lambda name, shape, dt=F32: es.enter_context(nc.psum_tensor(k.uname(name), shape, dt))
        R_t = R("s5tab")
        R_uTd = R("s5_uTd")
        with ExitStack() as es0:
            sb0 = lambda name, shape, dt: es0.enter_context(nc.sbuf_tensor(k.uname(name), shape, dt))
            ps0 = lambda name, shape, dt=F32: es0.enter_context(nc.psum_tensor(k.uname(name), shape, dt))
            modnorm = mk_modnorm(k, io, sb0, ps0)
            uT = sb0("s5_uT", [128, 8, 512], BF16)
            R_uT = R("s5_uT")
            for ti in range(len(TILES)):
                t0, N, _ = TILES[ti]
                modnorm(ti, li, 1, lambda j: uT[:, j, 0:N], R_uT)
                k.dma("sp", io["uT_d"][:, t0:t0 + N].rearrange("(c p) t -> p c t", p=128), uT[:, :, 0:N], [R_uT], [R_uTd])
        P.barrier()
        Wt = sb("s5_Wt", [128, 2, 32, 2, 128], BF16)
        Ct = sb("s5_Ct", [128, 2, 32, 2, 128], BF16)
        Ctab = sb("s5_Ctab", [128, 2, 32, TB], F32)
        Stab = sb("s5_Stab", [128, 2, 32, TB], F32)
        rt = sb("s5_rt", [128, 2, 32, TB], F32)
        w64 = sb("s5_w64", [128, 2, 64], F32)
        k.dma("sp", Wt[:, :, :, :, :].rearrange("p a b c d -> p (a b c d)"), io["s5p_Wt"], [], [R_t])
        k.dma("act", Ct[:, :, :, :, :].rearrange("p a b c d -> p (a b c d)"), io["s5p_Ct"], [], [R_t])
        k.dma("sp", Ctab[:, :, :, :].rearrange("p a b c -> p (a b c)"), io["s5p_Ctab"], [], [R_t])
        k.dma("act", Stab[:, :, :, :].rearrange("p a b c -> p (a b c)"), io["s5p_Stab"], [], [R_t])
        k.dma("sp", rt[:, :, :, :].rearrange("p a b c -> p (a b c)"), io["s5p_rt"], [], [R_t])
        k.dma("sp", w64[:, :, :].rearrange("p a b -> p (a b)"), io["s5p_w64"], [], [R_t])
        dq = lambda t2d: t2d.rearrange("p (d q) -> p d q", d=2)
        fl = lambda t3: t3.rearrange("p q t -> p (q t)")
        fl4 = lambda t4: t4.rearrange("p d q t -> p (d q t)")
        ub = [sb("s5_ub%d" % d, [128, 8, TB], BF16) for d in range(2)]
        Vr = sb("s5_Vr", [128, 2, 32, TB], F32)
        Vi = sb("s5_Vi", [128, 2, 32, TB], F32)
        T1 = sb("s5_T1", [128, 2, 32, TB], F32)
        T2 = sb("s5_T2", [128, 2, 32, TB], F32)
        Xb = sb("s5_Xb", [128, 2, 32, TB], BF16)
        zc = sb("s5_zc", [128, 2, 2, 32], F32)
        cw = sb("s5_cw", [128, 4, 2, 32], F32)
        ysb = sb("s5_ysb", [128, 8, TB], F32)
        pv = [[ps("s5_pv%d_%d" % (d, i), [128, 512]) for i in range(2)] for d in range(2)]
        pyy = [ps("s5_py%d" % d, [128, 512]) for d in range(2)]
        R_ub = [R("ub0"), R("ub1")]
        R_Xb, R_zc, R_cw, R_ys = [R(n) for n in "Xb zc cw ys".split()]
        R_Vr, R_Vi, R_T1, R_T2 = [[R(n + str(d)) for d in range(2)] for n in ("Vr", "Vi", "T1", "T2")]
        R_pv = [[R("pv%d%d" % (d, i)) for i in range(2)] for d in range(2)]
        R_pyy = [R("pyy0"), R("pyy1")]
        R_yd = [R("yd0"), R("yd1")]
        k.memset("dve", zc[:, :, :, :], 0.0, [R_zc])
        k.memset("dve", cw[:, :, :, :], 0.0, [R_cw])
        order = [list(range(NBLK)), [3, 2, 1, 0] + list(range(NBLK - 1, 3, -1))]
        C4, S4 = fl4(Ctab[:, :, :, :]), fl4(Stab[:, :, :, :])
        vr, vi, t1, t2 = fl4(Vr[:, :, :, :]), fl4(Vi[:, :, :, :]), fl4(T1[:, :, :, :]), fl4(T2[:, :, :, :])
        w4 = w64[:, :, :].rearrange("p r (d q) -> p r d q", d=2)
        FIRST = (0, TB - 1)
        LAST = (TB - 1, 0)
        nev = [0, 0]

        def emit_V(it, d):
            c0 = order[d][it] * TB
            k.dma("sp", ub[d][:, :, :], io["uT_d"][:, c0:c0 + TB].rearrange("(c p) t -> p c t", p=128), [R_uTd], [R_ub[d]])
            for w, (Vd, Rv) in enumerate(((Vr, R_Vr[d]), (Vi, R_Vi[d]))):
                for q8 in range(4):
                    bk = nev[d] % 2
                    nev[d] += 1
                    for qq in range(8):
                        q = q8 * 8 + qq
                        k.mm(pv[d][bk][:, qq * TB:(qq + 1) * TB], Wt[:, d, q, w, :], ub[d][:, q // 4, :], True, True,
                             [R_t, R_ub[d]], [R_pv[d][bk]], signal=(qq == 7))
                    k.copy("act", Vd[:, d, q8 * 8:(q8 + 1) * 8, :],
                           pv[d][bk][:, :].rearrange("p (a b) -> p a b", b=TB), [R_pv[d][bk]], [Rv])

        for d in range(2):
            emit_V(0, d)
        for it in range(NBLK):
            for d in range(2):
                Cd, Sd = Ctab[:, d, :, :], Stab[:, d, :, :]
                k.tt("dve", T1[:, d, :, :], Vr[:, d, :, :], Cd, ALU.mult, [R_Vr[d], R_t], [R_T1[d]])
                k.tt("dve", T2[:, d, :, :], Vi[:, d, :, :], Sd, ALU.mult, [R_Vi[d], R_t], [R_T2[d]])
                k.tt("dve", Vr[:, d, :, :], Vr[:, d, :, :], Sd, ALU.mult, [R_Vr[d], R_t], [R_Vr[d]])
                k.tt("dve", Vi[:, d, :, :], Vi[:, d, :, :], Cd, ALU.mult, [R_Vi[d], R_t], [R_Vi[d]])
                k.tt("dve", T1[:, d, :, :], T1[:, d, :, :], T2[:, d, :, :], ALU.add, [R_T1[d], R_T2[d]], [R_T1[d]])
                k.tt("dve", Vi[:, d, :, :], Vi[:, d, :, :], Vr[:, d, :, :], ALU.subtract, [R_Vi[d], R_Vr[d]], [R_Vi[d]])
            for d in range(2):
                k.tt("dve", T1[:, d, :, FIRST[d]], T1[:, d, :, FIRST[d]], cw[:, 0, d, :], ALU.add, [R_cw, R_T1[d]], [R_T1[d]])
                k.tt("dve", Vi[:, d, :, FIRST[d]], Vi[:, d, :, FIRST[d]], cw[:, 2, d, :], ALU.add, [R_cw, R_Vi[d]], [R_Vi[d]])
            for d in range(2):
                rv = (lambda a_: a_) if d == 0 else (lambda a_: a_[:, ::-1])
                rt2 = fl(rt[:, d, :, :])
                for (o_, i_, Ro, Ri) in ((T2, T1, R_T2[d], R_T1[d]), (Vr, Vi, R_Vr[d], R_Vi[d])):
                    P.op("dve", (lambda o, a0, a1: (lambda e: e.tensor_tensor_scan(out=o, data0=a0, data1=a1, initial=0.0,
                                                                                     op0=ALU.mult, op1=ALU.add)))(
                        rv(fl(o_[:, d, :, :])), rv(rt2), rv(fl(i_[:, d, :, :]))), [Ri, R_t], [Ro])
            for d in range(2):
                k.copy("pool", zc[:, 0, d, :], T2[:, d, :, LAST[d]], [R_T2[d]], [R_zc])
                k.copy("pool", zc[:, 1, d, :], Vr[:, d, :, LAST[d]], [R_Vr[d]], [R_zc])
            k.tt("pool", cw[:, 0, :, :], w4[:, 0, :, :], zc[:, 0, :, :], ALU.mult, [R_zc, R_t], [R_cw])
            k.tt("pool", cw[:, 1, :, :], w4[:, 1, :, :], zc[:, 1, :, :], ALU.mult, [R_zc, R_t], [R_cw])
            k.tt("pool", cw[:, 2, :, :], w4[:, 1, :, :], zc[:, 0, :, :], ALU.mult, [R_zc, R_t], [R_cw])
            k.tt("pool", cw[:, 3, :, :], w4[:, 0, :, :], zc[:, 1, :, :], ALU.mult, [R_zc, R_t], [R_cw])
            k.tt("pool", cw[:, 0, :, :], cw[:, 0, :, :], cw[:, 1, :, :], ALU.subtract, [R_cw], [R_cw])
            k.tt("pool", cw[:, 2, :, :], cw[:, 2, :, :], cw[:, 3, :, :], ALU.add, [R_cw], [R_cw])
            k.tt("dve", t1, t2, C4, ALU.mult, R_T2 + [R_t], R_T1)
            k.tt("dve", vi, vr, S4, ALU.mult, R_Vr + [R_t], R_Vi)
            k.tt("dve", t2, t2, S4, ALU.mult, R_T2 + [R_t], R_T2)
            k.tt("dve", vr, vr, C4, ALU.mult, R_Vr + [R_t], R_Vr)
            for d in range(2):
                c0 = order[d][it] * TB
                k.tt("dve", Xb[:, 0, :, :], T1[:, d, :, :], Vi[:, d, :, :], ALU.subtract, [R_T1[d], R_Vi[d]], [R_Xb])
                k.tt("dve", Xb[:, 1, :, :], Vr[:, d, :, :], T2[:, d, :, :], ALU.add, [R_Vr[d], R_T2[d]], [R_Xb])
                for j in range(8):
                    n_ = 0
                    for q4 in range(4):
                        for w in range(2):
                            k.mm(pyy[d][:, j * TB:(j + 1) * TB], Ct[:, d, 4 * j + q4, w, :], Xb[:, w, 4 * j + q4, :], n_ == 0, n_ == 7,
                                 [R_t, R_Xb], [R_pyy[d]], signal=(n_ == 7))
                            n_ += 1
                if it + 1 < NBLK:
                    emit_V(it + 1, d)
                k.copy("act", ysb[:, :, :], pyy[d][:, :].rearrange("p (a b) -> p a b", b=TB), [R_pyy[d]], [R_ys])
                k.dma("sp", io["yf_d"][d, :, c0:c0 + TB].rearrange("(c p) t -> p c t", p=128), ysb[:, :, :], [R_ys], [R_yd[d]])
    P.barrier()
    with ExitStack() as es:
        sb = lambda name, shape, dt: es.enter_context(nc.sbuf_tensor(k.uname(name), shape, dt))
        ps = lambda name, shape, dt=F32: es.enter_context(nc.psum_tensor(k.uname(name), shape, dt))
        resadd = mk_resadd(k, io, sb, li)
        wgl = sb("s5_wgl", [128, 8, 2 * D], BF16)
        R_wgl = R("wgl")
        for hh in range(4):
            k.dma("pool", wgl[:, :, hh * 512:(hh + 1) * 512],
                  io["s5_w_glu"].rearrange("(c p) n -> p c n", p=128)[:, :, hh * 512:(hh + 1) * 512], [], [R_wgl])
        yf = sb("s5_yf", [128, 8, 512], F32)
        yb = sb("s5_yb", [128, 8, 512], F32)
        uu = sb("s5_uu", [128, 8, 512], BF16)
        ge = sb("s5_ge", [128, 8, 512], BF16)
        w1 = sb("s5_w1", [128, 512], F32)
        w2 = sb("s5_w2", [128, 512], F32)
        sig = sb("s5_sig", [128, 512], F32)
        oo = sb("s5_oo", [128, 512], F32)
        R_yf, R_yb, R_uu, R_ge, R_w1, R_w2, R_sig, R_oo = [R(n) for n in "yf yb uu ge w1 w2 sig oo".split()]
        pa = ps("s5_pa", [128, 512])
        pb = ps("s5_pb", [128, 512])
        R_pa, R_pb = R("s5pa"), R("s5pb")
        for ti in range(len(TILES)):
            t0, N, isctx = TILES[ti]
            k.dma("sp", yf[:, :, 0:N], io["yf_d"][0, :, t0:t0 + N].rearrange("(c p) t -> p c t", p=128), [], [R_yf])
            k.dma("sp", yb[:, :, 0:N], io["yf_d"][1, :, t0:t0 + N].rearrange("(c p) t -> p c t", p=128), [], [R_yb])
            k.dma("sp", uu[:, :, 0:N], io["uT_d"][:, t0:t0 + N].rearrange("(c p) t -> p c t", p=128), [], [R_uu])
            for j in range(8):
                k.tt("dve", w1[:, 0:N], yf[:, j, 0:N], yb[:, j, 0:N], ALU.add, [R_yf, R_yb], [R_w1])
                k.stt("dve", w1[:, 0:N], uu[:, j, 0:N], k.vecT[:, VC_S5D + j:VC_S5D + j + 1], w1[:, 0:N], ALU.mult, ALU.add,
                      [R_uu, R_w1, k.R_vec], [R_w1])
                k.tt("dve", w2[:, 0:N], w1[:, 0:N], w1[:, 0:N], ALU.mult, [R_w1], [R_w2])
                k.ts("dve", w2[:, 0:N], w2[:, 0:N], 0.044715, 1.0, ALU.mult, ALU.add, [R_w2], [R_w2])
                k.tt("dve", w2[:, 0:N], w2[:, 0:N], w1[:, 0:N], ALU.mult, [R_w2, R_w1], [R_w2])
                k.act(w2[:, 0:N], w2[:, 0:N], AF.Sigmoid, [R_w2], [R_w2], scale=1.5957691216057308)
                k.tt("dve", ge[:, j, 0:N], w2[:, 0:N], w1[:, 0:N], ALU.mult, [R_w2, R_w1], [R_ge])
            for oc in range(8):
                for kc in range(8):
                    k.mm(pa[:, 0:N], wgl[:, kc, oc * 128:(oc + 1) * 128], ge[:, kc, 0:N], kc == 0, kc == 7, [R_wgl, R_ge], [R_pa],
                         signal=(kc == 7))
                for kc in range(8):
                    k.mm(pb[:, 0:N], wgl[:, kc, D + oc * 128:D + (oc + 1) * 128], ge[:, kc, 0:N], kc == 0, kc == 7, [R_wgl, R_ge],
                         [R_pb], signal=(kc == 7))
                k.act(sig[:, 0:N], pb[:, 0:N], AF.Sigmoid, [R_pb], [R_sig])
                k.tt("dve", oo[:, 0:N], pa[:, 0:N], sig[:, 0:N], ALU.mult, [R_pa, R_sig], [R_oo])
                resadd(ti, oc, 0, N, oo[:, 0:N], R_oo)
    P.barrier()


MIXERS[2] = s5_mixer
```
